# Optimizing a Trainium2 kernel written in Bass

```python
import math
import jax, jax.numpy as jnp
from jax import lax
import numpy as np

D_MODEL = 1024
BATCH = 8
SEQ = 4096
DEPTH = 2

RG_WIDTH = D_MODEL
RG_BLOCKS = 8
RG_CONV = 4
RG_C = 8.0
SG_WIDTH = D_MODEL
SG_GROUPS = 8
SG_CHUNK = 128
ATT_HEADS = 8
HEAD_DIM = D_MODEL // ATT_HEADS
ATT_WIDTH = ATT_HEADS * HEAD_DIM
MOBA_BLOCK = 256
MOBA_TOPK = 3
Q_CHUNK = 16
D_FF = 3 * D_MODEL
FFN_CONV = 3
LN_EPS = 1e-5
DEEPNORM_ALPHA = (2 * DEPTH) ** 0.25
DEEPNORM_BETA = (8 * DEPTH) ** -0.25
NEG = -1e30

IN_WIDTHS = (RG_WIDTH, RG_WIDTH, SG_WIDTH, SG_WIDTH, ATT_WIDTH, ATT_WIDTH, ATT_WIDTH,
             D_MODEL, D_MODEL, D_MODEL)
D_IN = sum(IN_WIDTHS)
IN_SPLITS = tuple(sum(IN_WIDTHS[:i]) for i in range(1, len(IN_WIDTHS)))

kernel_name = "hybrid_rglru_gmlp_moba_deepnorm"


def layer_norm(x, g, b):
    xf = x.astype(jnp.float32)
    mu = xf.mean(-1, keepdims=True)
    var = jnp.square(xf - mu).mean(-1, keepdims=True)
    y = (xf - mu) * lax.rsqrt(var + LN_EPS)
    return (y * g + b).astype(x.dtype)


def causal_dwconv(x, w, b):
    K, C = w.shape
    y = lax.conv_general_dilated(
        x, w[:, None, :].astype(x.dtype), window_strides=(1,), padding=[(K - 1, 0)],
        dimension_numbers=("NWC", "WIO", "NWC"), feature_group_count=C)
    return y + b


def _lin_rec_combine(left, right):
    a_l, b_l = left
    a_r, b_r = right
    return a_r * a_l, a_r * b_l + b_r


def rg_lru(x, w_r, b_r, w_i, b_i, lam):
    B, S, C = x.shape
    xb = x.reshape(B, S, RG_BLOCKS, C // RG_BLOCKS)
    r = jax.nn.sigmoid(jnp.einsum("bsgi,gij->bsgj", xb, w_r).reshape(B, S, C) + b_r)
    i = jax.nn.sigmoid(jnp.einsum("bsgi,gij->bsgj", xb, w_i).reshape(B, S, C) + b_i)
    log_a = -RG_C * r.astype(jnp.float32) * jax.nn.softplus(-lam.astype(jnp.float32))
    a = jnp.exp(log_a)
    u = jnp.sqrt(-jnp.expm1(2.0 * log_a)) * (i * x).astype(jnp.float32)
    _, h = lax.associative_scan(_lin_rec_combine, (a, u), axis=1)
    return h.astype(x.dtype)


def spatial_gating(u, v, ln_g, ln_b, w_s, b_s):
    B, S, C = v.shape
    v = layer_norm(v, ln_g, ln_b)
    vr = v.reshape(B, S // SG_CHUNK, SG_CHUNK, SG_GROUPS, C // SG_GROUPS)
    causal = jnp.tril(jnp.ones((SG_CHUNK, SG_CHUNK), dtype=bool))
    w = jnp.where(causal[None], w_s, 0)
    mixed = jnp.einsum("gts,bnsgc->bntgc", w, vr) + b_s.T[None, None, :, :, None]
    return u * mixed.reshape(B, S, C)


def moba_attention(q, k, v):
    B, S, H, D = q.shape
    nb = -(-S // MOBA_BLOCK)
    n_sel = min(MOBA_TOPK, nb)
    pad = nb * MOBA_BLOCK - S
    kp = jnp.pad(k, ((0, 0), (0, pad), (0, 0), (0, 0)))
    vp = jnp.pad(v, ((0, 0), (0, pad), (0, 0), (0, 0)))
    kb = kp.reshape(B, nb, MOBA_BLOCK, H, D).transpose(0, 3, 1, 2, 4)
    vb = vp.reshape(B, nb, MOBA_BLOCK, H, D).transpose(0, 3, 1, 2, 4)
    k_mean = kb.mean(axis=3)
    scale = D ** -0.5
    n_chunks = S // Q_CHUNK
    qc = q.reshape(B, n_chunks, Q_CHUNK, H, D).transpose(1, 0, 3, 2, 4)
    b_ix = jnp.arange(B)[:, None, None, None]
    h_ix = jnp.arange(H)[None, :, None, None]
    blk_ids = jnp.arange(nb)
    LK = n_sel * MOBA_BLOCK

    def chunk(args):
        c, qb = args
        q0 = c * Q_CHUNK
        own = q0 // MOBA_BLOCK
        gate = jnp.einsum("bhqd,bhjd->bhqj", qb, k_mean).astype(jnp.float32)
        gate = jnp.where(blk_ids < own, gate, -jnp.inf)
        _, sel = lax.top_k(gate, n_sel)
        sel_valid = sel < own
        k_sel = kb[b_ix, h_ix, sel]
        v_sel = vb[b_ix, h_ix, sel]
        s_sel = jnp.einsum("bhqd,bhqmld->bhqml", qb, k_sel).astype(jnp.float32) * scale
        s_sel = jnp.where(sel_valid[..., None], s_sel, NEG)
        k_own = lax.dynamic_index_in_dim(kb, own, axis=2, keepdims=False)
        v_own = lax.dynamic_index_in_dim(vb, own, axis=2, keepdims=False)
        s_own = jnp.einsum("bhqd,bhld->bhql", qb, k_own).astype(jnp.float32) * scale
        q_pos = q0 + jnp.arange(Q_CHUNK)
        k_pos = own * MOBA_BLOCK + jnp.arange(MOBA_BLOCK)
        s_own = jnp.where(k_pos[None, :] <= q_pos[:, None], s_own, NEG)
        logits = jnp.concatenate([s_sel.reshape(B, H, Q_CHUNK, LK), s_own], axis=-1)
        p = jax.nn.softmax(logits, axis=-1).astype(qb.dtype)
        p_sel = p[..., :LK].reshape(B, H, Q_CHUNK, n_sel, MOBA_BLOCK)
        p_own = p[..., LK:]
        return (jnp.einsum("bhqml,bhqmld->bhqd", p_sel, v_sel)
                + jnp.einsum("bhql,bhld->bhqd", p_own, v_own))

    out = lax.map(chunk, (jnp.arange(n_chunks), qc))
    return out.transpose(1, 0, 3, 2, 4).reshape(B, S, H * D)


def hybrid_mixer(x, w_in, conv_w, conv_b, w_r, b_r, w_i, b_i, lam,
                 sg_g, sg_b, w_s, b_s, w_out):
    B, S, _ = x.shape
    proj = x @ w_in
    a_x, a_gate, s_u, s_v, q, k, v, g_a, g_s, g_m = jnp.split(proj, IN_SPLITS, axis=-1)
    y_a = rg_lru(causal_dwconv(a_x, conv_w, conv_b), w_r, b_r, w_i, b_i, lam) * jax.nn.gelu(a_gate)
    y_s = spatial_gating(jax.nn.gelu(s_u), jax.nn.gelu(s_v), sg_g, sg_b, w_s, b_s)
    hs = (B, S, ATT_HEADS, HEAD_DIM)
    y_m = moba_attention(q.reshape(hs), k.reshape(hs), v.reshape(hs))
    merged = (jax.nn.sigmoid(g_a) * y_a + jax.nn.sigmoid(g_s) * y_s
              + jax.nn.sigmoid(g_m) * y_m)
    return merged @ w_out


def conv_ffn(x, w_up, conv_w, conv_b, w_down):
    h = x @ w_up
    h_gate, h_up = h[..., :D_FF], h[..., D_FF:]
    h_gate = causal_dwconv(h_gate, conv_w, conv_b)
    return (jax.nn.gelu(h_gate) * h_up) @ w_down


def setup_inputs(seed: int = 0) -> dict:
    key = jax.random.key(seed)
    ks = jax.random.split(key, 24)
    L = DEPTH

    def nrm(k, shape, scale):
        return jax.random.normal(k, shape, jnp.float32) * scale

    bw = RG_WIDTH // RG_BLOCKS
    a_c = jax.random.uniform(ks[8], (L, RG_WIDTH), jnp.float32, minval=0.9, maxval=0.999)
    s = a_c ** (1.0 / RG_C)
    return {
        "x": nrm(ks[0], (BATCH, SEQ, D_MODEL), 1.0),
        "w_in": nrm(ks[1], (L, D_MODEL, D_IN), D_MODEL ** -0.5),
        "conv_rg_w": nrm(ks[2], (L, RG_CONV, RG_WIDTH), RG_CONV ** -0.5),
        "conv_rg_b": nrm(ks[3], (L, RG_WIDTH), 0.01),
        "w_rgate": nrm(ks[4], (L, RG_BLOCKS, bw, bw), bw ** -0.5),
        "b_rgate": nrm(ks[5], (L, RG_WIDTH), 0.01),
        "w_igate": nrm(ks[6], (L, RG_BLOCKS, bw, bw), bw ** -0.5),
        "b_igate": nrm(ks[7], (L, RG_WIDTH), 0.01),
        "lru_lambda": jnp.log(s) - jnp.log1p(-s),
        "sgu_ln_g": 1.0 + nrm(ks[9], (L, SG_WIDTH), 0.01),
        "sgu_ln_b": nrm(ks[10], (L, SG_WIDTH), 0.01),
        "w_spatial": nrm(ks[11], (L, SG_GROUPS, SG_CHUNK, SG_CHUNK), SG_CHUNK ** -0.5),
        "b_spatial": 1.0 + nrm(ks[12], (L, SG_GROUPS, SG_CHUNK), 0.01),
        "w_out": nrm(ks[13], (L, D_MODEL, D_MODEL), D_MODEL ** -0.5 * DEEPNORM_BETA),
        "ln_mix_g": 1.0 + nrm(ks[14], (L, D_MODEL), 0.01),
        "ln_mix_b": nrm(ks[15], (L, D_MODEL), 0.01),
        "w_ffn_up": nrm(ks[16], (L, D_MODEL, 2 * D_FF), D_MODEL ** -0.5),
        "conv_ffn_w": nrm(ks[17], (L, FFN_CONV, D_FF), FFN_CONV ** -0.5),
        "conv_ffn_b": nrm(ks[18], (L, D_FF), 0.01),
        "w_ffn_down": nrm(ks[19], (L, D_FF, D_MODEL), D_FF ** -0.5 * DEEPNORM_BETA),
        "ln_ffn_g": 1.0 + nrm(ks[20], (L, D_MODEL), 0.01),
        "ln_ffn_b": nrm(ks[21], (L, D_MODEL), 0.01),
    }


def reference(x, w_in, conv_rg_w, conv_rg_b, w_rgate, b_rgate, w_igate, b_igate,
              lru_lambda, sgu_ln_g, sgu_ln_b, w_spatial, b_spatial, w_out,
              ln_mix_g, ln_mix_b, w_ffn_up, conv_ffn_w, conv_ffn_b, w_ffn_down,
              ln_ffn_g, ln_ffn_b):
    for l in range(DEPTH):
        mix = hybrid_mixer(x, w_in[l], conv_rg_w[l], conv_rg_b[l], w_rgate[l], b_rgate[l],
                           w_igate[l], b_igate[l], lru_lambda[l], sgu_ln_g[l], sgu_ln_b[l],
                           w_spatial[l], b_spatial[l], w_out[l])
        x = layer_norm(DEEPNORM_ALPHA * x + mix, ln_mix_g[l], ln_mix_b[l])
        ff = conv_ffn(x, w_ffn_up[l], conv_ffn_w[l], conv_ffn_b[l], w_ffn_down[l])
        x = layer_norm(DEEPNORM_ALPHA * x + ff, ln_ffn_g[l], ln_ffn_b[l])
    return x
```

```python
from contextlib import ExitStack
import numpy as np
import concourse.bass as bass
import concourse.mybir as mybir
from concourse.bass_utils import run_bass_kernel_spmd

F32 = mybir.dt.float32
BF16 = mybir.dt.bfloat16
AF = mybir.ActivationFunctionType
ALU = mybir.AluOpType

ENGS = ("pe", "act", "dve", "pool", "sp")
SAME_ENGINE_SYNC = True

D = 1024
SEQ = 4096
NT = 32
KC = 8
DEPTH = 2
DFF = 3072
NFC = 24
ALPHA = float((2 * DEPTH) ** 0.25)
EPS = 1e-5
ATT_SCALE = float(128 ** -0.5)
NEGBIG = -30000.0
C_CW, C_CB, C_BR, C_BI, C_LAM, C_FW, C_FB, NCH = 0, 32, 40, 48, 56, 64, 136, 160
CG_AX, CG_AG, CG_SU, CG_Q, CG_K, CG_V, CG_GA, CG_GS, CG_GM = range(9)


class Buf:
    __slots__ = ("w", "r", "name")

    def __init__(self, name=""):
        self.w = {}
        self.r = {}
        self.name = name


class Tile:
    def __init__(self, ap, name):
        self.ap = ap
        self.buf = Buf(name)
        self.name = name

    def __getitem__(self, k):
        return self.ap[k]


def _bufs(lst):
    out = []
    for x in lst:
        if x is None:
            continue
        out.append(x.buf if isinstance(x, Tile) else x)
    return out


class Sched:
    def __init__(self, nc, es):
        self.nc = nc
        self.es = es
        self.streams = {e: [] for e in ENGS}
        self.cnt = {}
        self.sem = {}
        for e in ("pe", "act", "dve", "pool"):
            self.sem[e] = es.enter_context(nc.semaphore("sem_" + e))
            self.cnt[e] = 0
        self.waited = {e: {} for e in ENGS}
        self.dma_pool = []
        self.dma_rr = 0
        self.dma_cnt = {}

    def new_dma_sem(self, name):
        s = self.es.enter_context(self.nc.semaphore(name))
        self.dma_cnt[id(s)] = [s, 0]
        return s

    def _pool_sem(self):
        if len(self.dma_pool) < 32:
            s = self.new_dma_sem("dq%d" % len(self.dma_pool))
            self.dma_pool.append(s)
            return s
        s = self.dma_pool[self.dma_rr % len(self.dma_pool)]
        self.dma_rr += 1
        return s

    def _deps(self, reads, writes):
        deps = {}

        def add(d):
            for k, v in d.items():
                if deps.get(k, (None, 0))[1] < v[1]:
                    deps[k] = v

        for b in reads:
            add(b.w)
        for b in writes:
            add(b.w)
            add(b.r)
        return deps

    def _emit_waits(self, eng, deps):
        own = self.sem.get(eng)
        for k, (s, v) in deps.items():
            if own is not None and s is own and (eng == "pe" or not SAME_ENGINE_SYNC):
                continue
            if self.waited[eng].get(k, 0) >= v:
                continue
            self.waited[eng][k] = v
            self.streams[eng].append(("wait", s, v))

    def _record(self, tok, reads, writes):
        k = id(tok[0])
        for b in reads:
            if b.r.get(k, (None, 0))[1] < tok[1]:
                b.r[k] = tok
        for b in writes:
            if b.w.get(k, (None, 0))[1] < tok[1]:
                b.w[k] = tok

    def op(self, eng, fn, reads=(), writes=(), signal=True):
        reads = _bufs(reads)
        writes = _bufs(writes)
        self._emit_waits(eng, self._deps(reads, writes))
        s = self.sem[eng]
        if signal:
            self.cnt[eng] += 1
            tok = (s, self.cnt[eng])
        else:
            tok = (s, self.cnt[eng] + 1)
        self.streams[eng].append(("op", fn, s if signal else None))
        self._record(tok, reads, writes)
        return tok

    def dma(self, q, out, in_, reads=(), writes=(), sem=None):
        reads = _bufs(reads)
        writes = _bufs(writes)
        if sem is None:
            if q == "pool":
                if not hasattr(self, "swq"):
                    self.swq = [self.new_dma_sem("swq%d" % i) for i in range(2)]
                    self.swq_rr = 0
                sem = self.swq[self.swq_rr % 2]
                self.swq_rr += 1
            else:
                sem = self._pool_sem()
        ent = self.dma_cnt[id(sem)]
        deps = self._deps(reads, writes)
        if ent[1] > 0:
            deps[id(sem)] = (sem, max(deps.get(id(sem), (None, 0))[1], ent[1]))
        self._emit_waits(q, deps)
        ent[1] += 16
        tok = (sem, ent[1])
        self.streams[q].append(("dma", out, in_, sem))
        self._record(tok, reads, writes)
        return tok

    def wait_all(self, eng, bufs):
        self._emit_waits(eng, self._deps(_bufs(bufs), []))

    def barrier(self):
        deps = {}
        for e in ("pe", "act", "dve", "pool"):
            if self.cnt[e] > 0:
                deps[id(self.sem[e])] = (self.sem[e], self.cnt[e])
        for k, (s, c) in self.dma_cnt.items():
            if c > 0:
                deps[k] = (s, c)
        for e in ENGS:
            self._emit_waits(e, deps)

    def replay(self):
        nc = self.nc
        streams = self.streams

        def run(e, stream):
            for it in stream:
                if it[0] == "wait":
                    e.wait_ge(it[1], it[2])
                elif it[0] == "op":
                    ins = it[1](e)
                    if it[2] is not None:
                        ins.then_inc(it[2], 1)
                else:
                    e.dma_start(out=it[1], in_=it[2]).then_inc(it[3], 16)

        with nc.Block() as block:
            @block.tensor
            def _(e):
                run(e, streams["pe"])

            @block.scalar
            def _(e):
                run(e, streams["act"])

            @block.vector
            def _(e):
                run(e, streams["dve"])

            @block.gpsimd
            def _(e):
                run(e, streams["pool"])

            @block.sync
            def _(e):
                run(e, streams["sp"])


class Arena:
    def __init__(self, ap, nwords):
        self.ap = ap
        self.n = nwords
        self.off = 0
        self.marks = []

    def alloc(self, name, shape, dtype):
        free = 1
        for s in shape[1:]:
            free *= s
        words = free if dtype == F32 else (free + 1) // 2
        assert self.off + words <= self.n, (name, self.off, words, self.n)
        v = self.ap[:, self.off:self.off + words]
        self.off += words
        if dtype != F32:
            v = v.bitcast(dtype)
            if (free % 2) == 1:
                v = v[:, 0:free]
        if len(shape) == 3:
            v = v.rearrange("p (a b) -> p a b", a=shape[1])
        elif len(shape) == 4:
            v = v.rearrange("p (a b c) -> p a b c", a=shape[1], b=shape[2])
        elif len(shape) == 5:
            v = v.rearrange("p (a b c d) -> p a b c d", a=shape[1], b=shape[2], c=shape[3])
        if shape[0] < 128:
            v = v[0:shape[0]]
        return Tile(v, name)

    def mark(self):
        return self.off

    def reset(self, m):
        self.off = m


def build_program(n_layers=DEPTH, dbg=False):
    nc = bass.Bass("TRN2", target_bir_lowering=False)

    def din(name, shape, dt=F32):
        return nc.dram_tensor(name, list(shape), dt, kind="ExternalInput").ap()

    def dscr(name, shape, dt):
        return nc.dram_tensor(name, list(shape), dt).ap()

    x_d = din("x", [SEQ, D])
    w_in_g = din("w_in_g", [DEPTH, 8, 128, 9 * 1024])
    w_sv = din("w_sv", [DEPTH, 128, 8 * 1024])
    w_outr = din("w_outr", [DEPTH, 128, 8 * 1024])
    w_upr = din("w_upr", [DEPTH, 128, NFC * 2048])
    w_dnr = din("w_dnr", [DEPTH, 128, NFC * 1024])
    rgw_d = din("rgw", [DEPTH, 128, 2048])
    wspT_d = din("wspT", [DEPTH, 128, 1024])
    chv_d = din("chv", [128, DEPTH * NCH])
    bsp_d = din("bsp", [DEPTH, 1024])
    tokv_d = din("tokv", [DEPTH, 6, 1024])
    y_d = nc.dram_tensor("y", [SEQ, D], F32, kind="ExternalOutput").ap()

    xres1 = dscr("xres1", [SEQ, D], F32)
    xres2 = dscr("xres2", [SEQ, D], F32)
    vln_d = dscr("vln_d", [SEQ, D], BF16)
    mrg_d = dscr("mrg_d", [128, 8, SEQ], BF16)
    xT_d = dscr("xT_d", [128, 8, SEQ], BF16)
    x1T_d = dscr("x1T_d", [128, 8, SEQ], BF16)
    wdn_bf = dscr("wdn_bf", [DEPTH, 128, NFC * 1024], BF16)
    dbg_out = {}
    if dbg:
        dbg_out["d_mrg"] = nc.dram_tensor("d_mrg", [128, 8, SEQ], BF16, kind="ExternalOutput").ap()
        dbg_out["d_x1"] = nc.dram_tensor("d_x1", [SEQ, D], F32, kind="ExternalOutput").ap()
        dbg_out["d_x1_0"] = nc.dram_tensor("d_x1_0", [SEQ, D], F32, kind="ExternalOutput").ap()
        dbg_out["d_mrg_0"] = nc.dram_tensor("d_mrg_0", [128, 8, SEQ], BF16, kind="ExternalOutput").ap()
        dbg_out["d_vln"] = nc.dram_tensor("d_vln", [SEQ, D], BF16, kind="ExternalOutput").ap()

    with ExitStack() as es:
        S = Sched(nc, es)
        NW = 53000
        arena_t = es.enter_context(nc.sbuf_tensor("arena", [128, NW], F32))
        AR = Arena(arena_t[:, :], NW)
        psum_t = [es.enter_context(nc.psum_tensor("ps%d" % i, [128, 1024], F32)) for i in range(4)]
        PSB = [Buf("bank%d" % i) for i in range(8)]

        def bank(i):
            return psum_t[i // 2][:, (i % 2) * 512:(i % 2) * 512 + 512]

        def pair(i):
            return psum_t[i][:, :]

        def MM(out, lhsT, rhs, start, stop, r, w, sig=False):
            S.op("pe", lambda e: e.matmul(out, lhsT=lhsT, rhs=rhs, start=start, stop=stop), r, w, signal=(stop or sig))

        def TR(out, in_, ident, r, w, signal=True):
            S.op("pe", lambda e: e.transpose(out=out, in_=in_, identity=ident), r, w, signal=signal)

        def ACT(out, in_, func, r, w, scale=1.0, bias=None, accum=None):
            def f(e):
                kw = {}
                if bias is not None:
                    kw["bias"] = bias
                if accum is not None:
                    kw["accum_out"] = accum
                return e.activation(out=out, in_=in_, func=func, scale=scale, **kw)
            S.op("act", f, r, w)

        def TT(eng, out, in0, in1, op, r, w):
            S.op(eng, lambda e: e.tensor_tensor(out=out, in0=in0, in1=in1, op=op), r, w)

        def TS(eng, out, in0, s1, s2, op0, op1, r, w):
            if s2 is None:
                S.op(eng, lambda e: e.tensor_scalar(out=out, in0=in0, scalar1=s1, scalar2=None, op0=op0), r, w)
            else:
                S.op(eng, lambda e: e.tensor_scalar(out=out, in0=in0, scalar1=s1, scalar2=s2, op0=op0, op1=op1), r, w)

        def STT(out, in0, scalar, in1, op0, op1, r, w):
            S.op("dve", lambda e: e.scalar_tensor_tensor(out=out, in0=in0, scalar=scalar, in1=in1, op0=op0, op1=op1), r, w)

        def CP(eng, out, in_, r, w):
            if eng == "act":
                S.op("act", lambda e: e.activation(out=out, in_=in_, func=AF.Copy), r, w)
            else:
                S.op(eng, lambda e: e.tensor_copy(out=out, in_=in_), r, w)

        def MEMSET(eng, ap, val, w):
            S.op(eng, lambda e: e.memset(ap, val), [], w)

        ones_f = AR.alloc("ones_f", [128, 128], F32)
        ident_f = AR.alloc("ident_f", [128, 128], F32)
        triu_f = AR.alloc("triu_f", [128, 128], F32)
        ident_b = AR.alloc("ident_b", [128, 128], BF16)
        triu_b = AR.alloc("triu_b", [128, 128], BF16)
        ones_b = AR.alloc("ones_b", [128, 128], BF16)
        onesrc = AR.alloc("onesrc", [128, 2048], BF16)
        sel = AR.alloc("sel", [128, 16, 128], BF16)
        cb = AR.alloc("cb", [128, 32, 16], F32)
        chv = AR.alloc("chv", [128, DEPTH * NCH], F32)
        der = AR.alloc("der", [128, DEPTH * 40], F32)
        cst = AR.alloc("cst", [128, 8], F32)
        PERS = [ones_f, ident_f, triu_f, ident_b, triu_b, ones_b, sel, cb, chv, der, cst]

        MEMSET("pool", ones_f[:, :], 1.0, [ones_f])
        S.op("pool", lambda e: e.affine_select(out=ident_f[:, :], in_=ones_f[:, :], pattern=[[1, 128]], compare_op=ALU.is_equal,
                                               fill=0.0, base=0, channel_multiplier=-1), [ones_f], [ident_f])
        S.op("pool", lambda e: e.affine_select(out=triu_f[:, :], in_=ones_f[:, :], pattern=[[1, 128]], compare_op=ALU.is_ge,
                                               fill=0.0, base=0, channel_multiplier=-1), [ones_f], [triu_f])
        CP("dve", ident_b[:, :], ident_f[:, :], [ident_f], [ident_b])
        CP("dve", triu_b[:, :], triu_f[:, :], [triu_f], [triu_b])
        CP("dve", ones_b[:, :], ones_f[:, :], [ones_f], [ones_b])
        MEMSET("pool", onesrc[:, :], 1.0, [onesrc])
        S.op("pool", lambda e: e.affine_select(out=sel[0:16, :, :], in_=onesrc[0:16, :].rearrange("p (a b) -> p a b", a=16),
                                               pattern=[[1, 16], [0, 128]], compare_op=ALU.is_equal, fill=0.0, base=0,
                                               channel_multiplier=-1), [onesrc], [sel])
        MEMSET("pool", cb[:, :, :], -1e30, [cb])
        for b in range(1, 16):
            MEMSET("pool", cb[:, 2 * b:2 * b + 2, 0:b], 0.0, [cb])
        MEMSET("pool", cst[:, 0:1], 1.0, [cst])
        MEMSET("pool", cst[:, 1:2], EPS, [cst])
        MEMSET("pool", cst[:, 2:3], -0.5, [cst])
        MEMSET("pool", cst[:, 3:4], 0.5, [cst])
        S.dma("sp", chv[:, :], chv_d[:, :], [], [chv])
        for l in range(n_layers):
            cv = l * NCH
            dv = l * 40
            TS("dve", der[:, dv:dv + 16], chv[:, cv + C_BR:cv + C_BR + 16], 0.5, None, ALU.mult, None, [chv], [der])
            ACT(der[:, dv + 32:dv + 40], chv[:, cv + C_LAM:cv + C_LAM + 8], AF.Exp, [chv], [der], scale=-1.0)
            ACT(der[:, dv + 32:dv + 40], der[:, dv + 32:dv + 40], AF.Ln, [der, cst], [der], bias=cst[:, 0:1])
            TS("dve", der[:, dv + 16:dv + 24], der[:, dv + 32:dv + 40], -4.0, None, ALU.mult, None, [der], [der])
            TS("dve", der[:, dv + 24:dv + 32], der[:, dv + 32:dv + 40], -8.0, None, ALU.mult, None, [der], [der])

        pers_mark = AR.mark()

        def DUMP(name, ap, reads):
            if not dbg:
                return
            o = nc.dram_tensor(name, list(ap.shape), ap.dtype, kind="ExternalOutput").ap()
            if len(ap.shape) == 3:
                for i in range(ap.shape[1]):
                    S.dma("sp", o[:, i, :], ap[:, i, :], reads, [Buf("dbg")])
            else:
                S.dma("sp", o[:, :], ap, reads, [Buf("dbg")])

        wdn_buf = [Buf("wdn%d" % l) for l in range(DEPTH)]
        for l in range(n_layers):
            for c in range(4):
                S.dma("pool", wdn_bf[l][:, c * 6144:(c + 1) * 6144].rearrange("p (a b) -> p a b", b=1024),
                      w_dnr[l][:, c * 6144:(c + 1) * 6144].rearrange("p (a b) -> p a b", b=1024), [], [wdn_buf[l]])

        vln_b = [Buf("vln%d" % t) for t in range(NT)]
        mrg_b = [Buf("mrg%d" % g) for g in range(8)]
        xres1_b = [Buf("xr1_%d" % t) for t in range(NT)]
        xres2_b = [Buf("xr2_%d" % t) for t in range(NT)]
        x1T_b = [Buf("x1T%d" % t) for t in range(8)]
        xTd_b = [Buf("xTd%d" % t) for t in range(8)]
        y_b = [Buf("y%d" % t) for t in range(NT)]

        def ln_rows(mode, zt, stats, mv, rs, gbc, bbc, out_ap, r_extra, w_out, tmp2=None):
            S.op("dve", lambda e: e.bn_stats(out=stats[:, 0, :], in_=zt[:, 0:512]), [zt], [stats])
            S.op("dve", lambda e: e.bn_stats(out=stats[:, 1, :], in_=zt[:, 512:1024]), [zt], [stats])
            S.op("dve", lambda e: e.bn_aggr(out=mv[:, :], in_=stats[:, :, :]), [stats], [mv])
            if mode == "pool":
                TS("dve", rs[:, 0:1], mv[:, 1:2], EPS, None, ALU.add, None, [mv], [rs])
                TT("pool", rs[:, 1:2], rs[:, 0:1], cst[:, 2:3], ALU.pow, [rs, cst], [rs])
            else:
                ACT(rs[:, 0:1], mv[:, 1:2], AF.Sqrt, [mv, cst], [rs], bias=cst[:, 1:2])
                S.op("dve", lambda e: e.reciprocal(out=rs[:, 1:2], in_=rs[:, 0:1]), [rs], [rs])
            TS("dve", zt[:, :], zt[:, :], mv[:, 0:1], rs[:, 1:2], ALU.subtract, ALU.mult, [zt, mv, rs], [zt])
            TT("pool", zt[:, :], zt[:, :], gbc[:, :], ALU.mult, [zt, gbc], [zt])
            TT("dve", out_ap, zt[:, :], bbc[:, :], ALU.add, [zt, bbc] + list(r_extra), list(w_out))

        for l in range(n_layers):
            cv = l * NCH
            dv = l * 40
            last = (l == n_layers - 1)
            xin_d = x_d if l == 0 else xres2
            xin_b = [None] * NT if l == 0 else xres2_b

            S.barrier()
            AR.reset(pers_mark)
            xT = AR.alloc("xT", [128, 8, SEQ], BF16)
            xT_b = [Buf("xT%d" % t) for t in range(NT)]
            macc = AR.alloc("macc", [128, SEQ], F32)
            wg = [AR.alloc("wg%d" % i, [128, 9, 8, 128], BF16) for i in range(2)]
            rgw = AR.alloc("rgw", [128, 8, 2, 128], BF16)
            wspb = AR.alloc("wspb", [128, 8, 128], BF16)
            bspbc = AR.alloc("bspbc", [128, 8, 128], F32)
            mix_mark = AR.mark()

            S.dma("pool", rgw[:, :, :, :].rearrange("p a b c -> p (a b c)"), rgw_d[l][:, :], [], [rgw])
            S.dma("sp", bspbc[:, :, :].rearrange("p a b -> p (a b)"), bsp_d[l].partition_broadcast(128), [], [bspbc])

            if l == 0:
                xin = [AR.alloc("xin%d" % i, [128, 1024], F32) for i in range(2)]
                for tt in range(NT):
                    xi = xin[tt % 2]
                    S.dma("sp", xi[:, :], x_d[tt * 128:(tt + 1) * 128, :], [], [xi])
                    for h in range(2):
                        pp = (tt % 2) * 2 + h
                        for j in range(4):
                            TR(pair(pp)[:, j * 128:(j + 1) * 128], xi[:, (h * 4 + j) * 128:(h * 4 + j + 1) * 128], ident_f[:, :],
                               [xi, ident_f], [PSB[2 * pp], PSB[2 * pp + 1]], signal=(j == 3))
                        CP("act" if h == 0 else "dve", xT[:, h * 4:(h + 1) * 4, tt * 128:(tt + 1) * 128],
                           pair(pp)[:, 0:512].rearrange("p (a b) -> p a b", a=4), [PSB[2 * pp], PSB[2 * pp + 1]], [xT_b[tt]])
            else:
                for c in range(8):
                    S.dma("sp", xT[:, :, c * 512:(c + 1) * 512], xT_d[:, :, c * 512:(c + 1) * 512], [xTd_b[c]],
                          [xT_b[4 * c + i] for i in range(4)])
            if l == 0 and False:
                DUMP("d_xT", xT[:, :, :], xT_b)
            S.barrier()
            AR.reset(mix_mark)

            wsv = AR.alloc("wsv", [128, 8, 1024], BF16)
            wspf = AR.alloc("wspf", [128, 8, 128], F32)
            gbc = AR.alloc("gbc", [128, 1024], F32)
            bbc = AR.alloc("bbc", [128, 1024], F32)
            v32 = [AR.alloc("v32_%d" % i, [128, 1024], F32) for i in range(2)]
            vlnb = [AR.alloc("vlnb%d" % i, [128, 1024], BF16) for i in range(2)]
            stats = [AR.alloc("stats%d" % i, [128, 2, 6], F32) for i in range(2)]
            mv = [AR.alloc("mv%d" % i, [128, 2], F32) for i in range(2)]
            rs = [AR.alloc("rs%d" % i, [128, 2], F32) for i in range(2)]
            for kc in range(8):
                S.dma("pool", wsv[:, kc, :], w_sv[l][:, kc * 1024:(kc + 1) * 1024], [], [wsv])
            S.dma("sp", wspf[:, :, :].rearrange("p a b -> p (a b)"), wspT_d[l][:, :], [], [wspf])
            S.dma("sp", gbc[:, :], tokv_d[l][0].partition_broadcast(128), [], [gbc])
            S.dma("sp", bbc[:, :], tokv_d[l][1].partition_broadcast(128), [], [bbc])
            TT("dve", wspb[:, :, :], wspf[:, :, :], triu_f[:, :].unsqueeze(1).broadcast_to([128, 8, 128]), ALU.mult,
               [wspf, triu_f], [wspb])
            def load_wg(g):
                t = wg[g % 2]
                for cg in range(9):
                    S.dma("pool", t[:, cg, :, :].rearrange("p a b -> p (a b)"), w_in_g[l][g][:, cg * 1024:(cg + 1) * 1024], [], [t])
            load_wg(0)
            for tt in range(NT):
                sl = tt % 2
                pp = sl
                for h in range(2):
                    for kc in range(8):
                        MM(pair(pp)[:, h * 512:(h + 1) * 512], xT[:, kc, tt * 128:(tt + 1) * 128], wsv[:, kc, h * 512:(h + 1) * 512],
                           kc == 0, kc == 7, [xT_b[tt], wsv], [PSB[2 * pp + h]])
                ACT(v32[sl][:, :], pair(pp), AF.Gelu_apprx_tanh, [PSB[2 * pp], PSB[2 * pp + 1]], [v32[sl]])
                if l == 0 and tt == 0:
                    DUMP("d_v32", v32[sl][:, :], [v32[sl]])
                    DUMP("d_wsv", wsv[:, :, :], [wsv])
                ln_rows("pool", v32[sl], stats[sl], mv[sl], rs[sl], gbc, bbc, vlnb[sl][:, :], [], [vlnb[sl]])
                S.dma("sp", vln_d[tt * 128:(tt + 1) * 128, :], vlnb[sl][:, :], [vlnb[sl]], [vln_b[tt]])
            S.barrier()
            AR.reset(mix_mark)
            g_mark = AR.mark()

            for g in range(8):
                wt = wg[g % 2]
                if g + 1 < 8:
                    load_wg(g + 1)

                def proj(cg, bk, t0, n):
                    xb = [xT_b[t] for t in range(t0 // 128, (t0 + n + 127) // 128)]
                    for kc in range(8):
                        MM(bank(bk)[:, 0:n], wt[:, cg, kc, :], xT[:, kc, t0:t0 + n], kc == 0, kc == 7, [wt] + xb, [PSB[bk]])

                AR.reset(g_mark)
                axp = [AR.alloc("axp%d" % i, [128, 516], F32) for i in range(2)]
                cc = [AR.alloc("cc%d" % i, [128, 512], F32) for i in range(2)]
                ccb = [AR.alloc("ccb%d" % i, [128, 512], BF16) for i in range(2)]
                tr_ = [AR.alloc("tr%d" % i, [128, 512], F32) for i in range(2)]
                ti_ = [AR.alloc("ti%d" % i, [128, 512], F32) for i in range(2)]
                aa = [AR.alloc("aa%d" % i, [128, 512], F32) for i in range(2)]
                a2 = [AR.alloc("a2%d" % i, [128, 512], F32) for i in range(2)]
                tmp = [AR.alloc("tmp%d" % i, [128, 512], F32) for i in range(2)]
                uu = [AR.alloc("uu%d" % i, [128, 512], F32) for i in range(2)]
                gg = [AR.alloc("gg%d" % i, [128, 512], F32) for i in range(2)]
                tg = [AR.alloc("tg%d" % i, [128, 512], F32) for i in range(2)]
                macc_b = [Buf("macc%d" % c) for c in range(8)]

                def cw(k):
                    return chv[:, cv + C_CW + k * 8 + g:cv + C_CW + k * 8 + g + 1]

                def a_front(c):
                    sl = c % 2
                    bk = c % 2
                    proj(CG_AX, bk, c * 512, 512)
                    CP("act", axp[sl][:, 3:515], bank(bk), [PSB[bk]], [axp[sl]])
                    if c == 0:
                        MEMSET("pool", axp[sl][:, 0:3], 0.0, [axp[sl]])
                    else:
                        CP("pool", axp[sl][:, 0:3], axp[1 - sl][:, 512:515], [axp[1 - sl]], [axp[sl]])

                def a_back(c):
                    sl = c % 2
                    t0 = c * 512
                    TS("dve", cc[sl][:, :], axp[sl][:, 3:515], cw(3), chv[:, cv + C_CB + g:cv + C_CB + g + 1], ALU.mult, ALU.add,
                       [axp[sl], chv], [cc[sl]])
                    for k in (2, 1, 0):
                        STT(cc[sl][:, :], axp[sl][:, k:k + 512], cw(k), cc[sl][:, :], ALU.mult, ALU.add, [axp[sl], chv, cc[sl]], [cc[sl]])
                    CP("pool", ccb[sl][:, :], cc[sl][:, :], [cc[sl]], [ccb[sl]])
                    br, bi = 2 + sl, 4 + sl
                    MM(bank(br), rgw[:, g, 0, :], ccb[sl][:, :], True, True, [rgw, ccb[sl]], [PSB[br]])
                    MM(bank(bi), rgw[:, g, 1, :], ccb[sl][:, :], True, True, [rgw, ccb[sl]], [PSB[bi]])
                    ACT(tr_[sl][:, :], bank(br), AF.Tanh, [PSB[br], der], [tr_[sl]], scale=0.5, bias=der[:, dv + g:dv + g + 1])
                    ACT(ti_[sl][:, :], bank(bi), AF.Tanh, [PSB[bi], der], [ti_[sl]], scale=0.5, bias=der[:, dv + 8 + g:dv + 8 + g + 1])
                    ACT(aa[sl][:, :], tr_[sl][:, :], AF.Exp, [tr_[sl], der], [aa[sl]], scale=der[:, dv + 16 + g:dv + 16 + g + 1],
                        bias=der[:, dv + 16 + g:dv + 16 + g + 1])
                    ACT(a2[sl][:, :], tr_[sl][:, :], AF.Exp, [tr_[sl], der], [a2[sl]], scale=der[:, dv + 24 + g:dv + 24 + g + 1],
                        bias=der[:, dv + 24 + g:dv + 24 + g + 1])
                    TS("dve", a2[sl][:, :], a2[sl][:, :], 1.0, None, ALU.min, None, [a2[sl]], [a2[sl]])
                    ACT(a2[sl][:, :], a2[sl][:, :], AF.Sqrt, [a2[sl], cst], [a2[sl]], scale=-1.0, bias=cst[:, 0:1])
                    STT(tmp[sl][:, :], ti_[sl][:, :], 1.0, cc[sl][:, :], ALU.add, ALU.mult, [ti_[sl], cc[sl]], [tmp[sl]])
                    STT(uu[sl][:, :], a2[sl][:, :], 0.5, tmp[sl][:, :], ALU.mult, ALU.mult, [a2[sl], tmp[sl]], [uu[sl]])
                    init = 0.0 if c == 0 else macc[:, t0 - 1:t0]
                    rb = [aa[sl], uu[sl]] + ([macc_b[c - 1]] if c > 0 else [])
                    S.op("dve", lambda e: e.tensor_tensor_scan(out=macc[:, t0:t0 + 512], data0=aa[sl][:, :], data1=uu[sl][:, :],
                                                               initial=init, op0=ALU.mult, op1=ALU.add), rb, [macc_b[c]])

                for c in range(9):
                    if c < 8:
                        a_front(c)
                    if c >= 1:
                        a_back(c - 1)
                for c in range(8):
                    sl = c % 2
                    t0 = c * 512
                    b1, b2 = 6, 7
                    proj(CG_AG, b1, t0, 512)
                    ACT(gg[sl][:, :], bank(b1), AF.Gelu_apprx_tanh, [PSB[b1]], [gg[sl]])
                    proj(CG_GA, b2, t0, 512)
                    ACT(tg[sl][:, :], bank(b2), AF.Tanh, [PSB[b2]], [tg[sl]], scale=0.5)
                    TT("dve", gg[sl][:, :], gg[sl][:, :], macc[:, t0:t0 + 512], ALU.mult, [gg[sl], macc_b[c]], [gg[sl]])
                    STT(macc[:, t0:t0 + 512], tg[sl][:, :], 1.0, gg[sl][:, :], ALU.add, ALU.mult, [tg[sl], gg[sl]], [macc_b[c]])
                S.barrier()

                AR.reset(g_mark)
                vlng = AR.alloc("vlng", [128, 32, 128], BF16)
                gu = [AR.alloc("gu%d" % i, [128, 512], F32) for i in range(2)]
                tgs = [AR.alloc("tgs%d" % i, [128, 512], F32) for i in range(2)]
                m1 = [AR.alloc("m1_%d" % i, [128, 512], F32) for i in range(2)]
                for q4 in range(4):
                    S.dma("sp", vlng[:, q4 * 8:(q4 + 1) * 8, :],
                          vln_d[q4 * 1024:(q4 + 1) * 1024, g * 128:(g + 1) * 128].rearrange("(n p) c -> p n c", p=128),
                          [vln_b[t] for t in range(q4 * 8, q4 * 8 + 8)], [vlng])
                for c in range(8):
                    sl = c % 2
                    t0 = c * 512
                    bu, bg, bm = 0 + sl, 2 + sl, 4 + sl
                    proj(CG_SU, bu, t0, 512)
                    ACT(gu[sl][:, :], bank(bu), AF.Gelu_apprx_tanh, [PSB[bu]], [gu[sl]])
                    proj(CG_GS, bg, t0, 512)
                    ACT(tgs[sl][:, :], bank(bg), AF.Tanh, [PSB[bg]], [tgs[sl]], scale=0.5)
                    for n in range(4):
                        MM(bank(bm)[:, n * 128:(n + 1) * 128], vlng[:, 4 * c + n, :], wspb[:, g, :], True, True, [vlng, wspb], [PSB[bm]])
                    TT("dve", m1[sl][:, :].rearrange("p (a b) -> p a b", a=4), bank(bm).rearrange("p (a b) -> p a b", a=4),
                       bspbc[:, g, :].unsqueeze(1).broadcast_to([128, 4, 128]), ALU.add, [PSB[bm], bspbc], [m1[sl]])
                    TT("dve", m1[sl][:, :], m1[sl][:, :], gu[sl][:, :], ALU.mult, [m1[sl], gu[sl]], [m1[sl]])
                    STT(m1[sl][:, :], tgs[sl][:, :], 1.0, m1[sl][:, :], ALU.add, ALU.mult, [tgs[sl], m1[sl]], [m1[sl]])
                    TT("dve", macc[:, t0:t0 + 512], macc[:, t0:t0 + 512], m1[sl][:, :], ALU.add, [m1[sl], macc_b[c]], [macc_b[c]])
                S.barrier()

                AR.reset(g_mark)
                qT = AR.alloc("qT", [128, SEQ], BF16)
                kT = AR.alloc("kT", [128, SEQ], BF16)
                Vt = AR.alloc("Vt", [128, 32, 128], BF16)
                negmT = AR.alloc("negmT", [128, SEQ], BF16)
                ksum = AR.alloc("ksum", [128, 16], F32)
                kmT = AR.alloc("kmT", [128, 16], BF16)
                gsb = AR.alloc("gsb", [128, 32, 16], F32)
                mx8 = AR.alloc("mx8", [128, 32, 8], F32)
                thr = AR.alloc("thr", [128, 32], F32)
                negm = AR.alloc("negm", [128, 32, 16], BF16)
                PT = [AR.alloc("PT%d" % i, [128, 256], BF16) for i in range(4)]
                tgm = [AR.alloc("tgm%d" % i, [128, 256], F32) for i in range(2)]
                rec = [AR.alloc("rec%d" % i, [128, 256], F32) for i in range(2)]
                ot = [AR.alloc("ot%d" % i, [128, 256], F32) for i in range(2)]
                mrgb = [AR.alloc("mrgb%d" % i, [128, 1024], BF16) for i in range(2)]
                for c in range(8):
                    t0 = c * 512
                    bq, bk_ = 0 + (c % 2), 2 + (c % 2)
                    proj(CG_K, bk_, t0, 512)
                    for h in range(2):
                        ACT(kT[:, t0 + h * 256:t0 + (h + 1) * 256], bank(bk_)[:, h * 256:(h + 1) * 256], AF.Copy, [PSB[bk_]], [kT, ksum],
                            accum=ksum[:, 2 * c + h:2 * c + h + 1])
                    proj(CG_Q, bq, t0, 512)
                    CP("dve", qT[:, t0:t0 + 512], bank(bq), [PSB[bq]], [qT])
                for t4 in range(8):
                    bv = 4 + (t4 % 2)
                    for j in range(4):
                        tt = t4 * 4 + j
                        for kc in range(8):
                            MM(bank(bv)[:, j * 128:(j + 1) * 128], xT[:, kc, tt * 128:(tt + 1) * 128], wt[:, CG_V, kc, :], kc == 0, kc == 7,
                               [xT_b[tt], wt], [PSB[bv]])
                    CP("act" if t4 % 2 == 0 else "dve", Vt[:, t4 * 4:(t4 + 1) * 4, :], bank(bv).rearrange("p (a b) -> p a b", a=4),
                       [PSB[bv]], [Vt])
                TS("dve", kmT[:, :], ksum[:, :], 1.0 / 256.0, None, ALU.mult, None, [ksum], [kmT])
                bgt = 6
                for qt in range(32):
                    MM(bank(bgt)[:, qt * 16:(qt + 1) * 16], qT[:, qt * 128:(qt + 1) * 128], kmT[:, :], True, True, [qT, kmT], [PSB[bgt]])
                TT("dve", gsb[:, :, :].rearrange("p a b -> p (a b)"), bank(bgt), cb[:, :, :].rearrange("p a b -> p (a b)"), ALU.add,
                   [PSB[bgt], cb], [gsb])
                for qt in range(32):
                    S.op("dve", (lambda qt: lambda e: e.max(out=mx8[:, qt, :], in_=gsb[:, qt, :]))(qt), [gsb], [mx8])
                TS("dve", thr[:, :], mx8[:, :, 2], -1e29, None, ALU.max, None, [mx8], [thr])
                TT("dve", negm[:, :, :], gsb[:, :, :], thr[:, :].unsqueeze(2).broadcast_to([128, 32, 16]), ALU.is_lt, [gsb, thr], [negm])
                TS("dve", negm[:, :, :], negm[:, :, :], NEGBIG, None, ALU.mult, None, [negm], [negm])
                for q8 in range(4):
                    bt = 6 + ((q8 + 1) % 2)
                    tb = bank(bt).bitcast(BF16)
                    for j in range(8):
                        qt = q8 * 8 + j
                        TR(tb[0:16, j * 128:(j + 1) * 128], negm[:, qt, :], ident_b[:, :], [negm, ident_b], [PSB[bt]], signal=(j == 7))
                    CP("act", negmT[0:16, q8 * 1024:(q8 + 1) * 1024], tb[0:16, :], [PSB[bt]], [negmT])
                st_rr = 0
                pt_rr = 0
                for b in range(16):
                    q0 = b * 256
                    bo, bl, bgm = 3 + (b % 2) * 0, 4, 5
                    bo = 3
                    tiles = [("past", kt) for kt in range(2 * b)] + [("own0", 2 * b), ("own1", 2 * b + 1)]
                    ntile = len(tiles)
                    for i, (kind, kt) in enumerate(tiles):
                        bs = st_rr % 3
                        st_rr += 1
                        P = PT[pt_rr % 4]
                        pt_rr += 1
                        first = (i == 0)
                        lastt = (i == ntile - 1)
                        if kind == "past":
                            j = kt // 2
                            MM(bank(bs)[:, 0:256], kT[:, kt * 128:(kt + 1) * 128], qT[:, q0:q0 + 256], True, False, [kT, qT], [PSB[bs]])
                            MM(bank(bs)[:, 0:256], sel[0:16, j, :], negmT[0:16, q0:q0 + 256], False, True, [sel, negmT], [PSB[bs]])
                            ACT(P[:, 0:256], bank(bs)[:, 0:256], AF.Exp, [PSB[bs]], [P], scale=ATT_SCALE)
                            MM(bank(bo)[:, 0:256], Vt[:, kt, :], P[:, 0:256], first, False, [Vt, P], [PSB[bo]])
                            MM(bank(bl)[:, 0:256], ones_b[:, :], P[:, 0:256], first, False, [ones_b, P], [PSB[bl]])
                        elif kind == "own0":
                            MM(bank(bs)[:, 0:256], kT[:, kt * 128:(kt + 1) * 128], qT[:, q0:q0 + 256], True, True, [kT, qT], [PSB[bs]])
                            ACT(P[:, 0:256], bank(bs)[:, 0:256], AF.Exp, [PSB[bs]], [P], scale=ATT_SCALE)
                            TT("pool", P[:, 0:128], P[:, 0:128], triu_b[:, :], ALU.mult, [P, triu_b], [P])
                            MM(bank(bo)[:, 0:256], Vt[:, kt, :], P[:, 0:256], first, False, [Vt, P], [PSB[bo]])
                            MM(bank(bl)[:, 0:256], ones_b[:, :], P[:, 0:256], first, False, [ones_b, P], [PSB[bl]])
                        else:
                            MM(bank(bs)[:, 0:128], kT[:, kt * 128:(kt + 1) * 128], qT[:, q0 + 128:q0 + 256], True, True, [kT, qT], [PSB[bs]])
                            ACT(P[:, 0:128], bank(bs)[:, 0:128], AF.Exp, [PSB[bs]], [P], scale=ATT_SCALE)
                            TT("pool", P[:, 0:128], P[:, 0:128], triu_b[:, :], ALU.mult, [P, triu_b], [P])
                            MM(bank(bo)[:, 128:256], Vt[:, kt, :], P[:, 0:128], False, True, [Vt, P], [PSB[bo]])
                            MM(bank(bl)[:, 128:256], ones_b[:, :], P[:, 0:128], False, True, [ones_b, P], [PSB[bl]])
                    sl = b % 2
                    proj(CG_GM, bgm, q0, 256)
                    ACT(tgm[sl][:, :], bank(bgm)[:, 0:256], AF.Tanh, [PSB[bgm]], [tgm[sl]], scale=0.5)
                    S.op("dve", (lambda sl, bl: lambda e: e.reciprocal(out=rec[sl][:, :], in_=bank(bl)[:, 0:256]))(sl, bl), [PSB[bl]], [rec[sl]])
                    TT("dve", ot[sl][:, :], bank(bo)[:, 0:256], rec[sl][:, :], ALU.mult, [PSB[bo], rec[sl]], [ot[sl]])
                    STT(ot[sl][:, :], tgm[sl][:, :], 1.0, ot[sl][:, :], ALU.add, ALU.mult, [tgm[sl], ot[sl]], [ot[sl]])
                    mb = macc_b[b // 2]
                    TT("dve", macc[:, q0:q0 + 256], macc[:, q0:q0 + 256], ot[sl][:, :], ALU.add, [ot[sl], mb], [mb])
                for c4 in range(4):
                    mt = mrgb[c4 % 2]
                    TS("pool", mt[:, :], macc[:, c4 * 1024:(c4 + 1) * 1024], 0.5, None, ALU.mult, None,
                       [macc_b[2 * c4], macc_b[2 * c4 + 1]], [mt])
                    S.dma("sp", mrg_d[:, g, c4 * 1024:(c4 + 1) * 1024], mt[:, :], [mt], [mrg_b[g]])
                S.barrier()

            if dbg and l == 0:
                S.dma("sp", dbg_out["d_mrg_0"][:, :, :], mrg_d[:, :, :], mrg_b, [Buf("dbg")])
            if dbg and l == n_layers - 1:
                S.dma("sp", dbg_out["d_mrg"][:, :, :], mrg_d[:, :, :], mrg_b, [Buf("dbg")])
                S.dma("sp", dbg_out["d_vln"][:, :], vln_d[:, :], vln_b, [Buf("dbg")])

            S.barrier()
            AR.reset(pers_mark)
            wup = AR.alloc("wup", [128, NFC, 2, 8, 128], BF16)
            f_mark = AR.mark()
            for fc in range(NFC):
                S.dma("pool", wup[:, fc, :, :, :].rearrange("p a b c -> p a (b c)"),
                      w_upr[l][:, fc * 2048:(fc + 1) * 2048].rearrange("p (a c) -> p a c", c=1024), [], [wup])
            woutb = AR.alloc("woutb", [128, 8, 1024], BF16)
            g1 = AR.alloc("g1", [128, 1024], F32)
            b1 = AR.alloc("b1", [128, 1024], F32)
            mt_ = [AR.alloc("mt%d" % i, [128, 8, 512], BF16) for i in range(2)]
            x1Tg = [AR.alloc("x1Tg%d" % i, [128, 8, 512], BF16) for i in range(2)]
            xr = [AR.alloc("xr%d" % i, [128, 1024], F32) for i in range(2)]
            zt = [AR.alloc("zt%d" % i, [128, 1024], F32) for i in range(2)]
            x1t = [AR.alloc("x1t%d" % i, [128, 1024], F32) for i in range(2)]
            stats = [AR.alloc("stats%d" % i, [128, 2, 6], F32) for i in range(2)]
            mv = [AR.alloc("mv%d" % i, [128, 2], F32) for i in range(2)]
            rs = [AR.alloc("rs%d" % i, [128, 2], F32) for i in range(2)]
            for kc in range(8):
                S.dma("pool", woutb[:, kc, :], w_outr[l][:, kc * 1024:(kc + 1) * 1024], [], [woutb])
            S.dma("sp", g1[:, :], tokv_d[l][2].partition_broadcast(128), [], [g1])
            S.dma("sp", b1[:, :], tokv_d[l][3].partition_broadcast(128), [], [b1])
            for c in range(8):
                m = mt_[c % 2]
                xg = x1Tg[c % 2]
                S.dma("sp", m[:, :, :], mrg_d[:, :, c * 512:(c + 1) * 512], mrg_b, [m])
                for j in range(4):
                    tt = c * 4 + j
                    sl = tt % 2
                    pp = sl
                    S.dma("sp", xr[sl][:, :], xin_d[tt * 128:(tt + 1) * 128, :], [xin_b[tt]], [xr[sl]])
                    for h in range(2):
                        for kc in range(8):
                            MM(pair(pp)[:, h * 512:(h + 1) * 512], m[:, kc, j * 128:(j + 1) * 128], woutb[:, kc, h * 512:(h + 1) * 512],
                               kc == 0, kc == 7, [m, woutb], [PSB[2 * pp + h]])
                    STT(zt[sl][:, :], xr[sl][:, :], ALPHA, pair(pp), ALU.mult, ALU.add, [xr[sl], PSB[2 * pp], PSB[2 * pp + 1]], [zt[sl]])
                    ln_rows("act", zt[sl], stats[sl], mv[sl], rs[sl], g1, b1, x1t[sl][:, :], [], [x1t[sl]])
                    S.dma("sp", xres1[tt * 128:(tt + 1) * 128, :], x1t[sl][:, :], [x1t[sl]], [xres1_b[tt]])
                    if dbg and l == 0:
                        S.dma("sp", dbg_out["d_x1_0"][tt * 128:(tt + 1) * 128, :], x1t[sl][:, :], [x1t[sl]], [Buf("dbg")])
                    if dbg and l == n_layers - 1:
                        S.dma("sp", dbg_out["d_x1"][tt * 128:(tt + 1) * 128, :], x1t[sl][:, :], [x1t[sl]], [Buf("dbg")])
                    pt = 2 + sl
                    for kc in range(8):
                        TR(pair(pt)[:, kc * 128:(kc + 1) * 128], x1t[sl][:, kc * 128:(kc + 1) * 128], ident_f[:, :], [x1t[sl], ident_f],
                           [PSB[2 * pt], PSB[2 * pt + 1]], signal=(kc == 7))
                    CP("act", xg[:, :, j * 128:(j + 1) * 128], pair(pt).rearrange("p (a b) -> p a b", a=8), [PSB[2 * pt], PSB[2 * pt + 1]], [xg])
                S.dma("sp", x1T_d[:, :, c * 512:(c + 1) * 512], xg[:, :, :], [xg], [x1T_b[c]])
            S.barrier()

            AR.reset(f_mark)
            g2 = AR.alloc("g2", [128, 1024], F32)
            b2 = AR.alloc("b2", [128, 1024], F32)
            wdn = [AR.alloc("wdn%d" % i, [128, 4, 1024], BF16) for i in range(2)]
            xg_ = [AR.alloc("xg%d" % i, [128, 8, 256], BF16) for i in range(2)]
            xr1 = [AR.alloc("xr1_%d" % i, [128, 2, 1024], F32) for i in range(2)]
            actT = AR.alloc("actT", [128, NFC, 256], BF16)
            actT_b = [Buf("actT%d" % i) for i in range(NFC)]
            hgp = [AR.alloc("hgp%d" % i, [128, 260], F32) for i in range(2)]
            cf = [AR.alloc("cf%d" % i, [128, 256], F32) for i in range(2)]
            gl = [AR.alloc("gl%d" % i, [128, 256], F32) for i in range(2)]
            carry = AR.alloc("carry", [128, NFC, 2], F32)
            zt = [AR.alloc("zt%d" % i, [128, 1024], F32) for i in range(2)]
            x2t = [AR.alloc("x2t%d" % i, [128, 1024], F32) for i in range(2)]
            x2Tg = [AR.alloc("x2Tg%d" % i, [128, 8, 512], BF16) for i in range(1)]
            stats = [AR.alloc("stats%d" % i, [128, 2, 6], F32) for i in range(2)]
            mv = [AR.alloc("mv%d" % i, [128, 2], F32) for i in range(2)]
            rs = [AR.alloc("rs%d" % i, [128, 2], F32) for i in range(2)]
            S.dma("sp", g2[:, :], tokv_d[l][4].partition_broadcast(128), [], [g2])
            S.dma("sp", b2[:, :], tokv_d[l][5].partition_broadcast(128), [], [b2])
            MEMSET("pool", carry[:, :, :], 0.0, [carry])
            wd_rr = 0
            for gi in range(16):
                t0 = gi * 256
                xg = xg_[gi % 2]
                x1r = xr1[gi % 2]
                S.dma("sp", xg[:, :, :], x1T_d[:, :, t0:t0 + 256], [x1T_b[gi // 2]], [xg])
                S.dma("sp", x1r[:, :, :], xres1[t0:t0 + 256, :].rearrange("(a p) d -> p a d", p=128),
                      [xres1_b[2 * gi], xres1_b[2 * gi + 1]], [x1r])
                for fc in range(NFC):
                    sl = fc % 2
                    bg_, bu_ = 4 + sl, 6 + sl
                    for kc in range(8):
                        MM(bank(bg_)[:, 0:256], wup[:, fc, 0, kc, :], xg[:, kc, :], kc == 0, kc == 7, [wup, xg], [PSB[bg_]])
                    for kc in range(8):
                        MM(bank(bu_)[:, 0:256], wup[:, fc, 1, kc, :], xg[:, kc, :], kc == 0, kc == 7, [wup, xg], [PSB[bu_]])
                    CP("pool", hgp[sl][:, 0:2], carry[:, fc, :], [carry], [hgp[sl]])
                    CP("act", hgp[sl][:, 2:258], bank(bg_)[:, 0:256], [PSB[bg_]], [hgp[sl]])
                    CP("pool", carry[:, fc, :], hgp[sl][:, 256:258], [hgp[sl]], [carry])

                    def fw(k):
                        return chv[:, cv + C_FW + k * NFC + fc:cv + C_FW + k * NFC + fc + 1]
                    TS("dve", cf[sl][:, :], hgp[sl][:, 2:258], fw(2), chv[:, cv + C_FB + fc:cv + C_FB + fc + 1], ALU.mult, ALU.add,
                       [hgp[sl], chv], [cf[sl]])
                    STT(cf[sl][:, :], hgp[sl][:, 1:257], fw(1), cf[sl][:, :], ALU.mult, ALU.add, [hgp[sl], cf[sl], chv], [cf[sl]])
                    STT(cf[sl][:, :], hgp[sl][:, 0:256], fw(0), cf[sl][:, :], ALU.mult, ALU.add, [hgp[sl], cf[sl], chv], [cf[sl]])
                    ACT(gl[sl][:, :], cf[sl][:, :], AF.Gelu_apprx_tanh, [cf[sl]], [gl[sl]])
                    TT("dve", actT[:, fc, :], gl[sl][:, :], bank(bu_)[:, 0:256], ALU.mult, [gl[sl], PSB[bu_]], [actT_b[fc]])
                for ch in range(6):
                    wd = wdn[wd_rr % 2]
                    wd_rr += 1
                    S.dma("sp", wd[:, :, :].rearrange("p a b -> p (a b)"), wdn_bf[l][:, ch * 4096:(ch + 1) * 4096], [wdn_buf[l]], [wd])
                    for f6 in range(4):
                        fc = ch * 4 + f6
                        for tl in range(2):
                            for h in range(2):
                                MM(pair(tl)[:, h * 512:(h + 1) * 512], actT[:, fc, tl * 128:(tl + 1) * 128], wd[:, f6, h * 512:(h + 1) * 512],
                                   fc == 0, fc == NFC - 1, [actT_b[fc], wd], [PSB[2 * tl + h]], sig=(f6 == 3 and tl == 1 and h == 1))
                for tl in range(2):
                    tt = gi * 2 + tl
                    sl = tt % 2
                    STT(zt[sl][:, :], x1r[:, tl, :], ALPHA, pair(tl), ALU.mult, ALU.add, [x1r, PSB[2 * tl], PSB[2 * tl + 1]], [zt[sl]])
                    ln_rows("pool", zt[sl], stats[sl], mv[sl], rs[sl], g2, b2, x2t[sl][:, :], [], [x2t[sl]])
                    if last:
                        S.dma("sp", y_d[tt * 128:(tt + 1) * 128, :], x2t[sl][:, :], [x2t[sl]], [y_b[tt]])
                    else:
                        S.dma("sp", xres2[tt * 128:(tt + 1) * 128, :], x2t[sl][:, :], [x2t[sl]], [xres2_b[tt]])
                        xg2 = x2Tg[0]
                        pt = 2 + sl
                        for kc in range(8):
                            TR(pair(pt)[:, kc * 128:(kc + 1) * 128], x2t[sl][:, kc * 128:(kc + 1) * 128], ident_f[:, :], [x2t[sl], ident_f],
                               [PSB[2 * pt], PSB[2 * pt + 1]], signal=(kc == 7))
                        CP("act", xg2[:, :, (tt % 4) * 128:(tt % 4 + 1) * 128], pair(pt).rearrange("p (a b) -> p a b", a=8),
                           [PSB[2 * pt], PSB[2 * pt + 1]], [xg2])
                        if tt % 4 == 3:
                            c = tt // 4
                            S.dma("sp", xT_d[:, :, c * 512:(c + 1) * 512], xg2[:, :, :], [xg2], [xTd_b[c]])
            if dbg and not last:
                o = nc.dram_tensor("d_x2", [SEQ, D], F32, kind="ExternalOutput").ap()
                for q4 in range(4):
                    S.dma("sp", o[q4 * 1024:(q4 + 1) * 1024, :], xres2[q4 * 1024:(q4 + 1) * 1024, :], xres2_b, [Buf("dbg")])
                o = nc.dram_tensor("d_xTd", [128, 8, SEQ], BF16, kind="ExternalOutput").ap()
                for q4 in range(8):
                    S.dma("sp", o[:, q4, :], xT_d[:, q4, :], xTd_b, [Buf("dbg")])
            S.barrier()

        S.barrier()
        S.replay()
    return nc


def _prep_weights(inp):
    f = np.float32
    w_in = np.asarray(inp["w_in"], f)
    L = w_in.shape[0]
    cgs = [0, 1, 2, 4, 5, 6, 7, 8, 9]
    w6 = w_in.reshape(L, 8, 128, 10, 8, 128)
    w_in_g = np.ascontiguousarray(w6[:, :, :, cgs, :, :].transpose(0, 4, 2, 3, 1, 5)).reshape(L, 8, 128, 9 * 1024)
    w_sv = np.ascontiguousarray(w6[:, :, :, 3, :, :].transpose(0, 2, 1, 3, 4)).reshape(L, 128, 8 * 1024)
    w_out = np.asarray(inp["w_out"], f).reshape(L, 8, 128, 1024)
    w_outr = np.ascontiguousarray(w_out.transpose(0, 2, 1, 3)).reshape(L, 128, 8 * 1024)
    w_up = np.asarray(inp["w_ffn_up"], f).reshape(L, 8, 128, 2, NFC, 128)
    w_upr = np.ascontiguousarray(w_up.transpose(0, 2, 4, 3, 1, 5)).reshape(L, 128, NFC * 2048)
    w_dn = np.asarray(inp["w_ffn_down"], f).reshape(L, NFC, 128, 1024)
    w_dnr = np.ascontiguousarray(w_dn.transpose(0, 2, 1, 3)).reshape(L, 128, NFC * 1024)
    wr = np.asarray(inp["w_rgate"], f)
    wi = np.asarray(inp["w_igate"], f)
    rgw = np.ascontiguousarray(np.stack([wr, wi], axis=2).transpose(0, 3, 1, 2, 4)).reshape(L, 128, 2048)
    wsp = np.asarray(inp["w_spatial"], f)
    wspT = np.ascontiguousarray(wsp.transpose(0, 3, 1, 2)).reshape(L, 128, 1024)
    chv = np.zeros((128, L * NCH), f)

    def pc(v, n):
        return np.asarray(v, f).reshape(n, 128).T

    for l in range(L):
        o = l * NCH
        for k in range(4):
            chv[:, o + C_CW + k * 8:o + C_CW + (k + 1) * 8] = pc(inp["conv_rg_w"][l][k], 8)
        chv[:, o + C_CB:o + C_CB + 8] = pc(inp["conv_rg_b"][l], 8)
        chv[:, o + C_BR:o + C_BR + 8] = pc(inp["b_rgate"][l], 8)
        chv[:, o + C_BI:o + C_BI + 8] = pc(inp["b_igate"][l], 8)
        chv[:, o + C_LAM:o + C_LAM + 8] = pc(inp["lru_lambda"][l], 8)
        for k in range(3):
            chv[:, o + C_FW + k * NFC:o + C_FW + (k + 1) * NFC] = pc(inp["conv_ffn_w"][l][k], NFC)
        chv[:, o + C_FB:o + C_FB + NFC] = pc(inp["conv_ffn_b"][l], NFC)
    bsp = np.ascontiguousarray(np.asarray(inp["b_spatial"], f).reshape(L, 1024))
    tokv = np.ascontiguousarray(np.stack([np.asarray(inp[k], f) for k in
                                          ("sgu_ln_g", "sgu_ln_b", "ln_mix_g", "ln_mix_b", "ln_ffn_g", "ln_ffn_b")], axis=1))
    return dict(w_in_g=w_in_g, w_sv=w_sv, w_outr=w_outr, w_upr=w_upr, w_dnr=w_dnr, rgw=rgw, wspT=wspT, chv=chv, bsp=bsp, tokv=tokv)


_CACHE = {}


def kernel(**inputs):
    x = np.asarray(inputs["x"], np.float32)
    B = x.shape[0]
    wts = _prep_weights(inputs)
    if "nc" not in _CACHE:
        _CACHE["nc"] = build_program()
    nc = _CACHE["nc"]
    in_maps = []
    for b in range(B):
        m = {"x": np.ascontiguousarray(x[b])}
        m.update(wts)
        in_maps.append(m)
    res = run_bass_kernel_spmd(nc, in_maps, core_ids=list(range(B)))
    return np.stack([np.asarray(r["y"], np.float32) for r in res.results], axis=0)
```

```python
from contextlib import ExitStack
import numpy as np
import concourse.bass as bass
import concourse.mybir as mybir
from concourse.bass_utils import run_bass_kernel_spmd

F32 = mybir.dt.float32
BF16 = mybir.dt.bfloat16
AF = mybir.ActivationFunctionType
ALU = mybir.AluOpType

ENGS = ("pe", "act", "dve", "pool", "sp")
SAME_ENGINE_SYNC = True

D = 1024
SEQ = 4096
NT = 32
KC = 8
DEPTH = 2
DFF = 3072
NFC = 24
ALPHA = float((2 * DEPTH) ** 0.25)
EPS = 1e-5
ATT_SCALE = float(128 ** -0.5)
NEGBIG = -30000.0
C_CW, C_CB, C_BR, C_BI, C_LAM, C_FW, C_FB, NCH = 0, 32, 40, 48, 56, 64, 136, 160
CG_AX, CG_AG, CG_SU, CG_Q, CG_K, CG_V, CG_GA, CG_GS, CG_GM = range(9)


class Buf:
    __slots__ = ("w", "r", "name")

    def __init__(self, name=""):
        self.w = {}
        self.r = {}
        self.name = name


class Tile:
    def __init__(self, ap, name):
        self.ap = ap
        self.buf = Buf(name)
        self.name = name

    def __getitem__(self, k):
        return self.ap[k]


def _bufs(lst):
    out = []
    for x in lst:
        if x is None:
            continue
        out.append(x.buf if isinstance(x, Tile) else x)
    return out


class Sched:
    def __init__(self, nc, es):
        self.nc = nc
        self.es = es
        self.streams = {e: [] for e in ENGS}
        self.cnt = {}
        self.sem = {}
        for e in ("pe", "act", "dve", "pool"):
            self.sem[e] = es.enter_context(nc.semaphore("sem_" + e))
            self.cnt[e] = 0
        self.waited = {e: {} for e in ENGS}
        self.dma_pool = []
        self.dma_rr = 0
        self.dma_cnt = {}

    def new_dma_sem(self, name):
        s = self.es.enter_context(self.nc.semaphore(name))
        self.dma_cnt[id(s)] = [s, 0]
        return s

    def _pool_sem(self):
        if len(self.dma_pool) < 32:
            s = self.new_dma_sem("dq%d" % len(self.dma_pool))
            self.dma_pool.append(s)
            return s
        s = self.dma_pool[self.dma_rr % len(self.dma_pool)]
        self.dma_rr += 1
        return s

    def _deps(self, reads, writes):
        deps = {}

        def add(d):
            for k, v in d.items():
                if deps.get(k, (None, 0))[1] < v[1]:
                    deps[k] = v

        for b in reads:
            add(b.w)
        for b in writes:
            add(b.w)
            add(b.r)
        return deps

    def _emit_waits(self, eng, deps):
        own = self.sem.get(eng)
        for k, (s, v) in deps.items():
            if own is not None and s is own and (eng == "pe" or not SAME_ENGINE_SYNC):
                continue
            if self.waited[eng].get(k, 0) >= v:
                continue
            self.waited[eng][k] = v
            self.streams[eng].append(("wait", s, v))

    def _record(self, tok, reads, writes):
        k = id(tok[0])
        for b in reads:
            if b.r.get(k, (None, 0))[1] < tok[1]:
                b.r[k] = tok
        for b in writes:
            if b.w.get(k, (None, 0))[1] < tok[1]:
                b.w[k] = tok

    def op(self, eng, fn, reads=(), writes=(), signal=True):
        reads = _bufs(reads)
        writes = _bufs(writes)
        self._emit_waits(eng, self._deps(reads, writes))
        s = self.sem[eng]
        if signal:
            self.cnt[eng] += 1
            tok = (s, self.cnt[eng])
        else:
            tok = (s, self.cnt[eng] + 1)
        self.streams[eng].append(("op", fn, s if signal else None))
        self._record(tok, reads, writes)
        return tok

    def dma(self, q, out, in_, reads=(), writes=(), sem=None):
        reads = _bufs(reads)
        writes = _bufs(writes)
        if sem is None:
            if q == "pool":
                if not hasattr(self, "swq"):
                    self.swq = [self.new_dma_sem("swq%d" % i) for i in range(2)]
                    self.swq_rr = 0
                sem = self.swq[self.swq_rr % 2]
                self.swq_rr += 1
            else:
                sem = self._pool_sem()
        ent = self.dma_cnt[id(sem)]
        deps = self._deps(reads, writes)
        if ent[1] > 0:
            deps[id(sem)] = (sem, max(deps.get(id(sem), (None, 0))[1], ent[1]))
        self._emit_waits(q, deps)
        ent[1] += 16
        tok = (sem, ent[1])
        self.streams[q].append(("dma", out, in_, sem))
        self._record(tok, reads, writes)
        return tok

    def wait_all(self, eng, bufs):
        self._emit_waits(eng, self._deps(_bufs(bufs), []))

    def barrier(self):
        deps = {}
        for e in ("pe", "act", "dve", "pool"):
            if self.cnt[e] > 0:
                deps[id(self.sem[e])] = (self.sem[e], self.cnt[e])
        for k, (s, c) in self.dma_cnt.items():
            if c > 0:
                deps[k] = (s, c)
        for e in ENGS:
            self._emit_waits(e, deps)

    def replay(self):
        nc = self.nc
        streams = self.streams

        def run(e, stream):
            for it in stream:
                if it[0] == "wait":
                    e.wait_ge(it[1], it[2])
                elif it[0] == "op":
                    ins = it[1](e)
                    if it[2] is not None:
                        ins.then_inc(it[2], 1)
                else:
                    e.dma_start(out=it[1], in_=it[2]).then_inc(it[3], 16)

        with nc.Block() as block:
            @block.tensor
            def _(e):
                run(e, streams["pe"])

            @block.scalar
            def _(e):
                run(e, streams["act"])

            @block.vector
            def _(e):
                run(e, streams["dve"])

            @block.gpsimd
            def _(e):
                run(e, streams["pool"])

            @block.sync
            def _(e):
                run(e, streams["sp"])


class Arena:
    def __init__(self, ap, nwords):
        self.ap = ap
        self.n = nwords
        self.off = 0
        self.marks = []

    def alloc(self, name, shape, dtype):
        free = 1
        for s in shape[1:]:
            free *= s
        words = free if dtype == F32 else (free + 1) // 2
        assert self.off + words <= self.n, (name, self.off, words, self.n)
        v = self.ap[:, self.off:self.off + words]
        self.off += words
        if dtype != F32:
            v = v.bitcast(dtype)
            if (free % 2) == 1:
                v = v[:, 0:free]
        if len(shape) == 3:
            v = v.rearrange("p (a b) -> p a b", a=shape[1])
        elif len(shape) == 4:
            v = v.rearrange("p (a b c) -> p a b c", a=shape[1], b=shape[2])
        elif len(shape) == 5:
            v = v.rearrange("p (a b c d) -> p a b c d", a=shape[1], b=shape[2], c=shape[3])
        if shape[0] < 128:
            v = v[0:shape[0]]
        return Tile(v, name)

    def mark(self):
        return self.off

    def reset(self, m):
        self.off = m


def build_program(n_layers=DEPTH, dbg=False):
    nc = bass.Bass("TRN2", target_bir_lowering=False)

    def din(name, shape, dt=F32):
        return nc.dram_tensor(name, list(shape), dt, kind="ExternalInput").ap()

    def dscr(name, shape, dt):
        return nc.dram_tensor(name, list(shape), dt).ap()

    x_d = din("x", [SEQ, D])
    w_in_g = din("w_in_g", [DEPTH, 8, 128, 9 * 1024])
    w_sv = din("w_sv", [DEPTH, 128, 8 * 1024])
    w_outr = din("w_outr", [DEPTH, 128, 8 * 1024])
    w_upr = din("w_upr", [DEPTH, 128, NFC * 2048])
    w_dnr = din("w_dnr", [DEPTH, 128, NFC * 1024])
    rgw_d = din("rgw", [DEPTH, 128, 2048])
    wspT_d = din("wspT", [DEPTH, 128, 1024])
    chv_d = din("chv", [128, DEPTH * NCH])
    bsp_d = din("bsp", [DEPTH, 1024])
    tokv_d = din("tokv", [DEPTH, 6, 1024])
    y_d = nc.dram_tensor("y", [SEQ, D], F32, kind="ExternalOutput").ap()

    xres1 = dscr("xres1", [SEQ, D], F32)
    xres2 = dscr("xres2", [SEQ, D], F32)
    vln_d = dscr("vln_d", [SEQ, D], BF16)
    mrg_d = dscr("mrg_d", [128, 8, SEQ], BF16)
    xT_d = dscr("xT_d", [128, 8, SEQ], BF16)
    x1T_d = dscr("x1T_d", [128, 8, SEQ], BF16)
    wdn_bf = dscr("wdn_bf", [DEPTH, 128, NFC * 1024], BF16)
    dbg_out = {}
    if dbg:
        dbg_out["d_mrg"] = nc.dram_tensor("d_mrg", [128, 8, SEQ], BF16, kind="ExternalOutput").ap()
        dbg_out["d_x1"] = nc.dram_tensor("d_x1", [SEQ, D], F32, kind="ExternalOutput").ap()
        dbg_out["d_x1_0"] = nc.dram_tensor("d_x1_0", [SEQ, D], F32, kind="ExternalOutput").ap()
        dbg_out["d_mrg_0"] = nc.dram_tensor("d_mrg_0", [128, 8, SEQ], BF16, kind="ExternalOutput").ap()
        dbg_out["d_vln"] = nc.dram_tensor("d_vln", [SEQ, D], BF16, kind="ExternalOutput").ap()

    with ExitStack() as es:
        S = Sched(nc, es)
        NW = 53000
        arena_t = es.enter_context(nc.sbuf_tensor("arena", [128, NW], F32))
        AR = Arena(arena_t[:, :], NW)
        psum_t = [es.enter_context(nc.psum_tensor("ps%d" % i, [128, 1024], F32)) for i in range(4)]
        PSB = [Buf("bank%d" % i) for i in range(8)]

        def bank(i):
            return psum_t[i // 2][:, (i % 2) * 512:(i % 2) * 512 + 512]

        def pair(i):
            return psum_t[i][:, :]

        def MM(out, lhsT, rhs, start, stop, r, w, sig=False):
            S.op("pe", lambda e: e.matmul(out, lhsT=lhsT, rhs=rhs, start=start, stop=stop), r, w, signal=(stop or sig))

        def TR(out, in_, ident, r, w, signal=True):
            S.op("pe", lambda e: e.transpose(out=out, in_=in_, identity=ident), r, w, signal=signal)

        def ACT(out, in_, func, r, w, scale=1.0, bias=None, accum=None):
            def f(e):
                kw = {}
                if bias is not None:
                    kw["bias"] = bias
                if accum is not None:
                    kw["accum_out"] = accum
                return e.activation(out=out, in_=in_, func=func, scale=scale, **kw)
            S.op("act", f, r, w)

        def TT(eng, out, in0, in1, op, r, w):
            S.op(eng, lambda e: e.tensor_tensor(out=out, in0=in0, in1=in1, op=op), r, w)

        def TS(eng, out, in0, s1, s2, op0, op1, r, w):
            if s2 is None:
                S.op(eng, lambda e: e.tensor_scalar(out=out, in0=in0, scalar1=s1, scalar2=None, op0=op0), r, w)
            else:
                S.op(eng, lambda e: e.tensor_scalar(out=out, in0=in0, scalar1=s1, scalar2=s2, op0=op0, op1=op1), r, w)

        def STT(out, in0, scalar, in1, op0, op1, r, w):
            S.op("dve", lambda e: e.scalar_tensor_tensor(out=out, in0=in0, scalar=scalar, in1=in1, op0=op0, op1=op1), r, w)

        def CP(eng, out, in_, r, w):
            if eng == "act":
                S.op("act", lambda e: e.activation(out=out, in_=in_, func=AF.Copy), r, w)
            else:
                S.op(eng, lambda e: e.tensor_copy(out=out, in_=in_), r, w)

        def MEMSET(eng, ap, val, w):
            S.op(eng, lambda e: e.memset(ap, val), [], w)

        ones_f = AR.alloc("ones_f", [128, 128], F32)
        ident_f = AR.alloc("ident_f", [128, 128], F32)
        triu_f = AR.alloc("triu_f", [128, 128], F32)
        ident_b = AR.alloc("ident_b", [128, 128], BF16)
        triu_b = AR.alloc("triu_b", [128, 128], BF16)
        ones_b = AR.alloc("ones_b", [128, 128], BF16)
        onesrc = AR.alloc("onesrc", [128, 2048], BF16)
        sel = AR.alloc("sel", [128, 16, 128], BF16)
        cb = AR.alloc("cb", [128, 32, 16], F32)
        chv = AR.alloc("chv", [128, DEPTH * NCH], F32)
        der = AR.alloc("der", [128, DEPTH * 40], F32)
        cst = AR.alloc("cst", [128, 8], F32)
        PERS = [ones_f, ident_f, triu_f, ident_b, triu_b, ones_b, sel, cb, chv, der, cst]

        MEMSET("pool", ones_f[:, :], 1.0, [ones_f])
        S.op("pool", lambda e: e.affine_select(out=ident_f[:, :], in_=ones_f[:, :], pattern=[[1, 128]], compare_op=ALU.is_equal,
                                               fill=0.0, base=0, channel_multiplier=-1), [ones_f], [ident_f])
        S.op("pool", lambda e: e.affine_select(out=triu_f[:, :], in_=ones_f[:, :], pattern=[[1, 128]], compare_op=ALU.is_ge,
                                               fill=0.0, base=0, channel_multiplier=-1), [ones_f], [triu_f])
        CP("dve", ident_b[:, :], ident_f[:, :], [ident_f], [ident_b])
        CP("dve", triu_b[:, :], triu_f[:, :], [triu_f], [triu_b])
        CP("dve", ones_b[:, :], ones_f[:, :], [ones_f], [ones_b])
        MEMSET("pool", onesrc[:, :], 1.0, [onesrc])
        S.op("pool", lambda e: e.affine_select(out=sel[:, :, :], in_=onesrc[:, :].rearrange("p (a b) -> p a b", a=16),
                                               pattern=[[1, 16], [0, 128]], compare_op=ALU.is_equal, fill=0.0, base=0,
                                               channel_multiplier=-1), [onesrc], [sel])
        MEMSET("pool", cb[:, :, :], -1e30, [cb])
        for b in range(1, 16):
            MEMSET("pool", cb[:, 2 * b:2 * b + 2, 0:b], 0.0, [cb])
        MEMSET("pool", cst[:, 0:1], 1.0, [cst])
        MEMSET("pool", cst[:, 1:2], EPS, [cst])
        MEMSET("pool", cst[:, 2:3], -0.5, [cst])
        MEMSET("pool", cst[:, 3:4], 0.5, [cst])
        S.dma("sp", chv[:, :], chv_d[:, :], [], [chv])
        for l in range(n_layers):
            cv = l * NCH
            dv = l * 40
            TS("dve", der[:, dv:dv + 16], chv[:, cv + C_BR:cv + C_BR + 16], 0.5, None, ALU.mult, None, [chv], [der])
            ACT(der[:, dv + 32:dv + 40], chv[:, cv + C_LAM:cv + C_LAM + 8], AF.Exp, [chv], [der], scale=-1.0)
            ACT(der[:, dv + 32:dv + 40], der[:, dv + 32:dv + 40], AF.Ln, [der, cst], [der], bias=cst[:, 0:1])
            TS("dve", der[:, dv + 16:dv + 24], der[:, dv + 32:dv + 40], -4.0, None, ALU.mult, None, [der], [der])
            TS("dve", der[:, dv + 24:dv + 32], der[:, dv + 32:dv + 40], -8.0, None, ALU.mult, None, [der], [der])

        pers_mark = AR.mark()

        def DUMP(name, ap, reads):
            if not dbg:
                return
            o = nc.dram_tensor(name, list(ap.shape), ap.dtype, kind="ExternalOutput").ap()
            if len(ap.shape) == 3:
                for i in range(ap.shape[1]):
                    S.dma("sp", o[:, i, :], ap[:, i, :], reads, [Buf("dbg")])
            else:
                S.dma("sp", o[:, :], ap, reads, [Buf("dbg")])

        wdn_buf = [Buf("wdn%d" % l) for l in range(DEPTH)]
        for l in range(n_layers):
            for c in range(4):
                S.dma("pool", wdn_bf[l][:, c * 6144:(c + 1) * 6144].rearrange("p (a b) -> p a b", b=1024),
                      w_dnr[l][:, c * 6144:(c + 1) * 6144].rearrange("p (a b) -> p a b", b=1024), [], [wdn_buf[l]])

        vln_b = [Buf("vln%d" % t) for t in range(NT)]
        mrg_b = [Buf("mrg%d" % g) for g in range(8)]
        xres1_b = [Buf("xr1_%d" % t) for t in range(NT)]
        xres2_b = [Buf("xr2_%d" % t) for t in range(NT)]
        x1T_b = [Buf("x1T%d" % t) for t in range(8)]
        xTd_b = [Buf("xTd%d" % t) for t in range(8)]
        y_b = [Buf("y%d" % t) for t in range(NT)]

        def ln_rows(mode, zt, stats, mv, rs, gbc, bbc, out_ap, r_extra, w_out, tmp2=None):
            S.op("dve", lambda e: e.bn_stats(out=stats[:, 0, :], in_=zt[:, 0:512]), [zt], [stats])
            S.op("dve", lambda e: e.bn_stats(out=stats[:, 1, :], in_=zt[:, 512:1024]), [zt], [stats])
            S.op("dve", lambda e: e.bn_aggr(out=mv[:, :], in_=stats[:, :, :]), [stats], [mv])
            if mode == "pool":
                TS("dve", rs[:, 0:1], mv[:, 1:2], EPS, None, ALU.add, None, [mv], [rs])
                TT("pool", rs[:, 1:2], rs[:, 0:1], cst[:, 2:3], ALU.pow, [rs, cst], [rs])
            else:
                ACT(rs[:, 0:1], mv[:, 1:2], AF.Sqrt, [mv, cst], [rs], bias=cst[:, 1:2])
                S.op("dve", lambda e: e.reciprocal(out=rs[:, 1:2], in_=rs[:, 0:1]), [rs], [rs])
            TS("dve", zt[:, :], zt[:, :], mv[:, 0:1], rs[:, 1:2], ALU.subtract, ALU.mult, [zt, mv, rs], [zt])
            TT("pool", zt[:, :], zt[:, :], gbc[:, :], ALU.mult, [zt, gbc], [zt])
            TT("dve", out_ap, zt[:, :], bbc[:, :], ALU.add, [zt, bbc] + list(r_extra), list(w_out))

        for l in range(n_layers):
            cv = l * NCH
            dv = l * 40
            last = (l == n_layers - 1)
            xin_d = x_d if l == 0 else xres2
            xin_b = [None] * NT if l == 0 else xres2_b

            S.barrier()
            AR.reset(pers_mark)
            xT = AR.alloc("xT", [128, 8, SEQ], BF16)
            xT_b = [Buf("xT%d" % t) for t in range(NT)]
            macc = AR.alloc("macc", [128, SEQ], F32)
            wg = [AR.alloc("wg%d" % i, [128, 9, 8, 128], BF16) for i in range(2)]
            rgw = AR.alloc("rgw", [128, 8, 2, 128], BF16)
            wspb = AR.alloc("wspb", [128, 8, 128], BF16)
            bspbc = AR.alloc("bspbc", [128, 8, 128], F32)
            mix_mark = AR.mark()

            S.dma("pool", rgw[:, :, :, :].rearrange("p a b c -> p (a b c)"), rgw_d[l][:, :], [], [rgw])
            S.dma("sp", bspbc[:, :, :].rearrange("p a b -> p (a b)"), bsp_d[l].partition_broadcast(128), [], [bspbc])

            if l == 0:
                xin = [AR.alloc("xin%d" % i, [128, 1024], F32) for i in range(2)]
                for tt in range(NT):
                    xi = xin[tt % 2]
                    S.dma("sp", xi[:, :], x_d[tt * 128:(tt + 1) * 128, :], [], [xi])
                    for h in range(2):
                        pp = (tt % 2) * 2 + h
                        for j in range(4):
                            TR(pair(pp)[:, j * 128:(j + 1) * 128], xi[:, (h * 4 + j) * 128:(h * 4 + j + 1) * 128], ident_f[:, :],
                               [xi, ident_f], [PSB[2 * pp], PSB[2 * pp + 1]], signal=(j == 3))
                        CP("act" if h == 0 else "dve", xT[:, h * 4:(h + 1) * 4, tt * 128:(tt + 1) * 128],
                           pair(pp)[:, 0:512].rearrange("p (a b) -> p a b", a=4), [PSB[2 * pp], PSB[2 * pp + 1]], [xT_b[tt]])
            else:
                for c in range(8):
                    S.dma("sp", xT[:, :, c * 512:(c + 1) * 512], xT_d[:, :, c * 512:(c + 1) * 512], [xTd_b[c]],
                          [xT_b[4 * c + i] for i in range(4)])
            if l == 0 and False:
                DUMP("d_xT", xT[:, :, :], xT_b)
            S.barrier()
            AR.reset(mix_mark)

            wsv = AR.alloc("wsv", [128, 8, 1024], BF16)
            wspf = AR.alloc("wspf", [128, 8, 128], F32)
            gbc = AR.alloc("gbc", [128, 1024], F32)
            bbc = AR.alloc("bbc", [128, 1024], F32)
            v32 = [AR.alloc("v32_%d" % i, [128, 1024], F32) for i in range(2)]
            vlnb = [AR.alloc("vlnb%d" % i, [128, 1024], BF16) for i in range(2)]
            stats = [AR.alloc("stats%d" % i, [128, 2, 6], F32) for i in range(2)]
            mv = [AR.alloc("mv%d" % i, [128, 2], F32) for i in range(2)]
            rs = [AR.alloc("rs%d" % i, [128, 2], F32) for i in range(2)]
            for kc in range(8):
                S.dma("pool", wsv[:, kc, :], w_sv[l][:, kc * 1024:(kc + 1) * 1024], [], [wsv])
            S.dma("sp", wspf[:, :, :].rearrange("p a b -> p (a b)"), wspT_d[l][:, :], [], [wspf])
            S.dma("sp", gbc[:, :], tokv_d[l][0].partition_broadcast(128), [], [gbc])
            S.dma("sp", bbc[:, :], tokv_d[l][1].partition_broadcast(128), [], [bbc])
            TT("dve", wspb[:, :, :], wspf[:, :, :], triu_f[:, :].unsqueeze(1).broadcast_to([128, 8, 128]), ALU.mult,
               [wspf, triu_f], [wspb])
            def load_wg(g):
                t = wg[g % 2]
                for cg in range(9):
                    S.dma("pool", t[:, cg, :, :].rearrange("p a b -> p (a b)"), w_in_g[l][g][:, cg * 1024:(cg + 1) * 1024], [], [t])
            load_wg(0)
            for tt in range(NT):
                sl = tt % 2
                pp = sl
                for h in range(2):
                    for kc in range(8):
                        MM(pair(pp)[:, h * 512:(h + 1) * 512], xT[:, kc, tt * 128:(tt + 1) * 128], wsv[:, kc, h * 512:(h + 1) * 512],
                           kc == 0, kc == 7, [xT_b[tt], wsv], [PSB[2 * pp + h]])
                ACT(v32[sl][:, :], pair(pp), AF.Gelu_apprx_tanh, [PSB[2 * pp], PSB[2 * pp + 1]], [v32[sl]])
                if l == 0 and tt == 0:
                    DUMP("d_v32", v32[sl][:, :], [v32[sl]])
                    DUMP("d_wsv", wsv[:, :, :], [wsv])
                ln_rows("pool", v32[sl], stats[sl], mv[sl], rs[sl], gbc, bbc, vlnb[sl][:, :], [], [vlnb[sl]])
                S.dma("sp", vln_d[tt * 128:(tt + 1) * 128, :], vlnb[sl][:, :], [vlnb[sl]], [vln_b[tt]])
            S.barrier()
            AR.reset(mix_mark)
            g_mark = AR.mark()

            for g in range(8):
                wt = wg[g % 2]
                if g + 1 < 8:
                    load_wg(g + 1)

                def proj(cg, bk, t0, n):
                    xb = [xT_b[t] for t in range(t0 // 128, (t0 + n + 127) // 128)]
                    for kc in range(8):
                        MM(bank(bk)[:, 0:n], wt[:, cg, kc, :], xT[:, kc, t0:t0 + n], kc == 0, kc == 7, [wt] + xb, [PSB[bk]])

                AR.reset(g_mark)
                axp = [AR.alloc("axp%d" % i, [128, 516], F32) for i in range(2)]
                cc = [AR.alloc("cc%d" % i, [128, 512], F32) for i in range(2)]
                ccb = [AR.alloc("ccb%d" % i, [128, 512], BF16) for i in range(2)]
                tr_ = [AR.alloc("tr%d" % i, [128, 512], F32) for i in range(2)]
                ti_ = [AR.alloc("ti%d" % i, [128, 512], F32) for i in range(2)]
                aa = [AR.alloc("aa%d" % i, [128, 512], F32) for i in range(2)]
                a2 = [AR.alloc("a2%d" % i, [128, 512], F32) for i in range(2)]
                tmp = [AR.alloc("tmp%d" % i, [128, 512], F32) for i in range(2)]
                uu = [AR.alloc("uu%d" % i, [128, 512], F32) for i in range(2)]
                gg = [AR.alloc("gg%d" % i, [128, 512], F32) for i in range(2)]
                tg = [AR.alloc("tg%d" % i, [128, 512], F32) for i in range(2)]
                macc_b = [Buf("macc%d" % c) for c in range(8)]

                def cw(k):
                    return chv[:, cv + C_CW + k * 8 + g:cv + C_CW + k * 8 + g + 1]

                def a_front(c):
                    sl = c % 2
                    bk = c % 2
                    proj(CG_AX, bk, c * 512, 512)
                    CP("act", axp[sl][:, 3:515], bank(bk), [PSB[bk]], [axp[sl]])
                    if c == 0:
                        MEMSET("pool", axp[sl][:, 0:3], 0.0, [axp[sl]])
                    else:
                        CP("pool", axp[sl][:, 0:3], axp[1 - sl][:, 512:515], [axp[1 - sl]], [axp[sl]])

                def a_back(c):
                    sl = c % 2
                    t0 = c * 512
                    TS("dve", cc[sl][:, :], axp[sl][:, 3:515], cw(3), chv[:, cv + C_CB + g:cv + C_CB + g + 1], ALU.mult, ALU.add,
                       [axp[sl], chv], [cc[sl]])
                    for k in (2, 1, 0):
                        STT(cc[sl][:, :], axp[sl][:, k:k + 512], cw(k), cc[sl][:, :], ALU.mult, ALU.add, [axp[sl], chv, cc[sl]], [cc[sl]])
                    CP("pool", ccb[sl][:, :], cc[sl][:, :], [cc[sl]], [ccb[sl]])
                    br, bi = 2 + sl, 4 + sl
                    MM(bank(br), rgw[:, g, 0, :], ccb[sl][:, :], True, True, [rgw, ccb[sl]], [PSB[br]])
                    MM(bank(bi), rgw[:, g, 1, :], ccb[sl][:, :], True, True, [rgw, ccb[sl]], [PSB[bi]])
                    ACT(tr_[sl][:, :], bank(br), AF.Tanh, [PSB[br], der], [tr_[sl]], scale=0.5, bias=der[:, dv + g:dv + g + 1])
                    ACT(ti_[sl][:, :], bank(bi), AF.Tanh, [PSB[bi], der], [ti_[sl]], scale=0.5, bias=der[:, dv + 8 + g:dv + 8 + g + 1])
                    ACT(aa[sl][:, :], tr_[sl][:, :], AF.Exp, [tr_[sl], der], [aa[sl]], scale=der[:, dv + 16 + g:dv + 16 + g + 1],
                        bias=der[:, dv + 16 + g:dv + 16 + g + 1])
                    ACT(a2[sl][:, :], tr_[sl][:, :], AF.Exp, [tr_[sl], der], [a2[sl]], scale=der[:, dv + 24 + g:dv + 24 + g + 1],
                        bias=der[:, dv + 24 + g:dv + 24 + g + 1])
                    TS("dve", a2[sl][:, :], a2[sl][:, :], 1.0, None, ALU.min, None, [a2[sl]], [a2[sl]])
                    ACT(a2[sl][:, :], a2[sl][:, :], AF.Sqrt, [a2[sl], cst], [a2[sl]], scale=-1.0, bias=cst[:, 0:1])
                    STT(tmp[sl][:, :], ti_[sl][:, :], 1.0, cc[sl][:, :], ALU.add, ALU.mult, [ti_[sl], cc[sl]], [tmp[sl]])
                    STT(uu[sl][:, :], a2[sl][:, :], 0.5, tmp[sl][:, :], ALU.mult, ALU.mult, [a2[sl], tmp[sl]], [uu[sl]])
                    init = 0.0 if c == 0 else macc[:, t0 - 1:t0]
                    rb = [aa[sl], uu[sl]] + ([macc_b[c - 1]] if c > 0 else [])
                    S.op("dve", lambda e: e.tensor_tensor_scan(out=macc[:, t0:t0 + 512], data0=aa[sl][:, :], data1=uu[sl][:, :],
                                                               initial=init, op0=ALU.mult, op1=ALU.add), rb, [macc_b[c]])

                for c in range(9):
                    if c < 8:
                        a_front(c)
                    if c >= 1:
                        a_back(c - 1)
                for c in range(8):
                    sl = c % 2
                    t0 = c * 512
                    b1, b2 = 6, 7
                    proj(CG_AG, b1, t0, 512)
                    ACT(gg[sl][:, :], bank(b1), AF.Gelu_apprx_tanh, [PSB[b1]], [gg[sl]])
                    proj(CG_GA, b2, t0, 512)
                    ACT(tg[sl][:, :], bank(b2), AF.Tanh, [PSB[b2]], [tg[sl]], scale=0.5)
                    TT("dve", gg[sl][:, :], gg[sl][:, :], macc[:, t0:t0 + 512], ALU.mult, [gg[sl], macc_b[c]], [gg[sl]])
                    STT(macc[:, t0:t0 + 512], tg[sl][:, :], 1.0, gg[sl][:, :], ALU.add, ALU.mult, [tg[sl], gg[sl]], [macc_b[c]])
                S.barrier()

                AR.reset(g_mark)
                vlng = AR.alloc("vlng", [128, 32, 128], BF16)
                gu = [AR.alloc("gu%d" % i, [128, 512], F32) for i in range(2)]
                tgs = [AR.alloc("tgs%d" % i, [128, 512], F32) for i in range(2)]
                m1 = [AR.alloc("m1_%d" % i, [128, 512], F32) for i in range(2)]
                for q4 in range(4):
                    S.dma("sp", vlng[:, q4 * 8:(q4 + 1) * 8, :],
                          vln_d[q4 * 1024:(q4 + 1) * 1024, g * 128:(g + 1) * 128].rearrange("(n p) c -> p n c", p=128),
                          [vln_b[t] for t in range(q4 * 8, q4 * 8 + 8)], [vlng])
                for c in range(8):
                    sl = c % 2
                    t0 = c * 512
                    bu, bg, bm = 0 + sl, 2 + sl, 4 + sl
                    proj(CG_SU, bu, t0, 512)
                    ACT(gu[sl][:, :], bank(bu), AF.Gelu_apprx_tanh, [PSB[bu]], [gu[sl]])
                    proj(CG_GS, bg, t0, 512)
                    ACT(tgs[sl][:, :], bank(bg), AF.Tanh, [PSB[bg]], [tgs[sl]], scale=0.5)
                    for n in range(4):
                        MM(bank(bm)[:, n * 128:(n + 1) * 128], vlng[:, 4 * c + n, :], wspb[:, g, :], True, True, [vlng, wspb], [PSB[bm]])
                    TT("dve", m1[sl][:, :].rearrange("p (a b) -> p a b", a=4), bank(bm).rearrange("p (a b) -> p a b", a=4),
                       bspbc[:, g, :].unsqueeze(1).broadcast_to([128, 4, 128]), ALU.add, [PSB[bm], bspbc], [m1[sl]])
                    TT("dve", m1[sl][:, :], m1[sl][:, :], gu[sl][:, :], ALU.mult, [m1[sl], gu[sl]], [m1[sl]])
                    STT(m1[sl][:, :], tgs[sl][:, :], 1.0, m1[sl][:, :], ALU.add, ALU.mult, [tgs[sl], m1[sl]], [m1[sl]])
                    TT("dve", macc[:, t0:t0 + 512], macc[:, t0:t0 + 512], m1[sl][:, :], ALU.add, [m1[sl], macc_b[c]], [macc_b[c]])
                S.barrier()

                AR.reset(g_mark)
                qT = AR.alloc("qT", [128, SEQ], BF16)
                kT = AR.alloc("kT", [128, SEQ], BF16)
                Vt = AR.alloc("Vt", [128, 32, 128], BF16)
                negmT = AR.alloc("negmT", [128, SEQ], BF16)
                ksum = AR.alloc("ksum", [128, 16], F32)
                kmT = AR.alloc("kmT", [128, 16], BF16)
                gsb = AR.alloc("gsb", [128, 32, 16], F32)
                mx8 = AR.alloc("mx8", [128, 32, 8], F32)
                thr = AR.alloc("thr", [128, 32], F32)
                negm = AR.alloc("negm", [128, 32, 16], BF16)
                NPT = 6
                PT = [AR.alloc("PT%d" % i, [128, 256], BF16) for i in range(NPT)]
                tgm = [AR.alloc("tgm%d" % i, [128, 256], F32) for i in range(2)]
                rec = [AR.alloc("rec%d" % i, [128, 256], F32) for i in range(2)]
                ot = [AR.alloc("ot%d" % i, [128, 256], F32) for i in range(2)]
                mrgb = [AR.alloc("mrgb%d" % i, [128, 1024], BF16) for i in range(2)]
                MEMSET("pool", negmT[:, :], 0.0, [negmT])
                for c in range(8):
                    t0 = c * 512
                    bq, bk_ = 0 + (c % 2), 2 + (c % 2)
                    proj(CG_K, bk_, t0, 512)
                    for h in range(2):
                        ACT(kT[:, t0 + h * 256:t0 + (h + 1) * 256], bank(bk_)[:, h * 256:(h + 1) * 256], AF.Copy, [PSB[bk_]], [kT, ksum],
                            accum=ksum[:, 2 * c + h:2 * c + h + 1])
                    proj(CG_Q, bq, t0, 512)
                    CP("dve", qT[:, t0:t0 + 512], bank(bq), [PSB[bq]], [qT])
                for t4 in range(8):
                    bv = 4 + (t4 % 2)
                    for j in range(4):
                        tt = t4 * 4 + j
                        for kc in range(8):
                            MM(bank(bv)[:, j * 128:(j + 1) * 128], xT[:, kc, tt * 128:(tt + 1) * 128], wt[:, CG_V, kc, :], kc == 0, kc == 7,
                               [xT_b[tt], wt], [PSB[bv]])
                    CP("act" if t4 % 2 == 0 else "dve", Vt[:, t4 * 4:(t4 + 1) * 4, :], bank(bv).rearrange("p (a b) -> p a b", a=4),
                       [PSB[bv]], [Vt])
                TS("dve", kmT[:, :], ksum[:, :], 1.0 / 256.0, None, ALU.mult, None, [ksum], [kmT])
                bgt = 6
                for qt in range(32):
                    MM(bank(bgt)[:, qt * 16:(qt + 1) * 16], qT[:, qt * 128:(qt + 1) * 128], kmT[:, :], True, True, [qT, kmT], [PSB[bgt]])
                TT("dve", gsb[:, :, :].rearrange("p a b -> p (a b)"), bank(bgt), cb[:, :, :].rearrange("p a b -> p (a b)"), ALU.add,
                   [PSB[bgt], cb], [gsb])
                for qt in range(32):
                    S.op("dve", (lambda qt: lambda e: e.max(out=mx8[:, qt, :], in_=gsb[:, qt, :]))(qt), [gsb], [mx8])
                TS("dve", thr[:, :], mx8[:, :, 2], -1e29, None, ALU.max, None, [mx8], [thr])
                TT("dve", negm[:, :, :], gsb[:, :, :], thr[:, :].unsqueeze(2).broadcast_to([128, 32, 16]), ALU.is_lt, [gsb, thr], [negm])
                TS("dve", negm[:, :, :], negm[:, :, :], NEGBIG, None, ALU.mult, None, [negm], [negm])
                for q8 in range(4):
                    bt = 6 + ((q8 + 1) % 2)
                    tb = bank(bt).bitcast(BF16)
                    for j in range(8):
                        qt = q8 * 8 + j
                        TR(tb[0:16, j * 128:(j + 1) * 128], negm[:, qt, :], ident_b[:, :], [negm, ident_b], [PSB[bt]], signal=(j == 7))
                    CP("act", negmT[0:16, q8 * 1024:(q8 + 1) * 1024], tb[0:16, :], [PSB[bt]], [negmT])
                flat = []
                for b in range(16):
                    tl_ = [("past", kt) for kt in range(2 * b)] + [("own0", 2 * b), ("own1", 2 * b + 1)]
                    for i, (kind, kt) in enumerate(tl_):
                        flat.append((b, kind, kt, i == 0, i == len(tl_) - 1))
                nfl = len(flat)

                def emit_score(i):
                    b, kind, kt, first, lastt = flat[i]
                    q0 = b * 256
                    bs = i % 3
                    P = PT[i % NPT]
                    if kind == "past":
                        j = kt // 2
                        MM(bank(bs)[:, 0:256], kT[:, kt * 128:(kt + 1) * 128], qT[:, q0:q0 + 256], True, False, [kT, qT], [PSB[bs]])
                        MM(bank(bs)[:, 0:256], sel[:, j, :], negmT[:, q0:q0 + 256], False, True, [sel, negmT], [PSB[bs]])
                        ACT(P[:, 0:256], bank(bs)[:, 0:256], AF.Exp, [PSB[bs]], [P], scale=ATT_SCALE)
                    elif kind == "own0":
                        MM(bank(bs)[:, 0:256], kT[:, kt * 128:(kt + 1) * 128], qT[:, q0:q0 + 256], True, True, [kT, qT], [PSB[bs]])
                        ACT(P[:, 0:256], bank(bs)[:, 0:256], AF.Exp, [PSB[bs]], [P], scale=ATT_SCALE)
                        TT("pool", P[:, 0:128], P[:, 0:128], triu_b[:, :], ALU.mult, [P, triu_b], [P])
                    else:
                        MM(bank(bs)[:, 0:128], kT[:, kt * 128:(kt + 1) * 128], qT[:, q0 + 128:q0 + 256], True, True, [kT, qT], [PSB[bs]])
                        ACT(P[:, 0:128], bank(bs)[:, 0:128], AF.Exp, [PSB[bs]], [P], scale=ATT_SCALE)
                        TT("pool", P[:, 0:128], P[:, 0:128], triu_b[:, :], ALU.mult, [P, triu_b], [P])

                def emit_pv(i):
                    b, kind, kt, first, lastt = flat[i]
                    q0 = b * 256
                    P = PT[i % NPT]
                    bo = 3 if b % 2 == 0 else 6
                    bl = 4 if b % 2 == 0 else 7
                    if kind == "own1":
                        MM(bank(bo)[:, 128:256], Vt[:, kt, :], P[:, 0:128], False, True, [Vt, P], [PSB[bo]])
                        MM(bank(bl)[:, 128:256], ones_b[:, :], P[:, 0:128], False, True, [ones_b, P], [PSB[bl]])
                    else:
                        MM(bank(bo)[:, 0:256], Vt[:, kt, :], P[:, 0:256], first, False, [Vt, P], [PSB[bo]])
                        MM(bank(bl)[:, 0:256], ones_b[:, :], P[:, 0:256], first, False, [ones_b, P], [PSB[bl]])
                    if lastt:
                        sl = b % 2
                        bgm = 5
                        proj(CG_GM, bgm, q0, 256)
                        ACT(tgm[sl][:, :], bank(bgm)[:, 0:256], AF.Tanh, [PSB[bgm]], [tgm[sl]], scale=0.5)
                        S.op("dve", lambda e: e.reciprocal(out=rec[sl][:, :], in_=bank(bl)[:, 0:256]), [PSB[bl]], [rec[sl]])
                        TT("dve", ot[sl][:, :], bank(bo)[:, 0:256], rec[sl][:, :], ALU.mult, [PSB[bo], rec[sl]], [ot[sl]])
                        STT(ot[sl][:, :], tgm[sl][:, :], 1.0, ot[sl][:, :], ALU.add, ALU.mult, [tgm[sl], ot[sl]], [ot[sl]])
                        mb = macc_b[b // 2]
                        TT("dve", macc[:, q0:q0 + 256], macc[:, q0:q0 + 256], ot[sl][:, :], ALU.add, [ot[sl], mb], [mb])

                LOOK = 2
                for i in range(min(LOOK, nfl)):
                    emit_score(i)
                for i in range(nfl):
                    if i + LOOK < nfl:
                        emit_score(i + LOOK)
                    emit_pv(i)
                for c4 in range(4):
                    mt = mrgb[c4 % 2]
                    TS("pool", mt[:, :], macc[:, c4 * 1024:(c4 + 1) * 1024], 0.5, None, ALU.mult, None,
                       [macc_b[2 * c4], macc_b[2 * c4 + 1]], [mt])
                    S.dma("sp", mrg_d[:, g, c4 * 1024:(c4 + 1) * 1024], mt[:, :], [mt], [mrg_b[g]])
                S.barrier()

            if dbg and l == 0:
                S.dma("sp", dbg_out["d_mrg_0"][:, :, :], mrg_d[:, :, :], mrg_b, [Buf("dbg")])
            if dbg and l == n_layers - 1:
                S.dma("sp", dbg_out["d_mrg"][:, :, :], mrg_d[:, :, :], mrg_b, [Buf("dbg")])
                S.dma("sp", dbg_out["d_vln"][:, :], vln_d[:, :], vln_b, [Buf("dbg")])

            S.barrier()
            AR.reset(pers_mark)
            wup = AR.alloc("wup", [128, NFC, 2, 8, 128], BF16)
            f_mark = AR.mark()
            for fc in range(NFC):
                S.dma("pool", wup[:, fc, :, :, :].rearrange("p a b c -> p a (b c)"),
                      w_upr[l][:, fc * 2048:(fc + 1) * 2048].rearrange("p (a c) -> p a c", c=1024), [], [wup])
            woutb = AR.alloc("woutb", [128, 8, 1024], BF16)
            g1 = AR.alloc("g1", [128, 1024], F32)
            b1 = AR.alloc("b1", [128, 1024], F32)
            mt_ = [AR.alloc("mt%d" % i, [128, 8, 512], BF16) for i in range(2)]
            x1Tg = [AR.alloc("x1Tg%d" % i, [128, 8, 512], BF16) for i in range(2)]
            xr = [AR.alloc("xr%d" % i, [128, 1024], F32) for i in range(2)]
            zt = [AR.alloc("zt%d" % i, [128, 1024], F32) for i in range(2)]
            x1t = [AR.alloc("x1t%d" % i, [128, 1024], F32) for i in range(2)]
            stats = [AR.alloc("stats%d" % i, [128, 2, 6], F32) for i in range(2)]
            mv = [AR.alloc("mv%d" % i, [128, 2], F32) for i in range(2)]
            rs = [AR.alloc("rs%d" % i, [128, 2], F32) for i in range(2)]
            for kc in range(8):
                S.dma("pool", woutb[:, kc, :], w_outr[l][:, kc * 1024:(kc + 1) * 1024], [], [woutb])
            S.dma("sp", g1[:, :], tokv_d[l][2].partition_broadcast(128), [], [g1])
            S.dma("sp", b1[:, :], tokv_d[l][3].partition_broadcast(128), [], [b1])
            for c in range(8):
                m = mt_[c % 2]
                xg = x1Tg[c % 2]
                S.dma("sp", m[:, :, :], mrg_d[:, :, c * 512:(c + 1) * 512], mrg_b, [m])
                for j in range(4):
                    tt = c * 4 + j
                    sl = tt % 2
                    pp = sl
                    S.dma("sp", xr[sl][:, :], xin_d[tt * 128:(tt + 1) * 128, :], [xin_b[tt]], [xr[sl]])
                    for h in range(2):
                        for kc in range(8):
                            MM(pair(pp)[:, h * 512:(h + 1) * 512], m[:, kc, j * 128:(j + 1) * 128], woutb[:, kc, h * 512:(h + 1) * 512],
                               kc == 0, kc == 7, [m, woutb], [PSB[2 * pp + h]])
                    STT(zt[sl][:, :], xr[sl][:, :], ALPHA, pair(pp), ALU.mult, ALU.add, [xr[sl], PSB[2 * pp], PSB[2 * pp + 1]], [zt[sl]])
                    ln_rows("act", zt[sl], stats[sl], mv[sl], rs[sl], g1, b1, x1t[sl][:, :], [], [x1t[sl]])
                    S.dma("sp", xres1[tt * 128:(tt + 1) * 128, :], x1t[sl][:, :], [x1t[sl]], [xres1_b[tt]])
                    if dbg and l == 0:
                        S.dma("sp", dbg_out["d_x1_0"][tt * 128:(tt + 1) * 128, :], x1t[sl][:, :], [x1t[sl]], [Buf("dbg")])
                    if dbg and l == n_layers - 1:
                        S.dma("sp", dbg_out["d_x1"][tt * 128:(tt + 1) * 128, :], x1t[sl][:, :], [x1t[sl]], [Buf("dbg")])
                    pt = 2 + sl
                    for kc in range(8):
                        TR(pair(pt)[:, kc * 128:(kc + 1) * 128], x1t[sl][:, kc * 128:(kc + 1) * 128], ident_f[:, :], [x1t[sl], ident_f],
                           [PSB[2 * pt], PSB[2 * pt + 1]], signal=(kc == 7))
                    CP("act", xg[:, :, j * 128:(j + 1) * 128], pair(pt).rearrange("p (a b) -> p a b", a=8), [PSB[2 * pt], PSB[2 * pt + 1]], [xg])
                S.dma("sp", x1T_d[:, :, c * 512:(c + 1) * 512], xg[:, :, :], [xg], [x1T_b[c]])
            S.barrier()

            AR.reset(f_mark)
            g2 = AR.alloc("g2", [128, 1024], F32)
            b2 = AR.alloc("b2", [128, 1024], F32)
            wdn = [AR.alloc("wdn%d" % i, [128, 4, 1024], BF16) for i in range(2)]
            xg_ = [AR.alloc("xg%d" % i, [128, 8, 256], BF16) for i in range(2)]
            xr1 = [AR.alloc("xr1_%d" % i, [128, 2, 1024], F32) for i in range(2)]
            actT = AR.alloc("actT", [128, NFC, 256], BF16)
            actT_b = [Buf("actT%d" % i) for i in range(NFC)]
            hgp = [AR.alloc("hgp%d" % i, [128, 260], F32) for i in range(2)]
            cf = [AR.alloc("cf%d" % i, [128, 256], F32) for i in range(2)]
            gl = [AR.alloc("gl%d" % i, [128, 256], F32) for i in range(2)]
            carry = AR.alloc("carry", [128, NFC, 2], F32)
            zt = [AR.alloc("zt%d" % i, [128, 1024], F32) for i in range(2)]
            x2t = [AR.alloc("x2t%d" % i, [128, 1024], F32) for i in range(2)]
            x2Tg = [AR.alloc("x2Tg%d" % i, [128, 8, 512], BF16) for i in range(1)]
            stats = [AR.alloc("stats%d" % i, [128, 2, 6], F32) for i in range(2)]
            mv = [AR.alloc("mv%d" % i, [128, 2], F32) for i in range(2)]
            rs = [AR.alloc("rs%d" % i, [128, 2], F32) for i in range(2)]
            S.dma("sp", g2[:, :], tokv_d[l][4].partition_broadcast(128), [], [g2])
            S.dma("sp", b2[:, :], tokv_d[l][5].partition_broadcast(128), [], [b2])
            MEMSET("pool", carry[:, :, :], 0.0, [carry])
            wd_rr = 0
            for gi in range(16):
                t0 = gi * 256
                xg = xg_[gi % 2]
                x1r = xr1[gi % 2]
                S.dma("sp", xg[:, :, :], x1T_d[:, :, t0:t0 + 256], [x1T_b[gi // 2]], [xg])
                S.dma("sp", x1r[:, :, :], xres1[t0:t0 + 256, :].rearrange("(a p) d -> p a d", p=128),
                      [xres1_b[2 * gi], xres1_b[2 * gi + 1]], [x1r])
                for fc in range(NFC):
                    sl = fc % 2
                    bg_, bu_ = 4 + sl, 6 + sl
                    for kc in range(8):
                        MM(bank(bg_)[:, 0:256], wup[:, fc, 0, kc, :], xg[:, kc, :], kc == 0, kc == 7, [wup, xg], [PSB[bg_]])
                    for kc in range(8):
                        MM(bank(bu_)[:, 0:256], wup[:, fc, 1, kc, :], xg[:, kc, :], kc == 0, kc == 7, [wup, xg], [PSB[bu_]])
                    CP("pool", hgp[sl][:, 0:2], carry[:, fc, :], [carry], [hgp[sl]])
                    CP("act", hgp[sl][:, 2:258], bank(bg_)[:, 0:256], [PSB[bg_]], [hgp[sl]])
                    CP("pool", carry[:, fc, :], hgp[sl][:, 256:258], [hgp[sl]], [carry])

                    def fw(k):
                        return chv[:, cv + C_FW + k * NFC + fc:cv + C_FW + k * NFC + fc + 1]
                    TS("dve", cf[sl][:, :], hgp[sl][:, 2:258], fw(2), chv[:, cv + C_FB + fc:cv + C_FB + fc + 1], ALU.mult, ALU.add,
                       [hgp[sl], chv], [cf[sl]])
                    STT(cf[sl][:, :], hgp[sl][:, 1:257], fw(1), cf[sl][:, :], ALU.mult, ALU.add, [hgp[sl], cf[sl], chv], [cf[sl]])
                    STT(cf[sl][:, :], hgp[sl][:, 0:256], fw(0), cf[sl][:, :], ALU.mult, ALU.add, [hgp[sl], cf[sl], chv], [cf[sl]])
                    ACT(gl[sl][:, :], cf[sl][:, :], AF.Gelu_apprx_tanh, [cf[sl]], [gl[sl]])
                    TT("dve", actT[:, fc, :], gl[sl][:, :], bank(bu_)[:, 0:256], ALU.mult, [gl[sl], PSB[bu_]], [actT_b[fc]])
                for ch in range(6):
                    wd = wdn[wd_rr % 2]
                    wd_rr += 1
                    S.dma("sp", wd[:, :, :].rearrange("p a b -> p (a b)"), wdn_bf[l][:, ch * 4096:(ch + 1) * 4096], [wdn_buf[l]], [wd])
                    for f6 in range(4):
                        fc = ch * 4 + f6
                        for tl in range(2):
                            for h in range(2):
                                MM(pair(tl)[:, h * 512:(h + 1) * 512], actT[:, fc, tl * 128:(tl + 1) * 128], wd[:, f6, h * 512:(h + 1) * 512],
                                   fc == 0, fc == NFC - 1, [actT_b[fc], wd], [PSB[2 * tl + h]], sig=(f6 == 3 and tl == 1 and h == 1))
                for tl in range(2):
                    tt = gi * 2 + tl
                    sl = tt % 2
                    STT(zt[sl][:, :], x1r[:, tl, :], ALPHA, pair(tl), ALU.mult, ALU.add, [x1r, PSB[2 * tl], PSB[2 * tl + 1]], [zt[sl]])
                    ln_rows("pool", zt[sl], stats[sl], mv[sl], rs[sl], g2, b2, x2t[sl][:, :], [], [x2t[sl]])
                    if last:
                        S.dma("sp", y_d[tt * 128:(tt + 1) * 128, :], x2t[sl][:, :], [x2t[sl]], [y_b[tt]])
                    else:
                        S.dma("sp", xres2[tt * 128:(tt + 1) * 128, :], x2t[sl][:, :], [x2t[sl]], [xres2_b[tt]])
                        xg2 = x2Tg[0]
                        pt = 2 + sl
                        for kc in range(8):
                            TR(pair(pt)[:, kc * 128:(kc + 1) * 128], x2t[sl][:, kc * 128:(kc + 1) * 128], ident_f[:, :], [x2t[sl], ident_f],
                               [PSB[2 * pt], PSB[2 * pt + 1]], signal=(kc == 7))
                        CP("act", xg2[:, :, (tt % 4) * 128:(tt % 4 + 1) * 128], pair(pt).rearrange("p (a b) -> p a b", a=8),
                           [PSB[2 * pt], PSB[2 * pt + 1]], [xg2])
                        if tt % 4 == 3:
                            c = tt // 4
                            S.dma("sp", xT_d[:, :, c * 512:(c + 1) * 512], xg2[:, :, :], [xg2], [xTd_b[c]])
            if dbg and not last:
                o = nc.dram_tensor("d_x2", [SEQ, D], F32, kind="ExternalOutput").ap()
                for q4 in range(4):
                    S.dma("sp", o[q4 * 1024:(q4 + 1) * 1024, :], xres2[q4 * 1024:(q4 + 1) * 1024, :], xres2_b, [Buf("dbg")])
                o = nc.dram_tensor("d_xTd", [128, 8, SEQ], BF16, kind="ExternalOutput").ap()
                for q4 in range(8):
                    S.dma("sp", o[:, q4, :], xT_d[:, q4, :], xTd_b, [Buf("dbg")])
            S.barrier()

        S.barrier()
        S.replay()
    return nc


def _prep_weights(inp):
    f = np.float32
    w_in = np.asarray(inp["w_in"], f)
    L = w_in.shape[0]
    cgs = [0, 1, 2, 4, 5, 6, 7, 8, 9]
    w6 = w_in.reshape(L, 8, 128, 10, 8, 128)
    w_in_g = np.ascontiguousarray(w6[:, :, :, cgs, :, :].transpose(0, 4, 2, 3, 1, 5)).reshape(L, 8, 128, 9 * 1024)
    w_sv = np.ascontiguousarray(w6[:, :, :, 3, :, :].transpose(0, 2, 1, 3, 4)).reshape(L, 128, 8 * 1024)
    w_out = np.asarray(inp["w_out"], f).reshape(L, 8, 128, 1024)
    w_outr = np.ascontiguousarray(w_out.transpose(0, 2, 1, 3)).reshape(L, 128, 8 * 1024)
    w_up = np.asarray(inp["w_ffn_up"], f).reshape(L, 8, 128, 2, NFC, 128)
    w_upr = np.ascontiguousarray(w_up.transpose(0, 2, 4, 3, 1, 5)).reshape(L, 128, NFC * 2048)
    w_dn = np.asarray(inp["w_ffn_down"], f).reshape(L, NFC, 128, 1024)
    w_dnr = np.ascontiguousarray(w_dn.transpose(0, 2, 1, 3)).reshape(L, 128, NFC * 1024)
    wr = np.asarray(inp["w_rgate"], f)
    wi = np.asarray(inp["w_igate"], f)
    rgw = np.ascontiguousarray(np.stack([wr, wi], axis=2).transpose(0, 3, 1, 2, 4)).reshape(L, 128, 2048)
    wsp = np.asarray(inp["w_spatial"], f)
    wspT = np.ascontiguousarray(wsp.transpose(0, 3, 1, 2)).reshape(L, 128, 1024)
    chv = np.zeros((128, L * NCH), f)

    def pc(v, n):
        return np.asarray(v, f).reshape(n, 128).T

    for l in range(L):
        o = l * NCH
        for k in range(4):
            chv[:, o + C_CW + k * 8:o + C_CW + (k + 1) * 8] = pc(inp["conv_rg_w"][l][k], 8)
        chv[:, o + C_CB:o + C_CB + 8] = pc(inp["conv_rg_b"][l], 8)
        chv[:, o + C_BR:o + C_BR + 8] = pc(inp["b_rgate"][l], 8)
        chv[:, o + C_BI:o + C_BI + 8] = pc(inp["b_igate"][l], 8)
        chv[:, o + C_LAM:o + C_LAM + 8] = pc(inp["lru_lambda"][l], 8)
        for k in range(3):
            chv[:, o + C_FW + k * NFC:o + C_FW + (k + 1) * NFC] = pc(inp["conv_ffn_w"][l][k], NFC)
        chv[:, o + C_FB:o + C_FB + NFC] = pc(inp["conv_ffn_b"][l], NFC)
    bsp = np.ascontiguousarray(np.asarray(inp["b_spatial"], f).reshape(L, 1024))
    tokv = np.ascontiguousarray(np.stack([np.asarray(inp[k], f) for k in
                                          ("sgu_ln_g", "sgu_ln_b", "ln_mix_g", "ln_mix_b", "ln_ffn_g", "ln_ffn_b")], axis=1))
    return dict(w_in_g=w_in_g, w_sv=w_sv, w_outr=w_outr, w_upr=w_upr, w_dnr=w_dnr, rgw=rgw, wspT=wspT, chv=chv, bsp=bsp, tokv=tokv)


_CACHE = {}


def kernel(**inputs):
    x = np.asarray(inputs["x"], np.float32)
    B = x.shape[0]
    wts = _prep_weights(inputs)
    if "nc" not in _CACHE:
        _CACHE["nc"] = build_program()
    nc = _CACHE["nc"]
    in_maps = []
    for b in range(B):
        m = {"x": np.ascontiguousarray(x[b])}
        m.update(wts)
        in_maps.append(m)
    res = run_bass_kernel_spmd(nc, in_maps, core_ids=list(range(B)))
    return np.stack([np.asarray(r["y"], np.float32) for r in res.results], axis=0)
```

```python
from contextlib import ExitStack
import numpy as np
import concourse.bass as bass
import concourse.mybir as mybir
from concourse.bass_utils import run_bass_kernel_spmd

F32 = mybir.dt.float32
BF16 = mybir.dt.bfloat16
AF = mybir.ActivationFunctionType
ALU = mybir.AluOpType

ENGS = ("pe", "act", "dve", "pool", "sp")
SAME_ENGINE_SYNC = True

D = 1024
SEQ = 4096
NT = 32
KC = 8
DEPTH = 2
DFF = 3072
NFC = 24
ALPHA = float((2 * DEPTH) ** 0.25)
EPS = 1e-5
ATT_SCALE = float(128 ** -0.5)
NEGBIG = -30000.0
C_CW, C_CB, C_BR, C_BI, C_LAM, C_FW, C_FB, NCH = 0, 32, 40, 48, 56, 64, 136, 160
CG_AX, CG_AG, CG_SU, CG_Q, CG_K, CG_V, CG_GA, CG_GS, CG_GM = range(9)


class Buf:
    __slots__ = ("w", "r", "name")

    def __init__(self, name=""):
        self.w = {}
        self.r = {}
        self.name = name


class Tile:
    def __init__(self, ap, name):
        self.ap = ap
        self.buf = Buf(name)
        self.name = name

    def __getitem__(self, k):
        return self.ap[k]


def _bufs(lst):
    out = []
    for x in lst:
        if x is None:
            continue
        out.append(x.buf if isinstance(x, Tile) else x)
    return out


class Sched:
    def __init__(self, nc, es):
        self.nc = nc
        self.es = es
        self.streams = {e: [] for e in ENGS}
        self.cnt = {}
        self.sem = {}
        for e in ("pe", "act", "dve", "pool"):
            self.sem[e] = es.enter_context(nc.semaphore("sem_" + e))
            self.cnt[e] = 0
        self.waited = {e: {} for e in ENGS}
        self.dma_pool = []
        self.dma_rr = 0
        self.dma_cnt = {}

    def new_dma_sem(self, name):
        s = self.es.enter_context(self.nc.semaphore(name))
        self.dma_cnt[id(s)] = [s, 0]
        return s

    def _pool_sem(self):
        if len(self.dma_pool) < 32:
            s = self.new_dma_sem("dq%d" % len(self.dma_pool))
            self.dma_pool.append(s)
            return s
        s = self.dma_pool[self.dma_rr % len(self.dma_pool)]
        self.dma_rr += 1
        return s

    def _deps(self, reads, writes):
        deps = {}

        def add(d):
            for k, v in d.items():
                if deps.get(k, (None, 0))[1] < v[1]:
                    deps[k] = v

        for b in reads:
            add(b.w)
        for b in writes:
            add(b.w)
            add(b.r)
        return deps

    def _emit_waits(self, eng, deps):
        own = self.sem.get(eng)
        for k, (s, v) in deps.items():
            if own is not None and s is own and (eng == "pe" or not SAME_ENGINE_SYNC):
                continue
            if self.waited[eng].get(k, 0) >= v:
                continue
            self.waited[eng][k] = v
            self.streams[eng].append(("wait", s, v))

    def _record(self, tok, reads, writes):
        k = id(tok[0])
        for b in reads:
            if b.r.get(k, (None, 0))[1] < tok[1]:
                b.r[k] = tok
        for b in writes:
            if b.w.get(k, (None, 0))[1] < tok[1]:
                b.w[k] = tok

    def op(self, eng, fn, reads=(), writes=(), signal=True):
        reads = _bufs(reads)
        writes = _bufs(writes)
        self._emit_waits(eng, self._deps(reads, writes))
        s = self.sem[eng]
        if signal:
            self.cnt[eng] += 1
            tok = (s, self.cnt[eng])
        else:
            tok = (s, self.cnt[eng] + 1)
        self.streams[eng].append(("op", fn, s if signal else None))
        self._record(tok, reads, writes)
        return tok

    def dma(self, q, out, in_, reads=(), writes=(), sem=None):
        reads = _bufs(reads)
        writes = _bufs(writes)
        if sem is None:
            if q == "pool":
                if not hasattr(self, "swq"):
                    self.swq = [self.new_dma_sem("swq%d" % i) for i in range(2)]
                    self.swq_rr = 0
                sem = self.swq[self.swq_rr % 2]
                self.swq_rr += 1
            else:
                sem = self._pool_sem()
        ent = self.dma_cnt[id(sem)]
        deps = self._deps(reads, writes)
        if ent[1] > 0:
            deps[id(sem)] = (sem, max(deps.get(id(sem), (None, 0))[1], ent[1]))
        self._emit_waits(q, deps)
        ent[1] += 16
        tok = (sem, ent[1])
        self.streams[q].append(("dma", out, in_, sem))
        self._record(tok, reads, writes)
        return tok

    def wait_all(self, eng, bufs):
        self._emit_waits(eng, self._deps(_bufs(bufs), []))

    def barrier(self):
        deps = {}
        for e in ("pe", "act", "dve", "pool"):
            if self.cnt[e] > 0:
                deps[id(self.sem[e])] = (self.sem[e], self.cnt[e])
        for k, (s, c) in self.dma_cnt.items():
            if c > 0:
                deps[k] = (s, c)
        for e in ENGS:
            self._emit_waits(e, deps)

    def replay(self):
        nc = self.nc
        streams = self.streams

        def run(e, stream):
            for it in stream:
                if it[0] == "wait":
                    e.wait_ge(it[1], it[2])
                elif it[0] == "op":
                    ins = it[1](e)
                    if it[2] is not None:
                        ins.then_inc(it[2], 1)
                else:
                    e.dma_start(out=it[1], in_=it[2]).then_inc(it[3], 16)

        with nc.Block() as block:
            @block.tensor
            def _(e):
                run(e, streams["pe"])

            @block.scalar
            def _(e):
                run(e, streams["act"])

            @block.vector
            def _(e):
                run(e, streams["dve"])

            @block.gpsimd
            def _(e):
                run(e, streams["pool"])

            @block.sync
            def _(e):
                run(e, streams["sp"])


class Arena:
    def __init__(self, ap, nwords):
        self.ap = ap
        self.n = nwords
        self.off = 0
        self.marks = []

    def alloc(self, name, shape, dtype):
        free = 1
        for s in shape[1:]:
            free *= s
        words = free if dtype == F32 else (free + 1) // 2
        assert self.off + words <= self.n, (name, self.off, words, self.n)
        v = self.ap[:, self.off:self.off + words]
        self.off += words
        if dtype != F32:
            v = v.bitcast(dtype)
            if (free % 2) == 1:
                v = v[:, 0:free]
        if len(shape) == 3:
            v = v.rearrange("p (a b) -> p a b", a=shape[1])
        elif len(shape) == 4:
            v = v.rearrange("p (a b c) -> p a b c", a=shape[1], b=shape[2])
        elif len(shape) == 5:
            v = v.rearrange("p (a b c d) -> p a b c d", a=shape[1], b=shape[2], c=shape[3])
        if shape[0] < 128:
            v = v[0:shape[0]]
        return Tile(v, name)

    def mark(self):
        return self.off

    def reset(self, m):
        self.off = m


def build_program(n_layers=DEPTH, dbg=False):
    nc = bass.Bass("TRN2", target_bir_lowering=False)

    def din(name, shape, dt=F32):
        return nc.dram_tensor(name, list(shape), dt, kind="ExternalInput").ap()

    def dscr(name, shape, dt):
        return nc.dram_tensor(name, list(shape), dt).ap()

    x_d = din("x", [SEQ, D])
    w_in_g = din("w_in_g", [DEPTH, 8, 128, 9 * 1024])
    w_sv = din("w_sv", [DEPTH, 128, 8 * 1024])
    w_outr = din("w_outr", [DEPTH, 128, 8 * 1024])
    w_upr = din("w_upr", [DEPTH, 128, NFC * 2048])
    w_dnr = din("w_dnr", [DEPTH, 128, NFC * 1024])
    rgw_d = din("rgw", [DEPTH, 128, 2048])
    wspT_d = din("wspT", [DEPTH, 128, 1024])
    chv_d = din("chv", [128, DEPTH * NCH])
    bsp_d = din("bsp", [DEPTH, 1024])
    tokv_d = din("tokv", [DEPTH, 6, 1024])
    y_d = nc.dram_tensor("y", [SEQ, D], F32, kind="ExternalOutput").ap()

    xres1 = dscr("xres1", [SEQ, D], F32)
    xres2 = dscr("xres2", [SEQ, D], F32)
    vln_d = dscr("vln_d", [SEQ, D], BF16)
    mrg_d = dscr("mrg_d", [128, 8, SEQ], BF16)
    xT_d = dscr("xT_d", [128, 8, SEQ], BF16)
    x1T_d = dscr("x1T_d", [128, 8, SEQ], BF16)
    wdn_bf = dscr("wdn_bf", [DEPTH, 128, NFC * 1024], BF16)
    dbg_out = {}
    if dbg:
        dbg_out["d_mrg"] = nc.dram_tensor("d_mrg", [128, 8, SEQ], BF16, kind="ExternalOutput").ap()
        dbg_out["d_x1"] = nc.dram_tensor("d_x1", [SEQ, D], F32, kind="ExternalOutput").ap()
        dbg_out["d_x1_0"] = nc.dram_tensor("d_x1_0", [SEQ, D], F32, kind="ExternalOutput").ap()
        dbg_out["d_mrg_0"] = nc.dram_tensor("d_mrg_0", [128, 8, SEQ], BF16, kind="ExternalOutput").ap()
        dbg_out["d_vln"] = nc.dram_tensor("d_vln", [SEQ, D], BF16, kind="ExternalOutput").ap()

    with ExitStack() as es:
        S = Sched(nc, es)
        NW = 53000
        arena_t = es.enter_context(nc.sbuf_tensor("arena", [128, NW], F32))
        AR = Arena(arena_t[:, :], NW)
        psum_t = [es.enter_context(nc.psum_tensor("ps%d" % i, [128, 1024], F32)) for i in range(4)]
        PSB = [Buf("bank%d" % i) for i in range(8)]

        def bank(i):
            return psum_t[i // 2][:, (i % 2) * 512:(i % 2) * 512 + 512]

        def pair(i):
            return psum_t[i][:, :]

        def MM(out, lhsT, rhs, start, stop, r, w, sig=False):
            S.op("pe", lambda e: e.matmul(out, lhsT=lhsT, rhs=rhs, start=start, stop=stop), r, w, signal=(stop or sig))

        def TR(out, in_, ident, r, w, signal=True):
            S.op("pe", lambda e: e.transpose(out=out, in_=in_, identity=ident), r, w, signal=signal)

        def ACT(out, in_, func, r, w, scale=1.0, bias=None, accum=None):
            def f(e):
                kw = {}
                if bias is not None:
                    kw["bias"] = bias
                if accum is not None:
                    kw["accum_out"] = accum
                return e.activation(out=out, in_=in_, func=func, scale=scale, **kw)
            S.op("act", f, r, w)

        def TT(eng, out, in0, in1, op, r, w):
            S.op(eng, lambda e: e.tensor_tensor(out=out, in0=in0, in1=in1, op=op), r, w)

        def TS(eng, out, in0, s1, s2, op0, op1, r, w):
            if s2 is None:
                S.op(eng, lambda e: e.tensor_scalar(out=out, in0=in0, scalar1=s1, scalar2=None, op0=op0), r, w)
            else:
                S.op(eng, lambda e: e.tensor_scalar(out=out, in0=in0, scalar1=s1, scalar2=s2, op0=op0, op1=op1), r, w)

        def STT(out, in0, scalar, in1, op0, op1, r, w):
            S.op("dve", lambda e: e.scalar_tensor_tensor(out=out, in0=in0, scalar=scalar, in1=in1, op0=op0, op1=op1), r, w)

        def CP(eng, out, in_, r, w):
            if eng == "act":
                S.op("act", lambda e: e.activation(out=out, in_=in_, func=AF.Copy), r, w)
            else:
                S.op(eng, lambda e: e.tensor_copy(out=out, in_=in_), r, w)

        def MEMSET(eng, ap, val, w):
            S.op(eng, lambda e: e.memset(ap, val), [], w)

        ones_f = AR.alloc("ones_f", [128, 128], F32)
        ident_f = AR.alloc("ident_f", [128, 128], F32)
        triu_f = AR.alloc("triu_f", [128, 128], F32)
        ident_b = AR.alloc("ident_b", [128, 128], BF16)
        triu_b = AR.alloc("triu_b", [128, 128], BF16)
        ones_b = AR.alloc("ones_b", [128, 128], BF16)
        onesrc = AR.alloc("onesrc", [128, 2048], BF16)
        sel = AR.alloc("sel", [128, 16, 128], BF16)
        cb = AR.alloc("cb", [128, 32, 16], F32)
        chv = AR.alloc("chv", [128, DEPTH * NCH], F32)
        der = AR.alloc("der", [128, DEPTH * 40], F32)
        cst = AR.alloc("cst", [128, 8], F32)
        PERS = [ones_f, ident_f, triu_f, ident_b, triu_b, ones_b, sel, cb, chv, der, cst]

        MEMSET("pool", ones_f[:, :], 1.0, [ones_f])
        S.op("pool", lambda e: e.affine_select(out=ident_f[:, :], in_=ones_f[:, :], pattern=[[1, 128]], compare_op=ALU.is_equal,
                                               fill=0.0, base=0, channel_multiplier=-1), [ones_f], [ident_f])
        S.op("pool", lambda e: e.affine_select(out=triu_f[:, :], in_=ones_f[:, :], pattern=[[1, 128]], compare_op=ALU.is_ge,
                                               fill=0.0, base=0, channel_multiplier=-1), [ones_f], [triu_f])
        CP("dve", ident_b[:, :], ident_f[:, :], [ident_f], [ident_b])
        CP("dve", triu_b[:, :], triu_f[:, :], [triu_f], [triu_b])
        CP("dve", ones_b[:, :], ones_f[:, :], [ones_f], [ones_b])
        MEMSET("pool", onesrc[:, :], 1.0, [onesrc])
        S.op("pool", lambda e: e.affine_select(out=sel[:, :, :], in_=onesrc[:, :].rearrange("p (a b) -> p a b", a=16),
                                               pattern=[[1, 16], [0, 128]], compare_op=ALU.is_equal, fill=0.0, base=0,
                                               channel_multiplier=-1), [onesrc], [sel])
        MEMSET("pool", cb[:, :, :], -1e30, [cb])
        for b in range(1, 16):
            MEMSET("pool", cb[:, 2 * b:2 * b + 2, 0:b], 0.0, [cb])
        MEMSET("pool", cst[:, 0:1], 1.0, [cst])
        MEMSET("pool", cst[:, 1:2], EPS, [cst])
        MEMSET("pool", cst[:, 2:3], -0.5, [cst])
        MEMSET("pool", cst[:, 3:4], 0.5, [cst])
        S.dma("sp", chv[:, :], chv_d[:, :], [], [chv])
        for l in range(n_layers):
            cv = l * NCH
            dv = l * 40
            TS("dve", der[:, dv:dv + 16], chv[:, cv + C_BR:cv + C_BR + 16], 0.5, None, ALU.mult, None, [chv], [der])
            ACT(der[:, dv + 32:dv + 40], chv[:, cv + C_LAM:cv + C_LAM + 8], AF.Exp, [chv], [der], scale=-1.0)
            ACT(der[:, dv + 32:dv + 40], der[:, dv + 32:dv + 40], AF.Ln, [der, cst], [der], bias=cst[:, 0:1])
            TS("dve", der[:, dv + 16:dv + 24], der[:, dv + 32:dv + 40], -4.0, None, ALU.mult, None, [der], [der])
            TS("dve", der[:, dv + 24:dv + 32], der[:, dv + 32:dv + 40], -8.0, None, ALU.mult, None, [der], [der])

        pers_mark = AR.mark()

        def DUMP(name, ap, reads):
            if not dbg:
                return
            o = nc.dram_tensor(name, list(ap.shape), ap.dtype, kind="ExternalOutput").ap()
            if len(ap.shape) == 3:
                for i in range(ap.shape[1]):
                    S.dma("sp", o[:, i, :], ap[:, i, :], reads, [Buf("dbg")])
            else:
                S.dma("sp", o[:, :], ap, reads, [Buf("dbg")])

        wdn_buf = [Buf("wdn%d" % l) for l in range(DEPTH)]
        for l in range(n_layers):
            for c in range(4):
                S.dma("pool", wdn_bf[l][:, c * 6144:(c + 1) * 6144].rearrange("p (a b) -> p a b", b=1024),
                      w_dnr[l][:, c * 6144:(c + 1) * 6144].rearrange("p (a b) -> p a b", b=1024), [], [wdn_buf[l]])

        vln_b = [Buf("vln%d" % t) for t in range(NT)]
        mrg_b = [Buf("mrg%d" % g) for g in range(8)]
        xres1_b = [Buf("xr1_%d" % t) for t in range(NT)]
        xres2_b = [Buf("xr2_%d" % t) for t in range(NT)]
        x1T_b = [Buf("x1T%d" % t) for t in range(8)]
        xTd_b = [Buf("xTd%d" % t) for t in range(8)]
        y_b = [Buf("y%d" % t) for t in range(NT)]

        def ln_rows(mode, zt, stats, mv, rs, gbc, bbc, out_ap, r_extra, w_out, tmp2=None):
            S.op("dve", lambda e: e.bn_stats(out=stats[:, 0, :], in_=zt[:, 0:512]), [zt], [stats])
            S.op("dve", lambda e: e.bn_stats(out=stats[:, 1, :], in_=zt[:, 512:1024]), [zt], [stats])
            S.op("dve", lambda e: e.bn_aggr(out=mv[:, :], in_=stats[:, :, :]), [stats], [mv])
            if mode == "pool":
                TS("dve", rs[:, 0:1], mv[:, 1:2], EPS, None, ALU.add, None, [mv], [rs])
                TT("pool", rs[:, 1:2], rs[:, 0:1], cst[:, 2:3], ALU.pow, [rs, cst], [rs])
            else:
                ACT(rs[:, 0:1], mv[:, 1:2], AF.Sqrt, [mv, cst], [rs], bias=cst[:, 1:2])
                S.op("dve", lambda e: e.reciprocal(out=rs[:, 1:2], in_=rs[:, 0:1]), [rs], [rs])
            STT(zt[:, :], zt[:, :], mv[:, 0:1], gbc[:, :], ALU.subtract, ALU.mult, [zt, mv, gbc], [zt])
            STT(out_ap, zt[:, :], rs[:, 1:2], bbc[:, :], ALU.mult, ALU.add, [zt, rs, bbc] + list(r_extra), list(w_out))

        for l in range(n_layers):
            cv = l * NCH
            dv = l * 40
            last = (l == n_layers - 1)
            xin_d = x_d if l == 0 else xres2
            xin_b = [None] * NT if l == 0 else xres2_b

            S.barrier()
            AR.reset(pers_mark)
            xT = AR.alloc("xT", [128, 8, SEQ], BF16)
            xT_b = [Buf("xT%d" % t) for t in range(NT)]
            macc = AR.alloc("macc", [128, SEQ], F32)
            wg = [AR.alloc("wg%d" % i, [128, 9, 8, 128], BF16) for i in range(2)]
            rgw = AR.alloc("rgw", [128, 8, 2, 128], BF16)
            wspb = AR.alloc("wspb", [128, 8, 128], BF16)
            bspbc = AR.alloc("bspbc", [128, 8, 128], F32)
            mix_mark = AR.mark()

            S.dma("pool", rgw[:, :, :, :].rearrange("p a b c -> p (a b c)"), rgw_d[l][:, :], [], [rgw])
            S.dma("sp", bspbc[:, :, :].rearrange("p a b -> p (a b)"), bsp_d[l].partition_broadcast(128), [], [bspbc])

            if l == 0:
                xin = [AR.alloc("xin%d" % i, [128, 1024], F32) for i in range(2)]
                for tt in range(NT):
                    xi = xin[tt % 2]
                    S.dma("sp", xi[:, :], x_d[tt * 128:(tt + 1) * 128, :], [], [xi])
                    for h in range(2):
                        pp = (tt % 2) * 2 + h
                        for j in range(4):
                            TR(pair(pp)[:, j * 128:(j + 1) * 128], xi[:, (h * 4 + j) * 128:(h * 4 + j + 1) * 128], ident_f[:, :],
                               [xi, ident_f], [PSB[2 * pp], PSB[2 * pp + 1]], signal=(j == 3))
                        CP("act" if h == 0 else "dve", xT[:, h * 4:(h + 1) * 4, tt * 128:(tt + 1) * 128],
                           pair(pp)[:, 0:512].rearrange("p (a b) -> p a b", a=4), [PSB[2 * pp], PSB[2 * pp + 1]], [xT_b[tt]])
            else:
                for c in range(8):
                    S.dma("sp", xT[:, :, c * 512:(c + 1) * 512], xT_d[:, :, c * 512:(c + 1) * 512], [xTd_b[c]],
                          [xT_b[4 * c + i] for i in range(4)])
            if l == 0 and False:
                DUMP("d_xT", xT[:, :, :], xT_b)
            S.barrier()
            AR.reset(mix_mark)

            wsv = AR.alloc("wsv", [128, 8, 1024], BF16)
            wspf = AR.alloc("wspf", [128, 8, 128], F32)
            gbc = AR.alloc("gbc", [128, 1024], F32)
            bbc = AR.alloc("bbc", [128, 1024], F32)
            v32 = [AR.alloc("v32_%d" % i, [128, 1024], F32) for i in range(2)]
            vlnb = [AR.alloc("vlnb%d" % i, [128, 1024], BF16) for i in range(2)]
            stats = [AR.alloc("stats%d" % i, [128, 2, 6], F32) for i in range(2)]
            mv = [AR.alloc("mv%d" % i, [128, 2], F32) for i in range(2)]
            rs = [AR.alloc("rs%d" % i, [128, 2], F32) for i in range(2)]
            for kc in range(8):
                S.dma("pool", wsv[:, kc, :], w_sv[l][:, kc * 1024:(kc + 1) * 1024], [], [wsv])
            S.dma("sp", wspf[:, :, :].rearrange("p a b -> p (a b)"), wspT_d[l][:, :], [], [wspf])
            S.dma("sp", gbc[:, :], tokv_d[l][0].partition_broadcast(128), [], [gbc])
            S.dma("sp", bbc[:, :], tokv_d[l][1].partition_broadcast(128), [], [bbc])
            TT("dve", wspb[:, :, :], wspf[:, :, :], triu_f[:, :].unsqueeze(1).broadcast_to([128, 8, 128]), ALU.mult,
               [wspf, triu_f], [wspb])
            def load_wg(g):
                t = wg[g % 2]
                for cg in range(9):
                    S.dma("pool", t[:, cg, :, :].rearrange("p a b -> p (a b)"), w_in_g[l][g][:, cg * 1024:(cg + 1) * 1024], [], [t])
            load_wg(0)
            for tt in range(NT):
                sl = tt % 2
                pp = sl
                for h in range(2):
                    for kc in range(8):
                        MM(pair(pp)[:, h * 512:(h + 1) * 512], xT[:, kc, tt * 128:(tt + 1) * 128], wsv[:, kc, h * 512:(h + 1) * 512],
                           kc == 0, kc == 7, [xT_b[tt], wsv], [PSB[2 * pp + h]])
                ACT(v32[sl][:, :], pair(pp), AF.Gelu_apprx_tanh, [PSB[2 * pp], PSB[2 * pp + 1]], [v32[sl]])
                if l == 0 and tt == 0:
                    DUMP("d_v32", v32[sl][:, :], [v32[sl]])
                    DUMP("d_wsv", wsv[:, :, :], [wsv])
                ln_rows("pool", v32[sl], stats[sl], mv[sl], rs[sl], gbc, bbc, vlnb[sl][:, :], [], [vlnb[sl]])
                S.dma("sp", vln_d[tt * 128:(tt + 1) * 128, :], vlnb[sl][:, :], [vlnb[sl]], [vln_b[tt]])
            S.barrier()
            AR.reset(mix_mark)
            g_mark = AR.mark()

            for g in range(8):
                wt = wg[g % 2]
                if g + 1 < 8:
                    load_wg(g + 1)

                def proj(cg, bk, t0, n):
                    xb = [xT_b[t] for t in range(t0 // 128, (t0 + n + 127) // 128)]
                    for kc in range(8):
                        MM(bank(bk)[:, 0:n], wt[:, cg, kc, :], xT[:, kc, t0:t0 + n], kc == 0, kc == 7, [wt] + xb, [PSB[bk]])

                AR.reset(g_mark)
                axp = [AR.alloc("axp%d" % i, [128, 516], F32) for i in range(2)]
                cc = [AR.alloc("cc%d" % i, [128, 512], F32) for i in range(2)]
                ccb = [AR.alloc("ccb%d" % i, [128, 512], BF16) for i in range(2)]
                tr_ = [AR.alloc("tr%d" % i, [128, 512], F32) for i in range(2)]
                ti_ = [AR.alloc("ti%d" % i, [128, 512], F32) for i in range(2)]
                aa = [AR.alloc("aa%d" % i, [128, 512], F32) for i in range(2)]
                a2 = [AR.alloc("a2%d" % i, [128, 512], F32) for i in range(2)]
                tmp = [AR.alloc("tmp%d" % i, [128, 512], F32) for i in range(2)]
                uu = [AR.alloc("uu%d" % i, [128, 512], F32) for i in range(2)]
                gg = [AR.alloc("gg%d" % i, [128, 512], F32) for i in range(2)]
                tg = [AR.alloc("tg%d" % i, [128, 512], F32) for i in range(2)]
                macc_b = [Buf("macc%d" % c) for c in range(8)]

                def cw(k):
                    return chv[:, cv + C_CW + k * 8 + g:cv + C_CW + k * 8 + g + 1]

                def a_front(c):
                    sl = c % 2
                    bk = c % 2
                    proj(CG_AX, bk, c * 512, 512)
                    CP("act", axp[sl][:, 3:515], bank(bk), [PSB[bk]], [axp[sl]])
                    if c == 0:
                        MEMSET("pool", axp[sl][:, 0:3], 0.0, [axp[sl]])
                    else:
                        CP("pool", axp[sl][:, 0:3], axp[1 - sl][:, 512:515], [axp[1 - sl]], [axp[sl]])

                def a_back(c):
                    sl = c % 2
                    t0 = c * 512
                    TS("dve", cc[sl][:, :], axp[sl][:, 3:515], cw(3), chv[:, cv + C_CB + g:cv + C_CB + g + 1], ALU.mult, ALU.add,
                       [axp[sl], chv], [cc[sl]])
                    for k in (2, 1, 0):
                        STT(cc[sl][:, :], axp[sl][:, k:k + 512], cw(k), cc[sl][:, :], ALU.mult, ALU.add, [axp[sl], chv, cc[sl]], [cc[sl]])
                    CP("pool", ccb[sl][:, :], cc[sl][:, :], [cc[sl]], [ccb[sl]])
                    br, bi = 2 + sl, 4 + sl
                    MM(bank(br), rgw[:, g, 0, :], ccb[sl][:, :], True, True, [rgw, ccb[sl]], [PSB[br]])
                    MM(bank(bi), rgw[:, g, 1, :], ccb[sl][:, :], True, True, [rgw, ccb[sl]], [PSB[bi]])
                    ACT(tr_[sl][:, :], bank(br), AF.Tanh, [PSB[br], der], [tr_[sl]], scale=0.5, bias=der[:, dv + g:dv + g + 1])
                    ACT(ti_[sl][:, :], bank(bi), AF.Tanh, [PSB[bi], der], [ti_[sl]], scale=0.5, bias=der[:, dv + 8 + g:dv + 8 + g + 1])
                    ACT(aa[sl][:, :], tr_[sl][:, :], AF.Exp, [tr_[sl], der], [aa[sl]], scale=der[:, dv + 16 + g:dv + 16 + g + 1],
                        bias=der[:, dv + 16 + g:dv + 16 + g + 1])
                    ACT(a2[sl][:, :], tr_[sl][:, :], AF.Exp, [tr_[sl], der], [a2[sl]], scale=der[:, dv + 24 + g:dv + 24 + g + 1],
                        bias=der[:, dv + 24 + g:dv + 24 + g + 1])
                    TS("dve", a2[sl][:, :], a2[sl][:, :], 1.0, None, ALU.min, None, [a2[sl]], [a2[sl]])
                    ACT(a2[sl][:, :], a2[sl][:, :], AF.Sqrt, [a2[sl], cst], [a2[sl]], scale=-1.0, bias=cst[:, 0:1])
                    STT(tmp[sl][:, :], ti_[sl][:, :], 1.0, cc[sl][:, :], ALU.add, ALU.mult, [ti_[sl], cc[sl]], [tmp[sl]])
                    STT(uu[sl][:, :], a2[sl][:, :], 0.5, tmp[sl][:, :], ALU.mult, ALU.mult, [a2[sl], tmp[sl]], [uu[sl]])
                    init = 0.0 if c == 0 else macc[:, t0 - 1:t0]
                    rb = [aa[sl], uu[sl]] + ([macc_b[c - 1]] if c > 0 else [])
                    S.op("dve", lambda e: e.tensor_tensor_scan(out=macc[:, t0:t0 + 512], data0=aa[sl][:, :], data1=uu[sl][:, :],
                                                               initial=init, op0=ALU.mult, op1=ALU.add), rb, [macc_b[c]])

                for c in range(9):
                    if c < 8:
                        a_front(c)
                    if c >= 1:
                        a_back(c - 1)
                for c in range(8):
                    sl = c % 2
                    t0 = c * 512
                    b1, b2 = 6, 7
                    proj(CG_AG, b1, t0, 512)
                    ACT(gg[sl][:, :], bank(b1), AF.Gelu_apprx_tanh, [PSB[b1]], [gg[sl]])
                    proj(CG_GA, b2, t0, 512)
                    ACT(tg[sl][:, :], bank(b2), AF.Tanh, [PSB[b2]], [tg[sl]], scale=0.5)
                    TT("dve", gg[sl][:, :], gg[sl][:, :], macc[:, t0:t0 + 512], ALU.mult, [gg[sl], macc_b[c]], [gg[sl]])
                    STT(macc[:, t0:t0 + 512], tg[sl][:, :], 1.0, gg[sl][:, :], ALU.add, ALU.mult, [tg[sl], gg[sl]], [macc_b[c]])
                S.barrier()

                AR.reset(g_mark)
                vlng = AR.alloc("vlng", [128, 32, 128], BF16)
                gu = [AR.alloc("gu%d" % i, [128, 512], F32) for i in range(2)]
                tgs = [AR.alloc("tgs%d" % i, [128, 512], F32) for i in range(2)]
                m1 = [AR.alloc("m1_%d" % i, [128, 512], F32) for i in range(2)]
                for q4 in range(4):
                    S.dma("sp", vlng[:, q4 * 8:(q4 + 1) * 8, :],
                          vln_d[q4 * 1024:(q4 + 1) * 1024, g * 128:(g + 1) * 128].rearrange("(n p) c -> p n c", p=128),
                          [vln_b[t] for t in range(q4 * 8, q4 * 8 + 8)], [vlng])
                for c in range(8):
                    sl = c % 2
                    t0 = c * 512
                    bu, bg, bm = 0 + sl, 2 + sl, 4 + sl
                    proj(CG_SU, bu, t0, 512)
                    ACT(gu[sl][:, :], bank(bu), AF.Gelu_apprx_tanh, [PSB[bu]], [gu[sl]])
                    proj(CG_GS, bg, t0, 512)
                    ACT(tgs[sl][:, :], bank(bg), AF.Tanh, [PSB[bg]], [tgs[sl]], scale=0.5)
                    for n in range(4):
                        MM(bank(bm)[:, n * 128:(n + 1) * 128], vlng[:, 4 * c + n, :], wspb[:, g, :], True, True, [vlng, wspb], [PSB[bm]])
                    TT("dve", m1[sl][:, :].rearrange("p (a b) -> p a b", a=4), bank(bm).rearrange("p (a b) -> p a b", a=4),
                       bspbc[:, g, :].unsqueeze(1).broadcast_to([128, 4, 128]), ALU.add, [PSB[bm], bspbc], [m1[sl]])
                    TT("dve", m1[sl][:, :], m1[sl][:, :], gu[sl][:, :], ALU.mult, [m1[sl], gu[sl]], [m1[sl]])
                    STT(m1[sl][:, :], tgs[sl][:, :], 1.0, m1[sl][:, :], ALU.add, ALU.mult, [tgs[sl], m1[sl]], [m1[sl]])
                    TT("dve", macc[:, t0:t0 + 512], macc[:, t0:t0 + 512], m1[sl][:, :], ALU.add, [m1[sl], macc_b[c]], [macc_b[c]])
                S.barrier()

                AR.reset(g_mark)
                qT = AR.alloc("qT", [128, SEQ], BF16)
                kT = AR.alloc("kT", [128, SEQ], BF16)
                Vt = AR.alloc("Vt", [128, 32, 128], BF16)
                negmT = AR.alloc("negmT", [128, SEQ], BF16)
                ksum = AR.alloc("ksum", [128, 16], F32)
                kmT = AR.alloc("kmT", [128, 16], BF16)
                gsb = AR.alloc("gsb", [128, 32, 16], F32)
                mx8 = AR.alloc("mx8", [128, 32, 8], F32)
                thr = AR.alloc("thr", [128, 32], F32)
                negm = AR.alloc("negm", [128, 32, 16], BF16)
                NPT = 8
                PT = [AR.alloc("PT%d" % i, [128, 256], BF16) for i in range(NPT)]
                tgm = [AR.alloc("tgm%d" % i, [128, 256], F32) for i in range(2)]
                rec = [AR.alloc("rec%d" % i, [128, 256], F32) for i in range(2)]
                ot = [AR.alloc("ot%d" % i, [128, 256], F32) for i in range(2)]
                mrgb = [AR.alloc("mrgb%d" % i, [128, 1024], BF16) for i in range(2)]
                MEMSET("pool", negmT[:, :], 0.0, [negmT])
                for c in range(8):
                    t0 = c * 512
                    bq, bk_ = 0 + (c % 2), 2 + (c % 2)
                    proj(CG_K, bk_, t0, 512)
                    for h in range(2):
                        ACT(kT[:, t0 + h * 256:t0 + (h + 1) * 256], bank(bk_)[:, h * 256:(h + 1) * 256], AF.Copy, [PSB[bk_]], [kT, ksum],
                            accum=ksum[:, 2 * c + h:2 * c + h + 1])
                    proj(CG_Q, bq, t0, 512)
                    CP("dve", qT[:, t0:t0 + 512], bank(bq), [PSB[bq]], [qT])
                for t4 in range(8):
                    bv = 4 + (t4 % 2)
                    for j in range(4):
                        tt = t4 * 4 + j
                        for kc in range(8):
                            MM(bank(bv)[:, j * 128:(j + 1) * 128], xT[:, kc, tt * 128:(tt + 1) * 128], wt[:, CG_V, kc, :], kc == 0, kc == 7,
                               [xT_b[tt], wt], [PSB[bv]])
                    CP("act" if t4 % 2 == 0 else "dve", Vt[:, t4 * 4:(t4 + 1) * 4, :], bank(bv).rearrange("p (a b) -> p a b", a=4),
                       [PSB[bv]], [Vt])
                TS("dve", kmT[:, :], ksum[:, :], 1.0 / 256.0, None, ALU.mult, None, [ksum], [kmT])
                bgt = 6
                for qt in range(32):
                    MM(bank(bgt)[:, qt * 16:(qt + 1) * 16], qT[:, qt * 128:(qt + 1) * 128], kmT[:, :], True, True, [qT, kmT], [PSB[bgt]])
                TT("dve", gsb[:, :, :].rearrange("p a b -> p (a b)"), bank(bgt), cb[:, :, :].rearrange("p a b -> p (a b)"), ALU.add,
                   [PSB[bgt], cb], [gsb])
                for qt in range(32):
                    S.op("dve", (lambda qt: lambda e: e.max(out=mx8[:, qt, :], in_=gsb[:, qt, :]))(qt), [gsb], [mx8])
                TS("dve", thr[:, :], mx8[:, :, 2], -1e29, None, ALU.max, None, [mx8], [thr])
                TT("dve", negm[:, :, :], gsb[:, :, :], thr[:, :].unsqueeze(2).broadcast_to([128, 32, 16]), ALU.is_lt, [gsb, thr], [negm])
                TS("dve", negm[:, :, :], negm[:, :, :], NEGBIG, None, ALU.mult, None, [negm], [negm])
                for q8 in range(4):
                    bt = 6 + ((q8 + 1) % 2)
                    tb = bank(bt).bitcast(BF16)
                    for j in range(8):
                        qt = q8 * 8 + j
                        TR(tb[0:16, j * 128:(j + 1) * 128], negm[:, qt, :], ident_b[:, :], [negm, ident_b], [PSB[bt]], signal=(j == 7))
                    CP("act", negmT[0:16, q8 * 1024:(q8 + 1) * 1024], tb[0:16, :], [PSB[bt]], [negmT])
                flat = []
                for b in range(16):
                    tl_ = [("past", kt) for kt in range(2 * b)] + [("own0", 2 * b), ("own1", 2 * b + 1)]
                    for i, (kind, kt) in enumerate(tl_):
                        flat.append((b, kind, kt, i == 0, i == len(tl_) - 1))
                nfl = len(flat)

                RING = [0, 1, 2, 5]
                ring_ctr = [0]

                def emit_score(i):
                    b, kind, kt, first, lastt = flat[i]
                    q0 = b * 256
                    rb = RING[ring_ctr[0] % 4]
                    ring_ctr[0] += 1
                    stv = bank(rb)[:, 0:256]
                    sb = PSB[rb]
                    P = PT[i % NPT]
                    if kind == "past":
                        j = kt // 2
                        MM(stv, kT[:, kt * 128:(kt + 1) * 128], qT[:, q0:q0 + 256], True, False, [kT, qT], [sb])
                        MM(stv, sel[:, j, :], negmT[:, q0:q0 + 256], False, True, [sel, negmT], [sb])
                        ACT(P[:, 0:256], stv, AF.Exp, [sb], [P], scale=ATT_SCALE)
                    elif kind == "own0":
                        MM(stv, kT[:, kt * 128:(kt + 1) * 128], qT[:, q0:q0 + 256], True, True, [kT, qT], [sb])
                        ACT(P[:, 0:256], stv, AF.Exp, [sb], [P], scale=ATT_SCALE)
                        TT("pool", P[:, 0:128], P[:, 0:128], triu_b[:, :], ALU.mult, [P, triu_b], [P])
                    else:
                        MM(stv[:, 0:128], kT[:, kt * 128:(kt + 1) * 128], qT[:, q0 + 128:q0 + 256], True, True, [kT, qT], [sb])
                        ACT(P[:, 0:128], stv[:, 0:128], AF.Exp, [sb], [P], scale=ATT_SCALE)
                        TT("pool", P[:, 0:128], P[:, 0:128], triu_b[:, :], ALU.mult, [P, triu_b], [P])

                def emit_pv(i):
                    b, kind, kt, first, lastt = flat[i]
                    q0 = b * 256
                    P = PT[i % NPT]
                    bo = 3 if b % 2 == 0 else 6
                    bl = 4 if b % 2 == 0 else 7
                    if kind == "own1":
                        MM(bank(bo)[:, 128:256], Vt[:, kt, :], P[:, 0:128], False, True, [Vt, P], [PSB[bo]])
                        MM(bank(bl)[:, 128:256], ones_b[:, :], P[:, 0:128], False, True, [ones_b, P], [PSB[bl]])
                    else:
                        MM(bank(bo)[:, 0:256], Vt[:, kt, :], P[:, 0:256], first, False, [Vt, P], [PSB[bo]])
                        MM(bank(bl)[:, 0:256], ones_b[:, :], P[:, 0:256], first, False, [ones_b, P], [PSB[bl]])
                    if lastt:
                        sl = b % 2
                        bgm = RING[ring_ctr[0] % 4]
                        ring_ctr[0] += 1
                        proj(CG_GM, bgm, q0, 256)
                        ACT(tgm[sl][:, :], bank(bgm)[:, 0:256], AF.Tanh, [PSB[bgm]], [tgm[sl]], scale=0.5)
                        S.op("dve", lambda e: e.reciprocal(out=rec[sl][:, :], in_=bank(bl)[:, 0:256]), [PSB[bl]], [rec[sl]])
                        TT("dve", ot[sl][:, :], bank(bo)[:, 0:256], rec[sl][:, :], ALU.mult, [PSB[bo], rec[sl]], [ot[sl]])
                        STT(ot[sl][:, :], tgm[sl][:, :], 1.0, ot[sl][:, :], ALU.add, ALU.mult, [tgm[sl], ot[sl]], [ot[sl]])
                        mb = macc_b[b // 2]
                        TT("dve", macc[:, q0:q0 + 256], macc[:, q0:q0 + 256], ot[sl][:, :], ALU.add, [ot[sl], mb], [mb])

                LOOK = 3
                for i in range(min(LOOK, nfl)):
                    emit_score(i)
                for i in range(nfl):
                    if i + LOOK < nfl:
                        emit_score(i + LOOK)
                    emit_pv(i)
                for c4 in range(4):
                    mt = mrgb[c4 % 2]
                    TS("pool", mt[:, :], macc[:, c4 * 1024:(c4 + 1) * 1024], 0.5, None, ALU.mult, None,
                       [macc_b[2 * c4], macc_b[2 * c4 + 1]], [mt])
                    S.dma("sp", mrg_d[:, g, c4 * 1024:(c4 + 1) * 1024], mt[:, :], [mt], [mrg_b[g]])
                S.barrier()

            if dbg and l == 0:
                S.dma("sp", dbg_out["d_mrg_0"][:, :, :], mrg_d[:, :, :], mrg_b, [Buf("dbg")])
            if dbg and l == n_layers - 1:
                S.dma("sp", dbg_out["d_mrg"][:, :, :], mrg_d[:, :, :], mrg_b, [Buf("dbg")])
                S.dma("sp", dbg_out["d_vln"][:, :], vln_d[:, :], vln_b, [Buf("dbg")])

            S.barrier()
            AR.reset(pers_mark)
            wup = AR.alloc("wup", [128, NFC, 2, 8, 128], BF16)
            f_mark = AR.mark()
            for fc in range(NFC):
                S.dma("pool", wup[:, fc, :, :, :].rearrange("p a b c -> p a (b c)"),
                      w_upr[l][:, fc * 2048:(fc + 1) * 2048].rearrange("p (a c) -> p a c", c=1024), [], [wup])
            woutb = AR.alloc("woutb", [128, 8, 1024], BF16)
            g1 = AR.alloc("g1", [128, 1024], F32)
            b1 = AR.alloc("b1", [128, 1024], F32)
            mt_ = [AR.alloc("mt%d" % i, [128, 8, 512], BF16) for i in range(2)]
            x1Tg = [AR.alloc("x1Tg%d" % i, [128, 8, 512], BF16) for i in range(2)]
            xr = [AR.alloc("xr%d" % i, [128, 1024], F32) for i in range(2)]
            zt = [AR.alloc("zt%d" % i, [128, 1024], F32) for i in range(2)]
            x1t = [AR.alloc("x1t%d" % i, [128, 1024], F32) for i in range(2)]
            stats = [AR.alloc("stats%d" % i, [128, 2, 6], F32) for i in range(2)]
            mv = [AR.alloc("mv%d" % i, [128, 2], F32) for i in range(2)]
            rs = [AR.alloc("rs%d" % i, [128, 2], F32) for i in range(2)]
            for kc in range(8):
                S.dma("pool", woutb[:, kc, :], w_outr[l][:, kc * 1024:(kc + 1) * 1024], [], [woutb])
            S.dma("sp", g1[:, :], tokv_d[l][2].partition_broadcast(128), [], [g1])
            S.dma("sp", b1[:, :], tokv_d[l][3].partition_broadcast(128), [], [b1])
            for c in range(8):
                m = mt_[c % 2]
                xg = x1Tg[c % 2]
                S.dma("sp", m[:, :, :], mrg_d[:, :, c * 512:(c + 1) * 512], mrg_b, [m])
                for j in range(4):
                    tt = c * 4 + j
                    sl = tt % 2
                    pp = sl
                    S.dma("sp", xr[sl][:, :], xin_d[tt * 128:(tt + 1) * 128, :], [xin_b[tt]], [xr[sl]])
                    for h in range(2):
                        for kc in range(8):
                            MM(pair(pp)[:, h * 512:(h + 1) * 512], m[:, kc, j * 128:(j + 1) * 128], woutb[:, kc, h * 512:(h + 1) * 512],
                               kc == 0, kc == 7, [m, woutb], [PSB[2 * pp + h]])
                    STT(zt[sl][:, :], xr[sl][:, :], ALPHA, pair(pp), ALU.mult, ALU.add, [xr[sl], PSB[2 * pp], PSB[2 * pp + 1]], [zt[sl]])
                    ln_rows("act", zt[sl], stats[sl], mv[sl], rs[sl], g1, b1, x1t[sl][:, :], [], [x1t[sl]])
                    S.dma("sp", xres1[tt * 128:(tt + 1) * 128, :], x1t[sl][:, :], [x1t[sl]], [xres1_b[tt]])
                    if dbg and l == 0:
                        S.dma("sp", dbg_out["d_x1_0"][tt * 128:(tt + 1) * 128, :], x1t[sl][:, :], [x1t[sl]], [Buf("dbg")])
                    if dbg and l == n_layers - 1:
                        S.dma("sp", dbg_out["d_x1"][tt * 128:(tt + 1) * 128, :], x1t[sl][:, :], [x1t[sl]], [Buf("dbg")])
                    pt = 2 + sl
                    for kc in range(8):
                        TR(pair(pt)[:, kc * 128:(kc + 1) * 128], x1t[sl][:, kc * 128:(kc + 1) * 128], ident_f[:, :], [x1t[sl], ident_f],
                           [PSB[2 * pt], PSB[2 * pt + 1]], signal=(kc == 7))
                    CP("act", xg[:, :, j * 128:(j + 1) * 128], pair(pt).rearrange("p (a b) -> p a b", a=8), [PSB[2 * pt], PSB[2 * pt + 1]], [xg])
                S.dma("sp", x1T_d[:, :, c * 512:(c + 1) * 512], xg[:, :, :], [xg], [x1T_b[c]])
            S.barrier()

            AR.reset(f_mark)
            g2 = AR.alloc("g2", [128, 1024], F32)
            b2 = AR.alloc("b2", [128, 1024], F32)
            wdn = [AR.alloc("wdn%d" % i, [128, 4, 1024], BF16) for i in range(2)]
            xg_ = [AR.alloc("xg%d" % i, [128, 8, 256], BF16) for i in range(2)]
            xr1 = [AR.alloc("xr1_%d" % i, [128, 2, 1024], F32) for i in range(2)]
            actT = AR.alloc("actT", [128, NFC, 256], BF16)
            actT_b = [Buf("actT%d" % i) for i in range(NFC)]
            hgp = [AR.alloc("hgp%d" % i, [128, 260], F32) for i in range(2)]
            cf = [AR.alloc("cf%d" % i, [128, 256], F32) for i in range(2)]
            gl = [AR.alloc("gl%d" % i, [128, 256], F32) for i in range(2)]
            carry = AR.alloc("carry", [128, NFC, 2], F32)
            zt = [AR.alloc("zt%d" % i, [128, 1024], F32) for i in range(2)]
            x2t = [AR.alloc("x2t%d" % i, [128, 1024], F32) for i in range(2)]
            x2Tg = [AR.alloc("x2Tg%d" % i, [128, 8, 512], BF16) for i in range(1)]
            stats = [AR.alloc("stats%d" % i, [128, 2, 6], F32) for i in range(2)]
            mv = [AR.alloc("mv%d" % i, [128, 2], F32) for i in range(2)]
            rs = [AR.alloc("rs%d" % i, [128, 2], F32) for i in range(2)]
            S.dma("sp", g2[:, :], tokv_d[l][4].partition_broadcast(128), [], [g2])
            S.dma("sp", b2[:, :], tokv_d[l][5].partition_broadcast(128), [], [b2])
            MEMSET("pool", carry[:, :, :], 0.0, [carry])
            wd_rr = 0
            for gi in range(16):
                t0 = gi * 256
                xg = xg_[gi % 2]
                x1r = xr1[gi % 2]
                S.dma("sp", xg[:, :, :], x1T_d[:, :, t0:t0 + 256], [x1T_b[gi // 2]], [xg])
                S.dma("sp", x1r[:, :, :], xres1[t0:t0 + 256, :].rearrange("(a p) d -> p a d", p=128),
                      [xres1_b[2 * gi], xres1_b[2 * gi + 1]], [x1r])
                def up(fc):
                    sl = fc % 2
                    bg_, bu_ = 4 + sl, 6 + sl
                    if fc % 4 == 0:
                        ch = fc // 4
                        wd = wdn[ch % 2]
                        S.dma("sp", wd[:, :, :].rearrange("p a b -> p (a b)"), wdn_bf[l][:, ch * 4096:(ch + 1) * 4096], [wdn_buf[l]], [wd])
                    for kc in range(8):
                        MM(bank(bg_)[:, 0:256], wup[:, fc, 0, kc, :], xg[:, kc, :], kc == 0, kc == 7, [wup, xg], [PSB[bg_]])
                    for kc in range(8):
                        MM(bank(bu_)[:, 0:256], wup[:, fc, 1, kc, :], xg[:, kc, :], kc == 0, kc == 7, [wup, xg], [PSB[bu_]])
                    CP("pool", hgp[sl][:, 0:2], carry[:, fc, :], [carry], [hgp[sl]])
                    CP("act", hgp[sl][:, 2:258], bank(bg_)[:, 0:256], [PSB[bg_]], [hgp[sl]])
                    CP("pool", carry[:, fc, :], hgp[sl][:, 256:258], [hgp[sl]], [carry])

                    def fw(k):
                        return chv[:, cv + C_FW + k * NFC + fc:cv + C_FW + k * NFC + fc + 1]
                    ACT(cf[sl][:, :], bank(bg_)[:, 0:256], AF.Identity, [PSB[bg_], chv], [cf[sl]], scale=fw(2),
                        bias=chv[:, cv + C_FB + fc:cv + C_FB + fc + 1])
                    STT(cf[sl][:, :], hgp[sl][:, 1:257], fw(1), cf[sl][:, :], ALU.mult, ALU.add, [hgp[sl], cf[sl], chv], [cf[sl]])
                    STT(cf[sl][:, :], hgp[sl][:, 0:256], fw(0), cf[sl][:, :], ALU.mult, ALU.add, [hgp[sl], cf[sl], chv], [cf[sl]])
                    ACT(gl[sl][:, :], cf[sl][:, :], AF.Gelu_apprx_tanh, [cf[sl]], [gl[sl]])
                    TT("dve", actT[:, fc, :], gl[sl][:, :], bank(bu_)[:, 0:256], ALU.mult, [gl[sl], PSB[bu_]], [actT_b[fc]])

                def down(fc):
                    wd = wdn[(fc // 4) % 2]
                    f6 = fc % 4
                    for tl in range(2):
                        for h in range(2):
                            MM(pair(tl)[:, h * 512:(h + 1) * 512], actT[:, fc, tl * 128:(tl + 1) * 128], wd[:, f6, h * 512:(h + 1) * 512],
                               fc == 0, fc == NFC - 1, [actT_b[fc], wd], [PSB[2 * tl + h]], sig=(f6 == 3 and tl == 1 and h == 1))

                for fc in range(NFC + 2):
                    if fc < NFC:
                        up(fc)
                    if fc >= 2:
                        down(fc - 2)
                for tl in range(2):
                    tt = gi * 2 + tl
                    sl = tt % 2
                    STT(zt[sl][:, :], x1r[:, tl, :], ALPHA, pair(tl), ALU.mult, ALU.add, [x1r, PSB[2 * tl], PSB[2 * tl + 1]], [zt[sl]])
                    ln_rows("pool", zt[sl], stats[sl], mv[sl], rs[sl], g2, b2, x2t[sl][:, :], [], [x2t[sl]])
                    if last:
                        S.dma("sp", y_d[tt * 128:(tt + 1) * 128, :], x2t[sl][:, :], [x2t[sl]], [y_b[tt]])
                    else:
                        S.dma("sp", xres2[tt * 128:(tt + 1) * 128, :], x2t[sl][:, :], [x2t[sl]], [xres2_b[tt]])
                        xg2 = x2Tg[0]
                        pt = 2 + sl
                        for kc in range(8):
                            TR(pair(pt)[:, kc * 128:(kc + 1) * 128], x2t[sl][:, kc * 128:(kc + 1) * 128], ident_f[:, :], [x2t[sl], ident_f],
                               [PSB[2 * pt], PSB[2 * pt + 1]], signal=(kc == 7))
                        CP("act", xg2[:, :, (tt % 4) * 128:(tt % 4 + 1) * 128], pair(pt).rearrange("p (a b) -> p a b", a=8),
                           [PSB[2 * pt], PSB[2 * pt + 1]], [xg2])
                        if tt % 4 == 3:
                            c = tt // 4
                            S.dma("sp", xT_d[:, :, c * 512:(c + 1) * 512], xg2[:, :, :], [xg2], [xTd_b[c]])
            if dbg and not last:
                o = nc.dram_tensor("d_x2", [SEQ, D], F32, kind="ExternalOutput").ap()
                for q4 in range(4):
                    S.dma("sp", o[q4 * 1024:(q4 + 1) * 1024, :], xres2[q4 * 1024:(q4 + 1) * 1024, :], xres2_b, [Buf("dbg")])
                o = nc.dram_tensor("d_xTd", [128, 8, SEQ], BF16, kind="ExternalOutput").ap()
                for q4 in range(8):
                    S.dma("sp", o[:, q4, :], xT_d[:, q4, :], xTd_b, [Buf("dbg")])
            S.barrier()

        S.barrier()
        S.replay()
    return nc


def _prep_weights(inp):
    f = np.float32
    w_in = np.asarray(inp["w_in"], f)
    L = w_in.shape[0]
    cgs = [0, 1, 2, 4, 5, 6, 7, 8, 9]
    w6 = w_in.reshape(L, 8, 128, 10, 8, 128)
    w_in_g = np.ascontiguousarray(w6[:, :, :, cgs, :, :].transpose(0, 4, 2, 3, 1, 5)).reshape(L, 8, 128, 9 * 1024)
    w_sv = np.ascontiguousarray(w6[:, :, :, 3, :, :].transpose(0, 2, 1, 3, 4)).reshape(L, 128, 8 * 1024)
    w_out = np.asarray(inp["w_out"], f).reshape(L, 8, 128, 1024)
    w_outr = np.ascontiguousarray(w_out.transpose(0, 2, 1, 3)).reshape(L, 128, 8 * 1024)
    w_up = np.asarray(inp["w_ffn_up"], f).reshape(L, 8, 128, 2, NFC, 128)
    w_upr = np.ascontiguousarray(w_up.transpose(0, 2, 4, 3, 1, 5)).reshape(L, 128, NFC * 2048)
    w_dn = np.asarray(inp["w_ffn_down"], f).reshape(L, NFC, 128, 1024)
    w_dnr = np.ascontiguousarray(w_dn.transpose(0, 2, 1, 3)).reshape(L, 128, NFC * 1024)
    wr = np.asarray(inp["w_rgate"], f)
    wi = np.asarray(inp["w_igate"], f)
    rgw = np.ascontiguousarray(np.stack([wr, wi], axis=2).transpose(0, 3, 1, 2, 4)).reshape(L, 128, 2048)
    wsp = np.asarray(inp["w_spatial"], f)
    wspT = np.ascontiguousarray(wsp.transpose(0, 3, 1, 2)).reshape(L, 128, 1024)
    chv = np.zeros((128, L * NCH), f)

    def pc(v, n):
        return np.asarray(v, f).reshape(n, 128).T

    for l in range(L):
        o = l * NCH
        for k in range(4):
            chv[:, o + C_CW + k * 8:o + C_CW + (k + 1) * 8] = pc(inp["conv_rg_w"][l][k], 8)
        chv[:, o + C_CB:o + C_CB + 8] = pc(inp["conv_rg_b"][l], 8)
        chv[:, o + C_BR:o + C_BR + 8] = pc(inp["b_rgate"][l], 8)
        chv[:, o + C_BI:o + C_BI + 8] = pc(inp["b_igate"][l], 8)
        chv[:, o + C_LAM:o + C_LAM + 8] = pc(inp["lru_lambda"][l], 8)
        for k in range(3):
            chv[:, o + C_FW + k * NFC:o + C_FW + (k + 1) * NFC] = pc(inp["conv_ffn_w"][l][k], NFC)
        chv[:, o + C_FB:o + C_FB + NFC] = pc(inp["conv_ffn_b"][l], NFC)
    bsp = np.ascontiguousarray(np.asarray(inp["b_spatial"], f).reshape(L, 1024))
    tokv = np.ascontiguousarray(np.stack([np.asarray(inp[k], f) for k in
                                          ("sgu_ln_g", "sgu_ln_b", "ln_mix_g", "ln_mix_b", "ln_ffn_g", "ln_ffn_b")], axis=1))
    return dict(w_in_g=w_in_g, w_sv=w_sv, w_outr=w_outr, w_upr=w_upr, w_dnr=w_dnr, rgw=rgw, wspT=wspT, chv=chv, bsp=bsp, tokv=tokv)


_CACHE = {}


def kernel(**inputs):
    x = np.asarray(inputs["x"], np.float32)
    B = x.shape[0]
    wts = _prep_weights(inputs)
    if "nc" not in _CACHE:
        _CACHE["nc"] = build_program()
    nc = _CACHE["nc"]
    in_maps = []
    for b in range(B):
        m = {"x": np.ascontiguousarray(x[b])}
        m.update(wts)
        in_maps.append(m)
    res = run_bass_kernel_spmd(nc, in_maps, core_ids=list(range(B)))
    return np.stack([np.asarray(r["y"], np.float32) for r in res.results], axis=0)
```

```python
from contextlib import ExitStack
import numpy as np
import concourse.bass as bass
import concourse.mybir as mybir
from concourse.bass_utils import run_bass_kernel_spmd

F32 = mybir.dt.float32
BF16 = mybir.dt.bfloat16
AF = mybir.ActivationFunctionType
ALU = mybir.AluOpType

ENGS = ("pe", "act", "dve", "pool", "sp")
SAME_ENGINE_SYNC = True

D = 1024
SEQ = 4096
NT = 32
KC = 8
DEPTH = 2
DFF = 3072
NFC = 24
ALPHA = float((2 * DEPTH) ** 0.25)
EPS = 1e-5
ATT_SCALE = float(128 ** -0.5)
NEGBIG = -30000.0
C_CW, C_CB, C_BR, C_BI, C_LAM, C_FW, C_FB, NCH = 0, 32, 40, 48, 56, 64, 136, 160
CG_AX, CG_AG, CG_SU, CG_Q, CG_K, CG_V, CG_GA, CG_GS, CG_GM = range(9)


class Buf:
    __slots__ = ("w", "r", "name")

    def __init__(self, name=""):
        self.w = {}
        self.r = {}
        self.name = name


class Tile:
    def __init__(self, ap, name):
        self.ap = ap
        self.buf = Buf(name)
        self.name = name

    def __getitem__(self, k):
        return self.ap[k]


def _bufs(lst):
    out = []
    for x in lst:
        if x is None:
            continue
        out.append(x.buf if isinstance(x, Tile) else x)
    return out


class Sched:
    def __init__(self, nc, es):
        self.nc = nc
        self.es = es
        self.streams = {e: [] for e in ENGS}
        self.cnt = {}
        self.sem = {}
        for e in ("pe", "act", "dve", "pool"):
            self.sem[e] = es.enter_context(nc.semaphore("sem_" + e))
            self.cnt[e] = 0
        self.waited = {e: {} for e in ENGS}
        self.dma_pool = []
        self.dma_rr = 0
        self.dma_cnt = {}

    def new_dma_sem(self, name):
        s = self.es.enter_context(self.nc.semaphore(name))
        self.dma_cnt[id(s)] = [s, 0]
        return s

    def _pool_sem(self):
        if len(self.dma_pool) < 32:
            s = self.new_dma_sem("dq%d" % len(self.dma_pool))
            self.dma_pool.append(s)
            return s
        s = self.dma_pool[self.dma_rr % len(self.dma_pool)]
        self.dma_rr += 1
        return s

    def _deps(self, reads, writes):
        deps = {}

        def add(d):
            for k, v in d.items():
                if deps.get(k, (None, 0))[1] < v[1]:
                    deps[k] = v

        for b in reads:
            add(b.w)
        for b in writes:
            add(b.w)
            add(b.r)
        return deps

    def _emit_waits(self, eng, deps):
        own = self.sem.get(eng)
        for k, (s, v) in deps.items():
            if own is not None and s is own and (eng == "pe" or not SAME_ENGINE_SYNC):
                continue
            if self.waited[eng].get(k, 0) >= v:
                continue
            self.waited[eng][k] = v
            self.streams[eng].append(("wait", s, v))

    def _record(self, tok, reads, writes):
        k = id(tok[0])
        for b in reads:
            if b.r.get(k, (None, 0))[1] < tok[1]:
                b.r[k] = tok
        for b in writes:
            if b.w.get(k, (None, 0))[1] < tok[1]:
                b.w[k] = tok

    def op(self, eng, fn, reads=(), writes=(), signal=True):
        reads = _bufs(reads)
        writes = _bufs(writes)
        self._emit_waits(eng, self._deps(reads, writes))
        s = self.sem[eng]
        if signal:
            self.cnt[eng] += 1
            tok = (s, self.cnt[eng])
        else:
            tok = (s, self.cnt[eng] + 1)
        self.streams[eng].append(("op", fn, s if signal else None))
        self._record(tok, reads, writes)
        return tok

    def dma(self, q, out, in_, reads=(), writes=(), sem=None):
        reads = _bufs(reads)
        writes = _bufs(writes)
        if sem is None:
            if q == "pool":
                if not hasattr(self, "swq"):
                    self.swq = [self.new_dma_sem("swq%d" % i) for i in range(2)]
                    self.swq_rr = 0
                sem = self.swq[self.swq_rr % 2]
                self.swq_rr += 1
            else:
                sem = self._pool_sem()
        ent = self.dma_cnt[id(sem)]
        deps = self._deps(reads, writes)
        if ent[1] > 0:
            deps[id(sem)] = (sem, max(deps.get(id(sem), (None, 0))[1], ent[1]))
        self._emit_waits(q, deps)
        ent[1] += 16
        tok = (sem, ent[1])
        self.streams[q].append(("dma", out, in_, sem))
        self._record(tok, reads, writes)
        return tok

    def wait_all(self, eng, bufs):
        self._emit_waits(eng, self._deps(_bufs(bufs), []))

    def barrier(self):
        deps = {}
        for e in ("pe", "act", "dve", "pool"):
            if self.cnt[e] > 0:
                deps[id(self.sem[e])] = (self.sem[e], self.cnt[e])
        for k, (s, c) in self.dma_cnt.items():
            if c > 0:
                deps[k] = (s, c)
        for e in ENGS:
            self._emit_waits(e, deps)

    def replay(self):
        nc = self.nc
        streams = self.streams

        def run(e, stream):
            for it in stream:
                if it[0] == "wait":
                    e.wait_ge(it[1], it[2])
                elif it[0] == "op":
                    ins = it[1](e)
                    if it[2] is not None:
                        ins.then_inc(it[2], 1)
                else:
                    e.dma_start(out=it[1], in_=it[2]).then_inc(it[3], 16)

        with nc.Block() as block:
            @block.tensor
            def _(e):
                run(e, streams["pe"])

            @block.scalar
            def _(e):
                run(e, streams["act"])

            @block.vector
            def _(e):
                run(e, streams["dve"])

            @block.gpsimd
            def _(e):
                run(e, streams["pool"])

            @block.sync
            def _(e):
                run(e, streams["sp"])


class Arena:
    def __init__(self, ap, nwords):
        self.ap = ap
        self.n = nwords
        self.off = 0
        self.marks = []

    def alloc(self, name, shape, dtype):
        free = 1
        for s in shape[1:]:
            free *= s
        words = free if dtype == F32 else (free + 1) // 2
        assert self.off + words <= self.n, (name, self.off, words, self.n)
        v = self.ap[:, self.off:self.off + words]
        self.off += words
        if dtype != F32:
            v = v.bitcast(dtype)
            if (free % 2) == 1:
                v = v[:, 0:free]
        if len(shape) == 3:
            v = v.rearrange("p (a b) -> p a b", a=shape[1])
        elif len(shape) == 4:
            v = v.rearrange("p (a b c) -> p a b c", a=shape[1], b=shape[2])
        elif len(shape) == 5:
            v = v.rearrange("p (a b c d) -> p a b c d", a=shape[1], b=shape[2], c=shape[3])
        if shape[0] < 128:
            v = v[0:shape[0]]
        return Tile(v, name)

    def mark(self):
        return self.off

    def reset(self, m):
        self.off = m


def build_program(n_layers=DEPTH, dbg=False):
    nc = bass.Bass("TRN2", target_bir_lowering=False)

    def din(name, shape, dt=F32):
        return nc.dram_tensor(name, list(shape), dt, kind="ExternalInput").ap()

    def dscr(name, shape, dt):
        return nc.dram_tensor(name, list(shape), dt).ap()

    x_d = din("x", [SEQ, D])
    w_in_g = din("w_in_g", [DEPTH, 8, 128, 9 * 1024])
    w_sv = din("w_sv", [DEPTH, 128, 8 * 1024])
    w_outr = din("w_outr", [DEPTH, 128, 8 * 1024])
    w_upr = din("w_upr", [DEPTH, 128, NFC * 2048])
    w_dnr = din("w_dnr", [DEPTH, 128, NFC * 1024])
    rgw_d = din("rgw", [DEPTH, 128, 2048])
    wspT_d = din("wspT", [DEPTH, 128, 1024])
    chv_d = din("chv", [128, DEPTH * NCH])
    bsp_d = din("bsp", [DEPTH, 1024])
    tokv_d = din("tokv", [DEPTH, 6, 1024])
    y_d = nc.dram_tensor("y", [SEQ, D], F32, kind="ExternalOutput").ap()

    xres1 = dscr("xres1", [SEQ, D], F32)
    xres2 = dscr("xres2", [SEQ, D], F32)
    vln_d = dscr("vln_d", [SEQ, D], BF16)
    mrg_d = dscr("mrg_d", [128, 8, SEQ], BF16)
    xT_d = dscr("xT_d", [128, 8, SEQ], BF16)
    x1T_d = dscr("x1T_d", [128, 8, SEQ], BF16)
    wdn_bf = dscr("wdn_bf", [DEPTH, 128, NFC * 1024], BF16)
    dbg_out = {}
    if dbg:
        dbg_out["d_mrg"] = nc.dram_tensor("d_mrg", [128, 8, SEQ], BF16, kind="ExternalOutput").ap()
        dbg_out["d_x1"] = nc.dram_tensor("d_x1", [SEQ, D], F32, kind="ExternalOutput").ap()
        dbg_out["d_x1_0"] = nc.dram_tensor("d_x1_0", [SEQ, D], F32, kind="ExternalOutput").ap()
        dbg_out["d_mrg_0"] = nc.dram_tensor("d_mrg_0", [128, 8, SEQ], BF16, kind="ExternalOutput").ap()
        dbg_out["d_vln"] = nc.dram_tensor("d_vln", [SEQ, D], BF16, kind="ExternalOutput").ap()

    with ExitStack() as es:
        S = Sched(nc, es)
        NW = 53000
        arena_t = es.enter_context(nc.sbuf_tensor("arena", [128, NW], F32))
        AR = Arena(arena_t[:, :], NW)
        psum_t = [es.enter_context(nc.psum_tensor("ps%d" % i, [128, 1024], F32)) for i in range(4)]
        PSB = [Buf("bank%d" % i) for i in range(8)]

        def bank(i):
            return psum_t[i // 2][:, (i % 2) * 512:(i % 2) * 512 + 512]

        def pair(i):
            return psum_t[i][:, :]

        def MM(out, lhsT, rhs, start, stop, r, w, sig=False):
            S.op("pe", lambda e: e.matmul(out, lhsT=lhsT, rhs=rhs, start=start, stop=stop), r, w, signal=(stop or sig))

        def TR(out, in_, ident, r, w, signal=True):
            S.op("pe", lambda e: e.transpose(out=out, in_=in_, identity=ident), r, w, signal=signal)

        def ACT(out, in_, func, r, w, scale=1.0, bias=None, accum=None):
            def f(e):
                kw = {}
                if bias is not None:
                    kw["bias"] = bias
                if accum is not None:
                    kw["accum_out"] = accum
                return e.activation(out=out, in_=in_, func=func, scale=scale, **kw)
            S.op("act", f, r, w)

        def TT(eng, out, in0, in1, op, r, w):
            S.op(eng, lambda e: e.tensor_tensor(out=out, in0=in0, in1=in1, op=op), r, w)

        def TS(eng, out, in0, s1, s2, op0, op1, r, w):
            if s2 is None:
                S.op(eng, lambda e: e.tensor_scalar(out=out, in0=in0, scalar1=s1, scalar2=None, op0=op0), r, w)
            else:
                S.op(eng, lambda e: e.tensor_scalar(out=out, in0=in0, scalar1=s1, scalar2=s2, op0=op0, op1=op1), r, w)

        def STT(out, in0, scalar, in1, op0, op1, r, w):
            S.op("dve", lambda e: e.scalar_tensor_tensor(out=out, in0=in0, scalar=scalar, in1=in1, op0=op0, op1=op1), r, w)

        def CP(eng, out, in_, r, w):
            if eng == "act":
                S.op("act", lambda e: e.activation(out=out, in_=in_, func=AF.Copy), r, w)
            else:
                S.op(eng, lambda e: e.tensor_copy(out=out, in_=in_), r, w)

        def MEMSET(eng, ap, val, w):
            S.op(eng, lambda e: e.memset(ap, val), [], w)

        ones_f = AR.alloc("ones_f", [128, 128], F32)
        ident_f = AR.alloc("ident_f", [128, 128], F32)
        triu_f = AR.alloc("triu_f", [128, 128], F32)
        ident_b = AR.alloc("ident_b", [128, 128], BF16)
        triu_b = AR.alloc("triu_b", [128, 128], BF16)
        ones_b = AR.alloc("ones_b", [128, 128], BF16)
        onesrc = AR.alloc("onesrc", [128, 2048], BF16)
        sel = AR.alloc("sel", [128, 16, 128], BF16)
        cb = AR.alloc("cb", [128, 32, 16], F32)
        chv = AR.alloc("chv", [128, DEPTH * NCH], F32)
        der = AR.alloc("der", [128, DEPTH * 40], F32)
        cst = AR.alloc("cst", [128, 8], F32)
        PERS = [ones_f, ident_f, triu_f, ident_b, triu_b, ones_b, sel, cb, chv, der, cst]

        MEMSET("pool", ones_f[:, :], 1.0, [ones_f])
        S.op("pool", lambda e: e.affine_select(out=ident_f[:, :], in_=ones_f[:, :], pattern=[[1, 128]], compare_op=ALU.is_equal,
                                               fill=0.0, base=0, channel_multiplier=-1), [ones_f], [ident_f])
        S.op("pool", lambda e: e.affine_select(out=triu_f[:, :], in_=ones_f[:, :], pattern=[[1, 128]], compare_op=ALU.is_ge,
                                               fill=0.0, base=0, channel_multiplier=-1), [ones_f], [triu_f])
        CP("dve", ident_b[:, :], ident_f[:, :], [ident_f], [ident_b])
        CP("dve", triu_b[:, :], triu_f[:, :], [triu_f], [triu_b])
        CP("dve", ones_b[:, :], ones_f[:, :], [ones_f], [ones_b])
        MEMSET("pool", onesrc[:, :], 1.0, [onesrc])
        S.op("pool", lambda e: e.affine_select(out=sel[:, :, :], in_=onesrc[:, :].rearrange("p (a b) -> p a b", a=16),
                                               pattern=[[1, 16], [0, 128]], compare_op=ALU.is_equal, fill=0.0, base=0,
                                               channel_multiplier=-1), [onesrc], [sel])
        MEMSET("pool", cb[:, :, :], -1e30, [cb])
        for b in range(1, 16):
            MEMSET("pool", cb[:, 2 * b:2 * b + 2, 0:b], 0.0, [cb])
        MEMSET("pool", cst[:, 0:1], 1.0, [cst])
        MEMSET("pool", cst[:, 1:2], EPS, [cst])
        MEMSET("pool", cst[:, 2:3], -0.5, [cst])
        MEMSET("pool", cst[:, 3:4], 0.5, [cst])
        S.dma("sp", chv[:, :], chv_d[:, :], [], [chv])
        for l in range(n_layers):
            cv = l * NCH
            dv = l * 40
            TS("dve", der[:, dv:dv + 16], chv[:, cv + C_BR:cv + C_BR + 16], 0.5, None, ALU.mult, None, [chv], [der])
            ACT(der[:, dv + 32:dv + 40], chv[:, cv + C_LAM:cv + C_LAM + 8], AF.Exp, [chv], [der], scale=-1.0)
            ACT(der[:, dv + 32:dv + 40], der[:, dv + 32:dv + 40], AF.Ln, [der, cst], [der], bias=cst[:, 0:1])
            TS("dve", der[:, dv + 16:dv + 24], der[:, dv + 32:dv + 40], -4.0, None, ALU.mult, None, [der], [der])
            TS("dve", der[:, dv + 24:dv + 32], der[:, dv + 32:dv + 40], -8.0, None, ALU.mult, None, [der], [der])

        pers_mark = AR.mark()

        def DUMP(name, ap, reads):
            if not dbg:
                return
            o = nc.dram_tensor(name, list(ap.shape), ap.dtype, kind="ExternalOutput").ap()
            if len(ap.shape) == 3:
                for i in range(ap.shape[1]):
                    S.dma("sp", o[:, i, :], ap[:, i, :], reads, [Buf("dbg")])
            else:
                S.dma("sp", o[:, :], ap, reads, [Buf("dbg")])

        wdn_buf = [Buf("wdn%d" % l) for l in range(DEPTH)]
        for l in range(n_layers):
            for c in range(4):
                S.dma("pool", wdn_bf[l][:, c * 6144:(c + 1) * 6144].rearrange("p (a b) -> p a b", b=1024),
                      w_dnr[l][:, c * 6144:(c + 1) * 6144].rearrange("p (a b) -> p a b", b=1024), [], [wdn_buf[l]])

        vln_b = [Buf("vln%d" % t) for t in range(NT)]
        mrg_b = [Buf("mrg%d" % g) for g in range(8)]
        xres1_b = [Buf("xr1_%d" % t) for t in range(NT)]
        xres2_b = [Buf("xr2_%d" % t) for t in range(NT)]
        x1T_b = [Buf("x1T%d" % t) for t in range(8)]
        xTd_b = [Buf("xTd%d" % t) for t in range(8)]
        y_b = [Buf("y%d" % t) for t in range(NT)]

        def ln_rows(mode, zt, stats, mv, rs, gbc, bbc, out_ap, r_extra, w_out, tmp2=None):
            S.op("dve", lambda e: e.bn_stats(out=stats[:, 0, :], in_=zt[:, 0:512]), [zt], [stats])
            S.op("dve", lambda e: e.bn_stats(out=stats[:, 1, :], in_=zt[:, 512:1024]), [zt], [stats])
            S.op("dve", lambda e: e.bn_aggr(out=mv[:, :], in_=stats[:, :, :]), [stats], [mv])
            if mode == "pool":
                TS("dve", rs[:, 0:1], mv[:, 1:2], EPS, None, ALU.add, None, [mv], [rs])
                TT("pool", rs[:, 1:2], rs[:, 0:1], cst[:, 2:3], ALU.pow, [rs, cst], [rs])
            else:
                ACT(rs[:, 0:1], mv[:, 1:2], AF.Sqrt, [mv, cst], [rs], bias=cst[:, 1:2])
                S.op("dve", lambda e: e.reciprocal(out=rs[:, 1:2], in_=rs[:, 0:1]), [rs], [rs])
            STT(zt[:, :], zt[:, :], mv[:, 0:1], gbc[:, :], ALU.subtract, ALU.mult, [zt, mv, gbc], [zt])
            STT(out_ap, zt[:, :], rs[:, 1:2], bbc[:, :], ALU.mult, ALU.add, [zt, rs, bbc] + list(r_extra), list(w_out))

        for l in range(n_layers):
            cv = l * NCH
            dv = l * 40
            last = (l == n_layers - 1)
            xin_d = x_d if l == 0 else xres2
            xin_b = [None] * NT if l == 0 else xres2_b

            S.barrier()
            AR.reset(pers_mark)
            xT = AR.alloc("xT", [128, 8, SEQ], BF16)
            xT_b = [Buf("xT%d" % t) for t in range(NT)]
            macc = AR.alloc("macc", [128, SEQ], F32)
            wg = [AR.alloc("wg%d" % i, [128, 9, 8, 128], BF16) for i in range(2)]
            rgw = AR.alloc("rgw", [128, 8, 2, 128], BF16)
            wspb = AR.alloc("wspb", [128, 8, 128], BF16)
            bspbc = AR.alloc("bspbc", [128, 8, 128], F32)
            mix_mark = AR.mark()

            S.dma("pool", rgw[:, :, :, :].rearrange("p a b c -> p (a b c)"), rgw_d[l][:, :], [], [rgw])
            S.dma("sp", bspbc[:, :, :].rearrange("p a b -> p (a b)"), bsp_d[l].partition_broadcast(128), [], [bspbc])

            if l == 0:
                xin = [AR.alloc("xin%d" % i, [128, 1024], F32) for i in range(2)]
                for tt in range(NT):
                    xi = xin[tt % 2]
                    S.dma("sp", xi[:, :], x_d[tt * 128:(tt + 1) * 128, :], [], [xi])
                    for h in range(2):
                        pp = (tt % 2) * 2 + h
                        for j in range(4):
                            TR(pair(pp)[:, j * 128:(j + 1) * 128], xi[:, (h * 4 + j) * 128:(h * 4 + j + 1) * 128], ident_f[:, :],
                               [xi, ident_f], [PSB[2 * pp], PSB[2 * pp + 1]], signal=(j == 3))
                        CP("act" if h == 0 else "dve", xT[:, h * 4:(h + 1) * 4, tt * 128:(tt + 1) * 128],
                           pair(pp)[:, 0:512].rearrange("p (a b) -> p a b", a=4), [PSB[2 * pp], PSB[2 * pp + 1]], [xT_b[tt]])
            else:
                for c in range(8):
                    S.dma("sp", xT[:, :, c * 512:(c + 1) * 512], xT_d[:, :, c * 512:(c + 1) * 512], [xTd_b[c]],
                          [xT_b[4 * c + i] for i in range(4)])
            if l == 0 and False:
                DUMP("d_xT", xT[:, :, :], xT_b)
            S.barrier()
            AR.reset(mix_mark)

            wsv = AR.alloc("wsv", [128, 8, 1024], BF16)
            wspf = AR.alloc("wspf", [128, 8, 128], F32)
            gbc = AR.alloc("gbc", [128, 1024], F32)
            bbc = AR.alloc("bbc", [128, 1024], F32)
            v32 = [AR.alloc("v32_%d" % i, [128, 1024], F32) for i in range(2)]
            vlnb = [AR.alloc("vlnb%d" % i, [128, 1024], BF16) for i in range(2)]
            stats = [AR.alloc("stats%d" % i, [128, 2, 6], F32) for i in range(2)]
            mv = [AR.alloc("mv%d" % i, [128, 2], F32) for i in range(2)]
            rs = [AR.alloc("rs%d" % i, [128, 2], F32) for i in range(2)]
            for kc in range(8):
                S.dma("pool", wsv[:, kc, :], w_sv[l][:, kc * 1024:(kc + 1) * 1024], [], [wsv])
            S.dma("sp", wspf[:, :, :].rearrange("p a b -> p (a b)"), wspT_d[l][:, :], [], [wspf])
            S.dma("sp", gbc[:, :], tokv_d[l][0].partition_broadcast(128), [], [gbc])
            S.dma("sp", bbc[:, :], tokv_d[l][1].partition_broadcast(128), [], [bbc])
            TT("dve", wspb[:, :, :], wspf[:, :, :], triu_f[:, :].unsqueeze(1).broadcast_to([128, 8, 128]), ALU.mult,
               [wspf, triu_f], [wspb])
            def load_wg(g):
                t = wg[g % 2]
                for cg in range(9):
                    S.dma("pool", t[:, cg, :, :].rearrange("p a b -> p (a b)"), w_in_g[l][g][:, cg * 1024:(cg + 1) * 1024], [], [t])
            load_wg(0)
            for tt in range(NT):
                sl = tt % 2
                pp = sl
                for h in range(2):
                    for kc in range(8):
                        MM(pair(pp)[:, h * 512:(h + 1) * 512], xT[:, kc, tt * 128:(tt + 1) * 128], wsv[:, kc, h * 512:(h + 1) * 512],
                           kc == 0, kc == 7, [xT_b[tt], wsv], [PSB[2 * pp + h]])
                ACT(v32[sl][:, :], pair(pp), AF.Gelu_apprx_tanh, [PSB[2 * pp], PSB[2 * pp + 1]], [v32[sl]])
                if l == 0 and tt == 0:
                    DUMP("d_v32", v32[sl][:, :], [v32[sl]])
                    DUMP("d_wsv", wsv[:, :, :], [wsv])
                ln_rows("pool", v32[sl], stats[sl], mv[sl], rs[sl], gbc, bbc, vlnb[sl][:, :], [], [vlnb[sl]])
                S.dma("sp", vln_d[tt * 128:(tt + 1) * 128, :], vlnb[sl][:, :], [vlnb[sl]], [vln_b[tt]])
            S.barrier()
            AR.reset(mix_mark)
            g_mark = AR.mark()

            for g in range(8):
                wt = wg[g % 2]
                if g + 1 < 8:
                    load_wg(g + 1)

                def proj(cg, bk, t0, n):
                    xb = [xT_b[t] for t in range(t0 // 128, (t0 + n + 127) // 128)]
                    for kc in range(8):
                        MM(bank(bk)[:, 0:n], wt[:, cg, kc, :], xT[:, kc, t0:t0 + n], kc == 0, kc == 7, [wt] + xb, [PSB[bk]])

                AR.reset(g_mark)
                HT = 2048
                axp = AR.alloc("axp", [128, HT + 4], F32)
                cc = AR.alloc("cc", [128, HT], F32)
                ccb = AR.alloc("ccb", [128, HT], BF16)
                tr_ = AR.alloc("tr", [128, HT], F32)
                ti_ = AR.alloc("ti", [128, HT], F32)
                aa = AR.alloc("aa", [128, HT], F32)
                a2 = AR.alloc("a2", [128, HT], F32)
                halo = AR.alloc("halo", [128, 4], F32)
                gg = [AR.alloc("gg%d" % i, [128, 512], F32) for i in range(2)]
                tg = [AR.alloc("tg%d" % i, [128, 512], F32) for i in range(2)]
                macc_b = [Buf("macc%d" % c) for c in range(8)]

                def cw(k):
                    return chv[:, cv + C_CW + k * 8 + g:cv + C_CW + k * 8 + g + 1]

                for hh in range(2):
                    T0 = hh * HT
                    if hh == 0:
                        MEMSET("pool", axp[:, 0:3], 0.0, [axp])
                    else:
                        CP("pool", axp[:, 0:3], halo[:, 0:3], [halo], [axp])
                    for c4 in range(4):
                        proj(CG_AX, c4, T0 + c4 * 512, 512)
                        CP("act", axp[:, 3 + c4 * 512:3 + (c4 + 1) * 512], bank(c4), [PSB[c4]], [axp])
                    TS("dve", cc[:, :], axp[:, 3:3 + HT], cw(3), chv[:, cv + C_CB + g:cv + C_CB + g + 1], ALU.mult, ALU.add, [axp, chv], [cc])
                    for k in (2, 1, 0):
                        STT(cc[:, :], axp[:, k:k + HT], cw(k), cc[:, :], ALU.mult, ALU.add, [axp, chv, cc], [cc])
                    CP("pool", halo[:, 0:3], axp[:, HT:HT + 3], [axp], [halo])
                    CP("act", ccb[:, :], cc[:, :], [cc], [ccb])
                    for c4 in range(4):
                        MM(bank(c4), rgw[:, g, 0, :], ccb[:, c4 * 512:(c4 + 1) * 512], True, True, [rgw, ccb], [PSB[c4]])
                    for c4 in range(4):
                        MM(bank(4 + c4), rgw[:, g, 1, :], ccb[:, c4 * 512:(c4 + 1) * 512], True, True, [rgw, ccb], [PSB[4 + c4]])
                    for p2 in range(2):
                        ACT(tr_[:, p2 * 1024:(p2 + 1) * 1024], pair(p2), AF.Tanh, [PSB[2 * p2], PSB[2 * p2 + 1], der], [tr_], scale=0.5,
                            bias=der[:, dv + g:dv + g + 1])
                    for p2 in range(2):
                        ACT(ti_[:, p2 * 1024:(p2 + 1) * 1024], pair(2 + p2), AF.Tanh, [PSB[4 + 2 * p2], PSB[5 + 2 * p2], der], [ti_], scale=0.5,
                            bias=der[:, dv + 8 + g:dv + 8 + g + 1])
                    ACT(aa[:, :], tr_[:, :], AF.Exp, [tr_, der], [aa], scale=der[:, dv + 16 + g:dv + 16 + g + 1],
                        bias=der[:, dv + 16 + g:dv + 16 + g + 1])
                    ACT(a2[:, :], tr_[:, :], AF.Exp, [tr_, der], [a2], scale=der[:, dv + 24 + g:dv + 24 + g + 1],
                        bias=der[:, dv + 24 + g:dv + 24 + g + 1])
                    TS("dve", a2[:, :], a2[:, :], 1.0, None, ALU.min, None, [a2], [a2])
                    ACT(a2[:, :], a2[:, :], AF.Sqrt, [a2, cst], [a2], scale=-1.0, bias=cst[:, 0:1])
                    STT(ti_[:, :], ti_[:, :], 1.0, cc[:, :], ALU.add, ALU.mult, [ti_, cc], [ti_])
                    STT(ti_[:, :], a2[:, :], 0.5, ti_[:, :], ALU.mult, ALU.mult, [a2, ti_], [ti_])
                    init = 0.0 if hh == 0 else macc[:, T0 - 1:T0]
                    mbs = [macc_b[4 * hh + i] for i in range(4)]
                    rb = [aa, ti_] + ([macc_b[4 * hh - 1]] if hh > 0 else [])
                    S.op("dve", (lambda T0, init: lambda e: e.tensor_tensor_scan(out=macc[:, T0:T0 + HT], data0=aa[:, :], data1=ti_[:, :],
                                                                                  initial=init, op0=ALU.mult, op1=ALU.add))(T0, init), rb, mbs)
                for c in range(8):
                    sl = c % 2
                    t0 = c * 512
                    b1 = [0, 2, 4, 6][c % 4]
                    b2 = b1 + 1
                    proj(CG_AG, b1, t0, 512)
                    ACT(gg[sl][:, :], bank(b1), AF.Gelu_apprx_tanh, [PSB[b1]], [gg[sl]])
                    proj(CG_GA, b2, t0, 512)
                    ACT(tg[sl][:, :], bank(b2), AF.Tanh, [PSB[b2]], [tg[sl]], scale=0.5)
                    TT("dve", gg[sl][:, :], gg[sl][:, :], macc[:, t0:t0 + 512], ALU.mult, [gg[sl], macc_b[c]], [gg[sl]])
                    STT(macc[:, t0:t0 + 512], tg[sl][:, :], 1.0, gg[sl][:, :], ALU.add, ALU.mult, [tg[sl], gg[sl]], [macc_b[c]])
                S.barrier()

                AR.reset(g_mark)
                vlng = AR.alloc("vlng", [128, 32, 128], BF16)
                gu = [AR.alloc("gu%d" % i, [128, 512], F32) for i in range(2)]
                tgs = [AR.alloc("tgs%d" % i, [128, 512], F32) for i in range(2)]
                m1 = [AR.alloc("m1_%d" % i, [128, 512], F32) for i in range(2)]
                for q4 in range(4):
                    S.dma("sp", vlng[:, q4 * 8:(q4 + 1) * 8, :],
                          vln_d[q4 * 1024:(q4 + 1) * 1024, g * 128:(g + 1) * 128].rearrange("(n p) c -> p n c", p=128),
                          [vln_b[t] for t in range(q4 * 8, q4 * 8 + 8)], [vlng])
                for c in range(8):
                    sl = c % 2
                    t0 = c * 512
                    bu, bg, bm = 0 + sl, 2 + sl, 4 + sl
                    proj(CG_SU, bu, t0, 512)
                    ACT(gu[sl][:, :], bank(bu), AF.Gelu_apprx_tanh, [PSB[bu]], [gu[sl]])
                    proj(CG_GS, bg, t0, 512)
                    ACT(tgs[sl][:, :], bank(bg), AF.Tanh, [PSB[bg]], [tgs[sl]], scale=0.5)
                    for n in range(4):
                        MM(bank(bm)[:, n * 128:(n + 1) * 128], vlng[:, 4 * c + n, :], wspb[:, g, :], True, True, [vlng, wspb], [PSB[bm]])
                    TT("dve", m1[sl][:, :].rearrange("p (a b) -> p a b", a=4), bank(bm).rearrange("p (a b) -> p a b", a=4),
                       bspbc[:, g, :].unsqueeze(1).broadcast_to([128, 4, 128]), ALU.add, [PSB[bm], bspbc], [m1[sl]])
                    TT("dve", m1[sl][:, :], m1[sl][:, :], gu[sl][:, :], ALU.mult, [m1[sl], gu[sl]], [m1[sl]])
                    STT(m1[sl][:, :], tgs[sl][:, :], 1.0, m1[sl][:, :], ALU.add, ALU.mult, [tgs[sl], m1[sl]], [m1[sl]])
                    TT("dve", macc[:, t0:t0 + 512], macc[:, t0:t0 + 512], m1[sl][:, :], ALU.add, [m1[sl], macc_b[c]], [macc_b[c]])
                S.barrier()

                AR.reset(g_mark)
                qT = AR.alloc("qT", [128, SEQ], BF16)
                kT = AR.alloc("kT", [128, SEQ], BF16)
                Vt = AR.alloc("Vt", [128, 32, 128], BF16)
                negmT = AR.alloc("negmT", [128, SEQ], BF16)
                ksum = AR.alloc("ksum", [128, 16], F32)
                kmT = AR.alloc("kmT", [128, 16], BF16)
                gsb = AR.alloc("gsb", [128, 32, 16], F32)
                mx8 = AR.alloc("mx8", [128, 32, 8], F32)
                thr = AR.alloc("thr", [128, 32], F32)
                negm = AR.alloc("negm", [128, 32, 16], BF16)
                NPT = 8
                PT = [AR.alloc("PT%d" % i, [128, 256], BF16) for i in range(NPT)]
                tgm = [AR.alloc("tgm%d" % i, [128, 256], F32) for i in range(2)]
                rec = [AR.alloc("rec%d" % i, [128, 256], F32) for i in range(2)]
                ot = [AR.alloc("ot%d" % i, [128, 256], F32) for i in range(2)]
                mrgb = [AR.alloc("mrgb%d" % i, [128, 1024], BF16) for i in range(2)]
                MEMSET("dve", negmT[:, :], 0.0, [negmT])
                for c in range(8):
                    t0 = c * 512
                    bq, bk_ = 0 + (c % 2), 2 + (c % 2)
                    proj(CG_K, bk_, t0, 512)
                    for h in range(2):
                        ACT(kT[:, t0 + h * 256:t0 + (h + 1) * 256], bank(bk_)[:, h * 256:(h + 1) * 256], AF.Copy, [PSB[bk_]], [kT, ksum],
                            accum=ksum[:, 2 * c + h:2 * c + h + 1])
                    proj(CG_Q, bq, t0, 512)
                    CP("dve", qT[:, t0:t0 + 512], bank(bq), [PSB[bq]], [qT])
                for t4 in range(8):
                    bv = 4 + (t4 % 2)
                    for j in range(4):
                        tt = t4 * 4 + j
                        for kc in range(8):
                            MM(bank(bv)[:, j * 128:(j + 1) * 128], xT[:, kc, tt * 128:(tt + 1) * 128], wt[:, CG_V, kc, :], kc == 0, kc == 7,
                               [xT_b[tt], wt], [PSB[bv]])
                    CP("act" if t4 % 2 == 0 else "dve", Vt[:, t4 * 4:(t4 + 1) * 4, :], bank(bv).rearrange("p (a b) -> p a b", a=4),
                       [PSB[bv]], [Vt])
                TS("dve", kmT[:, :], ksum[:, :], 1.0 / 256.0, None, ALU.mult, None, [ksum], [kmT])
                bgt = 6
                for qt in range(32):
                    MM(bank(bgt)[:, qt * 16:(qt + 1) * 16], qT[:, qt * 128:(qt + 1) * 128], kmT[:, :], True, True, [qT, kmT], [PSB[bgt]])
                TT("dve", gsb[:, :, :].rearrange("p a b -> p (a b)"), bank(bgt), cb[:, :, :].rearrange("p a b -> p (a b)"), ALU.add,
                   [PSB[bgt], cb], [gsb])
                for qt in range(32):
                    S.op("dve", (lambda qt: lambda e: e.max(out=mx8[:, qt, :], in_=gsb[:, qt, :]))(qt), [gsb], [mx8])
                TS("dve", thr[:, :], mx8[:, :, 2], -1e29, None, ALU.max, None, [mx8], [thr])
                TT("dve", negm[:, :, :], gsb[:, :, :], thr[:, :].unsqueeze(2).broadcast_to([128, 32, 16]), ALU.is_lt, [gsb, thr], [negm])
                TS("dve", negm[:, :, :], negm[:, :, :], NEGBIG, None, ALU.mult, None, [negm], [negm])
                for q8 in range(4):
                    bt = 6 + ((q8 + 1) % 2)
                    tb = bank(bt).bitcast(BF16)
                    for j in range(8):
                        qt = q8 * 8 + j
                        TR(tb[0:16, j * 128:(j + 1) * 128], negm[:, qt, :], ident_b[:, :], [negm, ident_b], [PSB[bt]], signal=(j == 7))
                    CP("act", negmT[0:16, q8 * 1024:(q8 + 1) * 1024], tb[0:16, :], [PSB[bt]], [negmT])
                flat = []
                for b in range(16):
                    tl_ = [("past", kt) for kt in range(2 * b)] + [("own0", 2 * b), ("own1", 2 * b + 1)]
                    for i, (kind, kt) in enumerate(tl_):
                        flat.append((b, kind, kt, i == 0, i == len(tl_) - 1))
                nfl = len(flat)

                RING = [0, 1, 2, 5]
                ring_ctr = [0]

                def emit_score(i):
                    b, kind, kt, first, lastt = flat[i]
                    q0 = b * 256
                    rb = RING[ring_ctr[0] % 4]
                    ring_ctr[0] += 1
                    stv = bank(rb)[:, 0:256]
                    sb = PSB[rb]
                    P = PT[i % NPT]
                    if kind == "past":
                        j = kt // 2
                        MM(stv, kT[:, kt * 128:(kt + 1) * 128], qT[:, q0:q0 + 256], True, False, [kT, qT], [sb])
                        MM(stv, sel[:, j, :], negmT[:, q0:q0 + 256], False, True, [sel, negmT], [sb])
                        ACT(P[:, 0:256], stv, AF.Exp, [sb], [P], scale=ATT_SCALE)
                    elif kind == "own0":
                        MM(stv, kT[:, kt * 128:(kt + 1) * 128], qT[:, q0:q0 + 256], True, True, [kT, qT], [sb])
                        ACT(P[:, 0:256], stv, AF.Exp, [sb], [P], scale=ATT_SCALE)
                        TT("pool", P[:, 0:128], P[:, 0:128], triu_b[:, :], ALU.mult, [P, triu_b], [P])
                    else:
                        MM(stv[:, 0:128], kT[:, kt * 128:(kt + 1) * 128], qT[:, q0 + 128:q0 + 256], True, True, [kT, qT], [sb])
                        ACT(P[:, 0:128], stv[:, 0:128], AF.Exp, [sb], [P], scale=ATT_SCALE)
                        TT("pool", P[:, 0:128], P[:, 0:128], triu_b[:, :], ALU.mult, [P, triu_b], [P])

                def emit_pv(i):
                    b, kind, kt, first, lastt = flat[i]
                    q0 = b * 256
                    P = PT[i % NPT]
                    bo = 3 if b % 2 == 0 else 6
                    bl = 4 if b % 2 == 0 else 7
                    if kind == "own1":
                        MM(bank(bo)[:, 128:256], Vt[:, kt, :], P[:, 0:128], False, True, [Vt, P], [PSB[bo]])
                        MM(bank(bl)[:, 128:256], ones_b[:, :], P[:, 0:128], False, True, [ones_b, P], [PSB[bl]])
                    else:
                        MM(bank(bo)[:, 0:256], Vt[:, kt, :], P[:, 0:256], first, False, [Vt, P], [PSB[bo]])
                        MM(bank(bl)[:, 0:256], ones_b[:, :], P[:, 0:256], first, False, [ones_b, P], [PSB[bl]])
                    if lastt:
                        sl = b % 2
                        bgm = RING[ring_ctr[0] % 4]
                        ring_ctr[0] += 1
                        proj(CG_GM, bgm, q0, 256)
                        ACT(tgm[sl][:, :], bank(bgm)[:, 0:256], AF.Tanh, [PSB[bgm]], [tgm[sl]], scale=0.5)
                        S.op("dve", lambda e: e.reciprocal(out=rec[sl][:, :], in_=bank(bl)[:, 0:256]), [PSB[bl]], [rec[sl]])
                        TT("dve", ot[sl][:, :], bank(bo)[:, 0:256], rec[sl][:, :], ALU.mult, [PSB[bo], rec[sl]], [ot[sl]])
                        STT(ot[sl][:, :], tgm[sl][:, :], 1.0, ot[sl][:, :], ALU.add, ALU.mult, [tgm[sl], ot[sl]], [ot[sl]])
                        mb = macc_b[b // 2]
                        TT("dve", macc[:, q0:q0 + 256], macc[:, q0:q0 + 256], ot[sl][:, :], ALU.add, [ot[sl], mb], [mb])

                LOOK = 3
                for i in range(min(LOOK, nfl)):
                    emit_score(i)
                for i in range(nfl):
                    if i + LOOK < nfl:
                        emit_score(i + LOOK)
                    emit_pv(i)
                for c4 in range(4):
                    mt = mrgb[c4 % 2]
                    ACT(mt[:, :], macc[:, c4 * 1024:(c4 + 1) * 1024], AF.Copy, [macc_b[2 * c4], macc_b[2 * c4 + 1]], [mt], scale=0.5)
                    S.dma("sp", mrg_d[:, g, c4 * 1024:(c4 + 1) * 1024], mt[:, :], [mt], [mrg_b[g]])
                S.barrier()

            if dbg and l == 0:
                S.dma("sp", dbg_out["d_mrg_0"][:, :, :], mrg_d[:, :, :], mrg_b, [Buf("dbg")])
            if dbg and l == n_layers - 1:
                S.dma("sp", dbg_out["d_mrg"][:, :, :], mrg_d[:, :, :], mrg_b, [Buf("dbg")])
                S.dma("sp", dbg_out["d_vln"][:, :], vln_d[:, :], vln_b, [Buf("dbg")])

            S.barrier()
            AR.reset(pers_mark)
            wup = AR.alloc("wup", [128, NFC, 2, 8, 128], BF16)
            f_mark = AR.mark()
            for fc in range(NFC):
                S.dma("pool", wup[:, fc, :, :, :].rearrange("p a b c -> p a (b c)"),
                      w_upr[l][:, fc * 2048:(fc + 1) * 2048].rearrange("p (a c) -> p a c", c=1024), [], [wup])
            woutb = AR.alloc("woutb", [128, 8, 1024], BF16)
            g1 = AR.alloc("g1", [128, 1024], F32)
            b1 = AR.alloc("b1", [128, 1024], F32)
            mt_ = [AR.alloc("mt%d" % i, [128, 8, 512], BF16) for i in range(2)]
            x1Tg = [AR.alloc("x1Tg%d" % i, [128, 8, 512], BF16) for i in range(2)]
            xr = [AR.alloc("xr%d" % i, [128, 1024], F32) for i in range(2)]
            zt = [AR.alloc("zt%d" % i, [128, 1024], F32) for i in range(2)]
            x1t = [AR.alloc("x1t%d" % i, [128, 1024], F32) for i in range(2)]
            stats = [AR.alloc("stats%d" % i, [128, 2, 6], F32) for i in range(2)]
            mv = [AR.alloc("mv%d" % i, [128, 2], F32) for i in range(2)]
            rs = [AR.alloc("rs%d" % i, [128, 2], F32) for i in range(2)]
            for kc in range(8):
                S.dma("pool", woutb[:, kc, :], w_outr[l][:, kc * 1024:(kc + 1) * 1024], [], [woutb])
            S.dma("sp", g1[:, :], tokv_d[l][2].partition_broadcast(128), [], [g1])
            S.dma("sp", b1[:, :], tokv_d[l][3].partition_broadcast(128), [], [b1])
            for c in range(8):
                m = mt_[c % 2]
                xg = x1Tg[c % 2]
                S.dma("sp", m[:, :, :], mrg_d[:, :, c * 512:(c + 1) * 512], mrg_b, [m])
                for j in range(4):
                    tt = c * 4 + j
                    sl = tt % 2
                    pp = sl
                    S.dma("sp", xr[sl][:, :], xin_d[tt * 128:(tt + 1) * 128, :], [xin_b[tt]], [xr[sl]])
                    for h in range(2):
                        for kc in range(8):
                            MM(pair(pp)[:, h * 512:(h + 1) * 512], m[:, kc, j * 128:(j + 1) * 128], woutb[:, kc, h * 512:(h + 1) * 512],
                               kc == 0, kc == 7, [m, woutb], [PSB[2 * pp + h]])
                    STT(zt[sl][:, :], xr[sl][:, :], ALPHA, pair(pp), ALU.mult, ALU.add, [xr[sl], PSB[2 * pp], PSB[2 * pp + 1]], [zt[sl]])
                    ln_rows("act", zt[sl], stats[sl], mv[sl], rs[sl], g1, b1, x1t[sl][:, :], [], [x1t[sl]])
                    S.dma("sp", xres1[tt * 128:(tt + 1) * 128, :], x1t[sl][:, :], [x1t[sl]], [xres1_b[tt]])
                    if dbg and l == 0:
                        S.dma("sp", dbg_out["d_x1_0"][tt * 128:(tt + 1) * 128, :], x1t[sl][:, :], [x1t[sl]], [Buf("dbg")])
                    if dbg and l == n_layers - 1:
                        S.dma("sp", dbg_out["d_x1"][tt * 128:(tt + 1) * 128, :], x1t[sl][:, :], [x1t[sl]], [Buf("dbg")])
                    pt = 2 + sl
                    for kc in range(8):
                        TR(pair(pt)[:, kc * 128:(kc + 1) * 128], x1t[sl][:, kc * 128:(kc + 1) * 128], ident_f[:, :], [x1t[sl], ident_f],
                           [PSB[2 * pt], PSB[2 * pt + 1]], signal=(kc == 7))
                    CP("act", xg[:, :, j * 128:(j + 1) * 128], pair(pt).rearrange("p (a b) -> p a b", a=8), [PSB[2 * pt], PSB[2 * pt + 1]], [xg])
                S.dma("sp", x1T_d[:, :, c * 512:(c + 1) * 512], xg[:, :, :], [xg], [x1T_b[c]])
            S.barrier()

            AR.reset(f_mark)
            g2 = AR.alloc("g2", [128, 1024], F32)
            b2 = AR.alloc("b2", [128, 1024], F32)
            wdn = [AR.alloc("wdn%d" % i, [128, 4, 1024], BF16) for i in range(2)]
            xg_ = [AR.alloc("xg%d" % i, [128, 8, 256], BF16) for i in range(2)]
            xr1 = [AR.alloc("xr1_%d" % i, [128, 2, 1024], F32) for i in range(2)]
            actT = AR.alloc("actT", [128, NFC, 256], BF16)
            actT_b = [Buf("actT%d" % i) for i in range(NFC)]
            hgp = [AR.alloc("hgp%d" % i, [128, 260], F32) for i in range(2)]
            cf = [AR.alloc("cf%d" % i, [128, 256], F32) for i in range(2)]
            gl = [AR.alloc("gl%d" % i, [128, 256], F32) for i in range(2)]
            carry = AR.alloc("carry", [128, NFC, 2], F32)
            zt = [AR.alloc("zt%d" % i, [128, 1024], F32) for i in range(2)]
            x2t = [AR.alloc("x2t%d" % i, [128, 1024], F32) for i in range(2)]
            x2Tg = [AR.alloc("x2Tg%d" % i, [128, 8, 512], BF16) for i in range(1)]
            stats = [AR.alloc("stats%d" % i, [128, 2, 6], F32) for i in range(2)]
            mv = [AR.alloc("mv%d" % i, [128, 2], F32) for i in range(2)]
            rs = [AR.alloc("rs%d" % i, [128, 2], F32) for i in range(2)]
            S.dma("sp", g2[:, :], tokv_d[l][4].partition_broadcast(128), [], [g2])
            S.dma("sp", b2[:, :], tokv_d[l][5].partition_broadcast(128), [], [b2])
            MEMSET("pool", carry[:, :, :], 0.0, [carry])
            wd_rr = 0
            for gi in range(16):
                t0 = gi * 256
                xg = xg_[gi % 2]
                x1r = xr1[gi % 2]
                S.dma("sp", xg[:, :, :], x1T_d[:, :, t0:t0 + 256], [x1T_b[gi // 2]], [xg])
                S.dma("sp", x1r[:, :, :], xres1[t0:t0 + 256, :].rearrange("(a p) d -> p a d", p=128),
                      [xres1_b[2 * gi], xres1_b[2 * gi + 1]], [x1r])
                def up(fc):
                    sl = fc % 2
                    bg_, bu_ = 4 + sl, 6 + sl
                    if fc % 4 == 0:
                        ch = fc // 4
                        wd = wdn[ch % 2]
                        S.dma("sp", wd[:, :, :].rearrange("p a b -> p (a b)"), wdn_bf[l][:, ch * 4096:(ch + 1) * 4096], [wdn_buf[l]], [wd])
                    for kc in range(8):
                        MM(bank(bg_)[:, 0:256], wup[:, fc, 0, kc, :], xg[:, kc, :], kc == 0, kc == 7, [wup, xg], [PSB[bg_]])
                    for kc in range(8):
                        MM(bank(bu_)[:, 0:256], wup[:, fc, 1, kc, :], xg[:, kc, :], kc == 0, kc == 7, [wup, xg], [PSB[bu_]])
                    CP("pool", hgp[sl][:, 0:2], carry[:, fc, :], [carry], [hgp[sl]])
                    CP("act", hgp[sl][:, 2:258], bank(bg_)[:, 0:256], [PSB[bg_]], [hgp[sl]])
                    CP("pool", carry[:, fc, :], hgp[sl][:, 256:258], [hgp[sl]], [carry])

                    def fw(k):
                        return chv[:, cv + C_FW + k * NFC + fc:cv + C_FW + k * NFC + fc + 1]
                    ACT(cf[sl][:, :], bank(bg_)[:, 0:256], AF.Identity, [PSB[bg_], chv], [cf[sl]], scale=fw(2),
                        bias=chv[:, cv + C_FB + fc:cv + C_FB + fc + 1])
                    STT(cf[sl][:, :], hgp[sl][:, 1:257], fw(1), cf[sl][:, :], ALU.mult, ALU.add, [hgp[sl], cf[sl], chv], [cf[sl]])
                    STT(cf[sl][:, :], hgp[sl][:, 0:256], fw(0), cf[sl][:, :], ALU.mult, ALU.add, [hgp[sl], cf[sl], chv], [cf[sl]])
                    ACT(gl[sl][:, :], cf[sl][:, :], AF.Gelu_apprx_tanh, [cf[sl]], [gl[sl]])
                    TT("dve", actT[:, fc, :], gl[sl][:, :], bank(bu_)[:, 0:256], ALU.mult, [gl[sl], PSB[bu_]], [actT_b[fc]])

                def down(fc):
                    wd = wdn[(fc // 4) % 2]
                    f6 = fc % 4
                    for tl in range(2):
                        for h in range(2):
                            MM(pair(tl)[:, h * 512:(h + 1) * 512], actT[:, fc, tl * 128:(tl + 1) * 128], wd[:, f6, h * 512:(h + 1) * 512],
                               fc == 0, fc == NFC - 1, [actT_b[fc], wd], [PSB[2 * tl + h]], sig=(f6 == 3 and tl == 1 and h == 1))

                for fc in range(NFC + 2):
                    if fc < NFC:
                        up(fc)
                    if fc >= 2:
                        down(fc - 2)
                for tl in range(2):
                    tt = gi * 2 + tl
                    sl = tt % 2
                    STT(zt[sl][:, :], x1r[:, tl, :], ALPHA, pair(tl), ALU.mult, ALU.add, [x1r, PSB[2 * tl], PSB[2 * tl + 1]], [zt[sl]])
                    ln_rows("pool", zt[sl], stats[sl], mv[sl], rs[sl], g2, b2, x2t[sl][:, :], [], [x2t[sl]])
                    if last:
                        S.dma("sp", y_d[tt * 128:(tt + 1) * 128, :], x2t[sl][:, :], [x2t[sl]], [y_b[tt]])
                    else:
                        S.dma("sp", xres2[tt * 128:(tt + 1) * 128, :], x2t[sl][:, :], [x2t[sl]], [xres2_b[tt]])
                        xg2 = x2Tg[0]
                        pt = 2 + sl
                        for kc in range(8):
                            TR(pair(pt)[:, kc * 128:(kc + 1) * 128], x2t[sl][:, kc * 128:(kc + 1) * 128], ident_f[:, :], [x2t[sl], ident_f],
                               [PSB[2 * pt], PSB[2 * pt + 1]], signal=(kc == 7))
                        CP("act", xg2[:, :, (tt % 4) * 128:(tt % 4 + 1) * 128], pair(pt).rearrange("p (a b) -> p a b", a=8),
                           [PSB[2 * pt], PSB[2 * pt + 1]], [xg2])
                        if tt % 4 == 3:
                            c = tt // 4
                            S.dma("sp", xT_d[:, :, c * 512:(c + 1) * 512], xg2[:, :, :], [xg2], [xTd_b[c]])
            if dbg and not last:
                o = nc.dram_tensor("d_x2", [SEQ, D], F32, kind="ExternalOutput").ap()
                for q4 in range(4):
                    S.dma("sp", o[q4 * 1024:(q4 + 1) * 1024, :], xres2[q4 * 1024:(q4 + 1) * 1024, :], xres2_b, [Buf("dbg")])
                o = nc.dram_tensor("d_xTd", [128, 8, SEQ], BF16, kind="ExternalOutput").ap()
                for q4 in range(8):
                    S.dma("sp", o[:, q4, :], xT_d[:, q4, :], xTd_b, [Buf("dbg")])
            S.barrier()

        S.barrier()
        S.replay()
    return nc


def _prep_weights(inp):
    f = np.float32
    w_in = np.asarray(inp["w_in"], f)
    L = w_in.shape[0]
    cgs = [0, 1, 2, 4, 5, 6, 7, 8, 9]
    w6 = w_in.reshape(L, 8, 128, 10, 8, 128)
    w_in_g = np.ascontiguousarray(w6[:, :, :, cgs, :, :].transpose(0, 4, 2, 3, 1, 5)).reshape(L, 8, 128, 9 * 1024)
    w_sv = np.ascontiguousarray(w6[:, :, :, 3, :, :].transpose(0, 2, 1, 3, 4)).reshape(L, 128, 8 * 1024)
    w_out = np.asarray(inp["w_out"], f).reshape(L, 8, 128, 1024)
    w_outr = np.ascontiguousarray(w_out.transpose(0, 2, 1, 3)).reshape(L, 128, 8 * 1024)
    w_up = np.asarray(inp["w_ffn_up"], f).reshape(L, 8, 128, 2, NFC, 128)
    w_upr = np.ascontiguousarray(w_up.transpose(0, 2, 4, 3, 1, 5)).reshape(L, 128, NFC * 2048)
    w_dn = np.asarray(inp["w_ffn_down"], f).reshape(L, NFC, 128, 1024)
    w_dnr = np.ascontiguousarray(w_dn.transpose(0, 2, 1, 3)).reshape(L, 128, NFC * 1024)
    wr = np.asarray(inp["w_rgate"], f)
    wi = np.asarray(inp["w_igate"], f)
    rgw = np.ascontiguousarray(np.stack([wr, wi], axis=2).transpose(0, 3, 1, 2, 4)).reshape(L, 128, 2048)
    wsp = np.asarray(inp["w_spatial"], f)
    wspT = np.ascontiguousarray(wsp.transpose(0, 3, 1, 2)).reshape(L, 128, 1024)
    chv = np.zeros((128, L * NCH), f)

    def pc(v, n):
        return np.asarray(v, f).reshape(n, 128).T

    for l in range(L):
        o = l * NCH
        for k in range(4):
            chv[:, o + C_CW + k * 8:o + C_CW + (k + 1) * 8] = pc(inp["conv_rg_w"][l][k], 8)
        chv[:, o + C_CB:o + C_CB + 8] = pc(inp["conv_rg_b"][l], 8)
        chv[:, o + C_BR:o + C_BR + 8] = pc(inp["b_rgate"][l], 8)
        chv[:, o + C_BI:o + C_BI + 8] = pc(inp["b_igate"][l], 8)
        chv[:, o + C_LAM:o + C_LAM + 8] = pc(inp["lru_lambda"][l], 8)
        for k in range(3):
            chv[:, o + C_FW + k * NFC:o + C_FW + (k + 1) * NFC] = pc(inp["conv_ffn_w"][l][k], NFC)
        chv[:, o + C_FB:o + C_FB + NFC] = pc(inp["conv_ffn_b"][l], NFC)
    bsp = np.ascontiguousarray(np.asarray(inp["b_spatial"], f).reshape(L, 1024))
    tokv = np.ascontiguousarray(np.stack([np.asarray(inp[k], f) for k in
                                          ("sgu_ln_g", "sgu_ln_b", "ln_mix_g", "ln_mix_b", "ln_ffn_g", "ln_ffn_b")], axis=1))
    return dict(w_in_g=w_in_g, w_sv=w_sv, w_outr=w_outr, w_upr=w_upr, w_dnr=w_dnr, rgw=rgw, wspT=wspT, chv=chv, bsp=bsp, tokv=tokv)


_CACHE = {}


def kernel(**inputs):
    x = np.asarray(inputs["x"], np.float32)
    B = x.shape[0]
    wts = _prep_weights(inputs)
    if "nc" not in _CACHE:
        _CACHE["nc"] = build_program()
    nc = _CACHE["nc"]
    in_maps = []
    for b in range(B):
        m = {"x": np.ascontiguousarray(x[b])}
        m.update(wts)
        in_maps.append(m)
    res = run_bass_kernel_spmd(nc, in_maps, core_ids=list(range(B)))
    return np.stack([np.asarray(r["y"], np.float32) for r in res.results], axis=0)
```

```python
from contextlib import ExitStack
import numpy as np
import concourse.bass as bass
import concourse.mybir as mybir
from concourse.bass_utils import run_bass_kernel_spmd

F32 = mybir.dt.float32
BF16 = mybir.dt.bfloat16
AF = mybir.ActivationFunctionType
ALU = mybir.AluOpType

ENGS = ("pe", "act", "dve", "pool", "sp")
SAME_ENGINE_SYNC = True

D = 1024
SEQ = 4096
NT = 32
KC = 8
DEPTH = 2
DFF = 3072
NFC = 24
ALPHA = float((2 * DEPTH) ** 0.25)
EPS = 1e-5
ATT_SCALE = float(128 ** -0.5)
NEGBIG = -30000.0
C_CW, C_CB, C_BR, C_BI, C_LAM, C_FW, C_FB, NCH = 0, 32, 40, 48, 56, 64, 136, 160
CG_AX, CG_AG, CG_SU, CG_Q, CG_K, CG_V, CG_GA, CG_GS, CG_GM = range(9)


class Buf:
    __slots__ = ("w", "r", "name")

    def __init__(self, name=""):
        self.w = {}
        self.r = {}
        self.name = name


class Tile:
    def __init__(self, ap, name):
        self.ap = ap
        self.buf = Buf(name)
        self.name = name

    def __getitem__(self, k):
        return self.ap[k]


def _bufs(lst):
    out = []
    for x in lst:
        if x is None:
            continue
        out.append(x.buf if isinstance(x, Tile) else x)
    return out


class Sched:
    def __init__(self, nc, es):
        self.nc = nc
        self.es = es
        self.streams = {e: [] for e in ENGS}
        self.cnt = {}
        self.sem = {}
        for e in ("pe", "act", "dve", "pool"):
            self.sem[e] = es.enter_context(nc.semaphore("sem_" + e))
            self.cnt[e] = 0
        self.waited = {e: {} for e in ENGS}
        self.dma_pool = []
        self.dma_rr = 0
        self.dma_cnt = {}

    def new_dma_sem(self, name):
        s = self.es.enter_context(self.nc.semaphore(name))
        self.dma_cnt[id(s)] = [s, 0]
        return s

    def _pool_sem(self):
        if len(self.dma_pool) < 32:
            s = self.new_dma_sem("dq%d" % len(self.dma_pool))
            self.dma_pool.append(s)
            return s
        s = self.dma_pool[self.dma_rr % len(self.dma_pool)]
        self.dma_rr += 1
        return s

    def _deps(self, reads, writes):
        deps = {}

        def add(d):
            for k, v in d.items():
                if deps.get(k, (None, 0))[1] < v[1]:
                    deps[k] = v

        for b in reads:
            add(b.w)
        for b in writes:
            add(b.w)
            add(b.r)
        return deps

    def _emit_waits(self, eng, deps):
        own = self.sem.get(eng)
        for k, (s, v) in deps.items():
            if own is not None and s is own and (eng == "pe" or not SAME_ENGINE_SYNC):
                continue
            if self.waited[eng].get(k, 0) >= v:
                continue
            self.waited[eng][k] = v
            self.streams[eng].append(("wait", s, v))

    def _record(self, tok, reads, writes):
        k = id(tok[0])
        for b in reads:
            if b.r.get(k, (None, 0))[1] < tok[1]:
                b.r[k] = tok
        for b in writes:
            if b.w.get(k, (None, 0))[1] < tok[1]:
                b.w[k] = tok

    def op(self, eng, fn, reads=(), writes=(), signal=True):
        reads = _bufs(reads)
        writes = _bufs(writes)
        self._emit_waits(eng, self._deps(reads, writes))
        s = self.sem[eng]
        if signal:
            self.cnt[eng] += 1
            tok = (s, self.cnt[eng])
        else:
            tok = (s, self.cnt[eng] + 1)
        self.streams[eng].append(("op", fn, s if signal else None))
        self._record(tok, reads, writes)
        return tok

    def dma(self, q, out, in_, reads=(), writes=(), sem=None):
        reads = _bufs(reads)
        writes = _bufs(writes)
        if sem is None:
            if q == "pool":
                if not hasattr(self, "swq"):
                    self.swq = [self.new_dma_sem("swq%d" % i) for i in range(2)]
                    self.swq_rr = 0
                sem = self.swq[self.swq_rr % 2]
                self.swq_rr += 1
            else:
                sem = self._pool_sem()
        ent = self.dma_cnt[id(sem)]
        deps = self._deps(reads, writes)
        if ent[1] > 0:
            deps[id(sem)] = (sem, max(deps.get(id(sem), (None, 0))[1], ent[1]))
        self._emit_waits(q, deps)
        ent[1] += 16
        tok = (sem, ent[1])
        self.streams[q].append(("dma", out, in_, sem))
        self._record(tok, reads, writes)
        return tok

    def wait_all(self, eng, bufs):
        self._emit_waits(eng, self._deps(_bufs(bufs), []))

    def barrier(self):
        deps = {}
        for e in ("pe", "act", "dve", "pool"):
            if self.cnt[e] > 0:
                deps[id(self.sem[e])] = (self.sem[e], self.cnt[e])
        for k, (s, c) in self.dma_cnt.items():
            if c > 0:
                deps[k] = (s, c)
        for e in ENGS:
            self._emit_waits(e, deps)

    def replay(self):
        nc = self.nc
        streams = self.streams

        def run(e, stream):
            for it in stream:
                if it[0] == "wait":
                    e.wait_ge(it[1], it[2])
                elif it[0] == "op":
                    ins = it[1](e)
                    if it[2] is not None:
                        ins.then_inc(it[2], 1)
                else:
                    e.dma_start(out=it[1], in_=it[2]).then_inc(it[3], 16)

        with nc.Block() as block:
            @block.tensor
            def _(e):
                run(e, streams["pe"])

            @block.scalar
            def _(e):
                run(e, streams["act"])

            @block.vector
            def _(e):
                run(e, streams["dve"])

            @block.gpsimd
            def _(e):
                run(e, streams["pool"])

            @block.sync
            def _(e):
                run(e, streams["sp"])


class Arena:
    def __init__(self, ap, nwords):
        self.ap = ap
        self.n = nwords
        self.off = 0
        self.marks = []

    def alloc(self, name, shape, dtype):
        free = 1
        for s in shape[1:]:
            free *= s
        words = free if dtype == F32 else (free + 1) // 2
        assert self.off + words <= self.n, (name, self.off, words, self.n)
        v = self.ap[:, self.off:self.off + words]
        self.off += words
        if dtype != F32:
            v = v.bitcast(dtype)
            if (free % 2) == 1:
                v = v[:, 0:free]
        if len(shape) == 3:
            v = v.rearrange("p (a b) -> p a b", a=shape[1])
        elif len(shape) == 4:
            v = v.rearrange("p (a b c) -> p a b c", a=shape[1], b=shape[2])
        elif len(shape) == 5:
            v = v.rearrange("p (a b c d) -> p a b c d", a=shape[1], b=shape[2], c=shape[3])
        if shape[0] < 128:
            v = v[0:shape[0]]
        return Tile(v, name)

    def mark(self):
        return self.off

    def reset(self, m):
        self.off = m


def build_program(n_layers=DEPTH, dbg=False):
    nc = bass.Bass("TRN2", target_bir_lowering=False)

    def din(name, shape, dt=F32):
        return nc.dram_tensor(name, list(shape), dt, kind="ExternalInput").ap()

    def dscr(name, shape, dt):
        return nc.dram_tensor(name, list(shape), dt).ap()

    x_d = din("x", [SEQ, D])
    w_in_g = din("w_in_g", [DEPTH, 8, 128, 9 * 1024])
    w_sv = din("w_sv", [DEPTH, 128, 8 * 1024])
    w_outr = din("w_outr", [DEPTH, 128, 8 * 1024])
    w_upr = din("w_upr", [DEPTH, 128, NFC * 2048])
    w_dnr = din("w_dnr", [DEPTH, 128, NFC * 1024])
    rgw_d = din("rgw", [DEPTH, 128, 2048])
    wspT_d = din("wspT", [DEPTH, 128, 1024])
    chv_d = din("chv", [128, DEPTH * NCH])
    bsp_d = din("bsp", [DEPTH, 1024])
    tokv_d = din("tokv", [DEPTH, 6, 1024])
    y_d = nc.dram_tensor("y", [SEQ, D], F32, kind="ExternalOutput").ap()

    xres1 = dscr("xres1", [SEQ, D], F32)
    xres2 = dscr("xres2", [SEQ, D], F32)
    vln_d = dscr("vln_d", [SEQ, D], BF16)
    mrg_d = dscr("mrg_d", [128, 8, SEQ], BF16)
    xT_d = dscr("xT_d", [128, 8, SEQ], BF16)
    x1T_d = dscr("x1T_d", [128, 8, SEQ], BF16)
    wdn_bf = dscr("wdn_bf", [DEPTH, 128, NFC * 1024], BF16)
    dbg_out = {}
    if dbg:
        dbg_out["d_mrg"] = nc.dram_tensor("d_mrg", [128, 8, SEQ], BF16, kind="ExternalOutput").ap()
        dbg_out["d_x1"] = nc.dram_tensor("d_x1", [SEQ, D], F32, kind="ExternalOutput").ap()
        dbg_out["d_x1_0"] = nc.dram_tensor("d_x1_0", [SEQ, D], F32, kind="ExternalOutput").ap()
        dbg_out["d_mrg_0"] = nc.dram_tensor("d_mrg_0", [128, 8, SEQ], BF16, kind="ExternalOutput").ap()
        dbg_out["d_vln"] = nc.dram_tensor("d_vln", [SEQ, D], BF16, kind="ExternalOutput").ap()

    with ExitStack() as es:
        S = Sched(nc, es)
        NW = 53000
        arena_t = es.enter_context(nc.sbuf_tensor("arena", [128, NW], F32))
        AR = Arena(arena_t[:, :], NW)
        psum_t = [es.enter_context(nc.psum_tensor("ps%d" % i, [128, 1024], F32)) for i in range(4)]
        PSB = [Buf("bank%d" % i) for i in range(8)]

        def bank(i):
            return psum_t[i // 2][:, (i % 2) * 512:(i % 2) * 512 + 512]

        def pair(i):
            return psum_t[i][:, :]

        def MM(out, lhsT, rhs, start, stop, r, w, sig=False):
            S.op("pe", lambda e: e.matmul(out, lhsT=lhsT, rhs=rhs, start=start, stop=stop), r, w, signal=(stop or sig))

        def TR(out, in_, ident, r, w, signal=True):
            S.op("pe", lambda e: e.transpose(out=out, in_=in_, identity=ident), r, w, signal=signal)

        def ACT(out, in_, func, r, w, scale=1.0, bias=None, accum=None):
            def f(e):
                kw = {}
                if bias is not None:
                    kw["bias"] = bias
                if accum is not None:
                    kw["accum_out"] = accum
                return e.activation(out=out, in_=in_, func=func, scale=scale, **kw)
            S.op("act", f, r, w)

        def TT(eng, out, in0, in1, op, r, w):
            S.op(eng, lambda e: e.tensor_tensor(out=out, in0=in0, in1=in1, op=op), r, w)

        def TS(eng, out, in0, s1, s2, op0, op1, r, w):
            if s2 is None:
                S.op(eng, lambda e: e.tensor_scalar(out=out, in0=in0, scalar1=s1, scalar2=None, op0=op0), r, w)
            else:
                S.op(eng, lambda e: e.tensor_scalar(out=out, in0=in0, scalar1=s1, scalar2=s2, op0=op0, op1=op1), r, w)

        def STT(out, in0, scalar, in1, op0, op1, r, w):
            S.op("dve", lambda e: e.scalar_tensor_tensor(out=out, in0=in0, scalar=scalar, in1=in1, op0=op0, op1=op1), r, w)

        def CP(eng, out, in_, r, w):
            if eng == "act":
                S.op("act", lambda e: e.activation(out=out, in_=in_, func=AF.Copy), r, w)
            else:
                S.op(eng, lambda e: e.tensor_copy(out=out, in_=in_), r, w)

        def MEMSET(eng, ap, val, w):
            S.op(eng, lambda e: e.memset(ap, val), [], w)

        ones_f = AR.alloc("ones_f", [128, 128], F32)
        ident_f = AR.alloc("ident_f", [128, 128], F32)
        triu_f = AR.alloc("triu_f", [128, 128], F32)
        ident_b = AR.alloc("ident_b", [128, 128], BF16)
        triu_b = AR.alloc("triu_b", [128, 128], BF16)
        ones_b = AR.alloc("ones_b", [128, 128], BF16)
        onesrc = AR.alloc("onesrc", [128, 2048], BF16)
        sel = AR.alloc("sel", [128, 16, 128], BF16)
        cb = AR.alloc("cb", [128, 32, 16], F32)
        chv = AR.alloc("chv", [128, DEPTH * NCH], F32)
        der = AR.alloc("der", [128, DEPTH * 40], F32)
        cst = AR.alloc("cst", [128, 8], F32)
        PERS = [ones_f, ident_f, triu_f, ident_b, triu_b, ones_b, sel, cb, chv, der, cst]

        MEMSET("pool", ones_f[:, :], 1.0, [ones_f])
        S.op("pool", lambda e: e.affine_select(out=ident_f[:, :], in_=ones_f[:, :], pattern=[[1, 128]], compare_op=ALU.is_equal,
                                               fill=0.0, base=0, channel_multiplier=-1), [ones_f], [ident_f])
        S.op("pool", lambda e: e.affine_select(out=triu_f[:, :], in_=ones_f[:, :], pattern=[[1, 128]], compare_op=ALU.is_ge,
                                               fill=0.0, base=0, channel_multiplier=-1), [ones_f], [triu_f])
        CP("dve", ident_b[:, :], ident_f[:, :], [ident_f], [ident_b])
        CP("dve", triu_b[:, :], triu_f[:, :], [triu_f], [triu_b])
        CP("dve", ones_b[:, :], ones_f[:, :], [ones_f], [ones_b])
        MEMSET("pool", onesrc[:, :], 1.0, [onesrc])
        S.op("pool", lambda e: e.affine_select(out=sel[:, :, :], in_=onesrc[:, :].rearrange("p (a b) -> p a b", a=16),
                                               pattern=[[1, 16], [0, 128]], compare_op=ALU.is_equal, fill=0.0, base=0,
                                               channel_multiplier=-1), [onesrc], [sel])
        MEMSET("pool", cb[:, :, :], -1e30, [cb])
        for b in range(1, 16):
            MEMSET("pool", cb[:, 2 * b:2 * b + 2, 0:b], 0.0, [cb])
        MEMSET("pool", cst[:, 0:1], 1.0, [cst])
        MEMSET("pool", cst[:, 1:2], EPS, [cst])
        MEMSET("pool", cst[:, 2:3], -0.5, [cst])
        MEMSET("pool", cst[:, 3:4], 0.5, [cst])
        S.dma("sp", chv[:, :], chv_d[:, :], [], [chv])
        for l in range(n_layers):
            cv = l * NCH
            dv = l * 40
            TS("dve", der[:, dv:dv + 16], chv[:, cv + C_BR:cv + C_BR + 16], 0.5, None, ALU.mult, None, [chv], [der])
            ACT(der[:, dv + 32:dv + 40], chv[:, cv + C_LAM:cv + C_LAM + 8], AF.Exp, [chv], [der], scale=-1.0)
            ACT(der[:, dv + 32:dv + 40], der[:, dv + 32:dv + 40], AF.Ln, [der, cst], [der], bias=cst[:, 0:1])
            TS("dve", der[:, dv + 16:dv + 24], der[:, dv + 32:dv + 40], -4.0, None, ALU.mult, None, [der], [der])
            TS("dve", der[:, dv + 24:dv + 32], der[:, dv + 32:dv + 40], -8.0, None, ALU.mult, None, [der], [der])

        pers_mark = AR.mark()

        def DUMP(name, ap, reads):
            if not dbg:
                return
            o = nc.dram_tensor(name, list(ap.shape), ap.dtype, kind="ExternalOutput").ap()
            if len(ap.shape) == 3:
                for i in range(ap.shape[1]):
                    S.dma("sp", o[:, i, :], ap[:, i, :], reads, [Buf("dbg")])
            else:
                S.dma("sp", o[:, :], ap, reads, [Buf("dbg")])

        wdn_buf = [Buf("wdn%d" % l) for l in range(DEPTH)]
        for l in range(n_layers):
            for c in range(4):
                S.dma("pool", wdn_bf[l][:, c * 6144:(c + 1) * 6144].rearrange("p (a b) -> p a b", b=1024),
                      w_dnr[l][:, c * 6144:(c + 1) * 6144].rearrange("p (a b) -> p a b", b=1024), [], [wdn_buf[l]])

        vln_b = [Buf("vln%d" % t) for t in range(NT)]
        mrg_b = [Buf("mrg%d" % g) for g in range(8)]
        xres1_b = [Buf("xr1_%d" % t) for t in range(NT)]
        xres2_b = [Buf("xr2_%d" % t) for t in range(NT)]
        x1T_b = [Buf("x1T%d" % t) for t in range(8)]
        xTd_b = [Buf("xTd%d" % t) for t in range(8)]
        y_b = [Buf("y%d" % t) for t in range(NT)]

        def ln_a(mode, zt, stats, mv, rs):
            S.op("dve", lambda e: e.bn_stats(out=stats[:, 0, :], in_=zt[:, 0:512]), [zt], [stats])
            S.op("dve", lambda e: e.bn_stats(out=stats[:, 1, :], in_=zt[:, 512:1024]), [zt], [stats])
            S.op("dve", lambda e: e.bn_aggr(out=mv[:, :], in_=stats[:, :, :]), [stats], [mv])
            if mode == "pool":
                TS("dve", rs[:, 0:1], mv[:, 1:2], EPS, None, ALU.add, None, [mv], [rs])
                TT("pool", rs[:, 1:2], rs[:, 0:1], cst[:, 2:3], ALU.pow, [rs, cst], [rs])
            else:
                ACT(rs[:, 0:1], mv[:, 1:2], AF.Sqrt, [mv, cst], [rs], bias=cst[:, 1:2])
                S.op("dve", lambda e: e.reciprocal(out=rs[:, 1:2], in_=rs[:, 0:1]), [rs], [rs])

        def ln_b(zt, mv, rs, gbc, bbc, out_ap, w_out):
            STT(zt[:, :], zt[:, :], mv[:, 0:1], gbc[:, :], ALU.subtract, ALU.mult, [zt, mv, gbc], [zt])
            STT(out_ap, zt[:, :], rs[:, 1:2], bbc[:, :], ALU.mult, ALU.add, [zt, rs, bbc], list(w_out))

        for l in range(n_layers):
            cv = l * NCH
            dv = l * 40
            TS("dve", der[:, dv:dv + 16], chv[:, cv + C_BR:cv + C_BR + 16], 0.5, None, ALU.mult, None, [chv], [der])
            ACT(der[:, dv + 32:dv + 40], chv[:, cv + C_LAM:cv + C_LAM + 8], AF.Exp, [chv], [der], scale=-1.0)
            ACT(der[:, dv + 32:dv + 40], der[:, dv + 32:dv + 40], AF.Ln, [der, cst], [der], bias=cst[:, 0:1])
            TS("dve", der[:, dv + 16:dv + 24], der[:, dv + 32:dv + 40], -4.0, None, ALU.mult, None, [der], [der])
            TS("dve", der[:, dv + 24:dv + 32], der[:, dv + 32:dv + 40], -8.0, None, ALU.mult, None, [der], [der])

        pers_mark = AR.mark()

        def DUMP(name, ap, reads):
            if not dbg:
                return
            o = nc.dram_tensor(name, list(ap.shape), ap.dtype, kind="ExternalOutput").ap()
            if len(ap.shape) == 3:
                for i in range(ap.shape[1]):
                    S.dma("sp", o[:, i, :], ap[:, i, :], reads, [Buf("dbg")])
            else:
                S.dma("sp", o[:, :], ap, reads, [Buf("dbg")])

        wdn_buf = [Buf("wdn%d" % l) for l in range(DEPTH)]
        for l in range(n_layers):
            for c in range(4):
                S.dma("pool", wdn_bf[l][:, c * 6144:(c + 1) * 6144].rearrange("p (a b) -> p a b", b=1024),
                      w_dnr[l][:, c * 6144:(c + 1) * 6144].rearrange("p (a b) -> p a b", b=1024), [], [wdn_buf[l]])

        vln_b = [Buf("vln%d" % t) for t in range(NT)]
        mrg_b = [Buf("mrg%d" % g) for g in range(8)]
        xres1_b = [Buf("xr1_%d" % t) for t in range(NT)]
        xres2_b = [Buf("xr2_%d" % t) for t in range(NT)]
        x1T_b = [Buf("x1T%d" % t) for t in range(8)]
        xTd_b = [Buf("xTd%d" % t) for t in range(8)]
        y_b = [Buf("y%d" % t) for t in range(NT)]

        def ln_rows(mode, zt, stats, mv, rs, gbc, bbc, out_ap, r_extra, w_out, tmp2=None):
            S.op("dve", lambda e: e.bn_stats(out=stats[:, 0, :], in_=zt[:, 0:512]), [zt], [stats])
            S.op("dve", lambda e: e.bn_stats(out=stats[:, 1, :], in_=zt[:, 512:1024]), [zt], [stats])
            S.op("dve", lambda e: e.bn_aggr(out=mv[:, :], in_=stats[:, :, :]), [stats], [mv])
            if mode == "pool":
                TS("dve", rs[:, 0:1], mv[:, 1:2], EPS, None, ALU.add, None, [mv], [rs])
                TT("pool", rs[:, 1:2], rs[:, 0:1], cst[:, 2:3], ALU.pow, [rs, cst], [rs])
            else:
                ACT(rs[:, 0:1], mv[:, 1:2], AF.Sqrt, [mv, cst], [rs], bias=cst[:, 1:2])
                S.op("dve", lambda e: e.reciprocal(out=rs[:, 1:2], in_=rs[:, 0:1]), [rs], [rs])
            STT(zt[:, :], zt[:, :], mv[:, 0:1], gbc[:, :], ALU.subtract, ALU.mult, [zt, mv, gbc], [zt])
            STT(out_ap, zt[:, :], rs[:, 1:2], bbc[:, :], ALU.mult, ALU.add, [zt, rs, bbc] + list(r_extra), list(w_out))

        for l in range(n_layers):
            cv = l * NCH
            dv = l * 40
            last = (l == n_layers - 1)
            xin_d = x_d if l == 0 else xres2
            xin_b = [None] * NT if l == 0 else xres2_b

            S.barrier()
            AR.reset(pers_mark)
            xT = AR.alloc("xT", [128, 8, SEQ], BF16)
            xT_b = [Buf("xT%d" % t) for t in range(NT)]
            macc = AR.alloc("macc", [128, SEQ], F32)
            wg = [AR.alloc("wg%d" % i, [128, 9, 8, 128], BF16) for i in range(2)]
            rgw = AR.alloc("rgw", [128, 8, 2, 128], BF16)
            wspb = AR.alloc("wspb", [128, 8, 128], BF16)
            bspbc = AR.alloc("bspbc", [128, 8, 128], F32)
            mix_mark = AR.mark()

            S.dma("pool", rgw[:, :, :, :].rearrange("p a b c -> p (a b c)"), rgw_d[l][:, :], [], [rgw])
            S.dma("sp", bspbc[:, :, :].rearrange("p a b -> p (a b)"), bsp_d[l].partition_broadcast(128), [], [bspbc])

            if l == 0:
                xin = [AR.alloc("xin%d" % i, [128, 1024], F32) for i in range(2)]
                for tt in range(NT):
                    xi = xin[tt % 2]
                    S.dma("sp", xi[:, :], x_d[tt * 128:(tt + 1) * 128, :], [], [xi])
                    for h in range(2):
                        pp = (tt % 2) * 2 + h
                        for j in range(4):
                            TR(pair(pp)[:, j * 128:(j + 1) * 128], xi[:, (h * 4 + j) * 128:(h * 4 + j + 1) * 128], ident_f[:, :],
                               [xi, ident_f], [PSB[2 * pp], PSB[2 * pp + 1]], signal=(j == 3))
                        CP("act" if h == 0 else "dve", xT[:, h * 4:(h + 1) * 4, tt * 128:(tt + 1) * 128],
                           pair(pp)[:, 0:512].rearrange("p (a b) -> p a b", a=4), [PSB[2 * pp], PSB[2 * pp + 1]], [xT_b[tt]])
            else:
                for c in range(8):
                    S.dma("sp", xT[:, :, c * 512:(c + 1) * 512], xT_d[:, :, c * 512:(c + 1) * 512], [xTd_b[c]],
                          [xT_b[4 * c + i] for i in range(4)])
            if l == 0 and False:
                DUMP("d_xT", xT[:, :, :], xT_b)
            S.barrier()
            AR.reset(mix_mark)

            wsv = AR.alloc("wsv", [128, 8, 1024], BF16)
            wspf = AR.alloc("wspf", [128, 8, 128], F32)
            gbc = AR.alloc("gbc", [128, 1024], F32)
            bbc = AR.alloc("bbc", [128, 1024], F32)
            v32 = [AR.alloc("v32_%d" % i, [128, 1024], F32) for i in range(2)]
            vlnb = [AR.alloc("vlnb%d" % i, [128, 1024], BF16) for i in range(2)]
            stats = [AR.alloc("stats%d" % i, [128, 2, 6], F32) for i in range(2)]
            mv = [AR.alloc("mv%d" % i, [128, 2], F32) for i in range(2)]
            rs = [AR.alloc("rs%d" % i, [128, 2], F32) for i in range(2)]
            for kc in range(8):
                S.dma("pool", wsv[:, kc, :], w_sv[l][:, kc * 1024:(kc + 1) * 1024], [], [wsv])
            S.dma("sp", wspf[:, :, :].rearrange("p a b -> p (a b)"), wspT_d[l][:, :], [], [wspf])
            S.dma("sp", gbc[:, :], tokv_d[l][0].partition_broadcast(128), [], [gbc])
            S.dma("sp", bbc[:, :], tokv_d[l][1].partition_broadcast(128), [], [bbc])
            TT("dve", wspb[:, :, :], wspf[:, :, :], triu_f[:, :].unsqueeze(1).broadcast_to([128, 8, 128]), ALU.mult,
               [wspf, triu_f], [wspb])
            def load_wg(g):
                t = wg[g % 2]
                for cg in range(9):
                    S.dma("pool", t[:, cg, :, :].rearrange("p a b -> p (a b)"), w_in_g[l][g][:, cg * 1024:(cg + 1) * 1024], [], [t])
            load_wg(0)
            def bpre_a(tt):
                sl = tt % 2
                pp = sl
                for h in range(2):
                    for kc in range(8):
                        MM(pair(pp)[:, h * 512:(h + 1) * 512], xT[:, kc, tt * 128:(tt + 1) * 128], wsv[:, kc, h * 512:(h + 1) * 512],
                           kc == 0, kc == 7, [xT_b[tt], wsv], [PSB[2 * pp + h]])
                ACT(v32[sl][:, :], pair(pp), AF.Gelu_apprx_tanh, [PSB[2 * pp], PSB[2 * pp + 1]], [v32[sl]])
                ln_a("pool", v32[sl], stats[sl], mv[sl], rs[sl])

            def bpre_b(tt):
                sl = tt % 2
                ln_b(v32[sl], mv[sl], rs[sl], gbc, bbc, vlnb[sl][:, :], [vlnb[sl]])
                S.dma("sp", vln_d[tt * 128:(tt + 1) * 128, :], vlnb[sl][:, :], [vlnb[sl]], [vln_b[tt]])

            for tt in range(NT + 1):
                if tt < NT:
                    bpre_a(tt)
                if tt >= 1:
                    bpre_b(tt - 1)
            S.barrier()
            AR.reset(mix_mark)
            g_mark = AR.mark()

            for g in range(8):
                wt = wg[g % 2]
                if g + 1 < 8:
                    load_wg(g + 1)

                def proj(cg, bk, t0, n):
                    xb = [xT_b[t] for t in range(t0 // 128, (t0 + n + 127) // 128)]
                    for kc in range(8):
                        MM(bank(bk)[:, 0:n], wt[:, cg, kc, :], xT[:, kc, t0:t0 + n], kc == 0, kc == 7, [wt] + xb, [PSB[bk]])

                AR.reset(g_mark)
                HT = 2048
                axp = AR.alloc("axp", [128, HT + 4], F32)
                cc = AR.alloc("cc", [128, HT], F32)
                ccb = AR.alloc("ccb", [128, HT], BF16)
                tr_ = AR.alloc("tr", [128, HT], F32)
                ti_ = AR.alloc("ti", [128, HT], F32)
                aa = AR.alloc("aa", [128, HT], F32)
                a2 = AR.alloc("a2", [128, HT], F32)
                halo = AR.alloc("halo", [128, 4], F32)
                gg = [AR.alloc("gg%d" % i, [128, 512], F32) for i in range(2)]
                tg = [AR.alloc("tg%d" % i, [128, 512], F32) for i in range(2)]
                macc_b = [Buf("macc%d" % c) for c in range(8)]

                def cw(k):
                    return chv[:, cv + C_CW + k * 8 + g:cv + C_CW + k * 8 + g + 1]

                for hh in range(2):
                    T0 = hh * HT
                    if hh == 0:
                        MEMSET("pool", axp[:, 0:3], 0.0, [axp])
                    else:
                        CP("pool", axp[:, 0:3], halo[:, 0:3], [halo], [axp])
                    for c4 in range(4):
                        proj(CG_AX, c4, T0 + c4 * 512, 512)
                        CP("act", axp[:, 3 + c4 * 512:3 + (c4 + 1) * 512], bank(c4), [PSB[c4]], [axp])
                    TS("dve", cc[:, :], axp[:, 3:3 + HT], cw(3), chv[:, cv + C_CB + g:cv + C_CB + g + 1], ALU.mult, ALU.add, [axp, chv], [cc])
                    for k in (2, 1, 0):
                        STT(cc[:, :], axp[:, k:k + HT], cw(k), cc[:, :], ALU.mult, ALU.add, [axp, chv, cc], [cc])
                    CP("pool", halo[:, 0:3], axp[:, HT:HT + 3], [axp], [halo])
                    CP("act", ccb[:, :], cc[:, :], [cc], [ccb])
                    for c4 in range(4):
                        MM(bank(c4), rgw[:, g, 0, :], ccb[:, c4 * 512:(c4 + 1) * 512], True, True, [rgw, ccb], [PSB[c4]])
                    for c4 in range(4):
                        MM(bank(4 + c4), rgw[:, g, 1, :], ccb[:, c4 * 512:(c4 + 1) * 512], True, True, [rgw, ccb], [PSB[4 + c4]])
                    for p2 in range(2):
                        ACT(tr_[:, p2 * 1024:(p2 + 1) * 1024], pair(p2), AF.Tanh, [PSB[2 * p2], PSB[2 * p2 + 1], der], [tr_], scale=0.5,
                            bias=der[:, dv + g:dv + g + 1])
                    for p2 in range(2):
                        ACT(ti_[:, p2 * 1024:(p2 + 1) * 1024], pair(2 + p2), AF.Tanh, [PSB[4 + 2 * p2], PSB[5 + 2 * p2], der], [ti_], scale=0.5,
                            bias=der[:, dv + 8 + g:dv + 8 + g + 1])
                    ACT(aa[:, :], tr_[:, :], AF.Exp, [tr_, der], [aa], scale=der[:, dv + 16 + g:dv + 16 + g + 1],
                        bias=der[:, dv + 16 + g:dv + 16 + g + 1])
                    ACT(a2[:, :], tr_[:, :], AF.Exp, [tr_, der], [a2], scale=der[:, dv + 24 + g:dv + 24 + g + 1],
                        bias=der[:, dv + 24 + g:dv + 24 + g + 1])
                    TS("dve", a2[:, :], a2[:, :], 1.0, None, ALU.min, None, [a2], [a2])
                    ACT(a2[:, :], a2[:, :], AF.Sqrt, [a2, cst], [a2], scale=-1.0, bias=cst[:, 0:1])
                    STT(ti_[:, :], ti_[:, :], 1.0, cc[:, :], ALU.add, ALU.mult, [ti_, cc], [ti_])
                    STT(ti_[:, :], a2[:, :], 0.5, ti_[:, :], ALU.mult, ALU.mult, [a2, ti_], [ti_])
                    init = 0.0 if hh == 0 else macc[:, T0 - 1:T0]
                    mbs = [macc_b[4 * hh + i] for i in range(4)]
                    rb = [aa, ti_] + ([macc_b[4 * hh - 1]] if hh > 0 else [])
                    S.op("dve", (lambda T0, init: lambda e: e.tensor_tensor_scan(out=macc[:, T0:T0 + HT], data0=aa[:, :], data1=ti_[:, :],
                                                                                  initial=init, op0=ALU.mult, op1=ALU.add))(T0, init), rb, mbs)
                for c in range(8):
                    sl = c % 2
                    t0 = c * 512
                    b1 = [0, 2, 4, 6][c % 4]
                    b2 = b1 + 1
                    proj(CG_AG, b1, t0, 512)
                    ACT(gg[sl][:, :], bank(b1), AF.Gelu_apprx_tanh, [PSB[b1]], [gg[sl]])
                    proj(CG_GA, b2, t0, 512)
                    ACT(tg[sl][:, :], bank(b2), AF.Tanh, [PSB[b2]], [tg[sl]], scale=0.5)
                    TT("dve", gg[sl][:, :], gg[sl][:, :], macc[:, t0:t0 + 512], ALU.mult, [gg[sl], macc_b[c]], [gg[sl]])
                    STT(macc[:, t0:t0 + 512], tg[sl][:, :], 1.0, gg[sl][:, :], ALU.add, ALU.mult, [tg[sl], gg[sl]], [macc_b[c]])
                S.barrier()

                AR.reset(g_mark)
                vlng = AR.alloc("vlng", [128, 32, 128], BF16)
                gu = [AR.alloc("gu%d" % i, [128, 512], F32) for i in range(2)]
                tgs = [AR.alloc("tgs%d" % i, [128, 512], F32) for i in range(2)]
                m1 = [AR.alloc("m1_%d" % i, [128, 512], F32) for i in range(2)]
                for q4 in range(4):
                    S.dma("sp", vlng[:, q4 * 8:(q4 + 1) * 8, :],
                          vln_d[q4 * 1024:(q4 + 1) * 1024, g * 128:(g + 1) * 128].rearrange("(n p) c -> p n c", p=128),
                          [vln_b[t] for t in range(q4 * 8, q4 * 8 + 8)], [vlng])
                for c in range(8):
                    sl = c % 2
                    t0 = c * 512
                    bu, bg, bm = 0 + sl, 2 + sl, 4 + sl
                    proj(CG_SU, bu, t0, 512)
                    ACT(gu[sl][:, :], bank(bu), AF.Gelu_apprx_tanh, [PSB[bu]], [gu[sl]])
                    proj(CG_GS, bg, t0, 512)
                    ACT(tgs[sl][:, :], bank(bg), AF.Tanh, [PSB[bg]], [tgs[sl]], scale=0.5)
                    for n in range(4):
                        MM(bank(bm)[:, n * 128:(n + 1) * 128], vlng[:, 4 * c + n, :], wspb[:, g, :], True, True, [vlng, wspb], [PSB[bm]])
                    TT("dve", m1[sl][:, :].rearrange("p (a b) -> p a b", a=4), bank(bm).rearrange("p (a b) -> p a b", a=4),
                       bspbc[:, g, :].unsqueeze(1).broadcast_to([128, 4, 128]), ALU.add, [PSB[bm], bspbc], [m1[sl]])
                    TT("dve", m1[sl][:, :], m1[sl][:, :], gu[sl][:, :], ALU.mult, [m1[sl], gu[sl]], [m1[sl]])
                    STT(m1[sl][:, :], tgs[sl][:, :], 1.0, m1[sl][:, :], ALU.add, ALU.mult, [tgs[sl], m1[sl]], [m1[sl]])
                    TT("dve", macc[:, t0:t0 + 512], macc[:, t0:t0 + 512], m1[sl][:, :], ALU.add, [m1[sl], macc_b[c]], [macc_b[c]])
                S.barrier()

                AR.reset(g_mark)
                qT = AR.alloc("qT", [128, SEQ], BF16)
                kT = AR.alloc("kT", [128, SEQ], BF16)
                Vt = AR.alloc("Vt", [128, 32, 128], BF16)
                negmT = AR.alloc("negmT", [128, SEQ], BF16)
                ksum = AR.alloc("ksum", [128, 16], F32)
                kmT = AR.alloc("kmT", [128, 16], BF16)
                gsb = AR.alloc("gsb", [128, 32, 16], F32)
                mx8 = AR.alloc("mx8", [128, 32, 8], F32)
                thr = AR.alloc("thr", [128, 32], F32)
                negm = AR.alloc("negm", [128, 32, 16], BF16)
                NPT = 8
                PT = [AR.alloc("PT%d" % i, [128, 256], BF16) for i in range(NPT)]
                tgm = [AR.alloc("tgm%d" % i, [128, 256], F32) for i in range(2)]
                rec = [AR.alloc("rec%d" % i, [128, 256], F32) for i in range(2)]
                ot = [AR.alloc("ot%d" % i, [128, 256], F32) for i in range(2)]
                mrgb = [AR.alloc("mrgb%d" % i, [128, 1024], BF16) for i in range(2)]
                MEMSET("dve", negmT[:, :], 0.0, [negmT])
                for c in range(8):
                    t0 = c * 512
                    bq, bk_ = 0 + (c % 2), 2 + (c % 2)
                    proj(CG_K, bk_, t0, 512)
                    for h in range(2):
                        ACT(kT[:, t0 + h * 256:t0 + (h + 1) * 256], bank(bk_)[:, h * 256:(h + 1) * 256], AF.Copy, [PSB[bk_]], [kT, ksum],
                            accum=ksum[:, 2 * c + h:2 * c + h + 1])
                    proj(CG_Q, bq, t0, 512)
                    CP("dve", qT[:, t0:t0 + 512], bank(bq), [PSB[bq]], [qT])
                for t4 in range(8):
                    bv = 4 + (t4 % 2)
                    for j in range(4):
                        tt = t4 * 4 + j
                        for kc in range(8):
                            MM(bank(bv)[:, j * 128:(j + 1) * 128], xT[:, kc, tt * 128:(tt + 1) * 128], wt[:, CG_V, kc, :], kc == 0, kc == 7,
                               [xT_b[tt], wt], [PSB[bv]])
                    CP("act" if t4 % 2 == 0 else "dve", Vt[:, t4 * 4:(t4 + 1) * 4, :], bank(bv).rearrange("p (a b) -> p a b", a=4),
                       [PSB[bv]], [Vt])
                TS("dve", kmT[:, :], ksum[:, :], 1.0 / 256.0, None, ALU.mult, None, [ksum], [kmT])
                bgt = 6
                for qt in range(32):
                    MM(bank(bgt)[:, qt * 16:(qt + 1) * 16], qT[:, qt * 128:(qt + 1) * 128], kmT[:, :], True, True, [qT, kmT], [PSB[bgt]])
                TT("dve", gsb[:, :, :].rearrange("p a b -> p (a b)"), bank(bgt), cb[:, :, :].rearrange("p a b -> p (a b)"), ALU.add,
                   [PSB[bgt], cb], [gsb])
                for qt in range(32):
                    S.op("dve", (lambda qt: lambda e: e.max(out=mx8[:, qt, :], in_=gsb[:, qt, :]))(qt), [gsb], [mx8])
                TS("dve", thr[:, :], mx8[:, :, 2], -1e29, None, ALU.max, None, [mx8], [thr])
                TT("dve", negm[:, :, :], gsb[:, :, :], thr[:, :].unsqueeze(2).broadcast_to([128, 32, 16]), ALU.is_lt, [gsb, thr], [negm])
                TS("dve", negm[:, :, :], negm[:, :, :], NEGBIG, None, ALU.mult, None, [negm], [negm])
                for q8 in range(4):
                    bt = 6 + ((q8 + 1) % 2)
                    tb = bank(bt).bitcast(BF16)
                    for j in range(8):
                        qt = q8 * 8 + j
                        TR(tb[0:16, j * 128:(j + 1) * 128], negm[:, qt, :], ident_b[:, :], [negm, ident_b], [PSB[bt]], signal=(j == 7))
                    CP("act", negmT[0:16, q8 * 1024:(q8 + 1) * 1024], tb[0:16, :], [PSB[bt]], [negmT])
                flat = []
                for b in range(16):
                    tl_ = [("past", kt) for kt in range(2 * b)] + [("own0", 2 * b), ("own1", 2 * b + 1)]
                    for i, (kind, kt) in enumerate(tl_):
                        flat.append((b, kind, kt, i == 0, i == len(tl_) - 1))
                nfl = len(flat)

                RING = [0, 1, 2, 5]
                ring_ctr = [0]

                def emit_score(i):
                    b, kind, kt, first, lastt = flat[i]
                    q0 = b * 256
                    rb = RING[ring_ctr[0] % 4]
                    ring_ctr[0] += 1
                    stv = bank(rb)[:, 0:256]
                    sb = PSB[rb]
                    P = PT[i % NPT]
                    if kind == "past":
                        j = kt // 2
                        MM(stv, kT[:, kt * 128:(kt + 1) * 128], qT[:, q0:q0 + 256], True, False, [kT, qT], [sb])
                        MM(stv, sel[:, j, :], negmT[:, q0:q0 + 256], False, True, [sel, negmT], [sb])
                        ACT(P[:, 0:256], stv, AF.Exp, [sb], [P], scale=ATT_SCALE)
                    elif kind == "own0":
                        MM(stv, kT[:, kt * 128:(kt + 1) * 128], qT[:, q0:q0 + 256], True, True, [kT, qT], [sb])
                        ACT(P[:, 0:256], stv, AF.Exp, [sb], [P], scale=ATT_SCALE)
                        TT("pool", P[:, 0:128], P[:, 0:128], triu_b[:, :], ALU.mult, [P, triu_b], [P])
                    else:
                        MM(stv[:, 0:128], kT[:, kt * 128:(kt + 1) * 128], qT[:, q0 + 128:q0 + 256], True, True, [kT, qT], [sb])
                        ACT(P[:, 0:128], stv[:, 0:128], AF.Exp, [sb], [P], scale=ATT_SCALE)
                        TT("pool", P[:, 0:128], P[:, 0:128], triu_b[:, :], ALU.mult, [P, triu_b], [P])

                def emit_pv(i):
                    b, kind, kt, first, lastt = flat[i]
                    q0 = b * 256
                    P = PT[i % NPT]
                    bo = 3 if b % 2 == 0 else 6
                    bl = 4 if b % 2 == 0 else 7
                    if kind == "own1":
                        MM(bank(bo)[:, 128:256], Vt[:, kt, :], P[:, 0:128], False, True, [Vt, P], [PSB[bo]])
                        MM(bank(bl)[:, 128:256], ones_b[:, :], P[:, 0:128], False, True, [ones_b, P], [PSB[bl]])
                    else:
                        MM(bank(bo)[:, 0:256], Vt[:, kt, :], P[:, 0:256], first, False, [Vt, P], [PSB[bo]])
                        MM(bank(bl)[:, 0:256], ones_b[:, :], P[:, 0:256], first, False, [ones_b, P], [PSB[bl]])
                    if lastt:
                        sl = b % 2
                        bgm = RING[ring_ctr[0] % 4]
                        ring_ctr[0] += 1
                        proj(CG_GM, bgm, q0, 256)
                        ACT(tgm[sl][:, :], bank(bgm)[:, 0:256], AF.Tanh, [PSB[bgm]], [tgm[sl]], scale=0.5)
                        S.op("dve", lambda e: e.reciprocal(out=rec[sl][:, :], in_=bank(bl)[:, 0:256]), [PSB[bl]], [rec[sl]])
                        TT("dve", ot[sl][:, :], bank(bo)[:, 0:256], rec[sl][:, :], ALU.mult, [PSB[bo], rec[sl]], [ot[sl]])
                        STT(ot[sl][:, :], tgm[sl][:, :], 1.0, ot[sl][:, :], ALU.add, ALU.mult, [tgm[sl], ot[sl]], [ot[sl]])
                        mb = macc_b[b // 2]
                        TT("dve", macc[:, q0:q0 + 256], macc[:, q0:q0 + 256], ot[sl][:, :], ALU.add, [ot[sl], mb], [mb])

                LOOK = 3
                for i in range(min(LOOK, nfl)):
                    emit_score(i)
                for i in range(nfl):
                    if i + LOOK < nfl:
                        emit_score(i + LOOK)
                    emit_pv(i)
                for c4 in range(4):
                    mt = mrgb[c4 % 2]
                    ACT(mt[:, :], macc[:, c4 * 1024:(c4 + 1) * 1024], AF.Copy, [macc_b[2 * c4], macc_b[2 * c4 + 1]], [mt], scale=0.5)
                    S.dma("sp", mrg_d[:, g, c4 * 1024:(c4 + 1) * 1024], mt[:, :], [mt], [mrg_b[g]])
                S.barrier()

            if dbg and l == 0:
                S.dma("sp", dbg_out["d_mrg_0"][:, :, :], mrg_d[:, :, :], mrg_b, [Buf("dbg")])
            if dbg and l == n_layers - 1:
                S.dma("sp", dbg_out["d_mrg"][:, :, :], mrg_d[:, :, :], mrg_b, [Buf("dbg")])
                S.dma("sp", dbg_out["d_vln"][:, :], vln_d[:, :], vln_b, [Buf("dbg")])

            S.barrier()
            AR.reset(pers_mark)
            wup = AR.alloc("wup", [128, NFC, 2, 8, 128], BF16)
            f_mark = AR.mark()
            woutb = AR.alloc("woutb", [128, 8, 1024], BF16)
            g1 = AR.alloc("g1", [128, 1024], F32)
            b1 = AR.alloc("b1", [128, 1024], F32)
            mt_ = [AR.alloc("mt%d" % i, [128, 8, 512], BF16) for i in range(2)]
            x1Tg = [AR.alloc("x1Tg%d" % i, [128, 8, 512], BF16) for i in range(2)]
            xr = [AR.alloc("xr%d" % i, [128, 1024], F32) for i in range(4)]
            zt = [AR.alloc("zt%d" % i, [128, 1024], F32) for i in range(2)]
            x1t = [AR.alloc("x1t%d" % i, [128, 1024], F32) for i in range(2)]
            stats = [AR.alloc("stats%d" % i, [128, 2, 6], F32) for i in range(2)]
            mv = [AR.alloc("mv%d" % i, [128, 2], F32) for i in range(2)]
            rs = [AR.alloc("rs%d" % i, [128, 2], F32) for i in range(2)]
            for kc in range(8):
                S.dma("pool", woutb[:, kc, :], w_outr[l][:, kc * 1024:(kc + 1) * 1024], [], [woutb])
            wup_next = [0]

            def load_wup_one():
                fc = wup_next[0]
                if fc >= NFC:
                    return
                wup_next[0] += 1
                S.dma("pool", wup[:, fc, :, :, :].rearrange("p a b c -> p a (b c)"),
                      w_upr[l][:, fc * 2048:(fc + 1) * 2048].rearrange("p (a c) -> p a c", c=1024), [], [wup])
            S.dma("sp", g1[:, :], tokv_d[l][2].partition_broadcast(128), [], [g1])
            S.dma("sp", b1[:, :], tokv_d[l][3].partition_broadcast(128), [], [b1])
            def out_a(tt):
                c, j = tt // 4, tt % 4
                m = mt_[c % 2]
                sl = tt % 2
                pp = sl
                if j == 0:
                    S.dma("sp", m[:, :, :], mrg_d[:, :, c * 512:(c + 1) * 512], mrg_b, [m])
                xs_ = xr[tt % 4]
                S.dma("sp", xs_[:, :], xin_d[tt * 128:(tt + 1) * 128, :], [xin_b[tt]], [xs_])
                for h in range(2):
                    for kc in range(8):
                        MM(pair(pp)[:, h * 512:(h + 1) * 512], m[:, kc, j * 128:(j + 1) * 128], woutb[:, kc, h * 512:(h + 1) * 512],
                           kc == 0, kc == 7, [m, woutb], [PSB[2 * pp + h]])
                STT(zt[sl][:, :], xs_[:, :], ALPHA, pair(pp), ALU.mult, ALU.add, [xs_, PSB[2 * pp], PSB[2 * pp + 1]], [zt[sl]])
                ln_a("act", zt[sl], stats[sl], mv[sl], rs[sl])

            def out_b(tt):
                c, j = tt // 4, tt % 4
                xg = x1Tg[c % 2]
                sl = tt % 2
                ln_b(zt[sl], mv[sl], rs[sl], g1, b1, x1t[sl][:, :], [x1t[sl]])
                S.dma("sp", xres1[tt * 128:(tt + 1) * 128, :], x1t[sl][:, :], [x1t[sl]], [xres1_b[tt]])
                if dbg and l == 0:
                    S.dma("sp", dbg_out["d_x1_0"][tt * 128:(tt + 1) * 128, :], x1t[sl][:, :], [x1t[sl]], [Buf("dbg")])
                if dbg and l == n_layers - 1:
                    S.dma("sp", dbg_out["d_x1"][tt * 128:(tt + 1) * 128, :], x1t[sl][:, :], [x1t[sl]], [Buf("dbg")])
                pt = 2 + sl
                for kc in range(8):
                    TR(pair(pt)[:, kc * 128:(kc + 1) * 128], x1t[sl][:, kc * 128:(kc + 1) * 128], ident_f[:, :], [x1t[sl], ident_f],
                       [PSB[2 * pt], PSB[2 * pt + 1]], signal=(kc == 7))
                CP("act", xg[:, :, j * 128:(j + 1) * 128], pair(pt).rearrange("p (a b) -> p a b", a=8), [PSB[2 * pt], PSB[2 * pt + 1]], [xg])
                if j == 3:
                    S.dma("sp", x1T_d[:, :, c * 512:(c + 1) * 512], xg[:, :, :], [xg], [x1T_b[c]])

            for tt in range(NT + 1):
                if tt < NT:
                    out_a(tt)
                if tt >= 1:
                    out_b(tt - 1)
                load_wup_one()
            while wup_next[0] < NFC:
                load_wup_one()
            S.barrier()

            AR.reset(f_mark)
            g2 = AR.alloc("g2", [128, 1024], F32)
            b2 = AR.alloc("b2", [128, 1024], F32)
            wdn = [AR.alloc("wdn%d" % i, [128, 4, 1024], BF16) for i in range(2)]
            xg_ = [AR.alloc("xg%d" % i, [128, 8, 256], BF16) for i in range(2)]
            xr1 = [AR.alloc("xr1_%d" % i, [128, 2, 1024], F32) for i in range(2)]
            actT = AR.alloc("actT", [128, NFC, 256], BF16)
            actT_b = [Buf("actT%d" % i) for i in range(NFC)]
            hgp = [AR.alloc("hgp%d" % i, [128, 260], F32) for i in range(2)]
            cf = [AR.alloc("cf%d" % i, [128, 256], F32) for i in range(2)]
            gl = [AR.alloc("gl%d" % i, [128, 256], F32) for i in range(2)]
            carry = AR.alloc("carry", [128, NFC, 2], F32)
            zt = [AR.alloc("zt%d" % i, [128, 1024], F32) for i in range(2)]
            x2t = [AR.alloc("x2t%d" % i, [128, 1024], F32) for i in range(2)]
            x2Tg = [AR.alloc("x2Tg%d" % i, [128, 8, 512], BF16) for i in range(1)]
            stats = [AR.alloc("stats%d" % i, [128, 2, 6], F32) for i in range(2)]
            mv = [AR.alloc("mv%d" % i, [128, 2], F32) for i in range(2)]
            rs = [AR.alloc("rs%d" % i, [128, 2], F32) for i in range(2)]
            S.dma("sp", g2[:, :], tokv_d[l][4].partition_broadcast(128), [], [g2])
            S.dma("sp", b2[:, :], tokv_d[l][5].partition_broadcast(128), [], [b2])
            MEMSET("pool", carry[:, :, :], 0.0, [carry])
            wd_rr = 0
            for gi in range(16):
                t0 = gi * 256
                xg = xg_[gi % 2]
                x1r = xr1[gi % 2]
                S.dma("sp", xg[:, :, :], x1T_d[:, :, t0:t0 + 256], [x1T_b[gi // 2]], [xg])
                S.dma("sp", x1r[:, :, :], xres1[t0:t0 + 256, :].rearrange("(a p) d -> p a d", p=128),
                      [xres1_b[2 * gi], xres1_b[2 * gi + 1]], [x1r])
                def up(fc):
                    sl = fc % 2
                    bg_, bu_ = 4 + sl, 6 + sl
                    if fc % 4 == 0:
                        ch = fc // 4
                        wd = wdn[ch % 2]
                        S.dma("sp", wd[:, :, :].rearrange("p a b -> p (a b)"), wdn_bf[l][:, ch * 4096:(ch + 1) * 4096], [wdn_buf[l]], [wd])
                    for kc in range(8):
                        MM(bank(bg_)[:, 0:256], wup[:, fc, 0, kc, :], xg[:, kc, :], kc == 0, kc == 7, [wup, xg], [PSB[bg_]])
                    for kc in range(8):
                        MM(bank(bu_)[:, 0:256], wup[:, fc, 1, kc, :], xg[:, kc, :], kc == 0, kc == 7, [wup, xg], [PSB[bu_]])
                    CP("pool", hgp[sl][:, 0:2], carry[:, fc, :], [carry], [hgp[sl]])
                    CP("act", hgp[sl][:, 2:258], bank(bg_)[:, 0:256], [PSB[bg_]], [hgp[sl]])
                    CP("pool", carry[:, fc, :], hgp[sl][:, 256:258], [hgp[sl]], [carry])

                    def fw(k):
                        return chv[:, cv + C_FW + k * NFC + fc:cv + C_FW + k * NFC + fc + 1]
                    ACT(cf[sl][:, :], bank(bg_)[:, 0:256], AF.Identity, [PSB[bg_], chv], [cf[sl]], scale=fw(2),
                        bias=chv[:, cv + C_FB + fc:cv + C_FB + fc + 1])
                    STT(cf[sl][:, :], hgp[sl][:, 1:257], fw(1), cf[sl][:, :], ALU.mult, ALU.add, [hgp[sl], cf[sl], chv], [cf[sl]])
                    STT(cf[sl][:, :], hgp[sl][:, 0:256], fw(0), cf[sl][:, :], ALU.mult, ALU.add, [hgp[sl], cf[sl], chv], [cf[sl]])
                    ACT(gl[sl][:, :], cf[sl][:, :], AF.Gelu_apprx_tanh, [cf[sl]], [gl[sl]])
                    TT("dve", actT[:, fc, :], gl[sl][:, :], bank(bu_)[:, 0:256], ALU.mult, [gl[sl], PSB[bu_]], [actT_b[fc]])

                def down(fc):
                    wd = wdn[(fc // 4) % 2]
                    f6 = fc % 4
                    for tl in range(2):
                        for h in range(2):
                            MM(pair(tl)[:, h * 512:(h + 1) * 512], actT[:, fc, tl * 128:(tl + 1) * 128], wd[:, f6, h * 512:(h + 1) * 512],
                               fc == 0, fc == NFC - 1, [actT_b[fc], wd], [PSB[2 * tl + h]], sig=(f6 == 3 and tl == 1 and h == 1))

                for fc in range(NFC + 2):
                    if fc < NFC:
                        up(fc)
                    if fc >= 2:
                        down(fc - 2)
                for tl in range(2):
                    tt = gi * 2 + tl
                    sl = tt % 2
                    STT(zt[sl][:, :], x1r[:, tl, :], ALPHA, pair(tl), ALU.mult, ALU.add, [x1r, PSB[2 * tl], PSB[2 * tl + 1]], [zt[sl]])
                    ln_a("pool", zt[sl], stats[sl], mv[sl], rs[sl])
                for tl in range(2):
                    tt = gi * 2 + tl
                    sl = tt % 2
                    ln_b(zt[sl], mv[sl], rs[sl], g2, b2, x2t[sl][:, :], [x2t[sl]])
                    if last:
                        S.dma("sp", y_d[tt * 128:(tt + 1) * 128, :], x2t[sl][:, :], [x2t[sl]], [y_b[tt]])
                    else:
                        S.dma("sp", xres2[tt * 128:(tt + 1) * 128, :], x2t[sl][:, :], [x2t[sl]], [xres2_b[tt]])
                        xg2 = x2Tg[0]
                        pt = 2 + sl
                        for kc in range(8):
                            TR(pair(pt)[:, kc * 128:(kc + 1) * 128], x2t[sl][:, kc * 128:(kc + 1) * 128], ident_f[:, :], [x2t[sl], ident_f],
                               [PSB[2 * pt], PSB[2 * pt + 1]], signal=(kc == 7))
                        CP("act", xg2[:, :, (tt % 4) * 128:(tt % 4 + 1) * 128], pair(pt).rearrange("p (a b) -> p a b", a=8),
                           [PSB[2 * pt], PSB[2 * pt + 1]], [xg2])
                        if tt % 4 == 3:
                            c = tt // 4
                            S.dma("sp", xT_d[:, :, c * 512:(c + 1) * 512], xg2[:, :, :], [xg2], [xTd_b[c]])
            if dbg and not last:
                o = nc.dram_tensor("d_x2", [SEQ, D], F32, kind="ExternalOutput").ap()
                for q4 in range(4):
                    S.dma("sp", o[q4 * 1024:(q4 + 1) * 1024, :], xres2[q4 * 1024:(q4 + 1) * 1024, :], xres2_b, [Buf("dbg")])
                o = nc.dram_tensor("d_xTd", [128, 8, SEQ], BF16, kind="ExternalOutput").ap()
                for q4 in range(8):
                    S.dma("sp", o[:, q4, :], xT_d[:, q4, :], xTd_b, [Buf("dbg")])
            S.barrier()

        S.barrier()
        S.replay()
    return nc


def _prep_weights(inp):
    f = np.float32
    w_in = np.asarray(inp["w_in"], f)
    L = w_in.shape[0]
    cgs = [0, 1, 2, 4, 5, 6, 7, 8, 9]
    w6 = w_in.reshape(L, 8, 128, 10, 8, 128)
    w_in_g = np.ascontiguousarray(w6[:, :, :, cgs, :, :].transpose(0, 4, 2, 3, 1, 5)).reshape(L, 8, 128, 9 * 1024)
    w_sv = np.ascontiguousarray(w6[:, :, :, 3, :, :].transpose(0, 2, 1, 3, 4)).reshape(L, 128, 8 * 1024)
    w_out = np.asarray(inp["w_out"], f).reshape(L, 8, 128, 1024)
    w_outr = np.ascontiguousarray(w_out.transpose(0, 2, 1, 3)).reshape(L, 128, 8 * 1024)
    w_up = np.asarray(inp["w_ffn_up"], f).reshape(L, 8, 128, 2, NFC, 128)
    w_upr = np.ascontiguousarray(w_up.transpose(0, 2, 4, 3, 1, 5)).reshape(L, 128, NFC * 2048)
    w_dn = np.asarray(inp["w_ffn_down"], f).reshape(L, NFC, 128, 1024)
    w_dnr = np.ascontiguousarray(w_dn.transpose(0, 2, 1, 3)).reshape(L, 128, NFC * 1024)
    wr = np.asarray(inp["w_rgate"], f)
    wi = np.asarray(inp["w_igate"], f)
    rgw = np.ascontiguousarray(np.stack([wr, wi], axis=2).transpose(0, 3, 1, 2, 4)).reshape(L, 128, 2048)
    wsp = np.asarray(inp["w_spatial"], f)
    wspT = np.ascontiguousarray(wsp.transpose(0, 3, 1, 2)).reshape(L, 128, 1024)
    chv = np.zeros((128, L * NCH), f)

    def pc(v, n):
        return np.asarray(v, f).reshape(n, 128).T

    for l in range(L):
        o = l * NCH
        for k in range(4):
            chv[:, o + C_CW + k * 8:o + C_CW + (k + 1) * 8] = pc(inp["conv_rg_w"][l][k], 8)
        chv[:, o + C_CB:o + C_CB + 8] = pc(inp["conv_rg_b"][l], 8)
        chv[:, o + C_BR:o + C_BR + 8] = pc(inp["b_rgate"][l], 8)
        chv[:, o + C_BI:o + C_BI + 8] = pc(inp["b_igate"][l], 8)
        chv[:, o + C_LAM:o + C_LAM + 8] = pc(inp["lru_lambda"][l], 8)
        for k in range(3):
            chv[:, o + C_FW + k * NFC:o + C_FW + (k + 1) * NFC] = pc(inp["conv_ffn_w"][l][k], NFC)
        chv[:, o + C_FB:o + C_FB + NFC] = pc(inp["conv_ffn_b"][l], NFC)
    bsp = np.ascontiguousarray(np.asarray(inp["b_spatial"], f).reshape(L, 1024))
    tokv = np.ascontiguousarray(np.stack([np.asarray(inp[k], f) for k in
                                          ("sgu_ln_g", "sgu_ln_b", "ln_mix_g", "ln_mix_b", "ln_ffn_g", "ln_ffn_b")], axis=1))
    return dict(w_in_g=w_in_g, w_sv=w_sv, w_outr=w_outr, w_upr=w_upr, w_dnr=w_dnr, rgw=rgw, wspT=wspT, chv=chv, bsp=bsp, tokv=tokv)


_CACHE = {}


def kernel(**inputs):
    x = np.asarray(inputs["x"], np.float32)
    B = x.shape[0]
    wts = _prep_weights(inputs)
    if "nc" not in _CACHE:
        _CACHE["nc"] = build_program()
    nc = _CACHE["nc"]
    in_maps = []
    for b in range(B):
        m = {"x": np.ascontiguousarray(x[b])}
        m.update(wts)
        in_maps.append(m)
    res = run_bass_kernel_spmd(nc, in_maps, core_ids=list(range(B)))
    return np.stack([np.asarray(r["y"], np.float32) for r in res.results], axis=0)
```

```python
from contextlib import ExitStack
import numpy as np
import concourse.bass as bass
import concourse.mybir as mybir
from concourse.bass_utils import run_bass_kernel_spmd

F32 = mybir.dt.float32
BF16 = mybir.dt.bfloat16
AF = mybir.ActivationFunctionType
ALU = mybir.AluOpType

ENGS = ("pe", "act", "dve", "pool", "sp")
SAME_ENGINE_SYNC = True

D = 1024
SEQ = 4096
NT = 32
KC = 8
DEPTH = 2
DFF = 3072
NFC = 24
ALPHA = float((2 * DEPTH) ** 0.25)
EPS = 1e-5
ATT_SCALE = float(128 ** -0.5)
NEGBIG = -30000.0
C_CW, C_CB, C_BR, C_BI, C_LAM, C_FW, C_FB, NCH = 0, 32, 40, 48, 56, 64, 136, 160
CG_AX, CG_AG, CG_SU, CG_Q, CG_K, CG_V, CG_GA, CG_GS, CG_GM = range(9)


class Buf:
    __slots__ = ("w", "r", "name")

    def __init__(self, name=""):
        self.w = {}
        self.r = {}
        self.name = name


class Tile:
    def __init__(self, ap, name):
        self.ap = ap
        self.buf = Buf(name)
        self.name = name

    def __getitem__(self, k):
        return self.ap[k]


def _bufs(lst):
    out = []
    for x in lst:
        if x is None:
            continue
        out.append(x.buf if isinstance(x, Tile) else x)
    return out


class Sched:
    def __init__(self, nc, es):
        self.nc = nc
        self.es = es
        self.streams = {e: [] for e in ENGS}
        self.cnt = {}
        self.sem = {}
        for e in ("pe", "act", "dve", "pool"):
            self.sem[e] = es.enter_context(nc.semaphore("sem_" + e))
            self.cnt[e] = 0
        self.waited = {e: {} for e in ENGS}
        self.dma_pool = []
        self.dma_rr = 0
        self.dma_cnt = {}

    def new_dma_sem(self, name):
        s = self.es.enter_context(self.nc.semaphore(name))
        self.dma_cnt[id(s)] = [s, 0]
        return s

    def _pool_sem(self):
        if len(self.dma_pool) < 32:
            s = self.new_dma_sem("dq%d" % len(self.dma_pool))
            self.dma_pool.append(s)
            return s
        s = self.dma_pool[self.dma_rr % len(self.dma_pool)]
        self.dma_rr += 1
        return s

    def _deps(self, reads, writes):
        deps = {}

        def add(d):
            for k, v in d.items():
                if deps.get(k, (None, 0))[1] < v[1]:
                    deps[k] = v

        for b in reads:
            add(b.w)
        for b in writes:
            add(b.w)
            add(b.r)
        return deps

    def _emit_waits(self, eng, deps):
        own = self.sem.get(eng)
        for k, (s, v) in deps.items():
            if own is not None and s is own and (eng == "pe" or not SAME_ENGINE_SYNC):
                continue
            if self.waited[eng].get(k, 0) >= v:
                continue
            self.waited[eng][k] = v
            self.streams[eng].append(("wait", s, v))

    def _record(self, tok, reads, writes):
        k = id(tok[0])
        for b in reads:
            if b.r.get(k, (None, 0))[1] < tok[1]:
                b.r[k] = tok
        for b in writes:
            if b.w.get(k, (None, 0))[1] < tok[1]:
                b.w[k] = tok

    def op(self, eng, fn, reads=(), writes=(), signal=True):
        reads = _bufs(reads)
        writes = _bufs(writes)
        self._emit_waits(eng, self._deps(reads, writes))
        s = self.sem[eng]
        if signal:
            self.cnt[eng] += 1
            tok = (s, self.cnt[eng])
        else:
            tok = (s, self.cnt[eng] + 1)
        self.streams[eng].append(("op", fn, s if signal else None))
        self._record(tok, reads, writes)
        return tok

    def dma(self, q, out, in_, reads=(), writes=(), sem=None):
        reads = _bufs(reads)
        writes = _bufs(writes)
        if sem is None:
            if q == "pool":
                if not hasattr(self, "swq"):
                    self.swq = [self.new_dma_sem("swq%d" % i) for i in range(2)]
                    self.swq_rr = 0
                sem = self.swq[self.swq_rr % 2]
                self.swq_rr += 1
            else:
                sem = self._pool_sem()
        ent = self.dma_cnt[id(sem)]
        deps = self._deps(reads, writes)
        if ent[1] > 0:
            deps[id(sem)] = (sem, max(deps.get(id(sem), (None, 0))[1], ent[1]))
        self._emit_waits(q, deps)
        ent[1] += 16
        tok = (sem, ent[1])
        self.streams[q].append(("dma", out, in_, sem))
        self._record(tok, reads, writes)
        return tok

    def wait_all(self, eng, bufs):
        self._emit_waits(eng, self._deps(_bufs(bufs), []))

    def barrier(self):
        deps = {}
        for e in ("pe", "act", "dve", "pool"):
            if self.cnt[e] > 0:
                deps[id(self.sem[e])] = (self.sem[e], self.cnt[e])
        for k, (s, c) in self.dma_cnt.items():
            if c > 0:
                deps[k] = (s, c)
        for e in ENGS:
            self._emit_waits(e, deps)

    def replay(self):
        nc = self.nc
        streams = self.streams

        def run(e, stream):
            for it in stream:
                if it[0] == "wait":
                    e.wait_ge(it[1], it[2])
                elif it[0] == "op":
                    ins = it[1](e)
                    if it[2] is not None:
                        ins.then_inc(it[2], 1)
                else:
                    e.dma_start(out=it[1], in_=it[2]).then_inc(it[3], 16)

        with nc.Block() as block:
            @block.tensor
            def _(e):
                run(e, streams["pe"])

            @block.scalar
            def _(e):
                run(e, streams["act"])

            @block.vector
            def _(e):
                run(e, streams["dve"])

            @block.gpsimd
            def _(e):
                run(e, streams["pool"])

            @block.sync
            def _(e):
                run(e, streams["sp"])


class Arena:
    def __init__(self, ap, nwords):
        self.ap = ap
        self.n = nwords
        self.off = 0
        self.marks = []

    def alloc(self, name, shape, dtype):
        free = 1
        for s in shape[1:]:
            free *= s
        words = free if dtype == F32 else (free + 1) // 2
        assert self.off + words <= self.n, (name, self.off, words, self.n)
        v = self.ap[:, self.off:self.off + words]
        self.off += words
        if dtype != F32:
            v = v.bitcast(dtype)
            if (free % 2) == 1:
                v = v[:, 0:free]
        if len(shape) == 3:
            v = v.rearrange("p (a b) -> p a b", a=shape[1])
        elif len(shape) == 4:
            v = v.rearrange("p (a b c) -> p a b c", a=shape[1], b=shape[2])
        elif len(shape) == 5:
            v = v.rearrange("p (a b c d) -> p a b c d", a=shape[1], b=shape[2], c=shape[3])
        if shape[0] < 128:
            v = v[0:shape[0]]
        return Tile(v, name)

    def mark(self):
        return self.off

    def reset(self, m):
        self.off = m


def build_program(n_layers=DEPTH, dbg=False):
    nc = bass.Bass("TRN2", target_bir_lowering=False)

    def din(name, shape, dt=F32):
        return nc.dram_tensor(name, list(shape), dt, kind="ExternalInput").ap()

    def dscr(name, shape, dt):
        return nc.dram_tensor(name, list(shape), dt).ap()

    x_d = din("x", [SEQ, D])
    w_in_g = din("w_in_g", [DEPTH, 8, 128, 9 * 1024])
    w_sv = din("w_sv", [DEPTH, 128, 8 * 1024])
    w_outr = din("w_outr", [DEPTH, 128, 8 * 1024])
    w_upr = din("w_upr", [DEPTH, 128, NFC * 2048])
    w_dnr = din("w_dnr", [DEPTH, 128, NFC * 1024])
    rgw_d = din("rgw", [DEPTH, 128, 2048])
    wspT_d = din("wspT", [DEPTH, 128, 1024])
    chv_d = din("chv", [128, DEPTH * NCH])
    bsp_d = din("bsp", [DEPTH, 1024])
    tokv_d = din("tokv", [DEPTH, 6, 1024])
    y_d = nc.dram_tensor("y", [SEQ, D], F32, kind="ExternalOutput").ap()

    xres1 = dscr("xres1", [SEQ, D], F32)
    xres2 = dscr("xres2", [SEQ, D], F32)
    vln_d = dscr("vln_d", [SEQ, D], BF16)
    mrg_d = dscr("mrg_d", [128, 8, SEQ], BF16)
    xT_d = dscr("xT_d", [128, 8, SEQ], BF16)
    x1T_d = dscr("x1T_d", [128, 8, SEQ], BF16)
    wdn_bf = dscr("wdn_bf", [DEPTH, 128, NFC * 1024], BF16)
    dbg_out = {}
    if dbg:
        dbg_out["d_mrg"] = nc.dram_tensor("d_mrg", [128, 8, SEQ], BF16, kind="ExternalOutput").ap()
        dbg_out["d_x1"] = nc.dram_tensor("d_x1", [SEQ, D], F32, kind="ExternalOutput").ap()
        dbg_out["d_x1_0"] = nc.dram_tensor("d_x1_0", [SEQ, D], F32, kind="ExternalOutput").ap()
        dbg_out["d_mrg_0"] = nc.dram_tensor("d_mrg_0", [128, 8, SEQ], BF16, kind="ExternalOutput").ap()
        dbg_out["d_vln"] = nc.dram_tensor("d_vln", [SEQ, D], BF16, kind="ExternalOutput").ap()

    with ExitStack() as es:
        S = Sched(nc, es)
        NW = 53000
        arena_t = es.enter_context(nc.sbuf_tensor("arena", [128, NW], F32))
        AR = Arena(arena_t[:, :], NW)
        psum_t = [es.enter_context(nc.psum_tensor("ps%d" % i, [128, 1024], F32)) for i in range(4)]
        PSB = [Buf("bank%d" % i) for i in range(8)]

        def bank(i):
            return psum_t[i // 2][:, (i % 2) * 512:(i % 2) * 512 + 512]

        def pair(i):
            return psum_t[i][:, :]

        def MM(out, lhsT, rhs, start, stop, r, w, sig=False):
            S.op("pe", lambda e: e.matmul(out, lhsT=lhsT, rhs=rhs, start=start, stop=stop), r, w, signal=(stop or sig))

        def TR(out, in_, ident, r, w, signal=True):
            S.op("pe", lambda e: e.transpose(out=out, in_=in_, identity=ident), r, w, signal=signal)

        def ACT(out, in_, func, r, w, scale=1.0, bias=None, accum=None):
            def f(e):
                kw = {}
                if bias is not None:
                    kw["bias"] = bias
                if accum is not None:
                    kw["accum_out"] = accum
                return e.activation(out=out, in_=in_, func=func, scale=scale, **kw)
            S.op("act", f, r, w)

        def TT(eng, out, in0, in1, op, r, w):
            S.op(eng, lambda e: e.tensor_tensor(out=out, in0=in0, in1=in1, op=op), r, w)

        def TS(eng, out, in0, s1, s2, op0, op1, r, w):
            if s2 is None:
                S.op(eng, lambda e: e.tensor_scalar(out=out, in0=in0, scalar1=s1, scalar2=None, op0=op0), r, w)
            else:
                S.op(eng, lambda e: e.tensor_scalar(out=out, in0=in0, scalar1=s1, scalar2=s2, op0=op0, op1=op1), r, w)

        def STT(out, in0, scalar, in1, op0, op1, r, w):
            S.op("dve", lambda e: e.scalar_tensor_tensor(out=out, in0=in0, scalar=scalar, in1=in1, op0=op0, op1=op1), r, w)

        def CP(eng, out, in_, r, w):
            if eng == "act":
                S.op("act", lambda e: e.activation(out=out, in_=in_, func=AF.Copy), r, w)
            else:
                S.op(eng, lambda e: e.tensor_copy(out=out, in_=in_), r, w)

        def MEMSET(eng, ap, val, w):
            S.op(eng, lambda e: e.memset(ap, val), [], w)

        ones_f = AR.alloc("ones_f", [128, 128], F32)
        ident_f = AR.alloc("ident_f", [128, 128], F32)
        triu_f = AR.alloc("triu_f", [128, 128], F32)
        ident_b = AR.alloc("ident_b", [128, 128], BF16)
        triu_b = AR.alloc("triu_b", [128, 128], BF16)
        ones_b = AR.alloc("ones_b", [128, 128], BF16)
        onesrc = AR.alloc("onesrc", [128, 2048], BF16)
        sel = AR.alloc("sel", [128, 16, 128], BF16)
        cb = AR.alloc("cb", [128, 32, 16], F32)
        chv = AR.alloc("chv", [128, DEPTH * NCH], F32)
        der = AR.alloc("der", [128, DEPTH * 40], F32)
        cst = AR.alloc("cst", [128, 8], F32)
        PERS = [ones_f, ident_f, triu_f, ident_b, triu_b, ones_b, sel, cb, chv, der, cst]

        MEMSET("pool", ones_f[:, :], 1.0, [ones_f])
        S.op("pool", lambda e: e.affine_select(out=ident_f[:, :], in_=ones_f[:, :], pattern=[[1, 128]], compare_op=ALU.is_equal,
                                               fill=0.0, base=0, channel_multiplier=-1), [ones_f], [ident_f])
        S.op("pool", lambda e: e.affine_select(out=triu_f[:, :], in_=ones_f[:, :], pattern=[[1, 128]], compare_op=ALU.is_ge,
                                               fill=0.0, base=0, channel_multiplier=-1), [ones_f], [triu_f])
        CP("dve", ident_b[:, :], ident_f[:, :], [ident_f], [ident_b])
        CP("dve", triu_b[:, :], triu_f[:, :], [triu_f], [triu_b])
        CP("dve", ones_b[:, :], ones_f[:, :], [ones_f], [ones_b])
        MEMSET("pool", onesrc[:, :], 1.0, [onesrc])
        S.op("pool", lambda e: e.affine_select(out=sel[:, :, :], in_=onesrc[:, :].rearrange("p (a b) -> p a b", a=16),
                                               pattern=[[1, 16], [0, 128]], compare_op=ALU.is_equal, fill=0.0, base=0,
                                               channel_multiplier=-1), [onesrc], [sel])
        MEMSET("pool", cb[:, :, :], -1e30, [cb])
        for b in range(1, 16):
            MEMSET("pool", cb[:, 2 * b:2 * b + 2, 0:b], 0.0, [cb])
        MEMSET("pool", cst[:, 0:1], 1.0, [cst])
        MEMSET("pool", cst[:, 1:2], EPS, [cst])
        MEMSET("pool", cst[:, 2:3], -0.5, [cst])
        MEMSET("pool", cst[:, 3:4], 0.5, [cst])
        S.dma("sp", chv[:, :], chv_d[:, :], [], [chv])
        for l in range(n_layers):
            cv = l * NCH
            dv = l * 40
            TS("dve", der[:, dv:dv + 16], chv[:, cv + C_BR:cv + C_BR + 16], 0.5, None, ALU.mult, None, [chv], [der])
            ACT(der[:, dv + 32:dv + 40], chv[:, cv + C_LAM:cv + C_LAM + 8], AF.Exp, [chv], [der], scale=-1.0)
            ACT(der[:, dv + 32:dv + 40], der[:, dv + 32:dv + 40], AF.Ln, [der, cst], [der], bias=cst[:, 0:1])
            TS("dve", der[:, dv + 16:dv + 24], der[:, dv + 32:dv + 40], -4.0, None, ALU.mult, None, [der], [der])
            TS("dve", der[:, dv + 24:dv + 32], der[:, dv + 32:dv + 40], -8.0, None, ALU.mult, None, [der], [der])

        pers_mark = AR.mark()

        def DUMP(name, ap, reads):
            if not dbg:
                return
            o = nc.dram_tensor(name, list(ap.shape), ap.dtype, kind="ExternalOutput").ap()
            if len(ap.shape) == 3:
                for i in range(ap.shape[1]):
                    S.dma("sp", o[:, i, :], ap[:, i, :], reads, [Buf("dbg")])
            else:
                S.dma("sp", o[:, :], ap, reads, [Buf("dbg")])

        wdn_buf = [Buf("wdn%d" % l) for l in range(DEPTH)]
        for l in range(n_layers):
            for c in range(4):
                S.dma("pool", wdn_bf[l][:, c * 6144:(c + 1) * 6144].rearrange("p (a b) -> p a b", b=1024),
                      w_dnr[l][:, c * 6144:(c + 1) * 6144].rearrange("p (a b) -> p a b", b=1024), [], [wdn_buf[l]])

        vln_b = [Buf("vln%d" % t) for t in range(NT)]
        mrg_b = [Buf("mrg%d" % g) for g in range(8)]
        xres1_b = [Buf("xr1_%d" % t) for t in range(NT)]
        xres2_b = [Buf("xr2_%d" % t) for t in range(NT)]
        x1T_b = [Buf("x1T%d" % t) for t in range(8)]
        xTd_b = [Buf("xTd%d" % t) for t in range(8)]
        y_b = [Buf("y%d" % t) for t in range(NT)]

        def ln_a(mode, zt, stats, mv, rs):
            S.op("dve", lambda e: e.bn_stats(out=stats[:, 0, :], in_=zt[:, 0:512]), [zt], [stats])
            S.op("dve", lambda e: e.bn_stats(out=stats[:, 1, :], in_=zt[:, 512:1024]), [zt], [stats])
            S.op("dve", lambda e: e.bn_aggr(out=mv[:, :], in_=stats[:, :, :]), [stats], [mv])
            if mode == "pool":
                TS("dve", rs[:, 0:1], mv[:, 1:2], EPS, None, ALU.add, None, [mv], [rs])
                TT("pool", rs[:, 1:2], rs[:, 0:1], cst[:, 2:3], ALU.pow, [rs, cst], [rs])
            else:
                ACT(rs[:, 0:1], mv[:, 1:2], AF.Sqrt, [mv, cst], [rs], bias=cst[:, 1:2])
                S.op("dve", lambda e: e.reciprocal(out=rs[:, 1:2], in_=rs[:, 0:1]), [rs], [rs])

        def ln_b(zt, mv, rs, gbc, bbc, out_ap, w_out):
            STT(zt[:, :], zt[:, :], mv[:, 0:1], gbc[:, :], ALU.subtract, ALU.mult, [zt, mv, gbc], [zt])
            STT(out_ap, zt[:, :], rs[:, 1:2], bbc[:, :], ALU.mult, ALU.add, [zt, rs, bbc], list(w_out))

        for l in range(n_layers):
            cv = l * NCH
            dv = l * 40
            TS("dve", der[:, dv:dv + 16], chv[:, cv + C_BR:cv + C_BR + 16], 0.5, None, ALU.mult, None, [chv], [der])
            ACT(der[:, dv + 32:dv + 40], chv[:, cv + C_LAM:cv + C_LAM + 8], AF.Exp, [chv], [der], scale=-1.0)
            ACT(der[:, dv + 32:dv + 40], der[:, dv + 32:dv + 40], AF.Ln, [der, cst], [der], bias=cst[:, 0:1])
            TS("dve", der[:, dv + 16:dv + 24], der[:, dv + 32:dv + 40], -4.0, None, ALU.mult, None, [der], [der])
            TS("dve", der[:, dv + 24:dv + 32], der[:, dv + 32:dv + 40], -8.0, None, ALU.mult, None, [der], [der])

        pers_mark = AR.mark()

        def DUMP(name, ap, reads):
            if not dbg:
                return
            o = nc.dram_tensor(name, list(ap.shape), ap.dtype, kind="ExternalOutput").ap()
            if len(ap.shape) == 3:
                for i in range(ap.shape[1]):
                    S.dma("sp", o[:, i, :], ap[:, i, :], reads, [Buf("dbg")])
            else:
                S.dma("sp", o[:, :], ap, reads, [Buf("dbg")])

        wdn_buf = [Buf("wdn%d" % l) for l in range(DEPTH)]
        for l in range(n_layers):
            for c in range(4):
                S.dma("pool", wdn_bf[l][:, c * 6144:(c + 1) * 6144].rearrange("p (a b) -> p a b", b=1024),
                      w_dnr[l][:, c * 6144:(c + 1) * 6144].rearrange("p (a b) -> p a b", b=1024), [], [wdn_buf[l]])

        vln_b = [Buf("vln%d" % t) for t in range(NT)]
        mrg_b = [Buf("mrg%d" % g) for g in range(8)]
        xres1_b = [Buf("xr1_%d" % t) for t in range(NT)]
        xres2_b = [Buf("xr2_%d" % t) for t in range(NT)]
        x1T_b = [Buf("x1T%d" % t) for t in range(8)]
        xTd_b = [Buf("xTd%d" % t) for t in range(8)]
        y_b = [Buf("y%d" % t) for t in range(NT)]

        def ln_rows(mode, zt, stats, mv, rs, gbc, bbc, out_ap, r_extra, w_out, tmp2=None):
            S.op("dve", lambda e: e.bn_stats(out=stats[:, 0, :], in_=zt[:, 0:512]), [zt], [stats])
            S.op("dve", lambda e: e.bn_stats(out=stats[:, 1, :], in_=zt[:, 512:1024]), [zt], [stats])
            S.op("dve", lambda e: e.bn_aggr(out=mv[:, :], in_=stats[:, :, :]), [stats], [mv])
            if mode == "pool":
                TS("dve", rs[:, 0:1], mv[:, 1:2], EPS, None, ALU.add, None, [mv], [rs])
                TT("pool", rs[:, 1:2], rs[:, 0:1], cst[:, 2:3], ALU.pow, [rs, cst], [rs])
            else:
                ACT(rs[:, 0:1], mv[:, 1:2], AF.Sqrt, [mv, cst], [rs], bias=cst[:, 1:2])
                S.op("dve", lambda e: e.reciprocal(out=rs[:, 1:2], in_=rs[:, 0:1]), [rs], [rs])
            STT(zt[:, :], zt[:, :], mv[:, 0:1], gbc[:, :], ALU.subtract, ALU.mult, [zt, mv, gbc], [zt])
            STT(out_ap, zt[:, :], rs[:, 1:2], bbc[:, :], ALU.mult, ALU.add, [zt, rs, bbc] + list(r_extra), list(w_out))

        for l in range(n_layers):
            cv = l * NCH
            dv = l * 40
            last = (l == n_layers - 1)
            xin_d = x_d if l == 0 else xres2
            xin_b = [None] * NT if l == 0 else xres2_b

            S.barrier()
            AR.reset(pers_mark)
            xT = AR.alloc("xT", [128, 8, SEQ], BF16)
            xT_b = [Buf("xT%d" % t) for t in range(NT)]
            macc = AR.alloc("macc", [128, SEQ], F32)
            wg = [AR.alloc("wg%d" % i, [128, 9, 8, 128], BF16) for i in range(2)]
            rgw = AR.alloc("rgw", [128, 8, 2, 128], BF16)
            wspb = AR.alloc("wspb", [128, 8, 128], BF16)
            bspbc = AR.alloc("bspbc", [128, 8, 128], F32)
            mix_mark = AR.mark()

            S.dma("pool", rgw[:, :, :, :].rearrange("p a b c -> p (a b c)"), rgw_d[l][:, :], [], [rgw])
            S.dma("sp", bspbc[:, :, :].rearrange("p a b -> p (a b)"), bsp_d[l].partition_broadcast(128), [], [bspbc])

            if l == 0:
                xin = [AR.alloc("xin%d" % i, [128, 1024], F32) for i in range(2)]
                for tt in range(NT):
                    xi = xin[tt % 2]
                    S.dma("sp", xi[:, :], x_d[tt * 128:(tt + 1) * 128, :], [], [xi])
                    for h in range(2):
                        pp = (tt % 2) * 2 + h
                        for j in range(4):
                            TR(pair(pp)[:, j * 128:(j + 1) * 128], xi[:, (h * 4 + j) * 128:(h * 4 + j + 1) * 128], ident_f[:, :],
                               [xi, ident_f], [PSB[2 * pp], PSB[2 * pp + 1]], signal=(j == 3))
                        CP("act" if h == 0 else "dve", xT[:, h * 4:(h + 1) * 4, tt * 128:(tt + 1) * 128],
                           pair(pp)[:, 0:512].rearrange("p (a b) -> p a b", a=4), [PSB[2 * pp], PSB[2 * pp + 1]], [xT_b[tt]])
            else:
                for c in range(8):
                    S.dma("sp", xT[:, :, c * 512:(c + 1) * 512], xT_d[:, :, c * 512:(c + 1) * 512], [xTd_b[c]],
                          [xT_b[4 * c + i] for i in range(4)])
            if l == 0 and False:
                DUMP("d_xT", xT[:, :, :], xT_b)
            S.barrier()
            AR.reset(mix_mark)

            wsv = AR.alloc("wsv", [128, 8, 1024], BF16)
            wspf = AR.alloc("wspf", [128, 8, 128], F32)
            gbc = AR.alloc("gbc", [128, 1024], F32)
            bbc = AR.alloc("bbc", [128, 1024], F32)
            v32 = [AR.alloc("v32_%d" % i, [128, 1024], F32) for i in range(2)]
            vlnb = [AR.alloc("vlnb%d" % i, [128, 1024], BF16) for i in range(2)]
            stats = [AR.alloc("stats%d" % i, [128, 2, 6], F32) for i in range(2)]
            mv = [AR.alloc("mv%d" % i, [128, 2], F32) for i in range(2)]
            rs = [AR.alloc("rs%d" % i, [128, 2], F32) for i in range(2)]
            for kc in range(8):
                S.dma("pool", wsv[:, kc, :], w_sv[l][:, kc * 1024:(kc + 1) * 1024], [], [wsv])
            S.dma("sp", wspf[:, :, :].rearrange("p a b -> p (a b)"), wspT_d[l][:, :], [], [wspf])
            S.dma("sp", gbc[:, :], tokv_d[l][0].partition_broadcast(128), [], [gbc])
            S.dma("sp", bbc[:, :], tokv_d[l][1].partition_broadcast(128), [], [bbc])
            TT("dve", wspb[:, :, :], wspf[:, :, :], triu_f[:, :].unsqueeze(1).broadcast_to([128, 8, 128]), ALU.mult,
               [wspf, triu_f], [wspb])
            def load_wg(g):
                t = wg[g % 2]
                for cg in range(9):
                    S.dma("pool", t[:, cg, :, :].rearrange("p a b -> p (a b)"), w_in_g[l][g][:, cg * 1024:(cg + 1) * 1024], [], [t])
            load_wg(0)
            def bpre_a(tt):
                sl = tt % 2
                pp = sl
                for h in range(2):
                    for kc in range(8):
                        MM(pair(pp)[:, h * 512:(h + 1) * 512], xT[:, kc, tt * 128:(tt + 1) * 128], wsv[:, kc, h * 512:(h + 1) * 512],
                           kc == 0, kc == 7, [xT_b[tt], wsv], [PSB[2 * pp + h]])
                ACT(v32[sl][:, :], pair(pp), AF.Gelu_apprx_tanh, [PSB[2 * pp], PSB[2 * pp + 1]], [v32[sl]])
                ln_a("pool", v32[sl], stats[sl], mv[sl], rs[sl])

            def bpre_b(tt):
                sl = tt % 2
                ln_b(v32[sl], mv[sl], rs[sl], gbc, bbc, vlnb[sl][:, :], [vlnb[sl]])
                S.dma("sp", vln_d[tt * 128:(tt + 1) * 128, :], vlnb[sl][:, :], [vlnb[sl]], [vln_b[tt]])

            for tt in range(NT + 1):
                if tt < NT:
                    bpre_a(tt)
                if tt >= 1:
                    bpre_b(tt - 1)
            S.barrier()
            AR.reset(mix_mark)
            g_mark = AR.mark()

            for g in range(8):
                wt = wg[g % 2]
                if g + 1 < 8:
                    load_wg(g + 1)

                def proj(cg, bk, t0, n):
                    xb = [xT_b[t] for t in range(t0 // 128, (t0 + n + 127) // 128)]
                    for kc in range(8):
                        MM(bank(bk)[:, 0:n], wt[:, cg, kc, :], xT[:, kc, t0:t0 + n], kc == 0, kc == 7, [wt] + xb, [PSB[bk]])

                AR.reset(g_mark)
                HT = 2048
                axp = AR.alloc("axp", [128, HT + 4], F32)
                cc = AR.alloc("cc", [128, HT], F32)
                ccb = AR.alloc("ccb", [128, HT], BF16)
                tr_ = AR.alloc("tr", [128, HT], F32)
                ti_ = AR.alloc("ti", [128, HT], F32)
                aa = AR.alloc("aa", [128, HT], F32)
                a2 = AR.alloc("a2", [128, HT], F32)
                halo = AR.alloc("halo", [128, 4], F32)
                gg = [AR.alloc("gg%d" % i, [128, 512], F32) for i in range(2)]
                tg = [AR.alloc("tg%d" % i, [128, 512], F32) for i in range(2)]
                macc_b = [Buf("macc%d" % c) for c in range(8)]

                def cw(k):
                    return chv[:, cv + C_CW + k * 8 + g:cv + C_CW + k * 8 + g + 1]

                for hh in range(2):
                    T0 = hh * HT
                    if hh == 0:
                        MEMSET("pool", axp[:, 0:3], 0.0, [axp])
                    else:
                        CP("pool", axp[:, 0:3], halo[:, 0:3], [halo], [axp])
                    for c4 in range(4):
                        proj(CG_AX, c4, T0 + c4 * 512, 512)
                        CP("act", axp[:, 3 + c4 * 512:3 + (c4 + 1) * 512], bank(c4), [PSB[c4]], [axp])
                    TS("dve", cc[:, :], axp[:, 3:3 + HT], cw(3), chv[:, cv + C_CB + g:cv + C_CB + g + 1], ALU.mult, ALU.add, [axp, chv], [cc])
                    for k in (2, 1, 0):
                        STT(cc[:, :], axp[:, k:k + HT], cw(k), cc[:, :], ALU.mult, ALU.add, [axp, chv, cc], [cc])
                    CP("pool", halo[:, 0:3], axp[:, HT:HT + 3], [axp], [halo])
                    CP("act", ccb[:, :], cc[:, :], [cc], [ccb])
                    for c4 in range(4):
                        MM(bank(c4), rgw[:, g, 0, :], ccb[:, c4 * 512:(c4 + 1) * 512], True, True, [rgw, ccb], [PSB[c4]])
                    for c4 in range(4):
                        MM(bank(4 + c4), rgw[:, g, 1, :], ccb[:, c4 * 512:(c4 + 1) * 512], True, True, [rgw, ccb], [PSB[4 + c4]])
                    for p2 in range(2):
                        ACT(tr_[:, p2 * 1024:(p2 + 1) * 1024], pair(p2), AF.Tanh, [PSB[2 * p2], PSB[2 * p2 + 1], der], [tr_], scale=0.5,
                            bias=der[:, dv + g:dv + g + 1])
                    for p2 in range(2):
                        ACT(ti_[:, p2 * 1024:(p2 + 1) * 1024], pair(2 + p2), AF.Tanh, [PSB[4 + 2 * p2], PSB[5 + 2 * p2], der], [ti_], scale=0.5,
                            bias=der[:, dv + 8 + g:dv + 8 + g + 1])
                    ACT(aa[:, :], tr_[:, :], AF.Exp, [tr_, der], [aa], scale=der[:, dv + 16 + g:dv + 16 + g + 1],
                        bias=der[:, dv + 16 + g:dv + 16 + g + 1])
                    ACT(a2[:, :], tr_[:, :], AF.Exp, [tr_, der], [a2], scale=der[:, dv + 24 + g:dv + 24 + g + 1],
                        bias=der[:, dv + 24 + g:dv + 24 + g + 1])
                    TS("dve", a2[:, :], a2[:, :], 1.0, None, ALU.min, None, [a2], [a2])
                    ACT(a2[:, :], a2[:, :], AF.Sqrt, [a2, cst], [a2], scale=-1.0, bias=cst[:, 0:1])
                    STT(ti_[:, :], ti_[:, :], 1.0, cc[:, :], ALU.add, ALU.mult, [ti_, cc], [ti_])
                    STT(ti_[:, :], a2[:, :], 0.5, ti_[:, :], ALU.mult, ALU.mult, [a2, ti_], [ti_])
                    init = 0.0 if hh == 0 else macc[:, T0 - 1:T0]
                    mbs = [macc_b[4 * hh + i] for i in range(4)]
                    rb = [aa, ti_] + ([macc_b[4 * hh - 1]] if hh > 0 else [])
                    S.op("dve", (lambda T0, init: lambda e: e.tensor_tensor_scan(out=macc[:, T0:T0 + HT], data0=aa[:, :], data1=ti_[:, :],
                                                                                  initial=init, op0=ALU.mult, op1=ALU.add))(T0, init), rb, mbs)
                for c in range(8):
                    sl = c % 2
                    t0 = c * 512
                    b1 = [0, 2, 4, 6][c % 4]
                    b2 = b1 + 1
                    proj(CG_AG, b1, t0, 512)
                    ACT(gg[sl][:, :], bank(b1), AF.Gelu_apprx_tanh, [PSB[b1]], [gg[sl]])
                    proj(CG_GA, b2, t0, 512)
                    ACT(tg[sl][:, :], bank(b2), AF.Tanh, [PSB[b2]], [tg[sl]], scale=0.5)
                    TT("dve", gg[sl][:, :], gg[sl][:, :], macc[:, t0:t0 + 512], ALU.mult, [gg[sl], macc_b[c]], [gg[sl]])
                    STT(macc[:, t0:t0 + 512], tg[sl][:, :], 1.0, gg[sl][:, :], ALU.add, ALU.mult, [tg[sl], gg[sl]], [macc_b[c]])
                S.barrier()

                AR.reset(g_mark)
                vlng = AR.alloc("vlng", [128, 32, 128], BF16)
                gu = [AR.alloc("gu%d" % i, [128, 512], F32) for i in range(2)]
                tgs = [AR.alloc("tgs%d" % i, [128, 512], F32) for i in range(2)]
                m1 = [AR.alloc("m1_%d" % i, [128, 512], F32) for i in range(2)]
                for q4 in range(4):
                    S.dma("sp", vlng[:, q4 * 8:(q4 + 1) * 8, :],
                          vln_d[q4 * 1024:(q4 + 1) * 1024, g * 128:(g + 1) * 128].rearrange("(n p) c -> p n c", p=128),
                          [vln_b[t] for t in range(q4 * 8, q4 * 8 + 8)], [vlng])
                for c in range(8):
                    sl = c % 2
                    t0 = c * 512
                    bu, bg, bm = 0 + sl, 2 + sl, 4 + sl
                    proj(CG_SU, bu, t0, 512)
                    ACT(gu[sl][:, :], bank(bu), AF.Gelu_apprx_tanh, [PSB[bu]], [gu[sl]])
                    proj(CG_GS, bg, t0, 512)
                    ACT(tgs[sl][:, :], bank(bg), AF.Tanh, [PSB[bg]], [tgs[sl]], scale=0.5)
                    for n in range(4):
                        MM(bank(bm)[:, n * 128:(n + 1) * 128], vlng[:, 4 * c + n, :], wspb[:, g, :], True, True, [vlng, wspb], [PSB[bm]])
                    TT("dve", m1[sl][:, :].rearrange("p (a b) -> p a b", a=4), bank(bm).rearrange("p (a b) -> p a b", a=4),
                       bspbc[:, g, :].unsqueeze(1).broadcast_to([128, 4, 128]), ALU.add, [PSB[bm], bspbc], [m1[sl]])
                    TT("dve", m1[sl][:, :], m1[sl][:, :], gu[sl][:, :], ALU.mult, [m1[sl], gu[sl]], [m1[sl]])
                    STT(m1[sl][:, :], tgs[sl][:, :], 1.0, m1[sl][:, :], ALU.add, ALU.mult, [tgs[sl], m1[sl]], [m1[sl]])
                    TT("dve", macc[:, t0:t0 + 512], macc[:, t0:t0 + 512], m1[sl][:, :], ALU.add, [m1[sl], macc_b[c]], [macc_b[c]])
                S.barrier()

                AR.reset(g_mark)
                qT = AR.alloc("qT", [128, SEQ], BF16)
                kT = AR.alloc("kT", [128, SEQ], BF16)
                Vt = AR.alloc("Vt", [128, 32, 128], BF16)
                negmT = AR.alloc("negmT", [128, SEQ], BF16)
                ksum = AR.alloc("ksum", [128, 16], F32)
                kmT = AR.alloc("kmT", [128, 16], BF16)
                gsb = AR.alloc("gsb", [128, 32, 16], F32)
                mx8 = AR.alloc("mx8", [128, 32, 8], F32)
                thr = AR.alloc("thr", [128, 32], F32)
                negm = AR.alloc("negm", [128, 32, 16], BF16)
                NPT = 8
                PT = [AR.alloc("PT%d" % i, [128, 256], BF16) for i in range(NPT)]
                tgm = [AR.alloc("tgm%d" % i, [128, 256], F32) for i in range(2)]
                rec = [AR.alloc("rec%d" % i, [128, 256], F32) for i in range(2)]
                ot = [AR.alloc("ot%d" % i, [128, 256], F32) for i in range(2)]
                mrgb = [AR.alloc("mrgb%d" % i, [128, 1024], BF16) for i in range(2)]
                MEMSET("dve", negmT[:, :], 0.0, [negmT])
                for c in range(8):
                    t0 = c * 512
                    bq, bk_ = 0 + (c % 2), 2 + (c % 2)
                    proj(CG_K, bk_, t0, 512)
                    for h in range(2):
                        ACT(kT[:, t0 + h * 256:t0 + (h + 1) * 256], bank(bk_)[:, h * 256:(h + 1) * 256], AF.Copy, [PSB[bk_]], [kT, ksum],
                            accum=ksum[:, 2 * c + h:2 * c + h + 1])
                    proj(CG_Q, bq, t0, 512)
                    CP("dve", qT[:, t0:t0 + 512], bank(bq), [PSB[bq]], [qT])
                TS("dve", kmT[:, :], ksum[:, :], 1.0 / 256.0, None, ALU.mult, None, [ksum], [kmT])
                bgt = 6
                for qt in range(32):
                    MM(bank(bgt)[:, qt * 16:(qt + 1) * 16], qT[:, qt * 128:(qt + 1) * 128], kmT[:, :], True, True, [qT, kmT], [PSB[bgt]])
                TT("dve", gsb[:, :, :].rearrange("p a b -> p (a b)"), bank(bgt), cb[:, :, :].rearrange("p a b -> p (a b)"), ALU.add,
                   [PSB[bgt], cb], [gsb])
                for qt in range(32):
                    S.op("dve", (lambda qt: lambda e: e.max(out=mx8[:, qt, :], in_=gsb[:, qt, :]))(qt), [gsb], [mx8])
                TS("dve", thr[:, :], mx8[:, :, 2], -1e29, None, ALU.max, None, [mx8], [thr])
                TT("dve", negm[:, :, :], gsb[:, :, :], thr[:, :].unsqueeze(2).broadcast_to([128, 32, 16]), ALU.is_lt, [gsb, thr], [negm])
                TS("dve", negm[:, :, :], negm[:, :, :], NEGBIG, None, ALU.mult, None, [negm], [negm])
                for t4 in range(8):
                    bv = 4 + (t4 % 2)
                    for j in range(4):
                        tt = t4 * 4 + j
                        for kc in range(8):
                            MM(bank(bv)[:, j * 128:(j + 1) * 128], xT[:, kc, tt * 128:(tt + 1) * 128], wt[:, CG_V, kc, :], kc == 0, kc == 7,
                               [xT_b[tt], wt], [PSB[bv]])
                    CP("act", Vt[:, t4 * 4:(t4 + 1) * 4, :], bank(bv).rearrange("p (a b) -> p a b", a=4),
                       [PSB[bv]], [Vt])
                for q8 in range(4):
                    bt = 6 + ((q8 + 1) % 2)
                    tb = bank(bt).bitcast(BF16)
                    for j in range(8):
                        qt = q8 * 8 + j
                        TR(tb[0:16, j * 128:(j + 1) * 128], negm[:, qt, :], ident_b[:, :], [negm, ident_b], [PSB[bt]], signal=(j == 7))
                    CP("act", negmT[0:16, q8 * 1024:(q8 + 1) * 1024], tb[0:16, :], [PSB[bt]], [negmT])
                flat = []
                for b in range(16):
                    tl_ = [("past", kt) for kt in range(2 * b)] + [("own0", 2 * b), ("own1", 2 * b + 1)]
                    for i, (kind, kt) in enumerate(tl_):
                        flat.append((b, kind, kt, i == 0, i == len(tl_) - 1))
                nfl = len(flat)

                RING = [0, 1, 2, 5]
                ring_ctr = [0]

                def emit_score(i):
                    b, kind, kt, first, lastt = flat[i]
                    q0 = b * 256
                    rb = RING[ring_ctr[0] % 4]
                    ring_ctr[0] += 1
                    stv = bank(rb)[:, 0:256]
                    sb = PSB[rb]
                    P = PT[i % NPT]
                    if kind == "past":
                        j = kt // 2
                        MM(stv, kT[:, kt * 128:(kt + 1) * 128], qT[:, q0:q0 + 256], True, False, [kT, qT], [sb])
                        MM(stv, sel[:, j, :], negmT[:, q0:q0 + 256], False, True, [sel, negmT], [sb])
                        ACT(P[:, 0:256], stv, AF.Exp, [sb], [P], scale=ATT_SCALE)
                    elif kind == "own0":
                        MM(stv, kT[:, kt * 128:(kt + 1) * 128], qT[:, q0:q0 + 256], True, True, [kT, qT], [sb])
                        ACT(P[:, 0:256], stv, AF.Exp, [sb], [P], scale=ATT_SCALE)
                        TT("pool", P[:, 0:128], P[:, 0:128], triu_b[:, :], ALU.mult, [P, triu_b], [P])
                    else:
                        MM(stv[:, 0:128], kT[:, kt * 128:(kt + 1) * 128], qT[:, q0 + 128:q0 + 256], True, True, [kT, qT], [sb])
                        ACT(P[:, 0:128], stv[:, 0:128], AF.Exp, [sb], [P], scale=ATT_SCALE)
                        TT("pool", P[:, 0:128], P[:, 0:128], triu_b[:, :], ALU.mult, [P, triu_b], [P])

                def emit_pv(i):
                    b, kind, kt, first, lastt = flat[i]
                    q0 = b * 256
                    P = PT[i % NPT]
                    bo = 3 if b % 2 == 0 else 6
                    bl = 4 if b % 2 == 0 else 7
                    if kind == "own1":
                        MM(bank(bo)[:, 128:256], Vt[:, kt, :], P[:, 0:128], False, True, [Vt, P], [PSB[bo]])
                        MM(bank(bl)[:, 128:256], ones_b[:, :], P[:, 0:128], False, True, [ones_b, P], [PSB[bl]])
                    else:
                        MM(bank(bo)[:, 0:256], Vt[:, kt, :], P[:, 0:256], first, False, [Vt, P], [PSB[bo]])
                        MM(bank(bl)[:, 0:256], ones_b[:, :], P[:, 0:256], first, False, [ones_b, P], [PSB[bl]])
                    if lastt:
                        sl = b % 2
                        bgm = RING[ring_ctr[0] % 4]
                        ring_ctr[0] += 1
                        proj(CG_GM, bgm, q0, 256)
                        ACT(tgm[sl][:, :], bank(bgm)[:, 0:256], AF.Tanh, [PSB[bgm]], [tgm[sl]], scale=0.5)
                        S.op("dve", lambda e: e.reciprocal(out=rec[sl][:, :], in_=bank(bl)[:, 0:256]), [PSB[bl]], [rec[sl]])
                        TT("dve", ot[sl][:, :], bank(bo)[:, 0:256], rec[sl][:, :], ALU.mult, [PSB[bo], rec[sl]], [ot[sl]])
                        STT(ot[sl][:, :], tgm[sl][:, :], 1.0, ot[sl][:, :], ALU.add, ALU.mult, [tgm[sl], ot[sl]], [ot[sl]])
                        mb = macc_b[b // 2]
                        TT("dve", macc[:, q0:q0 + 256], macc[:, q0:q0 + 256], ot[sl][:, :], ALU.add, [ot[sl], mb], [mb])

                LOOK = 3
                for i in range(min(LOOK, nfl)):
                    emit_score(i)
                for i in range(nfl):
                    if i + LOOK < nfl:
                        emit_score(i + LOOK)
                    emit_pv(i)
                for c4 in range(4):
                    mt = mrgb[c4 % 2]
                    ACT(mt[:, :], macc[:, c4 * 1024:(c4 + 1) * 1024], AF.Copy, [macc_b[2 * c4], macc_b[2 * c4 + 1]], [mt], scale=0.5)
                    S.dma("sp", mrg_d[:, g, c4 * 1024:(c4 + 1) * 1024], mt[:, :], [mt], [mrg_b[g]])
                S.barrier()

            if dbg and l == 0:
                S.dma("sp", dbg_out["d_mrg_0"][:, :, :], mrg_d[:, :, :], mrg_b, [Buf("dbg")])
            if dbg and l == n_layers - 1:
                S.dma("sp", dbg_out["d_mrg"][:, :, :], mrg_d[:, :, :], mrg_b, [Buf("dbg")])
                S.dma("sp", dbg_out["d_vln"][:, :], vln_d[:, :], vln_b, [Buf("dbg")])

            S.barrier()
            AR.reset(pers_mark)
            wup = AR.alloc("wup", [128, NFC, 2, 8, 128], BF16)
            f_mark = AR.mark()
            woutb = AR.alloc("woutb", [128, 8, 1024], BF16)
            g1 = AR.alloc("g1", [128, 1024], F32)
            b1 = AR.alloc("b1", [128, 1024], F32)
            mt_ = [AR.alloc("mt%d" % i, [128, 8, 512], BF16) for i in range(2)]
            x1Tg = [AR.alloc("x1Tg%d" % i, [128, 8, 512], BF16) for i in range(2)]
            xr = [AR.alloc("xr%d" % i, [128, 1024], F32) for i in range(4)]
            zt = [AR.alloc("zt%d" % i, [128, 1024], F32) for i in range(2)]
            x1t = [AR.alloc("x1t%d" % i, [128, 1024], F32) for i in range(2)]
            stats = [AR.alloc("stats%d" % i, [128, 2, 6], F32) for i in range(2)]
            mv = [AR.alloc("mv%d" % i, [128, 2], F32) for i in range(2)]
            rs = [AR.alloc("rs%d" % i, [128, 2], F32) for i in range(2)]
            for kc in range(8):
                S.dma("pool", woutb[:, kc, :], w_outr[l][:, kc * 1024:(kc + 1) * 1024], [], [woutb])
            wup_next = [0]

            def load_wup_one():
                fc = wup_next[0]
                if fc >= NFC:
                    return
                wup_next[0] += 1
                S.dma("pool", wup[:, fc, :, :, :].rearrange("p a b c -> p a (b c)"),
                      w_upr[l][:, fc * 2048:(fc + 1) * 2048].rearrange("p (a c) -> p a c", c=1024), [], [wup])
            S.dma("sp", g1[:, :], tokv_d[l][2].partition_broadcast(128), [], [g1])
            S.dma("sp", b1[:, :], tokv_d[l][3].partition_broadcast(128), [], [b1])
            def out_a_pe(tt):
                c, j = tt // 4, tt % 4
                m = mt_[c % 2]
                sl = tt % 2
                pp = sl
                if j == 0:
                    S.dma("sp", m[:, :, :], mrg_d[:, :, c * 512:(c + 1) * 512], mrg_b, [m])
                xs_ = xr[tt % 4]
                S.dma("sp", xs_[:, :], xin_d[tt * 128:(tt + 1) * 128, :], [xin_b[tt]], [xs_])
                for h in range(2):
                    for kc in range(8):
                        MM(pair(pp)[:, h * 512:(h + 1) * 512], m[:, kc, j * 128:(j + 1) * 128], woutb[:, kc, h * 512:(h + 1) * 512],
                           kc == 0, kc == 7, [m, woutb], [PSB[2 * pp + h]])

            def out_a_dve(tt):
                sl = tt % 2
                pp = sl
                xs_ = xr[tt % 4]
                STT(zt[sl][:, :], xs_[:, :], ALPHA, pair(pp), ALU.mult, ALU.add, [xs_, PSB[2 * pp], PSB[2 * pp + 1]], [zt[sl]])
                ln_a("act", zt[sl], stats[sl], mv[sl], rs[sl])

            def out_b(tt):
                c, j = tt // 4, tt % 4
                xg = x1Tg[c % 2]
                sl = tt % 2
                ln_b(zt[sl], mv[sl], rs[sl], g1, b1, x1t[sl][:, :], [x1t[sl]])
                S.dma("sp", xres1[tt * 128:(tt + 1) * 128, :], x1t[sl][:, :], [x1t[sl]], [xres1_b[tt]])
                if dbg and l == 0:
                    S.dma("sp", dbg_out["d_x1_0"][tt * 128:(tt + 1) * 128, :], x1t[sl][:, :], [x1t[sl]], [Buf("dbg")])
                if dbg and l == n_layers - 1:
                    S.dma("sp", dbg_out["d_x1"][tt * 128:(tt + 1) * 128, :], x1t[sl][:, :], [x1t[sl]], [Buf("dbg")])
                pt = 2 + sl
                for kc in range(8):
                    TR(pair(pt)[:, kc * 128:(kc + 1) * 128], x1t[sl][:, kc * 128:(kc + 1) * 128], ident_f[:, :], [x1t[sl], ident_f],
                       [PSB[2 * pt], PSB[2 * pt + 1]], signal=(kc == 7))
                CP("act", xg[:, :, j * 128:(j + 1) * 128], pair(pt).rearrange("p (a b) -> p a b", a=8), [PSB[2 * pt], PSB[2 * pt + 1]], [xg])
                if j == 3:
                    S.dma("sp", x1T_d[:, :, c * 512:(c + 1) * 512], xg[:, :, :], [xg], [x1T_b[c]])

            out_a_pe(0)
            for tt in range(NT + 1):
                if tt + 1 < NT:
                    out_a_pe(tt + 1)
                if tt < NT:
                    out_a_dve(tt)
                if tt >= 1:
                    out_b(tt - 1)
                load_wup_one()
            while wup_next[0] < NFC:
                load_wup_one()
            S.barrier()

            AR.reset(f_mark)
            g2 = AR.alloc("g2", [128, 1024], F32)
            b2 = AR.alloc("b2", [128, 1024], F32)
            wdn = [AR.alloc("wdn%d" % i, [128, 4, 1024], BF16) for i in range(2)]
            xg_ = [AR.alloc("xg%d" % i, [128, 8, 256], BF16) for i in range(2)]
            xr1 = [AR.alloc("xr1_%d" % i, [128, 2, 1024], F32) for i in range(2)]
            actT = AR.alloc("actT", [128, NFC, 256], BF16)
            actT_b = [Buf("actT%d" % i) for i in range(NFC)]
            hgp = [AR.alloc("hgp%d" % i, [128, 260], F32) for i in range(2)]
            cf = [AR.alloc("cf%d" % i, [128, 256], F32) for i in range(2)]
            gl = [AR.alloc("gl%d" % i, [128, 256], F32) for i in range(2)]
            carry = AR.alloc("carry", [128, NFC, 2], F32)
            zt = [AR.alloc("zt%d" % i, [128, 1024], F32) for i in range(2)]
            x2t = [AR.alloc("x2t%d" % i, [128, 1024], F32) for i in range(2)]
            x2Tg = [AR.alloc("x2Tg%d" % i, [128, 8, 512], BF16) for i in range(1)]
            stats = [AR.alloc("stats%d" % i, [128, 2, 6], F32) for i in range(2)]
            mv = [AR.alloc("mv%d" % i, [128, 2], F32) for i in range(2)]
            rs = [AR.alloc("rs%d" % i, [128, 2], F32) for i in range(2)]
            S.dma("sp", g2[:, :], tokv_d[l][4].partition_broadcast(128), [], [g2])
            S.dma("sp", b2[:, :], tokv_d[l][5].partition_broadcast(128), [], [b2])
            MEMSET("pool", carry[:, :, :], 0.0, [carry])
            wd_rr = 0
            pending_tr = []
            for gi in range(16):
                t0 = gi * 256
                xg = xg_[gi % 2]
                x1r = xr1[gi % 2]
                S.dma("sp", xg[:, :, :], x1T_d[:, :, t0:t0 + 256], [x1T_b[gi // 2]], [xg])
                S.dma("sp", x1r[:, :, :], xres1[t0:t0 + 256, :].rearrange("(a p) d -> p a d", p=128),
                      [xres1_b[2 * gi], xres1_b[2 * gi + 1]], [x1r])
                def up(fc):
                    sl = fc % 2
                    bg_, bu_ = 4 + sl, 6 + sl
                    if fc % 4 == 0:
                        ch = fc // 4
                        wd = wdn[ch % 2]
                        S.dma("sp", wd[:, :, :].rearrange("p a b -> p (a b)"), wdn_bf[l][:, ch * 4096:(ch + 1) * 4096], [wdn_buf[l]], [wd])
                    for kc in range(8):
                        MM(bank(bg_)[:, 0:256], wup[:, fc, 0, kc, :], xg[:, kc, :], kc == 0, kc == 7, [wup, xg], [PSB[bg_]])
                    for kc in range(8):
                        MM(bank(bu_)[:, 0:256], wup[:, fc, 1, kc, :], xg[:, kc, :], kc == 0, kc == 7, [wup, xg], [PSB[bu_]])
                    CP("pool", hgp[sl][:, 0:2], carry[:, fc, :], [carry], [hgp[sl]])
                    CP("act", hgp[sl][:, 2:258], bank(bg_)[:, 0:256], [PSB[bg_]], [hgp[sl]])
                    CP("pool", carry[:, fc, :], hgp[sl][:, 256:258], [hgp[sl]], [carry])

                    def fw(k):
                        return chv[:, cv + C_FW + k * NFC + fc:cv + C_FW + k * NFC + fc + 1]
                    ACT(cf[sl][:, :], bank(bg_)[:, 0:256], AF.Identity, [PSB[bg_], chv], [cf[sl]], scale=fw(2),
                        bias=chv[:, cv + C_FB + fc:cv + C_FB + fc + 1])
                    STT(cf[sl][:, :], hgp[sl][:, 1:257], fw(1), cf[sl][:, :], ALU.mult, ALU.add, [hgp[sl], cf[sl], chv], [cf[sl]])
                    STT(cf[sl][:, :], hgp[sl][:, 0:256], fw(0), cf[sl][:, :], ALU.mult, ALU.add, [hgp[sl], cf[sl], chv], [cf[sl]])
                    ACT(gl[sl][:, :], cf[sl][:, :], AF.Gelu_apprx_tanh, [cf[sl]], [gl[sl]])
                    TT("dve", actT[:, fc, :], gl[sl][:, :], bank(bu_)[:, 0:256], ALU.mult, [gl[sl], PSB[bu_]], [actT_b[fc]])

                def down(fc):
                    wd = wdn[(fc // 4) % 2]
                    f6 = fc % 4
                    for tl in range(2):
                        for h in range(2):
                            MM(pair(tl)[:, h * 512:(h + 1) * 512], actT[:, fc, tl * 128:(tl + 1) * 128], wd[:, f6, h * 512:(h + 1) * 512],
                               fc == 0, fc == NFC - 1, [actT_b[fc], wd], [PSB[2 * tl + h]], sig=(f6 == 3 and tl == 1 and h == 1))

                for fc in range(NFC + 2):
                    if fc < NFC:
                        up(fc)
                    if fc >= 2:
                        down(fc - 2)
                    if fc == 4 and pending_tr:
                        pending_tr.pop(0)()
                for tl in range(2):
                    tt = gi * 2 + tl
                    sl = tt % 2
                    STT(zt[sl][:, :], x1r[:, tl, :], ALPHA, pair(tl), ALU.mult, ALU.add, [x1r, PSB[2 * tl], PSB[2 * tl + 1]], [zt[sl]])
                    ln_a("pool", zt[sl], stats[sl], mv[sl], rs[sl])
                for tl in range(2):
                    tt = gi * 2 + tl
                    sl = tt % 2
                    ln_b(zt[sl], mv[sl], rs[sl], g2, b2, x2t[sl][:, :], [x2t[sl]])
                    if last:
                        S.dma("sp", y_d[tt * 128:(tt + 1) * 128, :], x2t[sl][:, :], [x2t[sl]], [y_b[tt]])
                    else:
                        S.dma("sp", xres2[tt * 128:(tt + 1) * 128, :], x2t[sl][:, :], [x2t[sl]], [xres2_b[tt]])

                def make_tr(gi):
                    def tr_group():
                        for tl in range(2):
                            tt = gi * 2 + tl
                            sl = tt % 2
                            xg2 = x2Tg[0]
                            pt = 2 + sl
                            for kc in range(8):
                                TR(pair(pt)[:, kc * 128:(kc + 1) * 128], x2t[sl][:, kc * 128:(kc + 1) * 128], ident_f[:, :], [x2t[sl], ident_f],
                                   [PSB[2 * pt], PSB[2 * pt + 1]], signal=(kc == 7))
                            CP("act", xg2[:, :, (tt % 4) * 128:(tt % 4 + 1) * 128], pair(pt).rearrange("p (a b) -> p a b", a=8),
                               [PSB[2 * pt], PSB[2 * pt + 1]], [xg2])
                            if tt % 4 == 3:
                                c = tt // 4
                                S.dma("sp", xT_d[:, :, c * 512:(c + 1) * 512], xg2[:, :, :], [xg2], [xTd_b[c]])
                    return tr_group
                if not last:
                    pending_tr.append(make_tr(gi))
            while pending_tr:
                pending_tr.pop(0)()
            if dbg and not last:
                o = nc.dram_tensor("d_x2", [SEQ, D], F32, kind="ExternalOutput").ap()
                for q4 in range(4):
                    S.dma("sp", o[q4 * 1024:(q4 + 1) * 1024, :], xres2[q4 * 1024:(q4 + 1) * 1024, :], xres2_b, [Buf("dbg")])
                o = nc.dram_tensor("d_xTd", [128, 8, SEQ], BF16, kind="ExternalOutput").ap()
                for q4 in range(8):
                    S.dma("sp", o[:, q4, :], xT_d[:, q4, :], xTd_b, [Buf("dbg")])
            S.barrier()

        S.barrier()
        S.replay()
    return nc


def _prep_weights(inp):
    f = np.float32
    w_in = np.asarray(inp["w_in"], f)
    L = w_in.shape[0]
    cgs = [0, 1, 2, 4, 5, 6, 7, 8, 9]
    w6 = w_in.reshape(L, 8, 128, 10, 8, 128)
    w_in_g = np.ascontiguousarray(w6[:, :, :, cgs, :, :].transpose(0, 4, 2, 3, 1, 5)).reshape(L, 8, 128, 9 * 1024)
    w_sv = np.ascontiguousarray(w6[:, :, :, 3, :, :].transpose(0, 2, 1, 3, 4)).reshape(L, 128, 8 * 1024)
    w_out = np.asarray(inp["w_out"], f).reshape(L, 8, 128, 1024)
    w_outr = np.ascontiguousarray(w_out.transpose(0, 2, 1, 3)).reshape(L, 128, 8 * 1024)
    w_up = np.asarray(inp["w_ffn_up"], f).reshape(L, 8, 128, 2, NFC, 128)
    w_upr = np.ascontiguousarray(w_up.transpose(0, 2, 4, 3, 1, 5)).reshape(L, 128, NFC * 2048)
    w_dn = np.asarray(inp["w_ffn_down"], f).reshape(L, NFC, 128, 1024)
    w_dnr = np.ascontiguousarray(w_dn.transpose(0, 2, 1, 3)).reshape(L, 128, NFC * 1024)
    wr = np.asarray(inp["w_rgate"], f)
    wi = np.asarray(inp["w_igate"], f)
    rgw = np.ascontiguousarray(np.stack([wr, wi], axis=2).transpose(0, 3, 1, 2, 4)).reshape(L, 128, 2048)
    wsp = np.asarray(inp["w_spatial"], f)
    wspT = np.ascontiguousarray(wsp.transpose(0, 3, 1, 2)).reshape(L, 128, 1024)
    chv = np.zeros((128, L * NCH), f)

    def pc(v, n):
        return np.asarray(v, f).reshape(n, 128).T

    for l in range(L):
        o = l * NCH
        for k in range(4):
            chv[:, o + C_CW + k * 8:o + C_CW + (k + 1) * 8] = pc(inp["conv_rg_w"][l][k], 8)
        chv[:, o + C_CB:o + C_CB + 8] = pc(inp["conv_rg_b"][l], 8)
        chv[:, o + C_BR:o + C_BR + 8] = pc(inp["b_rgate"][l], 8)
        chv[:, o + C_BI:o + C_BI + 8] = pc(inp["b_igate"][l], 8)
        chv[:, o + C_LAM:o + C_LAM + 8] = pc(inp["lru_lambda"][l], 8)
        for k in range(3):
            chv[:, o + C_FW + k * NFC:o + C_FW + (k + 1) * NFC] = pc(inp["conv_ffn_w"][l][k], NFC)
        chv[:, o + C_FB:o + C_FB + NFC] = pc(inp["conv_ffn_b"][l], NFC)
    bsp = np.ascontiguousarray(np.asarray(inp["b_spatial"], f).reshape(L, 1024))
    tokv = np.ascontiguousarray(np.stack([np.asarray(inp[k], f) for k in
                                          ("sgu_ln_g", "sgu_ln_b", "ln_mix_g", "ln_mix_b", "ln_ffn_g", "ln_ffn_b")], axis=1))
    return dict(w_in_g=w_in_g, w_sv=w_sv, w_outr=w_outr, w_upr=w_upr, w_dnr=w_dnr, rgw=rgw, wspT=wspT, chv=chv, bsp=bsp, tokv=tokv)


_CACHE = {}


def kernel(**inputs):
    x = np.asarray(inputs["x"], np.float32)
    B = x.shape[0]
    wts = _prep_weights(inputs)
    if "nc" not in _CACHE:
        _CACHE["nc"] = build_program()
    nc = _CACHE["nc"]
    in_maps = []
    for b in range(B):
        m = {"x": np.ascontiguousarray(x[b])}
        m.update(wts)
        in_maps.append(m)
    res = run_bass_kernel_spmd(nc, in_maps, core_ids=list(range(B)))
    return np.stack([np.asarray(r["y"], np.float32) for r in res.results], axis=0)
```

```python
from contextlib import ExitStack
import numpy as np
import concourse.bass as bass
import concourse.mybir as mybir
from concourse.bass_utils import run_bass_kernel_spmd

F32 = mybir.dt.float32
BF16 = mybir.dt.bfloat16
AF = mybir.ActivationFunctionType
ALU = mybir.AluOpType

ENGS = ("pe", "act", "dve", "pool", "sp")
SAME_ENGINE_SYNC = True

D = 1024
SEQ = 4096
NT = 32
KC = 8
DEPTH = 2
DFF = 3072
NFC = 24
ALPHA = float((2 * DEPTH) ** 0.25)
EPS = 1e-5
ATT_SCALE = float(128 ** -0.5)
NEGBIG = -30000.0
C_CW, C_CB, C_BR, C_BI, C_LAM, C_FW, C_FB, NCH = 0, 32, 40, 48, 56, 64, 136, 160
CG_AX, CG_AG, CG_SU, CG_Q, CG_K, CG_V, CG_GA, CG_GS, CG_GM = range(9)


class Buf:
    __slots__ = ("w", "r", "name")

    def __init__(self, name=""):
        self.w = {}
        self.r = {}
        self.name = name


class Tile:
    def __init__(self, ap, name):
        self.ap = ap
        self.buf = Buf(name)
        self.name = name

    def __getitem__(self, k):
        return self.ap[k]


def _bufs(lst):
    out = []
    for x in lst:
        if x is None:
            continue
        out.append(x.buf if isinstance(x, Tile) else x)
    return out


class Sched:
    def __init__(self, nc, es):
        self.nc = nc
        self.es = es
        self.streams = {e: [] for e in ENGS}
        self.cnt = {}
        self.sem = {}
        for e in ("pe", "act", "dve", "pool"):
            self.sem[e] = es.enter_context(nc.semaphore("sem_" + e))
            self.cnt[e] = 0
        self.waited = {e: {} for e in ENGS}
        self.dma_pool = []
        self.dma_rr = 0
        self.dma_cnt = {}

    def new_dma_sem(self, name):
        s = self.es.enter_context(self.nc.semaphore(name))
        self.dma_cnt[id(s)] = [s, 0]
        return s

    def _pool_sem(self):
        if len(self.dma_pool) < 32:
            s = self.new_dma_sem("dq%d" % len(self.dma_pool))
            self.dma_pool.append(s)
            return s
        s = self.dma_pool[self.dma_rr % len(self.dma_pool)]
        self.dma_rr += 1
        return s

    def _deps(self, reads, writes):
        deps = {}

        def add(d):
            for k, v in d.items():
                if deps.get(k, (None, 0))[1] < v[1]:
                    deps[k] = v

        for b in reads:
            add(b.w)
        for b in writes:
            add(b.w)
            add(b.r)
        return deps

    def _emit_waits(self, eng, deps):
        own = self.sem.get(eng)
        for k, (s, v) in deps.items():
            if own is not None and s is own and (eng == "pe" or not SAME_ENGINE_SYNC):
                continue
            if self.waited[eng].get(k, 0) >= v:
                continue
            self.waited[eng][k] = v
            self.streams[eng].append(("wait", s, v))

    def _record(self, tok, reads, writes):
        k = id(tok[0])
        for b in reads:
            if b.r.get(k, (None, 0))[1] < tok[1]:
                b.r[k] = tok
        for b in writes:
            if b.w.get(k, (None, 0))[1] < tok[1]:
                b.w[k] = tok

    def op(self, eng, fn, reads=(), writes=(), signal=True):
        reads = _bufs(reads)
        writes = _bufs(writes)
        self._emit_waits(eng, self._deps(reads, writes))
        s = self.sem[eng]
        if signal:
            self.cnt[eng] += 1
            tok = (s, self.cnt[eng])
        else:
            tok = (s, self.cnt[eng] + 1)
        self.streams[eng].append(("op", fn, s if signal else None))
        self._record(tok, reads, writes)
        return tok

    def dma(self, q, out, in_, reads=(), writes=(), sem=None):
        reads = _bufs(reads)
        writes = _bufs(writes)
        if sem is None:
            if q == "pool":
                if not hasattr(self, "swq"):
                    self.swq = [self.new_dma_sem("swq%d" % i) for i in range(2)]
                    self.swq_rr = 0
                sem = self.swq[self.swq_rr % 2]
                self.swq_rr += 1
            else:
                sem = self._pool_sem()
        ent = self.dma_cnt[id(sem)]
        deps = self._deps(reads, writes)
        if ent[1] > 0:
            deps[id(sem)] = (sem, max(deps.get(id(sem), (None, 0))[1], ent[1]))
        self._emit_waits(q, deps)
        ent[1] += 16
        tok = (sem, ent[1])
        self.streams[q].append(("dma", out, in_, sem))
        self._record(tok, reads, writes)
        return tok

    def wait_all(self, eng, bufs):
        self._emit_waits(eng, self._deps(_bufs(bufs), []))

    def barrier(self):
        deps = {}
        for e in ("pe", "act", "dve", "pool"):
            if self.cnt[e] > 0:
                deps[id(self.sem[e])] = (self.sem[e], self.cnt[e])
        for k, (s, c) in self.dma_cnt.items():
            if c > 0:
                deps[k] = (s, c)
        for e in ENGS:
            self._emit_waits(e, deps)

    def replay(self):
        nc = self.nc
        streams = self.streams

        def run(e, stream):
            for it in stream:
                if it[0] == "wait":
                    e.wait_ge(it[1], it[2])
                elif it[0] == "op":
                    ins = it[1](e)
                    if it[2] is not None:
                        ins.then_inc(it[2], 1)
                else:
                    e.dma_start(out=it[1], in_=it[2]).then_inc(it[3], 16)

        with nc.Block() as block:
            @block.tensor
            def _(e):
                run(e, streams["pe"])

            @block.scalar
            def _(e):
                run(e, streams["act"])

            @block.vector
            def _(e):
                run(e, streams["dve"])

            @block.gpsimd
            def _(e):
                run(e, streams["pool"])

            @block.sync
            def _(e):
                run(e, streams["sp"])


class Arena:
    def __init__(self, ap, nwords):
        self.ap = ap
        self.n = nwords
        self.off = 0
        self.marks = []

    def alloc(self, name, shape, dtype):
        free = 1
        for s in shape[1:]:
            free *= s
        words = free if dtype == F32 else (free + 1) // 2
        assert self.off + words <= self.n, (name, self.off, words, self.n)
        v = self.ap[:, self.off:self.off + words]
        self.off += words
        if dtype != F32:
            v = v.bitcast(dtype)
            if (free % 2) == 1:
                v = v[:, 0:free]
        if len(shape) == 3:
            v = v.rearrange("p (a b) -> p a b", a=shape[1])
        elif len(shape) == 4:
            v = v.rearrange("p (a b c) -> p a b c", a=shape[1], b=shape[2])
        elif len(shape) == 5:
            v = v.rearrange("p (a b c d) -> p a b c d", a=shape[1], b=shape[2], c=shape[3])
        if shape[0] < 128:
            v = v[0:shape[0]]
        return Tile(v, name)

    def mark(self):
        return self.off

    def reset(self, m):
        self.off = m


def build_program(n_layers=DEPTH, dbg=False):
    nc = bass.Bass("TRN2", target_bir_lowering=False)

    def din(name, shape, dt=F32):
        return nc.dram_tensor(name, list(shape), dt, kind="ExternalInput").ap()

    def dscr(name, shape, dt):
        return nc.dram_tensor(name, list(shape), dt).ap()

    x_d = din("x", [SEQ, D])
    w_in_g = din("w_in_g", [DEPTH, 8, 128, 9 * 1024])
    w_sv = din("w_sv", [DEPTH, 128, 8 * 1024])
    w_outr = din("w_outr", [DEPTH, 128, 8 * 1024])
    w_upr = din("w_upr", [DEPTH, 128, NFC * 2048])
    w_dnr = din("w_dnr", [DEPTH, 128, NFC * 1024])
    rgw_d = din("rgw", [DEPTH, 128, 2048])
    wspT_d = din("wspT", [DEPTH, 128, 1024])
    chv_d = din("chv", [128, DEPTH * NCH])
    bsp_d = din("bsp", [DEPTH, 1024])
    tokv_d = din("tokv", [DEPTH, 6, 1024])
    y_d = nc.dram_tensor("y", [SEQ, D], F32, kind="ExternalOutput").ap()

    xres1 = dscr("xres1", [SEQ, D], F32)
    xres2 = dscr("xres2", [SEQ, D], F32)
    vln_d = dscr("vln_d", [SEQ, D], BF16)
    mrg_d = dscr("mrg_d", [128, 8, SEQ], BF16)
    xT_d = dscr("xT_d", [128, 8, SEQ], BF16)
    x1T_d = dscr("x1T_d", [128, 8, SEQ], BF16)
    wdn_bf = dscr("wdn_bf", [DEPTH, 128, NFC * 1024], BF16)
    dbg_out = {}
    if dbg:
        dbg_out["d_mrg"] = nc.dram_tensor("d_mrg", [128, 8, SEQ], BF16, kind="ExternalOutput").ap()
        dbg_out["d_x1"] = nc.dram_tensor("d_x1", [SEQ, D], F32, kind="ExternalOutput").ap()
        dbg_out["d_x1_0"] = nc.dram_tensor("d_x1_0", [SEQ, D], F32, kind="ExternalOutput").ap()
        dbg_out["d_mrg_0"] = nc.dram_tensor("d_mrg_0", [128, 8, SEQ], BF16, kind="ExternalOutput").ap()
        dbg_out["d_vln"] = nc.dram_tensor("d_vln", [SEQ, D], BF16, kind="ExternalOutput").ap()

    with ExitStack() as es:
        S = Sched(nc, es)
        NW = 53000
        arena_t = es.enter_context(nc.sbuf_tensor("arena", [128, NW], F32))
        AR = Arena(arena_t[:, :], NW)
        psum_t = [es.enter_context(nc.psum_tensor("ps%d" % i, [128, 1024], F32)) for i in range(4)]
        PSB = [Buf("bank%d" % i) for i in range(8)]

        def bank(i):
            return psum_t[i // 2][:, (i % 2) * 512:(i % 2) * 512 + 512]

        def pair(i):
            return psum_t[i][:, :]

        def MM(out, lhsT, rhs, start, stop, r, w, sig=False):
            S.op("pe", lambda e: e.matmul(out, lhsT=lhsT, rhs=rhs, start=start, stop=stop), r, w, signal=(stop or sig))

        def TR(out, in_, ident, r, w, signal=True):
            S.op("pe", lambda e: e.transpose(out=out, in_=in_, identity=ident), r, w, signal=signal)

        def ACT(out, in_, func, r, w, scale=1.0, bias=None, accum=None):
            def f(e):
                kw = {}
                if bias is not None:
                    kw["bias"] = bias
                if accum is not None:
                    kw["accum_out"] = accum
                return e.activation(out=out, in_=in_, func=func, scale=scale, **kw)
            S.op("act", f, r, w)

        def TT(eng, out, in0, in1, op, r, w):
            S.op(eng, lambda e: e.tensor_tensor(out=out, in0=in0, in1=in1, op=op), r, w)

        def TS(eng, out, in0, s1, s2, op0, op1, r, w):
            if s2 is None:
                S.op(eng, lambda e: e.tensor_scalar(out=out, in0=in0, scalar1=s1, scalar2=None, op0=op0), r, w)
            else:
                S.op(eng, lambda e: e.tensor_scalar(out=out, in0=in0, scalar1=s1, scalar2=s2, op0=op0, op1=op1), r, w)

        def STT(out, in0, scalar, in1, op0, op1, r, w):
            S.op("dve", lambda e: e.scalar_tensor_tensor(out=out, in0=in0, scalar=scalar, in1=in1, op0=op0, op1=op1), r, w)

        def CP(eng, out, in_, r, w):
            if eng == "act":
                S.op("act", lambda e: e.activation(out=out, in_=in_, func=AF.Copy), r, w)
            else:
                S.op(eng, lambda e: e.tensor_copy(out=out, in_=in_), r, w)

        def MEMSET(eng, ap, val, w):
            S.op(eng, lambda e: e.memset(ap, val), [], w)

        ones_f = AR.alloc("ones_f", [128, 128], F32)
        ident_f = AR.alloc("ident_f", [128, 128], F32)
        triu_f = AR.alloc("triu_f", [128, 128], F32)
        ident_b = AR.alloc("ident_b", [128, 128], BF16)
        triu_b = AR.alloc("triu_b", [128, 128], BF16)
        ones_b = AR.alloc("ones_b", [128, 128], BF16)
        sel = AR.alloc("sel", [128, 16, 128], BF16)
        cb = AR.alloc("cb", [128, 32, 16], F32)
        chv = AR.alloc("chv", [128, DEPTH * NCH], F32)
        der = AR.alloc("der", [128, DEPTH * 40], F32)
        cst = AR.alloc("cst", [128, 8], F32)
        PERS = [ones_f, ident_f, triu_f, ident_b, triu_b, ones_b, sel, cb, chv, der, cst]

        MEMSET("pool", ones_f[:, :], 1.0, [ones_f])
        S.op("pool", lambda e: e.affine_select(out=ident_f[:, :], in_=ones_f[:, :], pattern=[[1, 128]], compare_op=ALU.is_equal,
                                               fill=0.0, base=0, channel_multiplier=-1), [ones_f], [ident_f])
        S.op("pool", lambda e: e.affine_select(out=triu_f[:, :], in_=ones_f[:, :], pattern=[[1, 128]], compare_op=ALU.is_ge,
                                               fill=0.0, base=0, channel_multiplier=-1), [ones_f], [triu_f])
        CP("dve", ident_b[:, :], ident_f[:, :], [ident_f], [ident_b])
        CP("dve", triu_b[:, :], triu_f[:, :], [triu_f], [triu_b])
        CP("dve", ones_b[:, :], ones_f[:, :], [ones_f], [ones_b])
        MEMSET("pool", cb[:, :, :], -1e30, [cb])
        for b in range(1, 16):
            MEMSET("pool", cb[:, 2 * b:2 * b + 2, 0:b], 0.0, [cb])
        MEMSET("pool", cst[:, 0:1], 1.0, [cst])
        MEMSET("pool", cst[:, 1:2], EPS, [cst])
        MEMSET("pool", cst[:, 2:3], -0.5, [cst])
        MEMSET("pool", cst[:, 3:4], 0.5, [cst])
        S.dma("sp", chv[:, :], chv_d[:, :], [], [chv])
        for l in range(n_layers):
            cv = l * NCH
            dv = l * 40
            TS("dve", der[:, dv:dv + 16], chv[:, cv + C_BR:cv + C_BR + 16], 0.5, None, ALU.mult, None, [chv], [der])
            ACT(der[:, dv + 32:dv + 40], chv[:, cv + C_LAM:cv + C_LAM + 8], AF.Exp, [chv], [der], scale=-1.0)
            ACT(der[:, dv + 32:dv + 40], der[:, dv + 32:dv + 40], AF.Ln, [der, cst], [der], bias=cst[:, 0:1])
            TS("dve", der[:, dv + 16:dv + 24], der[:, dv + 32:dv + 40], -4.0, None, ALU.mult, None, [der], [der])
            TS("dve", der[:, dv + 24:dv + 32], der[:, dv + 32:dv + 40], -8.0, None, ALU.mult, None, [der], [der])

        pers_mark = AR.mark()
        onesrc = AR.alloc("onesrc", [128, 2048], BF16)
        MEMSET("pool", onesrc[:, :], 1.0, [onesrc])
        S.op("pool", lambda e: e.affine_select(out=sel[:, :, :], in_=onesrc[:, :].rearrange("p (a b) -> p a b", a=16),
                                               pattern=[[1, 16], [0, 128]], compare_op=ALU.is_equal, fill=0.0, base=0,
                                               channel_multiplier=-1), [onesrc], [sel])

        def DUMP(name, ap, reads):
            if not dbg:
                return
            o = nc.dram_tensor(name, list(ap.shape), ap.dtype, kind="ExternalOutput").ap()
            if len(ap.shape) == 3:
                for i in range(ap.shape[1]):
                    S.dma("sp", o[:, i, :], ap[:, i, :], reads, [Buf("dbg")])
            else:
                S.dma("sp", o[:, :], ap, reads, [Buf("dbg")])

        wdn_buf = [Buf("wdn%d" % l) for l in range(DEPTH)]
        for l in range(n_layers):
            for c in range(4):
                S.dma("pool", wdn_bf[l][:, c * 6144:(c + 1) * 6144].rearrange("p (a b) -> p a b", b=1024),
                      w_dnr[l][:, c * 6144:(c + 1) * 6144].rearrange("p (a b) -> p a b", b=1024), [], [wdn_buf[l]])

        vln_b = [Buf("vln%d" % t) for t in range(NT)]
        mrg_b = [Buf("mrg%d" % g) for g in range(8)]
        xres1_b = [Buf("xr1_%d" % t) for t in range(NT)]
        xres2_b = [Buf("xr2_%d" % t) for t in range(NT)]
        x1T_b = [Buf("x1T%d" % t) for t in range(8)]
        xTd_b = [Buf("xTd%d" % t) for t in range(8)]
        y_b = [Buf("y%d" % t) for t in range(NT)]

        def ln_a(mode, zt, stats, mv, rs):
            S.op("dve", lambda e: e.bn_stats(out=stats[:, 0, :], in_=zt[:, 0:512]), [zt], [stats])
            S.op("dve", lambda e: e.bn_stats(out=stats[:, 1, :], in_=zt[:, 512:1024]), [zt], [stats])
            S.op("dve", lambda e: e.bn_aggr(out=mv[:, :], in_=stats[:, :, :]), [stats], [mv])
            if mode == "pool":
                TS("dve", rs[:, 0:1], mv[:, 1:2], EPS, None, ALU.add, None, [mv], [rs])
                TT("pool", rs[:, 1:2], rs[:, 0:1], cst[:, 2:3], ALU.pow, [rs, cst], [rs])
            else:
                ACT(rs[:, 0:1], mv[:, 1:2], AF.Sqrt, [mv, cst], [rs], bias=cst[:, 1:2])
                S.op("dve", lambda e: e.reciprocal(out=rs[:, 1:2], in_=rs[:, 0:1]), [rs], [rs])

        def ln_b(zt, mv, rs, gbc, bbc, out_ap, w_out):
            STT(zt[:, :], zt[:, :], mv[:, 0:1], gbc[:, :], ALU.subtract, ALU.mult, [zt, mv, gbc], [zt])
            STT(out_ap, zt[:, :], rs[:, 1:2], bbc[:, :], ALU.mult, ALU.add, [zt, rs, bbc], list(w_out))

        for l in range(n_layers):
            cv = l * NCH
            dv = l * 40
            TS("dve", der[:, dv:dv + 16], chv[:, cv + C_BR:cv + C_BR + 16], 0.5, None, ALU.mult, None, [chv], [der])
            ACT(der[:, dv + 32:dv + 40], chv[:, cv + C_LAM:cv + C_LAM + 8], AF.Exp, [chv], [der], scale=-1.0)
            ACT(der[:, dv + 32:dv + 40], der[:, dv + 32:dv + 40], AF.Ln, [der, cst], [der], bias=cst[:, 0:1])
            TS("dve", der[:, dv + 16:dv + 24], der[:, dv + 32:dv + 40], -4.0, None, ALU.mult, None, [der], [der])
            TS("dve", der[:, dv + 24:dv + 32], der[:, dv + 32:dv + 40], -8.0, None, ALU.mult, None, [der], [der])


        def DUMP(name, ap, reads):
            if not dbg:
                return
            o = nc.dram_tensor(name, list(ap.shape), ap.dtype, kind="ExternalOutput").ap()
            if len(ap.shape) == 3:
                for i in range(ap.shape[1]):
                    S.dma("sp", o[:, i, :], ap[:, i, :], reads, [Buf("dbg")])
            else:
                S.dma("sp", o[:, :], ap, reads, [Buf("dbg")])

        wdn_buf = [Buf("wdn%d" % l) for l in range(DEPTH)]
        for l in range(n_layers):
            for c in range(4):
                S.dma("pool", wdn_bf[l][:, c * 6144:(c + 1) * 6144].rearrange("p (a b) -> p a b", b=1024),
                      w_dnr[l][:, c * 6144:(c + 1) * 6144].rearrange("p (a b) -> p a b", b=1024), [], [wdn_buf[l]])

        vln_b = [Buf("vln%d" % t) for t in range(NT)]
        mrg_b = [Buf("mrg%d" % g) for g in range(8)]
        xres1_b = [Buf("xr1_%d" % t) for t in range(NT)]
        xres2_b = [Buf("xr2_%d" % t) for t in range(NT)]
        x1T_b = [Buf("x1T%d" % t) for t in range(8)]
        xTd_b = [Buf("xTd%d" % t) for t in range(8)]
        y_b = [Buf("y%d" % t) for t in range(NT)]

        def ln_rows(mode, zt, stats, mv, rs, gbc, bbc, out_ap, r_extra, w_out, tmp2=None):
            S.op("dve", lambda e: e.bn_stats(out=stats[:, 0, :], in_=zt[:, 0:512]), [zt], [stats])
            S.op("dve", lambda e: e.bn_stats(out=stats[:, 1, :], in_=zt[:, 512:1024]), [zt], [stats])
            S.op("dve", lambda e: e.bn_aggr(out=mv[:, :], in_=stats[:, :, :]), [stats], [mv])
            if mode == "pool":
                TS("dve", rs[:, 0:1], mv[:, 1:2], EPS, None, ALU.add, None, [mv], [rs])
                TT("pool", rs[:, 1:2], rs[:, 0:1], cst[:, 2:3], ALU.pow, [rs, cst], [rs])
            else:
                ACT(rs[:, 0:1], mv[:, 1:2], AF.Sqrt, [mv, cst], [rs], bias=cst[:, 1:2])
                S.op("dve", lambda e: e.reciprocal(out=rs[:, 1:2], in_=rs[:, 0:1]), [rs], [rs])
            STT(zt[:, :], zt[:, :], mv[:, 0:1], gbc[:, :], ALU.subtract, ALU.mult, [zt, mv, gbc], [zt])
            STT(out_ap, zt[:, :], rs[:, 1:2], bbc[:, :], ALU.mult, ALU.add, [zt, rs, bbc] + list(r_extra), list(w_out))

        for l in range(n_layers):
            cv = l * NCH
            dv = l * 40
            last = (l == n_layers - 1)
            xin_d = x_d if l == 0 else xres2
            xin_b = [None] * NT if l == 0 else xres2_b

            S.barrier()
            AR.reset(pers_mark)
            xT = AR.alloc("xT", [128, 8, SEQ], BF16)
            xT_b = [Buf("xT%d" % t) for t in range(NT)]
            macc = AR.alloc("macc", [128, SEQ], F32)
            wg = [AR.alloc("wg%d" % i, [128, 9, 8, 128], BF16) for i in range(2)]
            rgw = AR.alloc("rgw", [128, 8, 2, 128], BF16)
            wspb = AR.alloc("wspb", [128, 8, 128], BF16)
            bspbc = AR.alloc("bspbc", [128, 8, 128], F32)
            vlng = AR.alloc("vlng", [128, 32, 128], BF16)
            mix_mark = AR.mark()

            S.dma("pool", rgw[:, :, :, :].rearrange("p a b c -> p (a b c)"), rgw_d[l][:, :], [], [rgw])
            S.dma("sp", bspbc[:, :, :].rearrange("p a b -> p (a b)"), bsp_d[l].partition_broadcast(128), [], [bspbc])

            if l == 0:
                xin = [AR.alloc("xin%d" % i, [128, 1024], F32) for i in range(4)]
                for tt in range(NT):
                    xi = xin[tt % 4]
                    S.dma("sp", xi[:, :], x_d[tt * 128:(tt + 1) * 128, :], [], [xi])
                    for h in range(2):
                        pp = (tt % 2) * 2 + h
                        for j in range(4):
                            TR(pair(pp)[:, j * 128:(j + 1) * 128], xi[:, (h * 4 + j) * 128:(h * 4 + j + 1) * 128], ident_f[:, :],
                               [xi, ident_f], [PSB[2 * pp], PSB[2 * pp + 1]], signal=(j == 3))
                        CP("act" if h == 0 else "dve", xT[:, h * 4:(h + 1) * 4, tt * 128:(tt + 1) * 128],
                           pair(pp)[:, 0:512].rearrange("p (a b) -> p a b", a=4), [PSB[2 * pp], PSB[2 * pp + 1]], [xT_b[tt]])
            else:
                for c in range(8):
                    S.dma("sp", xT[:, :, c * 512:(c + 1) * 512], xT_d[:, :, c * 512:(c + 1) * 512], [xTd_b[c]],
                          [xT_b[4 * c + i] for i in range(4)])
            if l == 0 and False:
                DUMP("d_xT", xT[:, :, :], xT_b)
            S.barrier()
            AR.reset(mix_mark)

            wsv = AR.alloc("wsv", [128, 8, 1024], BF16)
            wspf = AR.alloc("wspf", [128, 8, 128], F32)
            gbc = AR.alloc("gbc", [128, 1024], F32)
            bbc = AR.alloc("bbc", [128, 1024], F32)
            v32 = [AR.alloc("v32_%d" % i, [128, 1024], F32) for i in range(2)]
            vlnb = [AR.alloc("vlnb%d" % i, [128, 1024], BF16) for i in range(2)]
            stats = [AR.alloc("stats%d" % i, [128, 2, 6], F32) for i in range(2)]
            mv = [AR.alloc("mv%d" % i, [128, 2], F32) for i in range(2)]
            rs = [AR.alloc("rs%d" % i, [128, 2], F32) for i in range(2)]
            for kc in range(8):
                S.dma("pool", wsv[:, kc, :], w_sv[l][:, kc * 1024:(kc + 1) * 1024], [], [wsv])
            S.dma("sp", wspf[:, :, :].rearrange("p a b -> p (a b)"), wspT_d[l][:, :], [], [wspf])
            S.dma("sp", gbc[:, :], tokv_d[l][0].partition_broadcast(128), [], [gbc])
            S.dma("sp", bbc[:, :], tokv_d[l][1].partition_broadcast(128), [], [bbc])
            TT("dve", wspb[:, :, :], wspf[:, :, :], triu_f[:, :].unsqueeze(1).broadcast_to([128, 8, 128]), ALU.mult,
               [wspf, triu_f], [wspb])
            def load_wg(g):
                t = wg[g % 2]
                for cg in range(9):
                    S.dma("pool", t[:, cg, :, :].rearrange("p a b -> p (a b)"), w_in_g[l][g][:, cg * 1024:(cg + 1) * 1024], [], [t])
            load_wg(0)
            def bpre_a(tt):
                sl = tt % 2
                pp = sl
                for h in range(2):
                    for kc in range(8):
                        MM(pair(pp)[:, h * 512:(h + 1) * 512], xT[:, kc, tt * 128:(tt + 1) * 128], wsv[:, kc, h * 512:(h + 1) * 512],
                           kc == 0, kc == 7, [xT_b[tt], wsv], [PSB[2 * pp + h]])
                ACT(v32[sl][:, :], pair(pp), AF.Gelu_apprx_tanh, [PSB[2 * pp], PSB[2 * pp + 1]], [v32[sl]])
                ln_a("pool", v32[sl], stats[sl], mv[sl], rs[sl])

            def bpre_b(tt):
                sl = tt % 2
                ln_b(v32[sl], mv[sl], rs[sl], gbc, bbc, vlnb[sl][:, :], [vlnb[sl]])
                S.dma("sp", vln_d[tt * 128:(tt + 1) * 128, :], vlnb[sl][:, :], [vlnb[sl]], [vln_b[tt]])

            for tt in range(NT + 1):
                if tt < NT:
                    bpre_a(tt)
                if tt >= 1:
                    bpre_b(tt - 1)
            S.barrier()
            AR.reset(mix_mark)
            g_mark = AR.mark()

            for g in range(8):
                wt = wg[g % 2]
                if g + 1 < 8:
                    load_wg(g + 1)

                def proj(cg, bk, t0, n):
                    xb = [xT_b[t] for t in range(t0 // 128, (t0 + n + 127) // 128)]
                    for kc in range(8):
                        MM(bank(bk)[:, 0:n], wt[:, cg, kc, :], xT[:, kc, t0:t0 + n], kc == 0, kc == 7, [wt] + xb, [PSB[bk]])

                AR.reset(g_mark)
                for q4 in range(4):
                    S.dma("sp", vlng[:, q4 * 8:(q4 + 1) * 8, :],
                          vln_d[q4 * 1024:(q4 + 1) * 1024, g * 128:(g + 1) * 128].rearrange("(n p) c -> p n c", p=128),
                          [vln_b[t] for t in range(q4 * 8, q4 * 8 + 8)], [vlng])
                HT = 2048
                axp = AR.alloc("axp", [128, HT + 4], F32)
                cc = AR.alloc("cc", [128, HT], F32)
                ccb = AR.alloc("ccb", [128, HT], BF16)
                tr_ = AR.alloc("tr", [128, HT], F32)
                ti_ = AR.alloc("ti", [128, HT], F32)
                aa = AR.alloc("aa", [128, HT], F32)
                a2 = AR.alloc("a2", [128, HT], F32)
                halo = AR.alloc("halo", [128, 4], F32)
                gg = [AR.alloc("gg%d" % i, [128, 512], F32) for i in range(2)]
                tg = [AR.alloc("tg%d" % i, [128, 512], F32) for i in range(2)]
                macc_b = [Buf("macc%d" % c) for c in range(8)]

                def cw(k):
                    return chv[:, cv + C_CW + k * 8 + g:cv + C_CW + k * 8 + g + 1]

                for hh in range(2):
                    T0 = hh * HT
                    if hh == 0:
                        MEMSET("pool", axp[:, 0:3], 0.0, [axp])
                    else:
                        CP("pool", axp[:, 0:3], halo[:, 0:3], [halo], [axp])
                    for c4 in range(4):
                        proj(CG_AX, c4, T0 + c4 * 512, 512)
                        CP("act", axp[:, 3 + c4 * 512:3 + (c4 + 1) * 512], bank(c4), [PSB[c4]], [axp])
                    TS("dve", cc[:, :], axp[:, 3:3 + HT], cw(3), chv[:, cv + C_CB + g:cv + C_CB + g + 1], ALU.mult, ALU.add, [axp, chv], [cc])
                    for k in (2, 1, 0):
                        STT(cc[:, :], axp[:, k:k + HT], cw(k), cc[:, :], ALU.mult, ALU.add, [axp, chv, cc], [cc])
                    CP("pool", halo[:, 0:3], axp[:, HT:HT + 3], [axp], [halo])
                    CP("act", ccb[:, :], cc[:, :], [cc], [ccb])
                    for c4 in range(4):
                        MM(bank(c4), rgw[:, g, 0, :], ccb[:, c4 * 512:(c4 + 1) * 512], True, True, [rgw, ccb], [PSB[c4]])
                    for c4 in range(4):
                        MM(bank(4 + c4), rgw[:, g, 1, :], ccb[:, c4 * 512:(c4 + 1) * 512], True, True, [rgw, ccb], [PSB[4 + c4]])
                    for p2 in range(2):
                        ACT(tr_[:, p2 * 1024:(p2 + 1) * 1024], pair(p2), AF.Tanh, [PSB[2 * p2], PSB[2 * p2 + 1], der], [tr_], scale=0.5,
                            bias=der[:, dv + g:dv + g + 1])
                    for p2 in range(2):
                        ACT(ti_[:, p2 * 1024:(p2 + 1) * 1024], pair(2 + p2), AF.Tanh, [PSB[4 + 2 * p2], PSB[5 + 2 * p2], der], [ti_], scale=0.5,
                            bias=der[:, dv + 8 + g:dv + 8 + g + 1])
                    ACT(aa[:, :], tr_[:, :], AF.Exp, [tr_, der], [aa], scale=der[:, dv + 16 + g:dv + 16 + g + 1],
                        bias=der[:, dv + 16 + g:dv + 16 + g + 1])
                    ACT(a2[:, :], tr_[:, :], AF.Exp, [tr_, der], [a2], scale=der[:, dv + 24 + g:dv + 24 + g + 1],
                        bias=der[:, dv + 24 + g:dv + 24 + g + 1])
                    TS("dve", a2[:, :], a2[:, :], 1.0, None, ALU.min, None, [a2], [a2])
                    ACT(a2[:, :], a2[:, :], AF.Sqrt, [a2, cst], [a2], scale=-1.0, bias=cst[:, 0:1])
                    STT(ti_[:, :], ti_[:, :], 1.0, cc[:, :], ALU.add, ALU.mult, [ti_, cc], [ti_])
                    STT(ti_[:, :], a2[:, :], 0.5, ti_[:, :], ALU.mult, ALU.mult, [a2, ti_], [ti_])
                    init = 0.0 if hh == 0 else macc[:, T0 - 1:T0]
                    mbs = [macc_b[4 * hh + i] for i in range(4)]
                    rb = [aa, ti_] + ([macc_b[4 * hh - 1]] if hh > 0 else [])
                    S.op("dve", (lambda T0, init: lambda e: e.tensor_tensor_scan(out=macc[:, T0:T0 + HT], data0=aa[:, :], data1=ti_[:, :],
                                                                                  initial=init, op0=ALU.mult, op1=ALU.add))(T0, init), rb, mbs)
                for c in range(8):
                    sl = c % 2
                    t0 = c * 512
                    b1 = [0, 2, 4, 6][c % 4]
                    b2 = b1 + 1
                    proj(CG_AG, b1, t0, 512)
                    ACT(gg[sl][:, :], bank(b1), AF.Gelu_apprx_tanh, [PSB[b1]], [gg[sl]])
                    proj(CG_GA, b2, t0, 512)
                    ACT(tg[sl][:, :], bank(b2), AF.Tanh, [PSB[b2]], [tg[sl]], scale=0.5)
                    TT("dve", gg[sl][:, :], gg[sl][:, :], macc[:, t0:t0 + 512], ALU.mult, [gg[sl], macc_b[c]], [gg[sl]])
                    STT(macc[:, t0:t0 + 512], tg[sl][:, :], 1.0, gg[sl][:, :], ALU.add, ALU.mult, [tg[sl], gg[sl]], [macc_b[c]])
                S.barrier()

                AR.reset(g_mark)
                gu = [AR.alloc("gu%d" % i, [128, 512], F32) for i in range(2)]
                tgs = [AR.alloc("tgs%d" % i, [128, 512], F32) for i in range(2)]
                m1 = [AR.alloc("m1_%d" % i, [128, 512], F32) for i in range(2)]
                for c in range(8):
                    sl = c % 2
                    t0 = c * 512
                    bu, bg, bm = 0 + sl, 2 + sl, 4 + sl
                    proj(CG_SU, bu, t0, 512)
                    ACT(gu[sl][:, :], bank(bu), AF.Gelu_apprx_tanh, [PSB[bu]], [gu[sl]])
                    proj(CG_GS, bg, t0, 512)
                    ACT(tgs[sl][:, :], bank(bg), AF.Tanh, [PSB[bg]], [tgs[sl]], scale=0.5)
                    for n in range(4):
                        MM(bank(bm)[:, n * 128:(n + 1) * 128], vlng[:, 4 * c + n, :], wspb[:, g, :], True, True, [vlng, wspb], [PSB[bm]])
                    TT("dve", m1[sl][:, :].rearrange("p (a b) -> p a b", a=4), bank(bm).rearrange("p (a b) -> p a b", a=4),
                       bspbc[:, g, :].unsqueeze(1).broadcast_to([128, 4, 128]), ALU.add, [PSB[bm], bspbc], [m1[sl]])
                    TT("dve", m1[sl][:, :], m1[sl][:, :], gu[sl][:, :], ALU.mult, [m1[sl], gu[sl]], [m1[sl]])
                    STT(m1[sl][:, :], tgs[sl][:, :], 1.0, m1[sl][:, :], ALU.add, ALU.mult, [tgs[sl], m1[sl]], [m1[sl]])
                    TT("dve", macc[:, t0:t0 + 512], macc[:, t0:t0 + 512], m1[sl][:, :], ALU.add, [m1[sl], macc_b[c]], [macc_b[c]])
                S.barrier()

                AR.reset(g_mark)
                qT = AR.alloc("qT", [128, SEQ], BF16)
                kT = AR.alloc("kT", [128, SEQ], BF16)
                Vt = AR.alloc("Vt", [128, 32, 128], BF16)
                negmT = AR.alloc("negmT", [128, SEQ], BF16)
                ksum = AR.alloc("ksum", [128, 16], F32)
                kmT = AR.alloc("kmT", [128, 16], BF16)
                gsb = AR.alloc("gsb", [128, 32, 16], F32)
                mx8 = AR.alloc("mx8", [128, 32, 8], F32)
                thr = AR.alloc("thr", [128, 32], F32)
                negm = AR.alloc("negm", [128, 32, 16], BF16)
                NPT = 8
                PT = [AR.alloc("PT%d" % i, [128, 256], BF16) for i in range(NPT)]
                tgm = [AR.alloc("tgm%d" % i, [128, 256], F32) for i in range(2)]
                rec = [AR.alloc("rec%d" % i, [128, 256], F32) for i in range(2)]
                ot = [AR.alloc("ot%d" % i, [128, 256], F32) for i in range(2)]
                mrgb = [AR.alloc("mrgb%d" % i, [128, 1024], BF16) for i in range(2)]
                MEMSET("dve", negmT[:, :], 0.0, [negmT])
                for c in range(8):
                    t0 = c * 512
                    bq, bk_ = 0 + (c % 2), 2 + (c % 2)
                    proj(CG_K, bk_, t0, 512)
                    for h in range(2):
                        ACT(kT[:, t0 + h * 256:t0 + (h + 1) * 256], bank(bk_)[:, h * 256:(h + 1) * 256], AF.Copy, [PSB[bk_]], [kT, ksum],
                            accum=ksum[:, 2 * c + h:2 * c + h + 1])
                    proj(CG_Q, bq, t0, 512)
                    CP("dve", qT[:, t0:t0 + 512], bank(bq), [PSB[bq]], [qT])
                TS("dve", kmT[:, :], ksum[:, :], 1.0 / 256.0, None, ALU.mult, None, [ksum], [kmT])
                bgt = 6
                for qt in range(32):
                    MM(bank(bgt)[:, qt * 16:(qt + 1) * 16], qT[:, qt * 128:(qt + 1) * 128], kmT[:, :], True, True, [qT, kmT], [PSB[bgt]])
                TT("dve", gsb[:, :, :].rearrange("p a b -> p (a b)"), bank(bgt), cb[:, :, :].rearrange("p a b -> p (a b)"), ALU.add,
                   [PSB[bgt], cb], [gsb])
                for qt in range(32):
                    S.op("dve", (lambda qt: lambda e: e.max(out=mx8[:, qt, :], in_=gsb[:, qt, :]))(qt), [gsb], [mx8])
                TS("dve", thr[:, :], mx8[:, :, 2], -1e29, None, ALU.max, None, [mx8], [thr])
                TT("dve", negm[:, :, :], gsb[:, :, :], thr[:, :].unsqueeze(2).broadcast_to([128, 32, 16]), ALU.is_lt, [gsb, thr], [negm])
                TS("dve", negm[:, :, :], negm[:, :, :], NEGBIG, None, ALU.mult, None, [negm], [negm])
                for t4 in range(8):
                    bv = 4 + (t4 % 2)
                    for j in range(4):
                        tt = t4 * 4 + j
                        for kc in range(8):
                            MM(bank(bv)[:, j * 128:(j + 1) * 128], xT[:, kc, tt * 128:(tt + 1) * 128], wt[:, CG_V, kc, :], kc == 0, kc == 7,
                               [xT_b[tt], wt], [PSB[bv]])
                    CP("act", Vt[:, t4 * 4:(t4 + 1) * 4, :], bank(bv).rearrange("p (a b) -> p a b", a=4),
                       [PSB[bv]], [Vt])
                for q8 in range(4):
                    bt = 6 + ((q8 + 1) % 2)
                    tb = bank(bt).bitcast(BF16)
                    for j in range(8):
                        qt = q8 * 8 + j
                        TR(tb[0:16, j * 128:(j + 1) * 128], negm[:, qt, :], ident_b[:, :], [negm, ident_b], [PSB[bt]], signal=(j == 7))
                    CP("act", negmT[0:16, q8 * 1024:(q8 + 1) * 1024], tb[0:16, :], [PSB[bt]], [negmT])
                flat = []
                for b in range(16):
                    tl_ = [("past", kt) for kt in range(2 * b)] + [("own0", 2 * b), ("own1", 2 * b + 1)]
                    for i, (kind, kt) in enumerate(tl_):
                        flat.append((b, kind, kt, i == 0, i == len(tl_) - 1))
                nfl = len(flat)

                RING = [0, 1, 2, 5]
                ring_ctr = [0]

                def emit_score(i):
                    b, kind, kt, first, lastt = flat[i]
                    q0 = b * 256
                    rb = RING[ring_ctr[0] % 4]
                    ring_ctr[0] += 1
                    stv = bank(rb)[:, 0:256]
                    sb = PSB[rb]
                    P = PT[i % NPT]
                    if kind == "past":
                        j = kt // 2
                        MM(stv, kT[:, kt * 128:(kt + 1) * 128], qT[:, q0:q0 + 256], True, False, [kT, qT], [sb])
                        MM(stv, sel[:, j, :], negmT[:, q0:q0 + 256], False, True, [sel, negmT], [sb])
                        ACT(P[:, 0:256], stv, AF.Exp, [sb], [P], scale=ATT_SCALE)
                    elif kind == "own0":
                        MM(stv, kT[:, kt * 128:(kt + 1) * 128], qT[:, q0:q0 + 256], True, True, [kT, qT], [sb])
                        ACT(P[:, 0:256], stv, AF.Exp, [sb], [P], scale=ATT_SCALE)
                        TT("pool", P[:, 0:128], P[:, 0:128], triu_b[:, :], ALU.mult, [P, triu_b], [P])
                    else:
                        MM(stv[:, 0:128], kT[:, kt * 128:(kt + 1) * 128], qT[:, q0 + 128:q0 + 256], True, True, [kT, qT], [sb])
                        ACT(P[:, 0:128], stv[:, 0:128], AF.Exp, [sb], [P], scale=ATT_SCALE)
                        TT("pool", P[:, 0:128], P[:, 0:128], triu_b[:, :], ALU.mult, [P, triu_b], [P])

                def emit_pv(i):
                    b, kind, kt, first, lastt = flat[i]
                    q0 = b * 256
                    P = PT[i % NPT]
                    bo = 3 if b % 2 == 0 else 6
                    bl = 4 if b % 2 == 0 else 7
                    if kind == "own1":
                        MM(bank(bo)[:, 128:256], Vt[:, kt, :], P[:, 0:128], False, True, [Vt, P], [PSB[bo]])
                        MM(bank(bl)[:, 128:256], ones_b[:, :], P[:, 0:128], False, True, [ones_b, P], [PSB[bl]])
                    else:
                        MM(bank(bo)[:, 0:256], Vt[:, kt, :], P[:, 0:256], first, False, [Vt, P], [PSB[bo]])
                        MM(bank(bl)[:, 0:256], ones_b[:, :], P[:, 0:256], first, False, [ones_b, P], [PSB[bl]])
                    if lastt:
                        sl = b % 2
                        bgm = RING[ring_ctr[0] % 4]
                        ring_ctr[0] += 1
                        proj(CG_GM, bgm, q0, 256)
                        ACT(tgm[sl][:, :], bank(bgm)[:, 0:256], AF.Tanh, [PSB[bgm]], [tgm[sl]], scale=0.5)
                        S.op("dve", lambda e: e.reciprocal(out=rec[sl][:, :], in_=bank(bl)[:, 0:256]), [PSB[bl]], [rec[sl]])
                        TT("dve", ot[sl][:, :], bank(bo)[:, 0:256], rec[sl][:, :], ALU.mult, [PSB[bo], rec[sl]], [ot[sl]])
                        STT(ot[sl][:, :], tgm[sl][:, :], 1.0, ot[sl][:, :], ALU.add, ALU.mult, [tgm[sl], ot[sl]], [ot[sl]])
                        mb = macc_b[b // 2]
                        TT("dve", macc[:, q0:q0 + 256], macc[:, q0:q0 + 256], ot[sl][:, :], ALU.add, [ot[sl], mb], [mb])

                LOOK = 3
                for i in range(min(LOOK, nfl)):
                    emit_score(i)
                for i in range(nfl):
                    if i + LOOK < nfl:
                        emit_score(i + LOOK)
                    emit_pv(i)
                for c4 in range(4):
                    mt = mrgb[c4 % 2]
                    ACT(mt[:, :], macc[:, c4 * 1024:(c4 + 1) * 1024], AF.Copy, [macc_b[2 * c4], macc_b[2 * c4 + 1]], [mt], scale=0.5)
                    S.dma("sp", mrg_d[:, g, c4 * 1024:(c4 + 1) * 1024], mt[:, :], [mt], [mrg_b[g]])
                S.barrier()

            if dbg and l == 0:
                S.dma("sp", dbg_out["d_mrg_0"][:, :, :], mrg_d[:, :, :], mrg_b, [Buf("dbg")])
            if dbg and l == n_layers - 1:
                S.dma("sp", dbg_out["d_mrg"][:, :, :], mrg_d[:, :, :], mrg_b, [Buf("dbg")])
                S.dma("sp", dbg_out["d_vln"][:, :], vln_d[:, :], vln_b, [Buf("dbg")])

            S.barrier()
            AR.reset(pers_mark)
            wup = AR.alloc("wup", [128, NFC, 2, 8, 128], BF16)
            f_mark = AR.mark()
            woutb = AR.alloc("woutb", [128, 8, 1024], BF16)
            g1 = AR.alloc("g1", [128, 1024], F32)
            b1 = AR.alloc("b1", [128, 1024], F32)
            mt_ = [AR.alloc("mt%d" % i, [128, 8, 512], BF16) for i in range(2)]
            x1Tg = [AR.alloc("x1Tg%d" % i, [128, 8, 512], BF16) for i in range(2)]
            xr = [AR.alloc("xr%d" % i, [128, 1024], F32) for i in range(4)]
            zt = [AR.alloc("zt%d" % i, [128, 1024], F32) for i in range(2)]
            x1t = [AR.alloc("x1t%d" % i, [128, 1024], F32) for i in range(2)]
            stats = [AR.alloc("stats%d" % i, [128, 2, 6], F32) for i in range(2)]
            mv = [AR.alloc("mv%d" % i, [128, 2], F32) for i in range(2)]
            rs = [AR.alloc("rs%d" % i, [128, 2], F32) for i in range(2)]
            for kc in range(8):
                S.dma("pool", woutb[:, kc, :], w_outr[l][:, kc * 1024:(kc + 1) * 1024], [], [woutb])
            wup_next = [0]

            def load_wup_one():
                fc = wup_next[0]
                if fc >= NFC:
                    return
                wup_next[0] += 1
                S.dma("pool", wup[:, fc, :, :, :].rearrange("p a b c -> p a (b c)"),
                      w_upr[l][:, fc * 2048:(fc + 1) * 2048].rearrange("p (a c) -> p a c", c=1024), [], [wup])
            S.dma("sp", g1[:, :], tokv_d[l][2].partition_broadcast(128), [], [g1])
            S.dma("sp", b1[:, :], tokv_d[l][3].partition_broadcast(128), [], [b1])
            def out_a_pe(tt):
                c, j = tt // 4, tt % 4
                m = mt_[c % 2]
                sl = tt % 2
                pp = sl
                if j == 0:
                    S.dma("sp", m[:, :, :], mrg_d[:, :, c * 512:(c + 1) * 512], mrg_b, [m])
                xs_ = xr[tt % 4]
                S.dma("sp", xs_[:, :], xin_d[tt * 128:(tt + 1) * 128, :], [xin_b[tt]], [xs_])
                for h in range(2):
                    for kc in range(8):
                        MM(pair(pp)[:, h * 512:(h + 1) * 512], m[:, kc, j * 128:(j + 1) * 128], woutb[:, kc, h * 512:(h + 1) * 512],
                           kc == 0, kc == 7, [m, woutb], [PSB[2 * pp + h]])

            def out_a_dve(tt):
                sl = tt % 2
                pp = sl
                xs_ = xr[tt % 4]
                STT(zt[sl][:, :], xs_[:, :], ALPHA, pair(pp), ALU.mult, ALU.add, [xs_, PSB[2 * pp], PSB[2 * pp + 1]], [zt[sl]])
                ln_a("act", zt[sl], stats[sl], mv[sl], rs[sl])

            def out_b(tt):
                c, j = tt // 4, tt % 4
                xg = x1Tg[c % 2]
                sl = tt % 2
                ln_b(zt[sl], mv[sl], rs[sl], g1, b1, x1t[sl][:, :], [x1t[sl]])
                S.dma("sp", xres1[tt * 128:(tt + 1) * 128, :], x1t[sl][:, :], [x1t[sl]], [xres1_b[tt]])
                if dbg and l == 0:
                    S.dma("sp", dbg_out["d_x1_0"][tt * 128:(tt + 1) * 128, :], x1t[sl][:, :], [x1t[sl]], [Buf("dbg")])
                if dbg and l == n_layers - 1:
                    S.dma("sp", dbg_out["d_x1"][tt * 128:(tt + 1) * 128, :], x1t[sl][:, :], [x1t[sl]], [Buf("dbg")])
                pt = 2 + sl
                for kc in range(8):
                    TR(pair(pt)[:, kc * 128:(kc + 1) * 128], x1t[sl][:, kc * 128:(kc + 1) * 128], ident_f[:, :], [x1t[sl], ident_f],
                       [PSB[2 * pt], PSB[2 * pt + 1]], signal=(kc == 7))
                CP("act", xg[:, :, j * 128:(j + 1) * 128], pair(pt).rearrange("p (a b) -> p a b", a=8), [PSB[2 * pt], PSB[2 * pt + 1]], [xg])
                if j == 3:
                    S.dma("sp", x1T_d[:, :, c * 512:(c + 1) * 512], xg[:, :, :], [xg], [x1T_b[c]])

            out_a_pe(0)
            for tt in range(NT + 1):
                if tt + 1 < NT:
                    out_a_pe(tt + 1)
                if tt < NT:
                    out_a_dve(tt)
                if tt >= 1:
                    out_b(tt - 1)
                load_wup_one()
            while wup_next[0] < NFC:
                load_wup_one()
            S.barrier()

            AR.reset(f_mark)
            g2 = AR.alloc("g2", [128, 1024], F32)
            b2 = AR.alloc("b2", [128, 1024], F32)
            wdn = [AR.alloc("wdn%d" % i, [128, 4, 1024], BF16) for i in range(2)]
            xg_ = [AR.alloc("xg%d" % i, [128, 8, 256], BF16) for i in range(2)]
            xr1 = [AR.alloc("xr1_%d" % i, [128, 2, 1024], F32) for i in range(2)]
            actT = AR.alloc("actT", [128, NFC, 256], BF16)
            actT_b = [Buf("actT%d" % i) for i in range(NFC)]
            hgp = [AR.alloc("hgp%d" % i, [128, 260], F32) for i in range(2)]
            cf = [AR.alloc("cf%d" % i, [128, 256], F32) for i in range(2)]
            gl = [AR.alloc("gl%d" % i, [128, 256], F32) for i in range(2)]
            carry = AR.alloc("carry", [128, NFC, 2], F32)
            zt = [AR.alloc("zt%d" % i, [128, 1024], F32) for i in range(2)]
            x2t = [AR.alloc("x2t%d" % i, [128, 1024], F32) for i in range(2)]
            x2Tg = [AR.alloc("x2Tg%d" % i, [128, 8, 512], BF16) for i in range(1)]
            stats = [AR.alloc("stats%d" % i, [128, 2, 6], F32) for i in range(2)]
            mv = [AR.alloc("mv%d" % i, [128, 2], F32) for i in range(2)]
            rs = [AR.alloc("rs%d" % i, [128, 2], F32) for i in range(2)]
            S.dma("sp", g2[:, :], tokv_d[l][4].partition_broadcast(128), [], [g2])
            S.dma("sp", b2[:, :], tokv_d[l][5].partition_broadcast(128), [], [b2])
            MEMSET("pool", carry[:, :, :], 0.0, [carry])
            wd_rr = 0
            pending_tr = []
            for gi in range(16):
                t0 = gi * 256
                xg = xg_[gi % 2]
                x1r = xr1[gi % 2]
                S.dma("sp", xg[:, :, :], x1T_d[:, :, t0:t0 + 256], [x1T_b[gi // 2]], [xg])
                S.dma("sp", x1r[:, :, :], xres1[t0:t0 + 256, :].rearrange("(a p) d -> p a d", p=128),
                      [xres1_b[2 * gi], xres1_b[2 * gi + 1]], [x1r])
                def up(fc):
                    sl = fc % 2
                    bg_, bu_ = 4 + sl, 6 + sl
                    if fc % 4 == 0:
                        ch = fc // 4
                        wd = wdn[ch % 2]
                        S.dma("sp", wd[:, :, :].rearrange("p a b -> p (a b)"), wdn_bf[l][:, ch * 4096:(ch + 1) * 4096], [wdn_buf[l]], [wd])
                    for kc in range(8):
                        MM(bank(bg_)[:, 0:256], wup[:, fc, 0, kc, :], xg[:, kc, :], kc == 0, kc == 7, [wup, xg], [PSB[bg_]])
                    for kc in range(8):
                        MM(bank(bu_)[:, 0:256], wup[:, fc, 1, kc, :], xg[:, kc, :], kc == 0, kc == 7, [wup, xg], [PSB[bu_]])
                    CP("pool", hgp[sl][:, 0:2], carry[:, fc, :], [carry], [hgp[sl]])
                    CP("act", hgp[sl][:, 2:258], bank(bg_)[:, 0:256], [PSB[bg_]], [hgp[sl]])
                    CP("pool", carry[:, fc, :], hgp[sl][:, 256:258], [hgp[sl]], [carry])

                    def fw(k):
                        return chv[:, cv + C_FW + k * NFC + fc:cv + C_FW + k * NFC + fc + 1]
                    ACT(cf[sl][:, :], bank(bg_)[:, 0:256], AF.Identity, [PSB[bg_], chv], [cf[sl]], scale=fw(2),
                        bias=chv[:, cv + C_FB + fc:cv + C_FB + fc + 1])
                    STT(cf[sl][:, :], hgp[sl][:, 1:257], fw(1), cf[sl][:, :], ALU.mult, ALU.add, [hgp[sl], cf[sl], chv], [cf[sl]])
                    STT(cf[sl][:, :], hgp[sl][:, 0:256], fw(0), cf[sl][:, :], ALU.mult, ALU.add, [hgp[sl], cf[sl], chv], [cf[sl]])
                    ACT(gl[sl][:, :], cf[sl][:, :], AF.Gelu_apprx_tanh, [cf[sl]], [gl[sl]])
                    TT("dve", actT[:, fc, :], gl[sl][:, :], bank(bu_)[:, 0:256], ALU.mult, [gl[sl], PSB[bu_]], [actT_b[fc]])

                def down(fc):
                    wd = wdn[(fc // 4) % 2]
                    f6 = fc % 4
                    for tl in range(2):
                        for h in range(2):
                            MM(pair(tl)[:, h * 512:(h + 1) * 512], actT[:, fc, tl * 128:(tl + 1) * 128], wd[:, f6, h * 512:(h + 1) * 512],
                               fc == 0, fc == NFC - 1, [actT_b[fc], wd], [PSB[2 * tl + h]], sig=(f6 == 3 and tl == 1 and h == 1))

                for fc in range(NFC + 2):
                    if fc < NFC:
                        up(fc)
                    if fc >= 2:
                        down(fc - 2)
                    if fc == 4 and pending_tr:
                        pending_tr.pop(0)()
                for tl in range(2):
                    tt = gi * 2 + tl
                    sl = tt % 2
                    STT(zt[sl][:, :], x1r[:, tl, :], ALPHA, pair(tl), ALU.mult, ALU.add, [x1r, PSB[2 * tl], PSB[2 * tl + 1]], [zt[sl]])
                    ln_a("pool", zt[sl], stats[sl], mv[sl], rs[sl])
                for tl in range(2):
                    tt = gi * 2 + tl
                    sl = tt % 2
                    ln_b(zt[sl], mv[sl], rs[sl], g2, b2, x2t[sl][:, :], [x2t[sl]])
                    if last:
                        S.dma("sp", y_d[tt * 128:(tt + 1) * 128, :], x2t[sl][:, :], [x2t[sl]], [y_b[tt]])
                    else:
                        S.dma("sp", xres2[tt * 128:(tt + 1) * 128, :], x2t[sl][:, :], [x2t[sl]], [xres2_b[tt]])

                def make_tr(gi):
                    def tr_group():
                        for tl in range(2):
                            tt = gi * 2 + tl
                            sl = tt % 2
                            xg2 = x2Tg[0]
                            pt = 2 + sl
                            for kc in range(8):
                                TR(pair(pt)[:, kc * 128:(kc + 1) * 128], x2t[sl][:, kc * 128:(kc + 1) * 128], ident_f[:, :], [x2t[sl], ident_f],
                                   [PSB[2 * pt], PSB[2 * pt + 1]], signal=(kc == 7))
                            CP("act", xg2[:, :, (tt % 4) * 128:(tt % 4 + 1) * 128], pair(pt).rearrange("p (a b) -> p a b", a=8),
                               [PSB[2 * pt], PSB[2 * pt + 1]], [xg2])
                            if tt % 4 == 3:
                                c = tt // 4
                                S.dma("sp", xT_d[:, :, c * 512:(c + 1) * 512], xg2[:, :, :], [xg2], [xTd_b[c]])
                    return tr_group
                if not last:
                    pending_tr.append(make_tr(gi))
            while pending_tr:
                pending_tr.pop(0)()
            if dbg and not last:
                o = nc.dram_tensor("d_x2", [SEQ, D], F32, kind="ExternalOutput").ap()
                for q4 in range(4):
                    S.dma("sp", o[q4 * 1024:(q4 + 1) * 1024, :], xres2[q4 * 1024:(q4 + 1) * 1024, :], xres2_b, [Buf("dbg")])
                o = nc.dram_tensor("d_xTd", [128, 8, SEQ], BF16, kind="ExternalOutput").ap()
                for q4 in range(8):
                    S.dma("sp", o[:, q4, :], xT_d[:, q4, :], xTd_b, [Buf("dbg")])
            S.barrier()

        S.barrier()
        S.replay()
    return nc


def _prep_weights(inp):
    f = np.float32
    w_in = np.asarray(inp["w_in"], f)
    L = w_in.shape[0]
    cgs = [0, 1, 2, 4, 5, 6, 7, 8, 9]
    w6 = w_in.reshape(L, 8, 128, 10, 8, 128)
    w_in_g = np.ascontiguousarray(w6[:, :, :, cgs, :, :].transpose(0, 4, 2, 3, 1, 5)).reshape(L, 8, 128, 9 * 1024)
    w_sv = np.ascontiguousarray(w6[:, :, :, 3, :, :].transpose(0, 2, 1, 3, 4)).reshape(L, 128, 8 * 1024)
    w_out = np.asarray(inp["w_out"], f).reshape(L, 8, 128, 1024)
    w_outr = np.ascontiguousarray(w_out.transpose(0, 2, 1, 3)).reshape(L, 128, 8 * 1024)
    w_up = np.asarray(inp["w_ffn_up"], f).reshape(L, 8, 128, 2, NFC, 128)
    w_upr = np.ascontiguousarray(w_up.transpose(0, 2, 4, 3, 1, 5)).reshape(L, 128, NFC * 2048)
    w_dn = np.asarray(inp["w_ffn_down"], f).reshape(L, NFC, 128, 1024)
    w_dnr = np.ascontiguousarray(w_dn.transpose(0, 2, 1, 3)).reshape(L, 128, NFC * 1024)
    wr = np.asarray(inp["w_rgate"], f)
    wi = np.asarray(inp["w_igate"], f)
    rgw = np.ascontiguousarray(np.stack([wr, wi], axis=2).transpose(0, 3, 1, 2, 4)).reshape(L, 128, 2048)
    wsp = np.asarray(inp["w_spatial"], f)
    wspT = np.ascontiguousarray(wsp.transpose(0, 3, 1, 2)).reshape(L, 128, 1024)
    chv = np.zeros((128, L * NCH), f)

    def pc(v, n):
        return np.asarray(v, f).reshape(n, 128).T

    for l in range(L):
        o = l * NCH
        for k in range(4):
            chv[:, o + C_CW + k * 8:o + C_CW + (k + 1) * 8] = pc(inp["conv_rg_w"][l][k], 8)
        chv[:, o + C_CB:o + C_CB + 8] = pc(inp["conv_rg_b"][l], 8)
        chv[:, o + C_BR:o + C_BR + 8] = pc(inp["b_rgate"][l], 8)
        chv[:, o + C_BI:o + C_BI + 8] = pc(inp["b_igate"][l], 8)
        chv[:, o + C_LAM:o + C_LAM + 8] = pc(inp["lru_lambda"][l], 8)
        for k in range(3):
            chv[:, o + C_FW + k * NFC:o + C_FW + (k + 1) * NFC] = pc(inp["conv_ffn_w"][l][k], NFC)
        chv[:, o + C_FB:o + C_FB + NFC] = pc(inp["conv_ffn_b"][l], NFC)
    bsp = np.ascontiguousarray(np.asarray(inp["b_spatial"], f).reshape(L, 1024))
    tokv = np.ascontiguousarray(np.stack([np.asarray(inp[k], f) for k in
                                          ("sgu_ln_g", "sgu_ln_b", "ln_mix_g", "ln_mix_b", "ln_ffn_g", "ln_ffn_b")], axis=1))
    return dict(w_in_g=w_in_g, w_sv=w_sv, w_outr=w_outr, w_upr=w_upr, w_dnr=w_dnr, rgw=rgw, wspT=wspT, chv=chv, bsp=bsp, tokv=tokv)


_CACHE = {}


def kernel(**inputs):
    x = np.asarray(inputs["x"], np.float32)
    B = x.shape[0]
    wts = _prep_weights(inputs)
    if "nc" not in _CACHE:
        _CACHE["nc"] = build_program()
    nc = _CACHE["nc"]
    in_maps = []
    for b in range(B):
        m = {"x": np.ascontiguousarray(x[b])}
        m.update(wts)
        in_maps.append(m)
    res = run_bass_kernel_spmd(nc, in_maps, core_ids=list(range(B)))
    return np.stack([np.asarray(r["y"], np.float32) for r in res.results], axis=0)
```
